# Optimizing a Trainium2 kernel written in Bass

```python
import math
import jax
import jax.numpy as jnp
from jax import lax
import numpy as np

D_MODEL = 1024
BATCH = 8
SEQ = 2048
DEPTH = 1
DEC_BATCH = 128
DEC_SEQ = 4
PAST_LEN = 2048
PAGE_SIZE = 128

HEAD_DIM = 64
N_ATTN_HEADS = D_MODEL // (2 * HEAD_DIM)
N_GDN_HEADS = D_MODEL // (2 * HEAD_DIM)
ATTN_WIDTH = N_ATTN_HEADS * HEAD_DIM
GDN_WIDTH = N_GDN_HEADS * HEAD_DIM
DILATED_PATTERNS = ((128, 1), (512, 4), (2048, 16))
MAX_WINDOW = 2048
T5_BUCKETS = 32
T5_MAX_EXACT = T5_BUCKETS // 2
T5_MAX_DIST = MAX_WINDOW
CONV_WIDTH = 4
GDN_CHUNK = 64
D_FF = 2816
IN_COLS = 3 * ATTN_WIDTH + 3 * GDN_WIDTH + GDN_WIDTH + 2 * N_GDN_HEADS
LN_EPS = 1e-5
RMS_EPS = 1e-6

kernel_name = 'hymba_dilated_gdn_macaron_deepnorm_step'


def _layernorm(x, g, b):
    xf = x.astype(jnp.float32)
    mu = jnp.mean(xf, axis=-1, keepdims=True)
    xc = xf - mu
    var = jnp.mean(xc * xc, axis=-1, keepdims=True)
    return (xc * lax.rsqrt(var + LN_EPS) * g.astype(jnp.float32) + b.astype(jnp.float32)).astype(x.dtype)


def _swiglu(x, w_gate, w_up, w_down):
    return (jax.nn.silu(x @ w_gate) * (x @ w_up)) @ w_down


def _l2norm(x):
    return x * lax.rsqrt(jnp.sum(x * x, axis=-1, keepdims=True) + RMS_EPS)


def _t5_bias(dist, rel_bias):
    n = jnp.maximum(dist, 0)
    nf = jnp.maximum(n, 1).astype(jnp.float32)
    large = T5_MAX_EXACT + (jnp.log(nf / T5_MAX_EXACT) / math.log(T5_MAX_DIST / T5_MAX_EXACT)
                            * (T5_BUCKETS - T5_MAX_EXACT)).astype(jnp.int32)
    large = jnp.minimum(large, T5_BUCKETS - 1)
    bucket = jnp.where(n < T5_MAX_EXACT, n, large)
    return rel_bias[bucket].astype(jnp.float32)


def _split_projection(h):
    B, L, _ = h.shape
    sizes = [ATTN_WIDTH, ATTN_WIDTH, ATTN_WIDTH, 3 * GDN_WIDTH, GDN_WIDTH, N_GDN_HEADS, N_GDN_HEADS]
    offs = [int(o) for o in np.cumsum(sizes)[:-1]]
    aq, ak, av, g_qkv, z, b, a = jnp.split(h, offs, axis=-1)
    heads = lambda t: t.reshape(B, L, N_ATTN_HEADS, HEAD_DIM)
    return heads(aq), heads(ak), heads(av), g_qkv, z, b, a


def _dilated_prompt(q, k, v, rel_bias, window, dilation):
    B, S, H, E = q.shape
    nb = window // dilation
    L = S // dilation
    nblk = -(-L // nb)
    lp = nblk * nb

    def classes(t):
        t = t.astype(jnp.float32).reshape(B, L, dilation, H, E).transpose(0, 2, 1, 3, 4)
        t = jnp.pad(t, ((0, 0), (0, 0), (0, lp - L), (0, 0), (0, 0)))
        return t.reshape(B, dilation, nblk, nb, H, E)

    qc = classes(q) * HEAD_DIM ** -0.5
    kc = classes(k)
    vc = classes(v)
    prev = lambda t: jnp.pad(t, ((0, 0), (0, 0), (1, 0), (0, 0), (0, 0), (0, 0)))[:, :, :-1]
    kk = jnp.concatenate([prev(kc), kc], axis=3)
    vv = jnp.concatenate([prev(vc), vc], axis=3)
    logits = jnp.einsum('brnqhe,brnkhe->brnhqk', qc, kk)
    qi = jnp.arange(nb)[:, None]
    ki = jnp.arange(2 * nb)[None, :]
    dist = qi + nb - ki
    key_cls = jnp.arange(nblk)[:, None, None] * nb - nb + ki[None]
    valid = (dist >= 0) & (dist <= nb) & (key_cls >= 0)
    bias = _t5_bias(dist * dilation, rel_bias).transpose(2, 0, 1)
    logits = jnp.where(valid[:, None], logits + bias, -jnp.inf)
    m = jnp.max(logits, axis=-1, keepdims=True)
    p = jnp.exp(logits - m)
    s = jnp.sum(p, axis=-1)
    o = jnp.einsum('brnhqk,brnkhe->brnqhe', p, vv) / s.transpose(0, 1, 2, 4, 3)[..., None]
    lse = (m[..., 0] + jnp.log(s)).transpose(0, 1, 2, 4, 3)

    def unclass(t):
        t = t.reshape((B, dilation, lp) + t.shape[4:])[:, :, :L]
        t = jnp.moveaxis(t, 1, 2)
        return t.reshape((B, S) + t.shape[3:])

    return unclass(o), unclass(lse)


def _dilated_sample(q, k_all, v_all, rel_bias, window, dilation):
    T = q.shape[1]
    w_past = k_all.shape[1] - T
    offs = jnp.arange(window // dilation + 1) * dilation
    idx = w_past + jnp.arange(T)[:, None] - offs[None, :]
    valid = idx >= 0
    idx = jnp.maximum(idx, 0)
    kg = k_all[:, idx].astype(jnp.float32)
    vg = v_all[:, idx].astype(jnp.float32)
    logits = jnp.einsum('bthe,btkhe->bthk', q.astype(jnp.float32) * HEAD_DIM ** -0.5, kg)
    logits = logits + _t5_bias(offs, rel_bias).T
    logits = jnp.where(valid[None, :, None, :], logits, -jnp.inf)
    m = jnp.max(logits, axis=-1, keepdims=True)
    p = jnp.exp(logits - m)
    s = jnp.sum(p, axis=-1)
    o = jnp.einsum('bthk,btkhe->bthe', p, vg) / s[..., None]
    return o, m[..., 0] + jnp.log(s)


def _merge_dilations(outs, lses):
    w = jax.nn.softmax(jnp.stack(lses), axis=0)
    return jnp.sum(w[..., None] * jnp.stack(outs), axis=0)


def _gdn_chunked(q, k, v, g, beta, s0, chunk):
    B, L, H, DK = q.shape
    DV = v.shape[-1]
    n = -(-L // chunk)
    pad = n * chunk - L
    padt = lambda t: jnp.pad(t, ((0, 0), (0, pad)) + ((0, 0),) * (t.ndim - 2))
    q, k, v, g, beta = [padt(t) for t in (q, k, v, g, beta)]
    ch4 = lambda t: t.reshape(B, n, chunk, H, t.shape[-1]).transpose(1, 0, 3, 2, 4)
    ch3 = lambda t: t.reshape(B, n, chunk, H).transpose(1, 0, 3, 2)
    q, k, v = ch4(q), ch4(k), ch4(v)
    g, beta = ch3(g), ch3(beta)
    gc = jnp.cumsum(g, axis=-1)
    eye = jnp.eye(chunk, dtype=jnp.float32)
    tril = jnp.tril(jnp.ones((chunk, chunk), dtype=bool))
    strict = tril & ~jnp.eye(chunk, dtype=bool)
    diff = gc[..., :, None] - gc[..., None, :]
    decay = jnp.where(tril, jnp.exp(jnp.where(tril, diff, 0.0)), 0.0)
    kb = k * beta[..., None]
    a = jnp.where(strict, jnp.einsum('nbhcd,nbhsd->nbhcs', kb, k) * decay, 0.0)
    t_inv = lax.linalg.triangular_solve(eye + a, jnp.broadcast_to(eye, a.shape),
                                        left_side=True, lower=True, unit_diagonal=True)
    u = jnp.einsum('nbhcs,nbhse->nbhce', t_inv, v * beta[..., None])
    w = jnp.einsum('nbhcs,nbhsd->nbhcd', t_inv, kb * jnp.exp(gc)[..., None])
    intra = jnp.where(tril, jnp.einsum('nbhcd,nbhsd->nbhcs', q, k) * decay, 0.0)
    qd = q * jnp.exp(gc)[..., None]
    kd = k * jnp.exp(gc[..., -1:] - gc)[..., None]
    glast = jnp.exp(gc[..., -1])

    def step(s, xs):
        u_i, w_i, intra_i, qd_i, kd_i, gl_i = xs
        v_new = u_i - jnp.einsum('bhcd,bhde->bhce', w_i, s)
        o_i = jnp.einsum('bhcd,bhde->bhce', qd_i, s) + jnp.einsum('bhcs,bhse->bhce', intra_i, v_new)
        s = s * gl_i[..., None, None] + jnp.einsum('bhcd,bhce->bhde', kd_i, v_new)
        return s, o_i

    s_final, o = lax.scan(step, s0, (u, w, intra, qd, kd, glast))
    o = o.transpose(1, 0, 3, 2, 4).reshape(B, n * chunk, H, DV)[:, :L]
    return o, s_final


def _gdn_branch(qkv_raw, z, b, a, conv_buf, s0, conv_w, a_log, dt_bias, norm_w):
    B, L, _ = qkv_raw.shape
    xp = jnp.concatenate([conv_buf.astype(qkv_raw.dtype), qkv_raw], axis=1)
    conv = sum(xp[:, j:j + L] * conv_w[j] for j in range(CONV_WIDTH))
    new_buf = xp[:, L:]
    qkv = jax.nn.silu(conv.astype(jnp.float32))
    q, k, v = [t.reshape(B, L, N_GDN_HEADS, HEAD_DIM) for t in jnp.split(qkv, 3, axis=-1)]
    q = _l2norm(q) * HEAD_DIM ** -0.5
    k = _l2norm(k)
    beta = jax.nn.sigmoid(b.astype(jnp.float32))
    g = -jnp.exp(a_log.astype(jnp.float32)) * jax.nn.softplus(a.astype(jnp.float32) + dt_bias.astype(jnp.float32))
    o, s_new = _gdn_chunked(q, k, v, g, beta, s0.astype(jnp.float32), min(GDN_CHUNK, L))
    o = o * lax.rsqrt(jnp.mean(o * o, axis=-1, keepdims=True) + RMS_EPS) * norm_w.astype(jnp.float32)
    o = o * jax.nn.silu(z.astype(jnp.float32).reshape(B, L, N_GDN_HEADS, HEAD_DIM))
    return o.reshape(B, L, GDN_WIDTH), s_new, new_buf


def _mixer_prompt(xn, w_in, rel_bias, conv_w, a_log, dt_bias, norm_w):
    B, L, _ = xn.shape
    aq, ak, av, g_qkv, z, b, a = _split_projection(xn @ w_in)
    res = [_dilated_prompt(aq, ak, av, rel_bias, wd, dl) for (wd, dl) in DILATED_PATTERNS]
    attn = _merge_dilations([r[0] for r in res], [r[1] for r in res])
    conv0 = jnp.zeros((B, CONV_WIDTH - 1, 3 * GDN_WIDTH), xn.dtype)
    s0 = jnp.zeros((B, N_GDN_HEADS, HEAD_DIM, HEAD_DIM), jnp.float32)
    gdn, s_new, conv_new = _gdn_branch(g_qkv, z, b, a, conv0, s0, conv_w, a_log, dt_bias, norm_w)
    heads = jnp.concatenate([attn.reshape(B, L, ATTN_WIDTH).astype(xn.dtype), gdn.astype(xn.dtype)], axis=-1)
    wp = min(MAX_WINDOW, L)
    return heads, (ak[:, L - wp:], av[:, L - wp:], s_new, conv_new)


def _mixer_sample(xn, k_past, v_past, s_past, conv_past, w_in, rel_bias, conv_w, a_log, dt_bias, norm_w):
    B, L, _ = xn.shape
    aq, ak, av, g_qkv, z, b, a = _split_projection(xn @ w_in)
    k_all = jnp.concatenate([k_past.astype(ak.dtype), ak], axis=1)
    v_all = jnp.concatenate([v_past.astype(av.dtype), av], axis=1)
    res = [_dilated_sample(aq, k_all, v_all, rel_bias, wd, dl) for (wd, dl) in DILATED_PATTERNS]
    attn = _merge_dilations([r[0] for r in res], [r[1] for r in res])
    gdn, s_new, conv_new = _gdn_branch(g_qkv, z, b, a, conv_past, s_past, conv_w, a_log, dt_bias, norm_w)
    heads = jnp.concatenate([attn.reshape(B, L, ATTN_WIDTH).astype(xn.dtype), gdn.astype(xn.dtype)], axis=-1)
    return heads, (ak, av, s_new, conv_new)


def _macaron_layer(x, mixer, alpha, ln1_g, ln1_b, ln2_g, ln2_b, ln3_g, ln3_b,
                   f1_gate, f1_up, f1_down, f2_gate, f2_up, f2_down, w_out):
    x = _layernorm(alpha * x + 0.5 * _swiglu(x, f1_gate, f1_up, f1_down), ln1_g, ln1_b)
    heads, states = mixer(x)
    x = _layernorm(alpha * x + heads @ w_out, ln2_g, ln2_b)
    x = _layernorm(alpha * x + 0.5 * _swiglu(x, f2_gate, f2_up, f2_down), ln3_g, ln3_b)
    return x, states


def setup_inputs(seed: int = 0) -> dict:
    key = jax.random.key(seed)
    ks = jax.random.split(key, 32)
    f32 = jnp.float32
    w_buf = min(MAX_WINDOW, PAST_LEN)
    out_scale = (8.0 * DEPTH) ** -0.25

    def dense(k, shape, fan_in, scale=1.0):
        return jax.random.normal(k, shape, f32) * (scale * fan_in ** -0.5)

    def gain(k, shape):
        return 1.0 + 0.05 * jax.random.normal(k, shape, f32)

    def small(k, shape):
        return 0.02 * jax.random.normal(k, shape, f32)

    dt = jnp.exp(jax.random.uniform(ks[24], (DEPTH, N_GDN_HEADS), f32, math.log(1e-3), math.log(1e-1)))
    return {
        'x_prompt': jax.random.normal(ks[0], (BATCH, SEQ, D_MODEL), f32),
        'x_sample': jax.random.normal(ks[1], (DEC_BATCH, DEC_SEQ, D_MODEL), f32),
        'cache_attn_k': jax.random.normal(ks[2], (DEPTH, DEC_BATCH, w_buf, N_ATTN_HEADS, HEAD_DIM), f32),
        'cache_attn_v': jax.random.normal(ks[3], (DEPTH, DEC_BATCH, w_buf, N_ATTN_HEADS, HEAD_DIM), f32),
        'state_gdn': 0.1 * jax.random.normal(ks[4], (DEPTH, DEC_BATCH, N_GDN_HEADS, HEAD_DIM, HEAD_DIM), f32),
        'state_conv': jax.random.normal(ks[5], (DEPTH, DEC_BATCH, CONV_WIDTH - 1, 3 * GDN_WIDTH), f32),
        'rel_bias': 0.5 * jax.random.normal(ks[6], (T5_BUCKETS, N_ATTN_HEADS), f32),
        'ln1_g': gain(ks[7], (DEPTH, D_MODEL)),
        'ln1_b': small(ks[8], (DEPTH, D_MODEL)),
        'ffn1_w_gate': dense(ks[9], (DEPTH, D_MODEL, D_FF), D_MODEL),
        'ffn1_w_up': dense(ks[10], (DEPTH, D_MODEL, D_FF), D_MODEL),
        'ffn1_w_down': dense(ks[11], (DEPTH, D_FF, D_MODEL), D_FF, out_scale),
        'w_in': dense(ks[12], (DEPTH, D_MODEL, IN_COLS), D_MODEL),
        'w_out': dense(ks[13], (DEPTH, D_MODEL, D_MODEL), D_MODEL, out_scale),
        'gdn_conv_w': dense(ks[14], (DEPTH, CONV_WIDTH, 3 * GDN_WIDTH), CONV_WIDTH),
        'gdn_a_log': jnp.log(jax.random.uniform(ks[15], (DEPTH, N_GDN_HEADS), f32, 1.0, 16.0)),
        'gdn_dt_bias': dt + jnp.log(-jnp.expm1(-dt)),
        'gdn_norm_w': gain(ks[16], (DEPTH, HEAD_DIM)),
        'ln2_g': gain(ks[17], (DEPTH, D_MODEL)),
        'ln2_b': small(ks[18], (DEPTH, D_MODEL)),
        'ffn2_w_gate': dense(ks[19], (DEPTH, D_MODEL, D_FF), D_MODEL),
        'ffn2_w_up': dense(ks[20], (DEPTH, D_MODEL, D_FF), D_MODEL),
        'ffn2_w_down': dense(ks[21], (DEPTH, D_FF, D_MODEL), D_FF, out_scale),
        'ln3_g': gain(ks[22], (DEPTH, D_MODEL)),
        'ln3_b': small(ks[23], (DEPTH, D_MODEL)),
    }


def reference(x_prompt, x_sample, cache_attn_k, cache_attn_v, state_gdn, state_conv, rel_bias,
              ln1_g, ln1_b, ffn1_w_gate, ffn1_w_up, ffn1_w_down, w_in, w_out,
              gdn_conv_w, gdn_a_log, gdn_dt_bias, gdn_norm_w, ln2_g, ln2_b,
              ffn2_w_gate, ffn2_w_up, ffn2_w_down, ln3_g, ln3_b):
    alpha = (2.0 * DEPTH) ** 0.25
    yp, ys = x_prompt, x_sample
    collected = [[] for _ in range(8)]
    for layer in range(DEPTH):
        common = (ln1_g[layer], ln1_b[layer], ln2_g[layer], ln2_b[layer], ln3_g[layer], ln3_b[layer],
                  ffn1_w_gate[layer], ffn1_w_up[layer], ffn1_w_down[layer],
                  ffn2_w_gate[layer], ffn2_w_up[layer], ffn2_w_down[layer], w_out[layer])
        mix_w = (w_in[layer], rel_bias, gdn_conv_w[layer], gdn_a_log[layer], gdn_dt_bias[layer], gdn_norm_w[layer])
        yp, st_p = _macaron_layer(yp, lambda xn: _mixer_prompt(xn, *mix_w), alpha, *common)
        ys, st_s = _macaron_layer(
            ys, lambda xn: _mixer_sample(xn, cache_attn_k[layer], cache_attn_v[layer],
                                         state_gdn[layer], state_conv[layer], *mix_w), alpha, *common)
        for lst, st in zip(collected, st_p + st_s):
            lst.append(st)
    k_p, v_p, sg_p, sc_p, k_s, v_s, sg_s, sc_s = [jnp.stack(t, axis=0) for t in collected]
    return (yp, ys, k_p, v_p, sg_p, sc_p, k_s, v_s, sg_s, sc_s)
```

```python
import contextlib
import math
import numpy as np
import concourse.bass as bass
import concourse.mybir as mybir
from concourse.bass_utils import run_bass_kernel_spmd

F32 = mybir.dt.float32
BF16 = mybir.dt.bfloat16
ALU = mybir.AluOpType
AF = mybir.ActivationFunctionType
AX = mybir.AxisListType

D = 1024
DFF = 2816
NFC = 22
NT = 17
NTOK = NT * 128
INC = 3600
ALPHA = 2.0 ** 0.25
LN_EPS = 1e-5
NEG = -30000.0


def sl(start, count, step=1):
    return slice(start, start + (count - 1) * step + 1, step)


class Sched:
    def __init__(self, nc, stack, ndma=6):
        self.nc = nc
        self.eng = {"pe": nc.tensor, "act": nc.scalar, "dve": nc.vector, "pool": nc.gpsimd, "sp": nc.sync}
        self.esem = {k: stack.enter_context(nc.semaphore("es_" + k)) for k in self.eng}
        self.tick = {k: 0 for k in self.eng}
        self.dsem = {q: [stack.enter_context(nc.semaphore("ds_%s%d" % (q, i))) for i in range(ndma)]
                     for q in ("sp", "pool", "act")}
        self.duse = {q: [0] * ndma for q in self.dsem}
        self.dcnt = {q: 0 for q in self.dsem}
        self.seen = {k: {} for k in self.eng}
        self.ops = []

    def op(self, eng, fn, reads=(), writes=()):
        self.ops.append(dict(eng=eng, fn=fn, reads=tuple(reads), writes=tuple(writes), dma=False))

    def dma(self, q, fn, reads=(), writes=()):
        self.ops.append(dict(eng=q, fn=fn, reads=tuple(reads), writes=tuple(writes), dma=True))

    def _wait(self, e, sem, val):
        key = id(sem)
        if self.seen[e].get(key, 0) < val:
            self.eng[e].wait_ge(sem, val)
            self.seen[e][key] = val

    def flush(self, barrier=True):
        ops = self.ops
        self.ops = []
        last_w = {}
        readers = {}
        needs = [False] * len(ops)
        for i, o in enumerate(ops):
            deps = set()

            def inorder(j):
                return ops[j]["eng"] == o["eng"] == "pe" and not ops[j]["dma"] and not o["dma"]

            for r in o["reads"]:
                j = last_w.get(r)
                if j is not None and not (inorder(j) and o["eng"] == "pe"):
                    deps.add(j)
            for w in o["writes"]:
                j = last_w.get(w)
                if j is not None and not inorder(j):
                    deps.add(j)
                for j in readers.get(w, ()):
                    if not inorder(j):
                        deps.add(j)
            o["deps"] = sorted(deps)
            for j in deps:
                needs[j] = True
            for r in o["reads"]:
                readers.setdefault(r, []).append(i)
            for w in o["writes"]:
                last_w[w] = i
                readers[w] = []
        lastop = {}
        for i, o in enumerate(ops):
            if not o["dma"]:
                lastop[o["eng"]] = i
        for i in lastop.values():
            needs[i] = True
        for i, o in enumerate(ops):
            e = o["eng"]
            for j in o["deps"]:
                ev = ops[j]["event"]
                self._wait(e, ev[0], ev[1])
            if o["dma"]:
                n = len(self.dsem[e])
                slot = self.dcnt[e] % n
                self.dcnt[e] += 1
                sem = self.dsem[e][slot]
                k = self.duse[e][slot]
                if k > 0:
                    self._wait(e, sem, 16 * k)
                ins = o["fn"](self.eng[e])
                ins.then_inc(sem, 16)
                self.duse[e][slot] = k + 1
                o["event"] = (sem, 16 * (k + 1))
            else:
                ins = o["fn"](self.eng[e])
                if needs[i]:
                    self.tick[e] += 1
                    ins.then_inc(self.esem[e], 1)
                    o["event"] = (self.esem[e], self.tick[e])
                else:
                    o["event"] = None
        if barrier:
            self.barrier()

    def barrier(self):
        for e in self.eng:
            for d in self.eng:
                if d != e and self.tick[d] > 0:
                    self._wait(e, self.esem[d], self.tick[d])
            for q in self.dsem:
                for s, k in zip(self.dsem[q], self.duse[q]):
                    if k > 0:
                        self._wait(e, s, 16 * k)


def build_nc(dbg=False, stages=("ffn1",)):
    nc = bass.Bass("TRN2", target_bir_lowering=False)

    def din(name, shape, dt=F32):
        return nc.dram_tensor(name, list(shape), dt, kind="ExternalInput").ap()

    def dout(name, shape, dt=F32):
        return nc.dram_tensor(name, list(shape), dt, kind="ExternalOutput").ap()

    def dscr(name, shape, dt=F32):
        return nc.dram_tensor(name, list(shape), dt, kind="ExternalOutput" if dbg else "Internal").ap()

    xin = din("xin", [NTOK, D])
    f1g = din("f1g", [D, DFF])
    f1u = din("f1u", [D, DFF])
    f1d = din("f1d", [DFF, D])
    lnp = din("lnp", [6, D])
    ident_d = din("ident", [128, 128])
    x1s = dscr("x1s", [NTOK, D])
    w_in = din("w_in", [D, INC])
    relb = din("relb", [32, 8])
    onehot = din("onehot", [3, 33, 384])
    antiid = din("antiid", [128, 128])
    o_kp = dout("o_kp", [2048, 512])
    o_vp = dout("o_vp", [2048, 512])
    o_ks = dout("o_ks", [64, 512])
    o_vs = dout("o_vs", [64, 512])
    fvd = dscr("fvd", [3, 8, 384])
    attn_s = dscr("attn_s", [8, 64, 2048], BF16)
    gconst = din("gconst", [64, 7, 64])
    convw = din("convw", [4, 1536])
    gvec = din("gvec", [3, 64])
    o_sgp = dout("o_sgp", [8, 64, 64])
    o_scp = dout("o_scp", [3, 1536])
    gdn_dbg = dscr("gdn_dbg", [128, 4, 2048]) if dbg else None
    ck = din("ck", [16, 2048, 512])
    cv = din("cv", [16, 2048, 512])
    sg_in = din("sg_in", [128, 4096])
    sc_in = din("sc_in", [16, 3, 1536])
    wcm = din("wcm", [32, 8, 4, 128])
    cws = din("cws", [128, 3, 4, 64])
    gvs = din("gvs", [128, 2])
    w_out = din("w_out", [D, D])
    f2g = din("f2g", [D, DFF])
    f2u = din("f2u", [D, DFF])
    f2d = din("f2d", [DFF, D])
    o_ys = dout("o_ys", [64, D])
    o_yp = dout("o_yp", [2048, D])
    o_sgs = dout("o_sgs", [128, 4096])
    o_scs = dout("o_scs", [16, 3, 1536])
    qs_s = dscr("qs_s", [64, 512])
    gq_s = dscr("gq_s", [64, 1536])
    z_s = dscr("z_s", [64, 512])
    ba_s = dscr("ba_s", [64, 16])
    heads_s = dscr("heads_s", [64, D])
    x2s = dscr("x2s", [NTOK, D])

    with contextlib.ExitStack() as stack:
        S = Sched(nc, stack)
        sb = lambda name, shape, dt=F32: stack.enter_context(nc.sbuf_tensor(name, list(shape), dt))
        psb = [stack.enter_context(nc.psum_tensor("psb%d" % i, [128, 512], F32)) for i in range(7)]
        pst = stack.enter_context(nc.psum_tensor("pst", [128, 1024], BF16))

        identf = sb("identf", [128, 128])
        identb = sb("identb", [128, 128], BF16)
        stp = contextlib.ExitStack()
        x1T = stp.enter_context(nc.sbuf_tensor("x1T", [128, 8, NTOK], BF16))

        S.dma("sp", lambda e: e.dma_start(out=identf[:], in_=ident_d[:, :]), writes=["identf"])
        S.op("dve", lambda e: e.tensor_copy(out=identb[:], in_=identf[:]), reads=["identf"], writes=["identb"])
        S.flush()

        def to_featmajor(src_bf, skey, dstT, col0, dkey):
            for kc in range(8):
                S.op("pe", lambda e, kc=kc: e.transpose(pst[:, kc * 128:(kc + 1) * 128],
                                                        src_bf[:, kc * 128:(kc + 1) * 128], identb[:]),
                     reads=[skey, "identb"], writes=["pst"])
            S.op("dve", lambda e: e.tensor_copy(out=dstT[:, :, col0:col0 + 128],
                                               in_=pst[:].rearrange("p (k c) -> p k c", k=8)),
                 reads=["pst"], writes=[dkey])

        def layernorm(r, rkey, lnrep, out, okey, tmp, epsmul=4.0):
            st, mv, rstd = tmp
            for c in range(2):
                S.op("dve", lambda e, c=c: e.bn_stats(out=st[:, c, :], in_=r[:, c * 512:(c + 1) * 512]),
                     reads=[rkey], writes=[("st", c)])
            S.op("dve", lambda e: e.bn_aggr(out=mv[:], in_=st[:].rearrange("p c s -> p (c s)")),
                 reads=[("st", 0), ("st", 1)], writes=["mv"])
            S.op("dve", lambda e: e.tensor_scalar(out=rstd[:], in0=mv[:, 1:2], scalar1=epsmul * LN_EPS, scalar2=None,
                                                  op0=ALU.add), reads=["mv"], writes=["rstd"])
            S.op("act", lambda e: e.sqrt(out=rstd[:], in_=rstd[:]), reads=["rstd"], writes=["rstd"])
            S.op("dve", lambda e: e.reciprocal(out=rstd[:], in_=rstd[:]), reads=["rstd"], writes=["rstd"])
            S.op("dve", lambda e: e.tensor_scalar(out=r[:], in0=r[:], scalar1=mv[:, 0:1], scalar2=rstd[:, 0:1],
                                                  op0=ALU.subtract, op1=ALU.mult),
                 reads=[rkey, "mv", "rstd"], writes=[rkey])
            S.op("pool", lambda e: e.tensor_tensor(out=r[:], in0=r[:], in1=lnrep[:, 0, :], op=ALU.mult),
                 reads=[rkey, ("lnrep", 0)], writes=[rkey])
            S.op("dve", lambda e: e.tensor_tensor(out=out[:], in0=r[:], in1=lnrep[:, 1, :], op=ALU.add),
                 reads=[rkey, ("lnrep", 1)], writes=[okey])

        def ffn(tag, xsrc, wg, wu, wd, gi, store, xTout):
            with contextlib.ExitStack() as st2:
                sb2 = lambda name, shape, dt=F32: st2.enter_context(nc.sbuf_tensor(tag + name, list(shape), dt))
                MT = 9
                xT = sb2("xT", [128, 8, MT * 128], BF16)
                hT = sb2("hT", [128, NFC, MT * 128], BF16)
                wgb = [sb2("wgb%d" % i, [128, 8, 256], BF16) for i in range(2)]
                wub = [sb2("wub%d" % i, [128, 8, 256], BF16) for i in range(2)]
                wdb = sb2("wdb", [128, NFC, D], BF16)
                xs = [sb2("xs%d" % i, [128, D]) for i in range(2)]
                xb = [sb2("xb%d" % i, [128, D], BF16) for i in range(2)]
                sg = [sb2("sg%d" % i, [128, 512]) for i in range(2)]
                rr = [sb2("rr%d" % i, [128, D]) for i in range(2)]
                oo = rr
                lnrep = sb2("ln", [128, 2, D])
                for i in range(2):
                    S.dma("sp", lambda e, i=i: e.dma_start(
                        out=lnrep[:, i, :], in_=lnp[gi + i:gi + i + 1, :].partition_broadcast(128)),
                          writes=[("lnrep", i)])
                stt = sb2("st", [128, 2, 6])
                mv = sb2("mv", [128, 2])
                rstd = sb2("rstd", [128, 1])
                wgv = wg.rearrange("(kc p) f -> p kc f", p=128)
                wuv = wu.rearrange("(kc p) f -> p kc f", p=128)
                wdv = wd.rearrange("(fc p) d -> p fc d", p=128)
                for mi, tiles in enumerate((list(range(0, MT)), list(range(MT, NT)))):
                    ntl = len(tiles)
                    for li, t in enumerate(tiles):
                        b = li % 2
                        S.dma("sp", lambda e, b=b, t=t: e.dma_start(out=xs[b][:], in_=xsrc(t)), writes=[("xs", b)])
                        S.op("act", lambda e, b=b: e.copy(out=xb[b][:], in_=xs[b][:]), reads=[("xs", b)],
                             writes=[("xb", b)])
                        to_featmajor(xb[b], ("xb", b), xT, li * 128, ("xT", li))
                    if mi == 0:
                        for q in range(2):
                            S.dma("pool", lambda e, q=q: e.dma_start(out=wdb[:, q * 11:(q + 1) * 11, :],
                                                                     in_=wdv[:, q * 11:(q + 1) * 11, :]),
                                  writes=[("wdb", q)])
                    tbs = [(c0, min(512, ntl * 128 - c0)) for c0 in range(0, ntl * 128, 512)]
                    for fb in range(11):
                        wbuf = fb % 2
                        S.dma("pool", lambda e, fb=fb, wbuf=wbuf: e.dma_start(
                            out=wgb[wbuf][:], in_=wgv[:, :, fb * 256:(fb + 1) * 256]), writes=[("wgb", wbuf)])
                        S.dma("pool", lambda e, fb=fb, wbuf=wbuf: e.dma_start(
                            out=wub[wbuf][:], in_=wuv[:, :, fb * 256:(fb + 1) * 256]), writes=[("wub", wbuf)])
                        for j in range(2):
                            fc = fb * 2 + j
                            for ti, (c0, cw) in enumerate(tbs):
                                pb = (fc * len(tbs) + ti) % 2
                                pg, pu = psb[pb], psb[2 + pb]
                                xkeys = [("xT", li) for li in range(c0 // 128, (c0 + cw) // 128)]
                                for kc in range(8):
                                    S.op("pe", lambda e, pg=pg, kc=kc, wbuf=wbuf, j=j, c0=c0, cw=cw: e.matmul(
                                        pg[:, 0:cw], wgb[wbuf][:, kc, j * 128:(j + 1) * 128], xT[:, kc, c0:c0 + cw],
                                        start=(kc == 0), stop=(kc == 7)),
                                         reads=[("wgb", wbuf)] + xkeys, writes=[("psb", pb)])
                                for kc in range(8):
                                    S.op("pe", lambda e, pu=pu, kc=kc, wbuf=wbuf, j=j, c0=c0, cw=cw: e.matmul(
                                        pu[:, 0:cw], wub[wbuf][:, kc, j * 128:(j + 1) * 128], xT[:, kc, c0:c0 + cw],
                                        start=(kc == 0), stop=(kc == 7)),
                                         reads=[("wub", wbuf)] + xkeys, writes=[("psb", 2 + pb)])
                                S.op("act", lambda e, pg=pg, pb=pb, cw=cw: e.activation(
                                    out=sg[pb][:, 0:cw], in_=pg[:, 0:cw], func=AF.Silu),
                                     reads=[("psb", pb)], writes=[("sg", pb)])
                                S.op("dve", lambda e, pu=pu, pb=pb, fc=fc, c0=c0, cw=cw: e.tensor_tensor(
                                    out=hT[:, fc, c0:c0 + cw], in0=sg[pb][:, 0:cw], in1=pu[:, 0:cw], op=ALU.mult),
                                     reads=[("sg", pb), ("psb", 2 + pb)], writes=[("hT", fc, ti)])
                    for li, t in enumerate(tiles):
                        b = li % 2
                        ti = li // 4
                        S.dma("sp", lambda e, b=b, t=t: e.dma_start(out=xs[b][:], in_=xsrc(t)), writes=[("xs", b)])
                        for half in range(2):
                            pd = psb[4 + half]
                            for fc in range(NFC):
                                S.op("pe", lambda e, pd=pd, fc=fc, li=li, half=half: e.matmul(
                                    pd[:, :], hT[:, fc, li * 128:(li + 1) * 128], wdb[:, fc, half * 512:(half + 1) * 512],
                                    start=(fc == 0), stop=(fc == NFC - 1)),
                                     reads=[("hT", fc, ti), ("wdb", fc // 11)], writes=[("psb", 4 + half)])
                            S.op("dve", lambda e, pd=pd, b=b, half=half: e.scalar_tensor_tensor(
                                out=rr[b][:, half * 512:(half + 1) * 512], in0=xs[b][:, half * 512:(half + 1) * 512],
                                scalar=2.0 * ALPHA, in1=pd[:, :], op0=ALU.mult, op1=ALU.add),
                                 reads=[("xs", b), ("psb", 4 + half)], writes=[("rr", b)])
                        layernorm(rr[b], ("rr", b), lnrep, rr[b], ("rr", b), (stt, mv, rstd))
                        store(t, rr[b], ("rr", b))
                        if xTout is not None:
                            S.op("act", lambda e, b=b: e.copy(out=xb[b][:], in_=rr[b][:]), reads=[("rr", b)],
                                 writes=[("xb", b)])
                            to_featmajor(xb[b], ("xb", b), xTout, t * 128, ("xTo", t))
                S.flush()

        PAT = ((128, 1), (512, 4), (2048, 16))

        def unit_tokens(d, u):
            nblk = 16 // d
            r, n = u // nblk, u % nblk
            return r, n, nblk, r + d * 128 * n

        def attention_stage():
            with contextlib.ExitStack() as st2:
                sb2 = lambda name, shape, dt=F32: st2.enter_context(nc.sbuf_tensor("at" + name, list(shape), dt))
                attnT = [sb2("attnT%d" % i, [64, 2048], BF16) for i in range(2)]
                qT = sb2("qT", [128, 4, 2048], BF16)
                kT = sb2("kT", [128, 4, 2048], BF16)
                vaug = [sb2("vaug%d" % i, [128, 16, 8, 65], BF16) for i in range(3)]
                brev = sb2("brev", [128, 24, 256], BF16)
                jb = sb2("jb", [128, 128], BF16)
                onesf = sb2("onesf", [128, 64])
                rbx = sb2("rbx", [33, 8])
                ohs = sb2("ohs", [33, 3, 384])
                fvs = sb2("fvs", [8, 3, 384])
                winv = w_in.rearrange("(kc p) f -> p kc f", p=128)
                S.dma("pool", lambda e: e.dma_start(out=jb[:], in_=antiid[:, :]), writes=["jb"])
                S.op("pool", lambda e: e.memset(onesf[:], 1.0), writes=["onesf"])
                S.op("pool", lambda e: e.memset(rbx[32:33, :], NEG), writes=["rbx1"])
                S.dma("sp", lambda e: e.dma_start(out=rbx[0:32, :], in_=relb[:, :]), writes=["rbx0"])
                S.dma("sp", lambda e: e.dma_start(out=ohs[:], in_=onehot.rearrange("p b m -> b p m")), writes=["ohs"])
                for p in range(3 if "notables" not in stages else 0):
                    S.op("pe", lambda e, p=p: e.matmul(psb[p][0:8, 0:384], rbx[:, :], ohs[:, p, :], start=True, stop=True),
                         reads=["rbx0", "rbx1", "ohs"], writes=[("psb", p)])
                    S.op("dve", lambda e, p=p: e.tensor_copy(out=fvs[:, p, :], in_=psb[p][0:8, 0:384]),
                         reads=[("psb", p)], writes=[("fvs", p)])
                if "notables" not in stages:
                    S.dma("sp", lambda e: e.dma_start(out=fvd.rearrange("p h m -> h p m"), in_=fvs[:]),
                          reads=[("fvs", p) for p in range(3)], writes=["fvd"])
                if "nohankel" not in stages:
                    S.dma("pool", lambda e: e.dma_start(
                        out=brev[:], in_=bass.AP(fvd.tensor, 0, [[1, 128], [384, 24], [1, 256]])),
                          reads=["fvd"], writes=["brev"])
                for i in range(3):
                    S.op("pool", lambda e, i=i: e.memset(vaug[i][:, :, :, 64:65], 1.0), writes=[("vone", i)])
                with contextlib.ExitStack() as st3:
                    sb3 = lambda name, shape, dt=F32: st3.enter_context(nc.sbuf_tensor("ap" + name, list(shape), dt))
                    wb = [sb3("wb%d" % i, [128, 8, 512], BF16) for i in range(3)]
                    kvo = [sb3("kvo%d" % i, [128, 512]) for i in range(2)]
                    for blk in range(3):
                        S.dma("pool", lambda e, blk=blk: e.dma_start(out=wb[blk][:], in_=winv[:, :, blk * 512:(blk + 1) * 512]),
                              writes=[("wb", blk)])
                    cnt = 0
                    if "noproj" in stages:
                        S.flush()
                        return
                    for blk, dst in (((0, qT), (1, kT)) if "noqk" not in stages else ()):
                        for pair in range(4):
                            for tb in range(4):
                                pb = cnt % 2
                                cnt += 1
                                for kc in range(8):
                                    S.op("pe", lambda e, pb=pb, blk=blk, pair=pair, tb=tb, kc=kc: e.matmul(
                                        psb[pb][:, :], wb[blk][:, kc, pair * 128:(pair + 1) * 128],
                                        x1T[:, kc, 128 + tb * 512:128 + (tb + 1) * 512], start=(kc == 0), stop=(kc == 7)),
                                         reads=[("wb", blk)], writes=[("psb", pb)])
                                if blk == 0:
                                    S.op("act", lambda e, pb=pb, pair=pair, tb=tb: e.mul(
                                        out=qT[:, pair, tb * 512:(tb + 1) * 512], in_=psb[pb][:, :], mul=0.125),
                                         reads=[("psb", pb)], writes=[("qT", pair)])
                                else:
                                    S.op("dve", lambda e, pb=pb, pair=pair, tb=tb: e.tensor_copy(
                                        out=kT[:, pair, tb * 512:(tb + 1) * 512], in_=psb[pb][:, :]),
                                         reads=[("psb", pb)], writes=[("kT", pair)])
                    for t in range(NT if "nokv" not in stages else 0):
                        for blk in (1, 2):
                            pb = 2 + (cnt % 2)
                            cnt += 1
                            ob = blk - 1
                            for kc in range(8):
                                S.op("pe", lambda e, pb=pb, blk=blk, t=t, kc=kc: e.matmul(
                                    psb[pb][:, :], x1T[:, kc, t * 128:(t + 1) * 128], wb[blk][:, kc, :],
                                    start=(kc == 0), stop=(kc == 7)),
                                     reads=[("wb", blk)], writes=[("psb", pb)])
                            S.op("act", lambda e, pb=pb, ob=ob: e.activation(out=kvo[ob][:], in_=psb[pb][:, :], func=AF.Copy),
                                 reads=[("psb", pb)], writes=[("kvo", ob)])
                            if blk == 2 and t >= 1:
                                S.op("dve", lambda e, ob=ob, t=t: e.tensor_copy(
                                    out=vaug[0][:, t - 1, :, 0:64], in_=kvo[ob][:, :].rearrange("p (h e) -> p h e", h=8)),
                                     reads=[("kvo", ob)], writes=[("vaug", 0, t - 1)])
                            if "nokvdma" in stages:
                                continue
                            if t == 0:
                                dst = (o_ks if blk == 1 else o_vs)[0:64, :]
                                S.dma("sp", lambda e, dst=dst, ob=ob: e.dma_start(out=dst, in_=kvo[ob][0:64, :]),
                                      reads=[("kvo", ob)], writes=[("okv", blk, t)])
                            else:
                                dst = (o_kp if blk == 1 else o_vp)[(t - 1) * 128:t * 128, :]
                                S.dma("sp", lambda e, dst=dst, ob=ob: e.dma_start(out=dst, in_=kvo[ob][:, :]),
                                      reads=[("kvo", ob)], writes=[("okv", blk, t)])
                    for pi in ((1, 2) if "nodil" not in stages else ()):
                        d = PAT[pi][1]
                        for u in range(16):
                            r, n, nblk, t0 = unit_tokens(d, u)
                            pb = 2 + (cnt % 2)
                            cnt += 1
                            for kc in range(8):
                                S.op("pe", lambda e, pb=pb, kc=kc, t0=t0, d=d: e.matmul(
                                    psb[pb][:, :], x1T[:, kc, sl(128 + t0, 128, d)], wb[2][:, kc, :],
                                    start=(kc == 0), stop=(kc == 7)),
                                     reads=[("wb", 2)], writes=[("psb", pb)])
                            S.op("dve", lambda e, pb=pb, pi=pi, u=u: e.tensor_copy(
                                out=vaug[pi][:, u, :, 0:64], in_=psb[pb][:, :].rearrange("p (h e) -> p h e", h=8)),
                                 reads=[("psb", pb)], writes=[("vaug", pi, u)])
                    S.flush()
                if "noattnmain" in stages:
                    return
                with contextlib.ExitStack() as st3:
                    sb3 = lambda name, shape, dt=F32: st3.enter_context(nc.sbuf_tensor("aa" + name, list(shape), dt))
                    acc = [sb3("acc%d" % i, [65, 2048]) for i in range(2)]
                    pts = [sb3("pt%d" % i, [128, 256], BF16) for i in range(4)]
                    rcp = sb3("rcp", [65, 2048])
                    ptc = 0
                    cnt = 0
                    for h in range(8):
                        pair, base = h // 2, (h % 2) * 64
                        ab = h % 2
                        A = acc[ab]
                        for pi, (win, d) in enumerate(PAT):
                            prev = None
                            for u in range(16):
                                r, n, nblk, t0 = unit_tokens(d, u)
                                W = 256 if n + 1 < nblk else 128
                                ps = cnt % 2
                                cnt += 1
                                pt = ptc % 4
                                ptc += 1
                                S.op("pe", lambda e, ps=ps, W=W, base=base, pair=pair, t0=t0, d=d: e.matmul(
                                    psb[ps][:, 0:W], kT[base:base + 64, pair, sl(t0, 128, d)],
                                    qT[base:base + 64, pair, sl(t0, W, d)], start=True, stop=False),
                                     reads=[("qT", pair), ("kT", pair)], writes=[("psb", ps)])
                                S.op("pe", lambda e, ps=ps, W=W, pi=pi, h=h: e.matmul(
                                    psb[ps][:, 0:W], jb[:, :], brev[:, pi * 8 + h, 0:W], start=False, stop=True),
                                     reads=["jb", "brev"], writes=[("psb", ps)])
                                S.op("act", lambda e, ps=ps, W=W, pt=pt: e.activation(
                                    out=pts[pt][:, 0:W], in_=psb[ps][:, 0:W], func=AF.Exp),
                                     reads=[("psb", ps)], writes=[("pt", pt)])
                                po = 2 + (cnt % 2)
                                first = (n == 0)
                                S.op("pe", lambda e, po=po, pi=pi, u=u, h=h, pt=pt, first=first: e.matmul(
                                    psb[po][0:65, 0:128], vaug[pi][:, u, h, :], pts[pt][:, 0:128], start=True, stop=first),
                                     reads=[("vaug", pi, u), ("vone", pi), ("pt", pt)], writes=[("psb", po)])
                                if not first:
                                    S.op("pe", lambda e, po=po, pi=pi, u=u, h=h, prev=prev: e.matmul(
                                        psb[po][0:65, 0:128], vaug[pi][:, u - 1, h, :], pts[prev][:, 128:256],
                                        start=False, stop=True),
                                         reads=[("vaug", pi, u - 1), ("vone", pi), ("pt", prev)], writes=[("psb", po)])
                                prev = pt
                                dst = A[:, sl(t0, 128, d)]
                                if pi == 0:
                                    S.op("dve", lambda e, dst=dst, po=po: e.tensor_copy(out=dst, in_=psb[po][0:65, 0:128]),
                                         reads=[("psb", po)], writes=[("acc", ab)])
                                else:
                                    S.op("dve", lambda e, dst=dst, po=po: e.tensor_tensor(
                                        out=dst, in0=dst, in1=psb[po][0:65, 0:128], op=ALU.add),
                                         reads=[("psb", po), ("acc", ab)], writes=[("acc", ab)])
                        S.op("dve", lambda e, A=A: e.reciprocal(out=rcp[64:65, :], in_=A[64:65, :]),
                             reads=[("acc", ab)], writes=["rcp"])
                        for tb in range(4):
                            pr = 4 + (tb % 2)
                            S.op("pe", lambda e, pr=pr, tb=tb: e.matmul(
                                psb[pr][0:64, :], onesf[64:65, 0:64], rcp[64:65, tb * 512:(tb + 1) * 512],
                                start=True, stop=True), reads=["rcp", "onesf"], writes=[("psb", pr)])
                            S.op("dve", lambda e, pr=pr, tb=tb, A=A, ab=ab: e.tensor_tensor(
                                out=attnT[ab][:, tb * 512:(tb + 1) * 512], in0=A[0:64, tb * 512:(tb + 1) * 512],
                                in1=psb[pr][0:64, :], op=ALU.mult),
                                 reads=[("psb", pr), ("acc", ab)], writes=[("attnT", ab)])
                        S.dma("sp", lambda e, h=h, ab=ab: e.dma_start(out=attn_s[h, :, :], in_=attnT[ab][:, :]),
                              reads=[("attnT", ab)], writes=[("attn_s", h)])
                    S.flush()

        def bl(ap, n=64):
            return ap.unsqueeze(2).broadcast_to([ap.shape[0], ap.shape[1], n])

        def bm(ap, n=8):
            return ap.unsqueeze(1).broadcast_to([ap.shape[0], n, ap.shape[1]])

        def v3(ap, h=8):
            return ap.rearrange("p (h x) -> p h x", h=h)

        def gdn_prompt_stage():
            with contextlib.ExitStack() as st2:
                sb2 = lambda name, shape, dt=F32: st2.enter_context(nc.sbuf_tensor("gd" + name, list(shape), dt))
                qh = sb2("qh", [64, 8, 2048], BF16)
                kh = sb2("kh", [64, 8, 2048], BF16)
                vT = sb2("vT", [128, 4, 2048], BF16)
                gcn = sb2("gcn", [64, 7, 64])
                NEGS, NEGT, MSKT, ID64, ONES, TRI, SEL = [gcn[:, i, :] for i in range(7)]
                cwr = sb2("cwr", [4, 1536])
                cwq = sb2("cwq", [64, 16, 4])
                cwv = sb2("cwv", [128, 4, 4])
                nwr = sb2("nwr", [64, 64])
                wz = sb2("wz", [128, 8, 512], BF16)
                wba = sb2("wba", [128, 8, 16], BF16)
                winv = w_in.rearrange("(kc p) f -> p kc f", p=128)
                S.dma("sp", lambda e: e.dma_start(out=gcn[:], in_=gconst[:, :, :]), writes=["gcn"])
                S.dma("sp", lambda e: e.dma_start(out=cwr[:], in_=convw[:, :]), writes=["cwr"])
                S.dma("sp", lambda e: e.dma_start(out=nwr[:], in_=gvec[2:3, :].partition_broadcast(64)), writes=["nwr"])
                S.dma("pool", lambda e: e.dma_start(out=wz[:], in_=winv[:, :, 3072:3584]), writes=["wz"])
                S.dma("pool", lambda e: e.dma_start(out=wba[:], in_=winv[:, :, 3584:3600]), writes=["wba"])
                for g in range(16):
                    S.op("pe", lambda e, g=g: e.transpose(psb[0][0:64, g * 4:(g + 1) * 4], cwr[0:4, g * 64:(g + 1) * 64],
                                                          identf[0:4, 0:4]), reads=["cwr", "identf"], writes=[("psb", 0)])
                for c in range(4):
                    S.op("pe", lambda e, c=c: e.transpose(psb[1][:, c * 4:(c + 1) * 4],
                                                          cwr[0:4, 1024 + c * 128:1024 + (c + 1) * 128], identf[0:4, 0:4]),
                         reads=["cwr", "identf"], writes=[("psb", 1)])
                S.op("dve", lambda e: e.tensor_copy(out=cwq[:], in_=v3(psb[0][0:64, 0:64], 16)), reads=[("psb", 0)],
                     writes=["cwq"])
                S.op("dve", lambda e: e.tensor_copy(out=cwv[:], in_=v3(psb[1][:, 0:16], 4)), reads=[("psb", 1)],
                     writes=["cwv"])
                S.flush()

                with contextlib.ExitStack() as st3:
                    sb3 = lambda name, shape, dt=F32: st3.enter_context(nc.sbuf_tensor("g1" + name, list(shape), dt))
                    wb = [sb3("wb%d" % i, [128, 8, 512], BF16) for i in range(2)]
                    raw = sb3("raw", [128, 2051])
                    cac = sb3("cac", [128, 2048])
                    rin = [sb3("rin%d" % i, [64, 512]) for i in range(2)]
                    scp = sb3("scp", [128, 12, 3])
                    S.op("pool", lambda e: e.memset(raw[:, 0:3], 0.0), writes=["raw0"])
                    cnt = 0
                    for blk in range(3):
                        wbuf = blk % 2
                        S.dma("pool", lambda e, blk=blk, wbuf=wbuf: e.dma_start(
                            out=wb[wbuf][:], in_=winv[:, :, 1536 + blk * 512:1536 + (blk + 1) * 512]),
                              writes=[("wb", wbuf)])
                        ngrp, P = (8, 64) if blk < 2 else (4, 128)
                        for g in range(ngrp):
                            for tb in range(4):
                                pb = cnt % 2
                                cnt += 1
                                for kc in range(8):
                                    S.op("pe", lambda e, pb=pb, wbuf=wbuf, g=g, P=P, tb=tb, kc=kc: e.matmul(
                                        psb[pb][0:P, :], wb[wbuf][:, kc, g * P:(g + 1) * P],
                                        x1T[:, kc, 128 + tb * 512:128 + (tb + 1) * 512], start=(kc == 0), stop=(kc == 7)),
                                         reads=[("wb", wbuf)], writes=[("psb", pb)])
                                S.op("act", lambda e, pb=pb, P=P, tb=tb: e.activation(
                                    out=raw[0:P, 3 + tb * 512:3 + (tb + 1) * 512], in_=psb[pb][0:P, :], func=AF.Copy),
                                     reads=[("psb", pb)], writes=["raw"])
                            ci = (blk * 512 + g * P) // 128
                            po = (blk * 512 + g * P) % 128
                            S.op("pool", lambda e, P=P, ci=ci, po=po: e.tensor_copy(
                                out=scp[po:po + P, ci, :], in_=raw[0:P, 2048:2051]) if po == 0 else e.tensor_copy(
                                out=scp[po:po + P, ci, :], in_=raw[0:P, 2048:2051]),
                                 reads=["raw"], writes=["scp"]) if False else None
                            col0 = blk * 512 + g * P
                            S.dma("sp", lambda e, P=P, col0=col0: e.dma_start(
                                out=o_scp[:, col0:col0 + P].rearrange("j p -> p j"), in_=raw[0:P, 2048:2051],
                                allow_slow_non_contiguous=True), reads=["raw"], writes=[("o_scp", col0)])
                            cwt = (cwq[:, blk * 8 + g, :] if blk < 2 else cwv[:, g, :])
                            S.op("dve", lambda e, P=P, cwt=cwt: e.tensor_scalar(
                                out=cac[0:P, :], in0=raw[0:P, 3:2051], scalar1=cwt[:, 3:4], scalar2=None, op0=ALU.mult),
                                 reads=["raw", "raw0", "cwq", "cwv"], writes=["cac"])
                            for j in (2, 1, 0):
                                S.op("dve", lambda e, P=P, cwt=cwt, j=j: e.scalar_tensor_tensor(
                                    out=cac[0:P, :], in0=raw[0:P, j:j + 2048], scalar=cwt[:, j:j + 1], in1=cac[0:P, :],
                                    op0=ALU.mult, op1=ALU.add), reads=["raw", "raw0", "cac"], writes=["cac"])
                            if blk == 2:
                                S.op("act", lambda e, g=g: e.activation(out=vT[:, g, :], in_=cac[:, :], func=AF.Silu),
                                     reads=["cac"], writes=[("vT", g)])
                                continue
                            S.op("act", lambda e: e.activation(out=cac[0:64, :], in_=cac[0:64, :], func=AF.Silu),
                                 reads=["cac"], writes=["cac"])
                            S.op("pool", lambda e: e.tensor_tensor(out=raw[0:64, 3:2051], in0=cac[0:64, :], in1=cac[0:64, :],
                                                                   op=ALU.mult), reads=["cac"], writes=["raw"])
                            dst = qh if blk == 0 else kh
                            for tb in range(4):
                                pb = 2 + (tb % 2)
                                rb = tb % 2
                                S.op("pe", lambda e, pb=pb, tb=tb: e.matmul(
                                    psb[pb][0:64, :], ONES, raw[0:64, 3 + tb * 512:3 + (tb + 1) * 512], start=True, stop=True),
                                     reads=["raw", "gcn"], writes=[("psb", pb)])
                                S.op("dve", lambda e, pb=pb, rb=rb: e.tensor_scalar(
                                    out=rin[rb][:], in0=psb[pb][0:64, :], scalar1=1e-6, scalar2=None, op0=ALU.add),
                                     reads=[("psb", pb)], writes=[("rin", rb)])
                                S.op("act", lambda e, rb=rb: e.sqrt(out=rin[rb][:], in_=rin[rb][:]), reads=[("rin", rb)],
                                     writes=[("rin", rb)])
                                S.op("dve", lambda e, rb=rb: e.reciprocal(out=rin[rb][:], in_=rin[rb][:]),
                                     reads=[("rin", rb)], writes=[("rin", rb)])
                                sc = 0.125 if blk == 0 else 1.0
                                S.op("dve", lambda e, rb=rb, tb=tb, dst=dst, g=g, sc=sc: e.scalar_tensor_tensor(
                                    out=dst[:, g, tb * 512:(tb + 1) * 512], in0=cac[0:64, tb * 512:(tb + 1) * 512], scalar=sc,
                                    in1=rin[rb][:], op0=ALU.mult, op1=ALU.mult),
                                     reads=["cac", ("rin", rb)], writes=[("qk", blk, g)])
                    S.flush()

                gt = lambda name: sb2(name, [64, 32, 8])
                ba = sb2("ba", [64, 32, 16])
                beta, nbeta, gg, gc, gcl, eg, egl, ekd, nbeg, alr, dtr = [gt(n) for n in (
                    "beta", "nbeta", "gg", "gc", "gcl", "eg", "egl", "ekd", "nbeg", "alr", "dtr")]
                S.dma("sp", lambda e: e.dma_start(out=alr[:], in_=bass.AP(gvec.tensor, 0, [[0, 64], [0, 32], [1, 8]])),
                      writes=["alr"])
                S.dma("sp", lambda e: e.dma_start(out=dtr[:], in_=bass.AP(gvec.tensor, 64, [[0, 64], [0, 32], [1, 8]])),
                      writes=["dtr"])
                for n in range(32):
                    for kc in range(8):
                        S.op("pe", lambda e, n=n, kc=kc: e.matmul(
                            psb[0][0:64, n * 16:(n + 1) * 16], x1T[:, kc, 128 + n * 64:128 + (n + 1) * 64], wba[:, kc, :],
                            start=(kc == 0), stop=(kc == 7)), reads=["wba"], writes=[("psb", 0)])
                S.op("dve", lambda e: e.tensor_copy(out=ba[:], in_=v3(psb[0][0:64, :], 32)), reads=[("psb", 0)],
                     writes=["ba"])
                S.op("act", lambda e: e.activation(out=beta[:], in_=ba[:, :, 0:8], func=AF.Sigmoid), reads=["ba"],
                     writes=["beta"])
                S.op("dve", lambda e: e.tensor_scalar(out=nbeta[:], in0=beta[:], scalar1=-1.0, scalar2=None, op0=ALU.mult),
                     reads=["beta"], writes=["nbeta"])
                S.op("dve", lambda e: e.tensor_tensor(out=gg[:], in0=ba[:, :, 8:16], in1=dtr[:], op=ALU.add),
                     reads=["ba", "dtr"], writes=["gg"])
                S.op("act", lambda e: e.activation(out=gg[:], in_=gg[:], func=AF.Exp), reads=["gg"], writes=["gg"])
                S.op("act", lambda e: e.activation(out=gg[:], in_=gg[:], func=AF.Ln, bias=ONES[:, 0:1]),
                     reads=["gg", "gcn"], writes=["gg"])
                S.op("act", lambda e: e.activation(out=alr[:], in_=alr[:], func=AF.Exp), reads=["alr"], writes=["alr"])
                S.op("dve", lambda e: e.scalar_tensor_tensor(out=gg[:], in0=gg[:], scalar=-1.0, in1=alr[:], op0=ALU.mult,
                                                             op1=ALU.mult), reads=["gg", "alr"], writes=["gg"])
                gg2 = lambda t: t[:].rearrange("p n h -> p (n h)")
                S.op("pe", lambda e: e.matmul(psb[1][0:64, 0:256], TRI, gg2(gg), start=True, stop=True),
                     reads=["gg", "gcn"], writes=[("psb", 1)])
                S.op("dve", lambda e: e.tensor_copy(out=gg2(gc), in_=psb[1][0:64, 0:256]), reads=[("psb", 1)],
                     writes=["gc"])
                S.op("pe", lambda e: e.matmul(psb[2][0:64, 0:256], SEL, gg2(gc), start=True, stop=True),
                     reads=["gc", "gcn"], writes=[("psb", 2)])
                S.op("dve", lambda e: e.tensor_copy(out=gg2(gcl), in_=psb[2][0:64, 0:256]), reads=[("psb", 2)],
                     writes=["gcl"])
                S.op("act", lambda e: e.activation(out=eg[:], in_=gc[:], func=AF.Exp), reads=["gc"], writes=["eg"])
                S.op("act", lambda e: e.activation(out=egl[:], in_=gcl[:], func=AF.Exp), reads=["gcl"], writes=["egl"])
                S.op("dve", lambda e: e.tensor_tensor(out=ekd[:], in0=gcl[:], in1=gc[:], op=ALU.subtract),
                     reads=["gc", "gcl"], writes=["ekd"])
                S.op("act", lambda e: e.activation(out=ekd[:], in_=ekd[:], func=AF.Exp), reads=["ekd"], writes=["ekd"])
                S.op("dve", lambda e: e.tensor_tensor(out=nbeg[:], in0=nbeta[:], in1=eg[:], op=ALU.mult),
                     reads=["nbeta", "eg"], writes=["nbeg"])
                S.flush()

                f3 = lambda name: sb2(name, [64, 8, 64])
                b3 = lambda name: sb2(name, [64, 8, 64], BF16)
                kvn = sb2("kvn", [64, 1024], BF16)
                zs = sb2("zs", [64, 512])
                Dg, Db, m1, e1, e2, b1, c1, Tf, vb, tt, ob, osq, Sf = [f3(n) for n in (
                    "Dg", "Db", "m1", "e1", "e2", "b1", "c1", "Tf", "vb", "tt", "ob", "osq", "Sf")]
                Bb = [b3("Bb0"), b3("Bb1")]
                Cb = [b3("Cb0"), b3("Cb1")]
                intraT, Tb, rn, vn, vns, Sb = [b3(n) for n in ("intraT", "Tb", "rn", "vn", "vns", "Sb")]
                go = sb2("go", [64, 512], BF16)
                ss = sb2("ss", [64, 8])
                S.op("pool", lambda e: e.memset(Sf[:], 0.0), writes=["Sf"])
                S.op("pool", lambda e: e.memset(Sb[:], 0.0), writes=["Sb"])
                ps3 = lambda i: v3(psb[i][0:64, :])
                R = lambda *k: list(k)
                for n in range(32):
                    c0 = n * 64
                    xc0 = 128 + c0
                    for h in range(8):
                        S.op("pe", lambda e, h=h, c0=c0: e.transpose(pst[0:64, h * 64:(h + 1) * 64], kh[:, h, c0:c0 + 64],
                                                                     identb[0:64, 0:64]),
                             reads=[("qk", 1, h), "identb"], writes=["pst"])
                    for pr in range(4):
                        S.op("pe", lambda e, pr=pr, c0=c0: e.transpose(pst[0:64, 512 + pr * 128:512 + (pr + 1) * 128],
                                                                       vT[:, pr, c0:c0 + 64], identb[:, :]),
                             reads=[("vT", pr), "identb"], writes=["pst"])
                    S.op("dve", lambda e: e.tensor_copy(out=kvn[:], in_=pst[0:64, :]), reads=["pst"], writes=["kvn"])
                    for kc in range(8):
                        S.op("pe", lambda e, kc=kc, xc0=xc0: e.matmul(psb[2][0:64, :], x1T[:, kc, xc0:xc0 + 64], wz[:, kc, :],
                                                                      start=(kc == 0), stop=(kc == 7)),
                             reads=["wz"], writes=[("psb", 2)])
                    S.op("act", lambda e: e.activation(out=zs[:], in_=psb[2][0:64, :], func=AF.Silu), reads=[("psb", 2)],
                         writes=["zs"])
                    for h in range(8):
                        S.op("pe", lambda e, h=h, c0=c0: e.matmul(psb[0][0:64, h * 64:(h + 1) * 64], kh[:, h, c0:c0 + 64],
                                                                  kh[:, h, c0:c0 + 64], start=True, stop=True),
                             reads=[("qk", 1, h)], writes=[("psb", 0)])
                    for h in range(8):
                        S.op("pe", lambda e, h=h, c0=c0: e.matmul(psb[1][0:64, h * 64:(h + 1) * 64], kh[:, h, c0:c0 + 64],
                                                                  qh[:, h, c0:c0 + 64], start=True, stop=True),
                             reads=[("qk", 1, h), ("qk", 0, h)], writes=[("psb", 1)])
                    gcn_ = gc[:, n, :]
                    S.op("pool", lambda e, gcn_=gcn_: e.tensor_tensor(out=Dg[:], in0=bm(ID64), in1=bl(gcn_), op=ALU.mult),
                         reads=["gc", "gcn"], writes=["Dg"])
                    S.op("pool", lambda e, n=n: e.tensor_tensor(out=Db[:], in0=bm(ID64), in1=bl(beta[:, n, :]), op=ALU.mult),
                         reads=["beta", "gcn"], writes=["Db"])
                    S.op("pe", lambda e: e.matmul(psb[3][0:64, :], ONES, Dg[:].rearrange("p h s -> p (h s)"), start=True,
                                                  stop=True), reads=["Dg", "gcn"], writes=[("psb", 3)])
                    S.op("pe", lambda e: e.matmul(psb[4][0:64, :], ONES, Db[:].rearrange("p h s -> p (h s)"), start=True,
                                                  stop=True), reads=["Db", "gcn"], writes=[("psb", 4)])
                    S.op("pool", lambda e, gcn_=gcn_: e.tensor_tensor(out=m1[:], in0=bl(gcn_), in1=bm(NEGS), op=ALU.add),
                         reads=["gc", "gcn"], writes=["m1"])
                    S.op("dve", lambda e: e.scalar_tensor_tensor(out=e1[:], in0=ps3(3), scalar=-1.0, in1=m1[:], op0=ALU.mult,
                                                                 op1=ALU.add), reads=[("psb", 3), "m1"], writes=["e1"])
                    S.op("act", lambda e: e.activation(out=e1[:], in_=e1[:], func=AF.Exp), reads=["e1"], writes=["e1"])
                    S.op("pool", lambda e, gcn_=gcn_: e.tensor_tensor(out=m1[:], in0=bm(NEGT), in1=bl(gcn_), op=ALU.subtract),
                         reads=["gc", "gcn", "e1"], writes=["m1"])
                    S.op("dve", lambda e: e.tensor_tensor(out=e2[:], in0=ps3(3), in1=m1[:], op=ALU.add),
                         reads=[("psb", 3), "m1"], writes=["e2"])
                    S.op("act", lambda e: e.activation(out=e2[:], in_=e2[:], func=AF.Exp), reads=["e2"], writes=["e2"])
                    S.op("dve", lambda e: e.tensor_tensor(out=b1[:], in0=ps3(0), in1=e1[:], op=ALU.mult),
                         reads=[("psb", 0), "e1"], writes=["b1"])
                    S.op("pool", lambda e, n=n: e.tensor_tensor(out=Bb[0][:], in0=b1[:], in1=bl(nbeta[:, n, :]), op=ALU.mult),
                         reads=["b1", "nbeta"], writes=[("Bb", 0)])
                    S.op("dve", lambda e: e.tensor_tensor(out=c1[:], in0=ps3(0), in1=e2[:], op=ALU.mult),
                         reads=[("psb", 0), "e2"], writes=["c1"])
                    S.op("dve", lambda e: e.tensor_tensor(out=c1[:], in0=c1[:], in1=ps3(4), op=ALU.mult),
                         reads=[("psb", 4), "c1"], writes=["c1"])
                    S.op("pool", lambda e: e.tensor_tensor(out=c1[:], in0=c1[:], in1=bm(MSKT), op=ALU.mult),
                         reads=["c1", "gcn"], writes=["c1"])
                    S.op("act", lambda e: e.copy(out=Cb[0][:], in_=c1[:]), reads=["c1"], writes=[("Cb", 0)])
                    S.op("dve", lambda e: e.tensor_tensor(out=intraT[:], in0=ps3(1), in1=e2[:], op=ALU.mult),
                         reads=[("psb", 1), "e2"], writes=["intraT"])
                    S.op("pool", lambda e: e.tensor_tensor(out=Tf[:], in0=c1[:], in1=bm(ID64), op=ALU.add),
                         reads=["c1", "gcn"], writes=["Tf"])
                    S.op("act", lambda e: e.copy(out=Tb[:], in_=Tf[:]), reads=["Tf"], writes=["Tb"])
                    for k in range(1, 6):
                        cur, nxt = (k - 1) % 2, k % 2
                        for h in range(8):
                            S.op("pe", lambda e, h=h, cur=cur: e.matmul(psb[5][0:64, h * 64:(h + 1) * 64], Cb[cur][:, h, :],
                                                                        Bb[cur][:, h, :], start=True, stop=True),
                                 reads=[("Cb", cur), ("Bb", cur)], writes=[("psb", 5)])
                        if k < 5:
                            for h in range(8):
                                S.op("pe", lambda e, h=h, cur=cur: e.matmul(psb[6][0:64, h * 64:(h + 1) * 64], Bb[cur][:, h, :],
                                                                            Cb[cur][:, h, :], start=True, stop=True),
                                     reads=[("Cb", cur), ("Bb", cur)], writes=[("psb", 6)])
                        S.op("act", lambda e, nxt=nxt: e.activation(out=Bb[nxt][:], in_=ps3(5), func=AF.Copy),
                             reads=[("psb", 5)], writes=[("Bb", nxt)])
                        if k < 5:
                            S.op("dve", lambda e, nxt=nxt: e.tensor_copy(out=Cb[nxt][:], in_=ps3(6)),
                                 reads=[("psb", 6)], writes=[("Cb", nxt)])
                        for h in range(8):
                            S.op("pe", lambda e, h=h, nxt=nxt: e.matmul(psb[2][0:64, h * 64:(h + 1) * 64], Bb[nxt][:, h, :],
                                                                        Tb[:, h, :], start=True, stop=True),
                                 reads=[("Bb", nxt), "Tb"], writes=[("psb", 2)])
                        S.op("dve", lambda e: e.tensor_tensor(out=Tf[:], in0=Tf[:], in1=ps3(2), op=ALU.add),
                             reads=[("psb", 2), "Tf"], writes=["Tf"])
                        S.op("act", lambda e: e.copy(out=Tb[:], in_=Tf[:]), reads=["Tf"], writes=["Tb"])
                    S.op("pool", lambda e, n=n: e.tensor_tensor(out=vb[:], in0=v3(kvn[:, 512:1024]), in1=bl(beta[:, n, :]),
                                                                op=ALU.mult), reads=["kvn", "beta"], writes=["vb"])
                    for h in range(8):
                        S.op("pe", lambda e, h=h, c0=c0: e.matmul(psb[3][0:64, h * 64:(h + 1) * 64], kh[:, h, c0:c0 + 64],
                                                                  Sb[:, h, :], start=True, stop=True),
                             reads=[("qk", 1, h), "Sb"], writes=[("psb", 3)])
                    S.op("dve", lambda e, n=n: e.tensor_tensor(out=tt[:], in0=ps3(3), in1=bl(nbeg[:, n, :]), op=ALU.mult),
                         reads=[("psb", 3), "nbeg"], writes=["tt"])
                    S.op("pool", lambda e: e.tensor_tensor(out=rn[:], in0=tt[:], in1=vb[:], op=ALU.add),
                         reads=["tt", "vb"], writes=["rn"])
                    for h in range(8):
                        S.op("pe", lambda e, h=h: e.matmul(psb[4][0:64, h * 64:(h + 1) * 64], Tb[:, h, :], rn[:, h, :],
                                                           start=True, stop=True),
                             reads=["Tb", "rn"], writes=[("psb", 4)])
                    S.op("act", lambda e: e.activation(out=vn[:], in_=ps3(4), func=AF.Copy), reads=[("psb", 4)],
                         writes=["vn"])
                    S.op("pool", lambda e, n=n: e.tensor_tensor(out=vns[:], in0=vn[:], in1=bl(ekd[:, n, :]), op=ALU.mult),
                         reads=["vn", "ekd"], writes=["vns"])
                    for h in range(8):
                        S.op("pe", lambda e, h=h, c0=c0: e.matmul(psb[5][0:64, h * 64:(h + 1) * 64], qh[:, h, c0:c0 + 64],
                                                                  Sb[:, h, :], start=True, stop=True),
                             reads=[("qk", 0, h), "Sb"], writes=[("psb", 5)])
                    for h in range(8):
                        S.op("pe", lambda e, h=h: e.matmul(psb[6][0:64, h * 64:(h + 1) * 64], intraT[:, h, :], vn[:, h, :],
                                                           start=True, stop=True),
                             reads=["intraT", "vn"], writes=[("psb", 6)])
                    S.op("dve", lambda e, n=n: e.tensor_tensor(out=ob[:], in0=ps3(5), in1=bl(eg[:, n, :]), op=ALU.mult),
                         reads=[("psb", 5), "eg"], writes=["ob"])
                    S.op("dve", lambda e: e.tensor_tensor(out=ob[:], in0=ob[:], in1=ps3(6), op=ALU.add),
                         reads=[("psb", 6), "ob"], writes=["ob"])
                    for h in range(8):
                        S.op("pe", lambda e, h=h: e.matmul(psb[0][0:64, h * 64:(h + 1) * 64], kvn[:, h * 64:(h + 1) * 64],
                                                           vns[:, h, :], start=True, stop=True),
                             reads=["kvn", "vns"], writes=[("psb", 0)])
                    S.op("pool", lambda e, n=n: e.tensor_tensor(out=Sf[:], in0=Sf[:], in1=bl(egl[:, n, :]), op=ALU.mult),
                         reads=["Sf", "egl"], writes=["Sf"])
                    S.op("dve", lambda e: e.tensor_tensor(out=Sf[:], in0=Sf[:], in1=ps3(0), op=ALU.add),
                         reads=[("psb", 0), "Sf"], writes=["Sf"])
                    S.op("act", lambda e: e.copy(out=Sb[:], in_=Sf[:]), reads=["Sf"], writes=["Sb"])
                    S.op("pool", lambda e: e.tensor_tensor(out=osq[:], in0=ob[:], in1=ob[:], op=ALU.mult), reads=["ob"],
                         writes=["osq"])
                    S.op("dve", lambda e: e.tensor_reduce(out=ss[:], in_=osq[:], axis=AX.X, op=ALU.add), reads=["osq"],
                         writes=["ss"])
                    S.op("dve", lambda e: e.tensor_scalar(out=ss[:], in0=ss[:], scalar1=1.0 / 64.0, scalar2=1e-6,
                                                          op0=ALU.mult, op1=ALU.add), reads=["ss"], writes=["ss"])
                    S.op("act", lambda e: e.sqrt(out=ss[:], in_=ss[:]), reads=["ss"], writes=["ss"])
                    S.op("dve", lambda e: e.reciprocal(out=ss[:], in_=ss[:]), reads=["ss"], writes=["ss"])
                    S.op("dve", lambda e: e.tensor_tensor(out=ob[:], in0=ob[:], in1=bl(ss[:, :]), op=ALU.mult),
                         reads=["ob", "ss"], writes=["ob"])
                    S.op("pool", lambda e: e.tensor_tensor(out=ob[:], in0=ob[:], in1=bm(nwr[:, :]), op=ALU.mult),
                         reads=["ob", "nwr"], writes=["ob"])
                    S.op("dve", lambda e: e.tensor_tensor(out=v3(go[:, :]), in0=ob[:], in1=v3(zs[:, :]), op=ALU.mult),
                         reads=["ob", "zs"], writes=["go"])
                    for pr in range(4):
                        S.op("pe", lambda e, pr=pr: e.transpose(pst[:, pr * 64:(pr + 1) * 64], go[:, pr * 128:(pr + 1) * 128],
                                                                identb[0:64, 0:64]),
                             reads=["go", "identb"], writes=["pst"])
                    S.op("dve", lambda e, c0=c0: e.tensor_copy(out=gdnT[:, :, c0:c0 + 64], in_=v3(pst[:, 0:256], 4)),
                         reads=["pst"], writes=[("gdnT", n)])
                S.dma("sp", lambda e: e.dma_start(out=o_sgp.rearrange("h d e -> d h e"), in_=Sf[:]), reads=["Sf"],
                      writes=["o_sgp"])
                if dbg:
                    S.dma("pool", lambda e: e.dma_start(out=gdn_dbg[:, :, :], in_=gdnT[:]),
                          reads=[("gdnT", n) for n in range(32)], writes=["gdn_dbg"])
                S.flush()

        def sample_proj_stage():
            with contextlib.ExitStack() as st2:
                sb2 = lambda name, shape, dt=F32: st2.enter_context(nc.sbuf_tensor("sp" + name, list(shape), dt))
                wb = [sb2("wb%d" % i, [128, 8, 512], BF16) for i in range(2)]
                ot = [sb2("ot%d" % i, [128, 512]) for i in range(2)]
                winv = w_in.rearrange("(kc p) f -> p kc f", p=128)
                blocks = [(0, 512, qs_s[:, :], 0.125), (1536, 512, gq_s[:, 0:512], 1.0), (2048, 512, gq_s[:, 512:1024], 1.0),
                          (2560, 512, gq_s[:, 1024:1536], 1.0), (3072, 512, z_s[:, :], 1.0), (3584, 16, ba_s[:, :], 1.0)]
                for i, (c0, w, dst, sc) in enumerate(blocks):
                    b = i % 2
                    S.dma("pool", lambda e, b=b, c0=c0, w=w: e.dma_start(out=wb[b][:, :, 0:w], in_=winv[:, :, c0:c0 + w]),
                          writes=[("wb", b)])
                    for kc in range(8):
                        S.op("pe", lambda e, b=b, kc=kc, w=w: e.matmul(psb[b][:, 0:w], x1T[:, kc, 0:128], wb[b][:, kc, 0:w],
                                                                       start=(kc == 0), stop=(kc == 7)),
                             reads=[("wb", b)], writes=[("psb", b)])
                    S.op("act", lambda e, b=b, w=w, sc=sc: e.mul(out=ot[b][:, 0:w], in_=psb[b][:, 0:w], mul=sc),
                         reads=[("psb", b)], writes=[("ot", b)])
                    S.dma("sp", lambda e, b=b, w=w, dst=dst: e.dma_start(out=dst, in_=ot[b][0:64, 0:w]),
                          reads=[("ot", b)], writes=[("sscr", i)])
                S.dma("sp", lambda e: e.dma_start(
                    out=o_scs[:, :, :], in_=bass.AP(gq_s.tensor, 1536, [[4 * 1536, 16], [1536, 3], [1, 1536]])),
                      reads=[("sscr", 1), ("sscr", 2), ("sscr", 3)], writes=["o_scs"])
                S.flush()

        def sample_attn_stage():
            with contextlib.ExitStack() as st2:
                sb2 = lambda name, shape, dt=F32: st2.enter_context(nc.sbuf_tensor("sa" + name, list(shape), dt))
                kt = [sb2("kt%d" % i, [128, 8, 512]) for i in range(2)]
                vt = [sb2("vt%d" % i, [128, 8, 512]) for i in range(2)]
                qrep = [sb2("qrep%d" % i, [128, 4, 512]) for i in range(2)]
                prod = [sb2("prod%d" % i, [128, 4, 512]) for i in range(2)]
                L = sb2("L", [128, 8, 32])
                Pm = sb2("Pm", [128, 8, 32])
                Pj = sb2("Pj", [128, 32])
                mtab = sb2("mtab", [128, 8, 32])
                wcs = sb2("wcs", [32, 8, 4, 128])
                eb = sb2("eb", [32, 8])
                ones1 = sb2("ones1", [128, 1])
                osb = sb2("osb", [4, 512])
                rs = sb2("rs", [4, 8])
                S.dma("sp", lambda e: e.dma_start(out=wcs[:], in_=wcm[:, :, :, :]), writes=["wcs"])
                S.dma("sp", lambda e: e.dma_start(out=eb[:], in_=relb[:, :]), writes=["eb"])
                S.op("act", lambda e: e.activation(out=eb[:], in_=eb[:], func=AF.Exp), reads=["eb"], writes=["eb"])
                S.op("pool", lambda e: e.memset(ones1[:], 1.0), writes=["ones1"])
                for i in range(2):
                    S.op("pool", lambda e, i=i: e.memset(kt[i][:, 7, :], 0.0), writes=[("kt7", i)])
                    S.op("pool", lambda e, i=i: e.memset(vt[i][:, 7, :], 0.0), writes=[("vt7", i)])
                for j in range(8):
                    for t in range(4):
                        S.op("pe", lambda e, j=j, t=t: e.matmul(psb[0][:, (j * 4 + t) * 8:(j * 4 + t + 1) * 8], wcs[:, j, t, :],
                                                                eb[:, :], start=True, stop=True),
                             reads=["wcs", "eb"], writes=[("psb", 0)])
                S.op("dve", lambda e: e.tensor_copy(out=mtab[:].rearrange("p j x -> p (j x)"), in_=psb[0][:, 0:256]),
                     reads=[("psb", 0)], writes=["mtab"])
                for b in range(16):
                    bb = b % 2
                    for src, dstt, onew, nm in ((ck, kt[bb], o_ks, "kt"), (cv, vt[bb], o_vs, "vt")):
                        S.dma("sp", lambda e, src=src, dstt=dstt, b=b: e.dma_start(
                            out=dstt[:, 0:4, :], in_=src[b, 1536:2048, :].rearrange("(j p) e -> p j e", p=128)),
                              writes=[(nm, bb)])
                        for r in range(4):
                            S.dma("sp", lambda e, src=src, dstt=dstt, b=b, r=r: e.dma_start(
                                out=dstt[r:128:4, 4:7, :],
                                in_=bass.AP(src.tensor, b * 2048 * 512 + r * 512, [[16 * 512, 32], [32 * 16 * 512, 3], [1, 512]])),
                                  writes=[(nm, bb)])
                        S.dma("sp", lambda e, dstt=dstt, onew=onew, b=b: e.dma_start(
                            out=dstt[0:4, 7, :], in_=onew[b * 4:(b + 1) * 4, :]),
                              reads=[("okv", 1, 0), ("okv", 2, 0), (nm + "7", bb)], writes=[(nm, bb)])
                    S.dma("sp", lambda e, b=b, bb=bb: e.dma_start(
                        out=qrep[bb][:], in_=bass.AP(qs_s.tensor, b * 4 * 512, [[0, 128], [512, 4], [1, 512]])),
                          reads=[("sscr", 0)], writes=[("qrep", bb)])
                    for j in range(8):
                        pb = j % 2
                        eng = "pool" if j % 2 else "dve"
                        S.op(eng, lambda e, bb=bb, j=j, pb=pb: e.tensor_tensor(
                            out=prod[pb][:], in0=bm(kt[bb][:, j, :], 4), in1=qrep[bb][:], op=ALU.mult),
                             reads=[("kt", bb), ("qrep", bb)], writes=[("prod", pb)])
                        S.op("dve", lambda e, j=j, pb=pb: e.tensor_reduce(
                            out=L[:, j, :], in_=prod[pb][:].rearrange("p t (h x) -> p (t h) x", h=8), axis=AX.X, op=ALU.add),
                             reads=[("prod", pb)], writes=["L"])
                    S.op("act", lambda e: e.activation(out=Pm[:], in_=L[:], func=AF.Exp), reads=["L"], writes=["Pm"])
                    S.op("dve", lambda e: e.tensor_tensor(out=Pm[:], in0=Pm[:], in1=mtab[:], op=ALU.mult),
                         reads=["Pm", "mtab"], writes=["Pm"])
                    S.op("dve", lambda e: e.tensor_reduce(out=Pj[:], in_=Pm[:].rearrange("p j x -> p x j"), axis=AX.X,
                                                          op=ALU.add), reads=["Pm"], writes=["Pj"])
                    for h in range(8):
                        for j in range(8):
                            S.op("pe", lambda e, h=h, j=j, bb=bb: e.matmul(
                                psb[1][0:4, h * 64:(h + 1) * 64], Pm[:, j, h:32:8], vt[bb][:, j, h * 64:(h + 1) * 64],
                                start=(j == 0), stop=(j == 7)), reads=["Pm", ("vt", bb)], writes=[("psb", 1)])
                        S.op("pe", lambda e, h=h: e.matmul(psb[2][0:4, h:h + 1], Pj[:, h:32:8], ones1[:, :], start=True,
                                                           stop=True), reads=["Pj", "ones1"], writes=[("psb", 2)])
                    S.op("dve", lambda e: e.reciprocal(out=rs[:], in_=psb[2][0:4, 0:8]), reads=[("psb", 2)], writes=["rs"])
                    S.op("dve", lambda e: e.tensor_tensor(out=v3(osb[:, :]), in0=v3(psb[1][0:4, :]), in1=bl(rs[:, :]),
                                                          op=ALU.mult), reads=[("psb", 1), "rs"], writes=["osb"])
                    S.dma("pool", lambda e, b=b: e.dma_start(out=heads_s[b * 4:(b + 1) * 4, 0:512], in_=osb[:, :]),
                          reads=["osb"], writes=[("heads_a", b)])
                if dbg:
                    dL = dscr("dbg_L", [128, 256]); dP = dscr("dbg_Pm", [128, 256]); dM = dscr("dbg_mtab", [128, 256])
                    dK = dscr("dbg_kt", [128, 8, 512]); dQ = dscr("dbg_qrep", [128, 4, 512])
                    S.dma("sp", lambda e: e.dma_start(out=dL[:, :], in_=L[:].rearrange("p j x -> p (j x)")), reads=["L"], writes=["dL"])
                    S.dma("sp", lambda e: e.dma_start(out=dP[:, :], in_=Pm[:].rearrange("p j x -> p (j x)")), reads=["Pm"], writes=["dP"])
                    S.dma("sp", lambda e: e.dma_start(out=dM[:, :], in_=mtab[:].rearrange("p j x -> p (j x)")), reads=["mtab"], writes=["dM"])
                    S.dma("sp", lambda e: e.dma_start(out=dK[:, :, :], in_=kt[1][:]), reads=[("kt", 1)], writes=["dK"])
                    S.dma("sp", lambda e: e.dma_start(out=dQ[:, :, :], in_=qrep[1][:]), reads=[("qrep", 1)], writes=["dQ"])
                S.flush()

        def sample_gdn_stage():
            with contextlib.ExitStack() as st2:
                sb2 = lambda name, shape, dt=F32: st2.enter_context(nc.sbuf_tensor("sg" + name, list(shape), dt))
                Sx = sb2("S", [128, 64, 64])
                tmp = sb2("tmp", [128, 64, 64])
                xq = sb2("xq", [128, 3, 7, 64])
                zz = sb2("zz", [128, 4, 64])
                bav = sb2("bav", [128, 4, 2])
                cw = sb2("cw", [128, 3, 4, 64])
                gv = sb2("gv", [128, 2])
                nwr = sb2("nwr", [128, 64])
                cq = sb2("cq", [128, 3, 4, 64])
                ct = sb2("ct", [128, 4, 64])
                ssq = sb2("ssq", [128, 3, 4])
                beta = sb2("beta", [128, 4])
                gg = sb2("gg", [128, 4])
                eg = sb2("eg", [128, 4])
                neg = sb2("neg", [128, 4])
                ks = sb2("ks", [128, 64])
                dl = sb2("dl", [128, 64])
                oo = sb2("oo", [128, 4, 64])
                one = sb2("one", [128, 1])
                S.op("pool", lambda e: e.memset(one[:], 1.0), writes=["one"])
                S.dma("sp", lambda e: e.dma_start(out=Sx[:].rearrange("p a b -> p (a b)"), in_=sg_in[:, :]), writes=["S"])
                S.dma("sp", lambda e: e.dma_start(out=cw[:], in_=cws[:, :, :, :]), writes=["cw"])
                S.dma("sp", lambda e: e.dma_start(out=gv[:], in_=gvs[:, :]), writes=["gv"])
                S.dma("sp", lambda e: e.dma_start(out=nwr[:], in_=gvec[2:3, :].partition_broadcast(128)), writes=["nwr"])
                for b in range(16):
                    p0 = b * 8
                    for sec in range(3):
                        q = "act" if sec == 1 else "sp"
                        S.dma(q, lambda e, b=b, p0=p0, sec=sec: e.dma_start(
                            out=xq[p0:p0 + 8, sec, 0:3, :],
                            in_=bass.AP(sc_in.tensor, b * 3 * 1536 + sec * 512, [[64, 8], [1536, 3], [1, 64]])),
                              writes=["xq"])
                        S.dma(q, lambda e, b=b, p0=p0, sec=sec: e.dma_start(
                            out=xq[p0:p0 + 8, sec, 3:7, :],
                            in_=bass.AP(gq_s.tensor, b * 4 * 1536 + sec * 512, [[64, 8], [1536, 4], [1, 64]])),
                              reads=[("sscr", 1 + sec)], writes=["xq"])
                    S.dma("act", lambda e, b=b, p0=p0: e.dma_start(
                        out=zz[p0:p0 + 8, :, :], in_=bass.AP(z_s.tensor, b * 4 * 512, [[64, 8], [512, 4], [1, 64]])),
                          reads=[("sscr", 4)], writes=["zz"])
                    S.dma("act", lambda e, b=b, p0=p0: e.dma_start(
                        out=bav[p0:p0 + 8, :, :], in_=bass.AP(ba_s.tensor, b * 4 * 16, [[1, 8], [16, 4], [8, 2]]),
                        allow_slow_non_contiguous=True), reads=[("sscr", 5)], writes=["bav"])
                for sec in range(3):
                    for j in range(4):
                        wv = cw[:, sec, j, :].unsqueeze(1).broadcast_to([128, 4, 64])
                        if j == 0:
                            S.op("dve", lambda e, sec=sec, wv=wv: e.tensor_tensor(
                                out=cq[:, sec, :, :], in0=xq[:, sec, 0:4, :], in1=wv, op=ALU.mult),
                                 reads=["xq", "cw"], writes=["cq"])
                        else:
                            S.op("pool", lambda e, sec=sec, wv=wv, j=j: e.tensor_tensor(
                                out=ct[:], in0=xq[:, sec, j:j + 4, :], in1=wv, op=ALU.mult),
                                 reads=["xq", "cw"], writes=["ct"])
                            S.op("dve", lambda e, sec=sec: e.tensor_tensor(
                                out=cq[:, sec, :, :], in0=cq[:, sec, :, :], in1=ct[:], op=ALU.add),
                                 reads=["cq", "ct"], writes=["cq"])
                S.op("act", lambda e: e.activation(out=cq[:], in_=cq[:], func=AF.Silu), reads=["cq"], writes=["cq"])
                S.op("act", lambda e: e.activation(out=zz[:], in_=zz[:], func=AF.Silu), reads=["zz"], writes=["zz"])
                for sec in range(2):
                    S.op("pool", lambda e, sec=sec: e.tensor_tensor(out=ct[:], in0=cq[:, sec, :, :], in1=cq[:, sec, :, :],
                                                                    op=ALU.mult), reads=["cq"], writes=["ct"])
                    S.op("dve", lambda e, sec=sec: e.tensor_reduce(out=ssq[:, sec, :], in_=ct[:], axis=AX.X, op=ALU.add),
                         reads=["ct"], writes=["ssq"])
                    S.op("dve", lambda e, sec=sec: e.tensor_scalar(out=ssq[:, sec, :], in0=ssq[:, sec, :], scalar1=1e-6,
                                                                   scalar2=None, op0=ALU.add), reads=["ssq"], writes=["ssq"])
                    S.op("act", lambda e, sec=sec: e.sqrt(out=ssq[:, sec, :], in_=ssq[:, sec, :]), reads=["ssq"],
                         writes=["ssq"])
                    S.op("dve", lambda e, sec=sec: e.reciprocal(out=ssq[:, sec, :], in_=ssq[:, sec, :]), reads=["ssq"],
                         writes=["ssq"])
                    sc = 0.125 if sec == 0 else 1.0
                    S.op("dve", lambda e, sec=sec, sc=sc: e.scalar_tensor_tensor(
                        out=cq[:, sec, :, :], in0=cq[:, sec, :, :], scalar=sc, in1=bl(ssq[:, sec, :]), op0=ALU.mult,
                        op1=ALU.mult), reads=["cq", "ssq"], writes=["cq"])
                S.op("act", lambda e: e.activation(out=beta[:], in_=bav[:, :, 0], func=AF.Sigmoid), reads=["bav"],
                     writes=["beta"])
                S.op("dve", lambda e: e.tensor_scalar(out=gg[:], in0=bav[:, :, 1], scalar1=gv[:, 1:2], scalar2=None,
                                                      op0=ALU.add), reads=["bav", "gv"], writes=["gg"])
                S.op("act", lambda e: e.activation(out=gg[:], in_=gg[:], func=AF.Exp), reads=["gg"], writes=["gg"])
                S.op("act", lambda e: e.activation(out=gg[:], in_=gg[:], func=AF.Ln, bias=one[:, 0:1]), reads=["gg", "one"],
                     writes=["gg"])
                S.op("act", lambda e: e.activation(out=gv[:, 0:1], in_=gv[:, 0:1], func=AF.Exp), reads=["gv"], writes=["gv"])
                S.op("dve", lambda e: e.tensor_scalar(out=gg[:], in0=gg[:], scalar1=gv[:, 0:1], scalar2=-1.0, op0=ALU.mult,
                                                      op1=ALU.mult), reads=["gg", "gv"], writes=["gg"])
                S.op("act", lambda e: e.activation(out=eg[:], in_=gg[:], func=AF.Exp), reads=["gg"], writes=["eg"])
                S.op("dve", lambda e: e.tensor_scalar(out=neg[:], in0=eg[:], scalar1=-1.0, scalar2=None, op0=ALU.mult),
                     reads=["eg"], writes=["neg"])
                ST = Sx[:].rearrange("p a b -> p b a")
                for t in range(4):
                    qv, kv, vv = cq[:, 0, t, :], cq[:, 1, t, :], cq[:, 2, t, :]
                    S.op("dve", lambda e, kv=kv: e.tensor_tensor(out=tmp[:], in0=ST, in1=bm(kv, 64), op=ALU.mult),
                         reads=["S", "cq"], writes=["tmp"])
                    S.op("dve", lambda e: e.tensor_reduce(out=ks[:], in_=tmp[:], axis=AX.X, op=ALU.add), reads=["tmp"],
                         writes=["ks"])
                    S.op("dve", lambda e, t=t, vv=vv: e.scalar_tensor_tensor(
                        out=dl[:], in0=ks[:], scalar=neg[:, t:t + 1], in1=vv, op0=ALU.mult, op1=ALU.add),
                         reads=["ks", "neg", "cq"], writes=["dl"])
                    S.op("dve", lambda e, t=t: e.tensor_scalar(out=dl[:], in0=dl[:], scalar1=beta[:, t:t + 1], scalar2=None,
                                                               op0=ALU.mult), reads=["dl", "beta"], writes=["dl"])
                    S.op("pool", lambda e, kv=kv: e.tensor_tensor(out=tmp[:], in0=bl(kv, 64), in1=bm(dl[:, :], 64),
                                                                  op=ALU.mult), reads=["cq", "dl"], writes=["tmp"])
                    S.op("dve", lambda e, t=t: e.scalar_tensor_tensor(
                        out=Sx[:], in0=Sx[:], scalar=eg[:, t:t + 1], in1=tmp[:], op0=ALU.mult, op1=ALU.add),
                         reads=["S", "eg", "tmp"], writes=["S"])
                    S.op("pool", lambda e, qv=qv: e.tensor_tensor(out=tmp[:], in0=ST, in1=bm(qv, 64), op=ALU.mult),
                         reads=["S", "cq"], writes=["tmp"])
                    S.op("dve", lambda e, t=t: e.tensor_reduce(out=oo[:, t, :], in_=tmp[:], axis=AX.X, op=ALU.add),
                         reads=["tmp"], writes=["oo"])
                S.dma("sp", lambda e: e.dma_start(out=o_sgs[:, :], in_=Sx[:].rearrange("p a b -> p (a b)")), reads=["S"],
                      writes=["o_sgs"])
                S.op("pool", lambda e: e.tensor_tensor(out=ct[:], in0=oo[:], in1=oo[:], op=ALU.mult), reads=["oo"],
                     writes=["ct"])
                S.op("dve", lambda e: e.tensor_reduce(out=ssq[:, 2, :], in_=ct[:], axis=AX.X, op=ALU.add), reads=["ct"],
                     writes=["ssq"])
                S.op("dve", lambda e: e.tensor_scalar(out=ssq[:, 2, :], in0=ssq[:, 2, :], scalar1=1.0 / 64.0, scalar2=1e-6,
                                                      op0=ALU.mult, op1=ALU.add), reads=["ssq"], writes=["ssq"])
                S.op("act", lambda e: e.sqrt(out=ssq[:, 2, :], in_=ssq[:, 2, :]), reads=["ssq"], writes=["ssq"])
                S.op("dve", lambda e: e.reciprocal(out=ssq[:, 2, :], in_=ssq[:, 2, :]), reads=["ssq"], writes=["ssq"])
                S.op("dve", lambda e: e.tensor_tensor(out=oo[:], in0=oo[:], in1=bl(ssq[:, 2, :]), op=ALU.mult),
                     reads=["oo", "ssq"], writes=["oo"])
                S.op("pool", lambda e: e.tensor_tensor(out=oo[:], in0=oo[:], in1=bm(nwr[:, :], 4), op=ALU.mult),
                     reads=["oo", "nwr"], writes=["oo"])
                S.op("dve", lambda e: e.tensor_tensor(out=oo[:], in0=oo[:], in1=zz[:], op=ALU.mult), reads=["oo", "zz"],
                     writes=["oo"])
                if dbg:
                    dC = dscr("dbg_cq", [128, 3, 4, 64]); dG = dscr("dbg_gates", [128, 3, 4])
                    S.dma("sp", lambda e: e.dma_start(out=dC[:, :, :, :], in_=cq[:]), reads=["cq"], writes=["dC"])
                    S.dma("sp", lambda e: e.dma_start(out=dG[:, 0, :], in_=beta[:]), reads=["beta"], writes=["dG0"])
                    S.dma("sp", lambda e: e.dma_start(out=dG[:, 1, :], in_=gg[:]), reads=["gg"], writes=["dG1"])
                    S.dma("sp", lambda e: e.dma_start(out=dG[:, 2, :], in_=eg[:]), reads=["eg"], writes=["dG2"])
                for b in range(16):
                    S.dma("sp", lambda e, b=b: e.dma_start(
                        out=bass.AP(heads_s.tensor, b * 4 * D + 512, [[64, 8], [D, 4], [1, 64]]), in_=oo[b * 8:(b + 1) * 8, :, :]),
                          reads=["oo"], writes=[("heads_g", b)])
                S.flush()

        def wout_stage():
            with contextlib.ExitStack() as st2:
                sb2 = lambda name, shape, dt=F32: st2.enter_context(nc.sbuf_tensor("wo" + name, list(shape), dt))
                attnT = sb2("attnT", [64, 8, 2048], BF16)
                woa = sb2("woa", [64, 8, D], BF16)
                wog = sb2("wog", [128, 4, D], BF16)
                won = sb2("won", [128, 8, D], BF16)
                hsf = sb2("hsf", [128, D])
                hsb = sb2("hsb", [128, D], BF16)
                hsT = sb2("hsT", [128, 8, 128], BF16)
                xs = [sb2("xs%d" % i, [128, D]) for i in range(2)]
                rr = [sb2("rr%d" % i, [128, D]) for i in range(2)]
                lnrep = sb2("ln", [128, 2, D])
                stt = sb2("st", [128, 2, 6])
                mv = sb2("mv", [128, 2])
                rstd = sb2("rstd", [128, 1])
                for i in range(2):
                    S.dma("sp", lambda e, i=i: e.dma_start(out=lnrep[:, i, :], in_=lnp[2 + i:3 + i, :].partition_broadcast(128)),
                          writes=[("lnrep", i)])
                S.dma("sp", lambda e: e.dma_start(out=attnT[:], in_=attn_s.rearrange("h d t -> d h t")),
                      reads=[("attn_s", h) for h in range(8)], writes=["attnT"])
                S.dma("pool", lambda e: e.dma_start(out=woa[:], in_=w_out[0:512, :].rearrange("(h d) o -> d h o", d=64)),
                      writes=["woa"])
                S.dma("pool", lambda e: e.dma_start(out=wog[:], in_=w_out[512:1024, :].rearrange("(c p) o -> p c o", p=128)),
                      writes=["wog"])
                S.dma("pool", lambda e: e.dma_start(out=won[:], in_=w_out.rearrange("(c p) o -> p c o", p=128)),
                      writes=["won"])
                S.op("pool", lambda e: e.memset(hsf[:], 0.0), writes=["hsf0"])
                S.dma("sp", lambda e: e.dma_start(out=hsf[0:64, :], in_=heads_s[:, :]),
                      reads=[("heads_a", b) for b in range(16)] + [("heads_g", b) for b in range(16)] + ["hsf0"],
                      writes=["hsf"])
                S.op("act", lambda e: e.copy(out=hsb[:], in_=hsf[:]), reads=["hsf"], writes=["hsb"])
                to_featmajor(hsb, "hsb", hsT, 0, "hsT")
                for t in range(NT):
                    b = t % 2
                    S.dma("sp", lambda e, b=b, t=t: e.dma_start(out=xs[b][:], in_=x1s[t * 128:(t + 1) * 128, :]),
                          reads=[("x1s", t)], writes=[("xs", b)])
                    for half in range(2):
                        pd = psb[4 + half]
                        hs_ = slice(half * 512, (half + 1) * 512)
                        if t == 0:
                            for kc in range(8):
                                S.op("pe", lambda e, pd=pd, kc=kc, hs_=hs_: e.matmul(
                                    pd[:, :], hsT[:, kc, :], won[:, kc, hs_], start=(kc == 0), stop=(kc == 7)),
                                     reads=["hsT", "won"], writes=[("psb", 4 + half)])
                        else:
                            ts_ = slice((t - 1) * 128, t * 128)
                            for h in range(8):
                                S.op("pe", lambda e, pd=pd, h=h, hs_=hs_, ts_=ts_: e.matmul(
                                    pd[:, :], attnT[:, h, ts_], woa[:, h, hs_], start=(h == 0), stop=False),
                                     reads=["attnT", "woa"], writes=[("psb", 4 + half)])
                            for c in range(4):
                                S.op("pe", lambda e, pd=pd, c=c, hs_=hs_, ts_=ts_: e.matmul(
                                    pd[:, :], gdnT[:, c, ts_], wog[:, c, hs_], start=False, stop=(c == 3)),
                                     reads=[("gdnT", n) for n in range(32)] + ["wog"], writes=[("psb", 4 + half)])
                        S.op("dve", lambda e, pd=pd, b=b, hs_=hs_: e.scalar_tensor_tensor(
                            out=rr[b][:, hs_], in0=xs[b][:, hs_], scalar=ALPHA, in1=pd[:, :], op0=ALU.mult, op1=ALU.add),
                             reads=[("xs", b), ("psb", 4 + half)], writes=[("rr", b)])
                    layernorm(rr[b], ("rr", b), lnrep, rr[b], ("rr", b), (stt, mv, rstd), epsmul=1.0)
                    S.dma("pool", lambda e, b=b, t=t: e.dma_start(out=x2s[t * 128:(t + 1) * 128, :], in_=rr[b][:]),
                          reads=[("rr", b)], writes=[("x2s", t)])
                S.flush()

        if "ffn1" not in stages:
            x1Tin = din("x1Tin", [128, 8, NTOK])
            S.dma("pool", lambda e: e.dma_start(out=x1T[:], in_=x1Tin[:, :, :]), writes=["x1Tinit"])
            S.flush()
        if "ffn1" in stages:
            def store1(t, tile, key):
                S.dma("pool", lambda e: e.dma_start(out=x1s[t * 128:(t + 1) * 128, :], in_=tile[:]), reads=[key],
                      writes=[("x1s", t)])
            ffn("f1", lambda t: xin[t * 128:(t + 1) * 128, :], f1g, f1u, f1d, 0, store1, x1T)

        gdnT = stp.enter_context(nc.sbuf_tensor("gdnT", [128, 4, 2048], BF16))
        if "attn" in stages:
            attention_stage()
        if "sproj" in stages:
            sample_proj_stage()
        if "sattn" in stages:
            sample_attn_stage()
        if "gdn" in stages:
            gdn_prompt_stage()
        if "sgdn" in stages:
            sample_gdn_stage()
        if "wout" in stages:
            wout_stage()
        S.flush()
        stp.close()
        if "ffn2" in stages:
            def store2(t, tile, key):
                if t == 0:
                    S.dma("pool", lambda e: e.dma_start(out=o_ys[:, :], in_=tile[0:64, :]), reads=[key], writes=[("oy", t)])
                else:
                    S.dma("pool", lambda e: e.dma_start(out=o_yp[(t - 1) * 128:t * 128, :], in_=tile[:]), reads=[key],
                          writes=[("oy", t)])
            ffn("f2", lambda t: x2s[t * 128:(t + 1) * 128, :], f2g, f2u, f2d, 4, store2, None)
        S.flush()
    return nc


def used_inputs(nc):
    names = set()
    for a in nc.allocations:
        try:
            if a.kind == "ExternalInput":
                names.add(a.name)
        except Exception:
            pass
    return names


def _t5_bucket(n):
    n = np.asarray(n, np.int64)
    nf = np.maximum(n, 1).astype(np.float64)
    large = 16 + np.floor(np.log(nf / 16.0) / math.log(2048 / 16.0) * 16.0 + 1e-9).astype(np.int64)
    large = np.minimum(large, 31)
    return np.where(n < 16, n, large)


def _onehot():
    oh = np.zeros((3, 33, 384), np.float32)
    for p, d in enumerate((1, 4, 16)):
        for m in range(383):
            dl = m - 127
            b = int(_t5_bucket(dl * d)) if 0 <= dl <= 128 else 32
            oh[p, b, m] = 1.0
    return oh


def _gconst():
    i = np.arange(64)
    P, Fr = i[:, None], i[None, :]
    g = np.zeros((64, 7, 64), np.float32)
    g[:, 0] = np.where(Fr < P, 0.0, NEG)
    g[:, 1] = np.where(Fr >= P, 0.0, NEG)
    g[:, 2] = np.where(Fr > P, -1.0, 0.0)
    g[:, 3] = np.eye(64)
    g[:, 4] = 1.0
    g[:, 5] = np.where(P <= Fr, 1.0, 0.0)
    g[:, 6] = np.where(P == 63, 1.0, 0.0)
    return g


def _gvec(inp):
    g = np.zeros((3, 64), np.float32)
    g[0, 0:8] = inp["gdn_a_log"][0]
    g[1, 0:8] = inp["gdn_dt_bias"][0]
    g[2, :] = inp["gdn_norm_w"][0]
    return g


def _wcm():
    w = np.zeros((32, 8, 4, 128), np.float32)
    for j in range(8):
        for p in range(128):
            if j < 4:
                pos = 1536 + j * 128 + p
            elif j < 7:
                pos = 16 * ((j - 4) * 32 + p // 4) + p % 4
            elif p < 4:
                pos = 2048 + p
            else:
                continue
            for t in range(4):
                dist = 2048 + t - pos
                if dist < 0:
                    continue
                for (win, d) in ((128, 1), (512, 4), (2048, 16)):
                    if dist % d == 0 and dist <= win:
                        w[int(_t5_bucket(dist)), j, t, p] += 1.0
    return w


def core_inputs(inp, c, big=None):
    f = np.float32
    xin = np.zeros((NTOK, D), f)
    xin[0:64] = inp["x_sample"][16 * c:16 * c + 16].reshape(64, D)
    xin[128:] = inp["x_prompt"][c]
    lnp = np.stack([inp["ln1_g"][0], inp["ln1_b"][0], inp["ln2_g"][0], inp["ln2_b"][0],
                    inp["ln3_g"][0], inp["ln3_b"][0]]).astype(f)
    m = {
        "xin": xin,
        "f1g": np.ascontiguousarray(inp["ffn1_w_gate"][0]), "f1u": np.ascontiguousarray(inp["ffn1_w_up"][0]),
        "f1d": np.ascontiguousarray(inp["ffn1_w_down"][0]),
        "lnp": lnp, "ident": np.eye(128, dtype=f),
        "w_in": np.ascontiguousarray(inp["w_in"][0]), "relb": np.ascontiguousarray(inp["rel_bias"]),
        "onehot": _onehot(), "antiid": np.ascontiguousarray(np.eye(128, dtype=f)[::-1]),
        "gconst": _gconst(), "convw": np.ascontiguousarray(inp["gdn_conv_w"][0]),
        "gvec": _gvec(inp),
        "sg_in": np.ascontiguousarray(inp["state_gdn"][0, 16 * c:16 * c + 16]).reshape(128, 4096),
        "sc_in": np.ascontiguousarray(inp["state_conv"][0, 16 * c:16 * c + 16]),
        "wcm": _wcm(),
        "cws": np.ascontiguousarray(np.broadcast_to(
            inp["gdn_conv_w"][0].reshape(4, 3, 8, 64).transpose(2, 1, 0, 3)[None], (16, 8, 3, 4, 64)).reshape(128, 3, 4, 64)),
        "gvs": np.ascontiguousarray(np.tile(np.stack([inp["gdn_a_log"][0], inp["gdn_dt_bias"][0]], axis=1), (16, 1))).astype(f),
        "w_out": np.ascontiguousarray(inp["w_out"][0]),
        "f2g": np.ascontiguousarray(inp["ffn2_w_gate"][0]), "f2u": np.ascontiguousarray(inp["ffn2_w_up"][0]),
        "f2d": np.ascontiguousarray(inp["ffn2_w_down"][0]),
    }
    if big is not None:
        m["ck"] = np.ascontiguousarray(big["cache_attn_k"][0, 16 * c:16 * c + 16]).reshape(16, 2048, 512)
        m["cv"] = np.ascontiguousarray(big["cache_attn_v"][0, 16 * c:16 * c + 16]).reshape(16, 2048, 512)
    return m


ALL_STAGES = ("ffn1", "attn", "sproj", "sattn", "gdn", "sgdn", "wout", "ffn2")
_NC_CACHE = {}


def gather_outputs(results):
    n = len(results)
    f = np.float32
    yp = np.stack([r["o_yp"] for r in results]).astype(f)
    ys = np.concatenate([r["o_ys"].reshape(16, 4, D) for r in results]).astype(f)
    kp = np.stack([r["o_kp"].reshape(2048, 8, 64) for r in results])[None].astype(f)
    vp = np.stack([r["o_vp"].reshape(2048, 8, 64) for r in results])[None].astype(f)
    sgp = np.stack([r["o_sgp"] for r in results])[None].astype(f)
    scp = np.stack([r["o_scp"] for r in results])[None].astype(f)
    ks = np.concatenate([r["o_ks"].reshape(16, 4, 8, 64) for r in results])[None].astype(f)
    vs = np.concatenate([r["o_vs"].reshape(16, 4, 8, 64) for r in results])[None].astype(f)
    sgs = np.concatenate([r["o_sgs"].reshape(16, 8, 64, 64) for r in results])[None].astype(f)
    scs = np.concatenate([r["o_scs"] for r in results])[None].astype(f)
    return (yp, ys, kp, vp, sgp, scp, ks, vs, sgs, scs)


def kernel(**inputs):
    inp = {k: np.asarray(v) for k, v in inputs.items()}
    if "nc" not in _NC_CACHE:
        _NC_CACHE["nc"] = build_nc(dbg=False, stages=ALL_STAGES)
    nc = _NC_CACHE["nc"]
    in_maps = [core_inputs(inp, c, inp) for c in range(8)]
    res = run_bass_kernel_spmd(nc, in_maps, core_ids=list(range(8)))
    return gather_outputs(res.results)
```

```python
import contextlib
import math
import numpy as np
import concourse.bass as bass
import concourse.mybir as mybir
from concourse.bass_utils import run_bass_kernel_spmd

F32 = mybir.dt.float32
BF16 = mybir.dt.bfloat16
ALU = mybir.AluOpType
AF = mybir.ActivationFunctionType
AX = mybir.AxisListType

D = 1024
DFF = 2816
NFC = 22
NT = 17
NTOK = NT * 128
INC = 3600
ALPHA = 2.0 ** 0.25
LN_EPS = 1e-5
NEG = -30000.0


def sl(start, count, step=1):
    return slice(start, start + (count - 1) * step + 1, step)


class Sched:
    def __init__(self, nc, stack, ndma=6):
        self.nc = nc
        self.eng = {"pe": nc.tensor, "act": nc.scalar, "dve": nc.vector, "pool": nc.gpsimd, "sp": nc.sync}
        self.esem = {k: stack.enter_context(nc.semaphore("es_" + k)) for k in self.eng}
        self.tick = {k: 0 for k in self.eng}
        self.dsem = {q: [stack.enter_context(nc.semaphore("ds_%s%d" % (q, i))) for i in range(ndma)]
                     for q in ("sp", "pool", "act")}
        self.duse = {q: [0] * ndma for q in self.dsem}
        self.dcnt = {q: 0 for q in self.dsem}
        self.seen = {k: {} for k in self.eng}
        self.ops = []

    def op(self, eng, fn, reads=(), writes=()):
        self.ops.append(dict(eng=eng, fn=fn, reads=tuple(reads), writes=tuple(writes), dma=False))

    def dma(self, q, fn, reads=(), writes=()):
        self.ops.append(dict(eng=q, fn=fn, reads=tuple(reads), writes=tuple(writes), dma=True))

    def capture(self, fn):
        saved = self.ops
        self.ops = []
        fn()
        got = self.ops
        self.ops = saved
        return got

    def emit_merged(self, a, b):
        i = j = 0
        while i < len(a) or j < len(b):
            if j >= len(b) or (i < len(a) and i * len(b) <= j * len(a)):
                self.ops.append(a[i])
                i += 1
            else:
                self.ops.append(b[j])
                j += 1

    def _wait(self, e, sem, val):
        key = id(sem)
        if self.seen[e].get(key, 0) < val:
            self.eng[e].wait_ge(sem, val)
            self.seen[e][key] = val

    def flush(self, barrier=True):
        ops = self.ops
        self.ops = []
        last_w = {}
        readers = {}
        needs = [False] * len(ops)
        for i, o in enumerate(ops):
            deps = set()

            def inorder(j):
                return ops[j]["eng"] == o["eng"] == "pe" and not ops[j]["dma"] and not o["dma"]

            for r in o["reads"]:
                j = last_w.get(r)
                if j is not None and not (inorder(j) and o["eng"] == "pe"):
                    deps.add(j)
            for w in o["writes"]:
                j = last_w.get(w)
                if j is not None and not inorder(j):
                    deps.add(j)
                for j in readers.get(w, ()):
                    if not inorder(j):
                        deps.add(j)
            o["deps"] = sorted(deps)
            for j in deps:
                needs[j] = True
            for r in o["reads"]:
                readers.setdefault(r, []).append(i)
            for w in o["writes"]:
                last_w[w] = i
                readers[w] = []
        lastop = {}
        for i, o in enumerate(ops):
            if not o["dma"]:
                lastop[o["eng"]] = i
        for i in lastop.values():
            needs[i] = True
        for i, o in enumerate(ops):
            e = o["eng"]
            for j in o["deps"]:
                ev = ops[j]["event"]
                self._wait(e, ev[0], ev[1])
            if o["dma"]:
                n = len(self.dsem[e])
                slot = self.dcnt[e] % n
                self.dcnt[e] += 1
                sem = self.dsem[e][slot]
                k = self.duse[e][slot]
                if k > 0:
                    self._wait(e, sem, 16 * k)
                ins = o["fn"](self.eng[e])
                ins.then_inc(sem, 16)
                self.duse[e][slot] = k + 1
                o["event"] = (sem, 16 * (k + 1))
            else:
                ins = o["fn"](self.eng[e])
                if needs[i]:
                    self.tick[e] += 1
                    ins.then_inc(self.esem[e], 1)
                    o["event"] = (self.esem[e], self.tick[e])
                else:
                    o["event"] = None
        if barrier:
            self.barrier()

    def barrier(self):
        for e in self.eng:
            for d in self.eng:
                if d != e and self.tick[d] > 0:
                    self._wait(e, self.esem[d], self.tick[d])
            for q in self.dsem:
                for s, k in zip(self.dsem[q], self.duse[q]):
                    if k > 0:
                        self._wait(e, s, 16 * k)


def build_nc(dbg=False, stages=("ffn1",)):
    nc = bass.Bass("TRN2", target_bir_lowering=False)

    def din(name, shape, dt=F32):
        return nc.dram_tensor(name, list(shape), dt, kind="ExternalInput").ap()

    def dout(name, shape, dt=F32):
        return nc.dram_tensor(name, list(shape), dt, kind="ExternalOutput").ap()

    def dscr(name, shape, dt=F32):
        return nc.dram_tensor(name, list(shape), dt, kind="ExternalOutput" if dbg else "Internal").ap()

    xin = din("xin", [NTOK, D])
    f1g = din("f1g", [D, DFF])
    f1u = din("f1u", [D, DFF])
    f1d = din("f1d", [DFF, D])
    lnp = din("lnp", [6, D])
    ident_d = din("ident", [128, 128])
    x1s = dscr("x1s", [NTOK, D])
    w_in = din("w_in", [D, INC])
    relb = din("relb", [32, 8])
    onehot = din("onehot", [3, 33, 384])
    antiid = din("antiid", [128, 128])
    o_kp = dout("o_kp", [2048, 512])
    o_vp = dout("o_vp", [2048, 512])
    o_ks = dout("o_ks", [64, 512])
    o_vs = dout("o_vs", [64, 512])
    fvd = dscr("fvd", [3, 8, 384])
    attn_s = dscr("attn_s", [8, 64, 2048], BF16)
    gconst = din("gconst", [64, 7, 64])
    convw = din("convw", [4, 1536])
    gvec = din("gvec", [3, 64])
    o_sgp = dout("o_sgp", [8, 64, 64])
    o_scp = dout("o_scp", [3, 1536])
    gdn_dbg = dscr("gdn_dbg", [128, 4, 2048]) if dbg else None
    ck = din("ck", [16, 2048, 512])
    cv = din("cv", [16, 2048, 512])
    sg_in = din("sg_in", [128, 4096])
    sc_in = din("sc_in", [16, 3, 1536])
    wcm = din("wcm", [32, 8, 4, 128])
    cws = din("cws", [128, 3, 4, 64])
    gvs = din("gvs", [128, 2])
    w_out = din("w_out", [D, D])
    f2g = din("f2g", [D, DFF])
    f2u = din("f2u", [D, DFF])
    f2d = din("f2d", [DFF, D])
    o_ys = dout("o_ys", [64, D])
    o_yp = dout("o_yp", [2048, D])
    o_sgs = dout("o_sgs", [128, 4096])
    o_scs = dout("o_scs", [16, 3, 1536])
    qs_s = dscr("qs_s", [64, 512])
    gq_s = dscr("gq_s", [64, 1536])
    z_s = dscr("z_s", [64, 512])
    ba_s = dscr("ba_s", [64, 16])
    heads_s = dscr("heads_s", [64, D])
    x2s = dscr("x2s", [NTOK, D])

    with contextlib.ExitStack() as stack:
        S = Sched(nc, stack)
        sb = lambda name, shape, dt=F32: stack.enter_context(nc.sbuf_tensor(name, list(shape), dt))
        psb = [stack.enter_context(nc.psum_tensor("psb%d" % i, [128, 512], F32)) for i in range(7)]
        pst = stack.enter_context(nc.psum_tensor("pst", [128, 1024], BF16))

        identf = sb("identf", [128, 128])
        identb = sb("identb", [128, 128], BF16)
        stp = contextlib.ExitStack()
        x1T = stp.enter_context(nc.sbuf_tensor("x1T", [128, 8, NTOK], BF16))

        S.dma("sp", lambda e: e.dma_start(out=identf[:], in_=ident_d[:, :]), writes=["identf"])
        S.op("dve", lambda e: e.tensor_copy(out=identb[:], in_=identf[:]), reads=["identf"], writes=["identb"])
        S.flush()

        def to_featmajor(src_bf, skey, dstT, col0, dkey):
            for kc in range(8):
                S.op("pe", lambda e, kc=kc: e.transpose(pst[:, kc * 128:(kc + 1) * 128],
                                                        src_bf[:, kc * 128:(kc + 1) * 128], identb[:]),
                     reads=[skey, "identb"], writes=["pst"])
            S.op("dve", lambda e: e.tensor_copy(out=dstT[:, :, col0:col0 + 128],
                                               in_=pst[:].rearrange("p (k c) -> p k c", k=8)),
                 reads=["pst"], writes=[dkey])

        def layernorm(r, rkey, lnrep, out, okey, tmp, epsmul=4.0):
            st, mv, rstd = tmp
            for c in range(2):
                S.op("dve", lambda e, c=c: e.bn_stats(out=st[:, c, :], in_=r[:, c * 512:(c + 1) * 512]),
                     reads=[rkey], writes=[("st", c)])
            S.op("dve", lambda e: e.bn_aggr(out=mv[:], in_=st[:].rearrange("p c s -> p (c s)")),
                 reads=[("st", 0), ("st", 1)], writes=["mv"])
            S.op("dve", lambda e: e.tensor_scalar(out=rstd[:], in0=mv[:, 1:2], scalar1=epsmul * LN_EPS, scalar2=None,
                                                  op0=ALU.add), reads=["mv"], writes=["rstd"])
            S.op("act", lambda e: e.sqrt(out=rstd[:], in_=rstd[:]), reads=["rstd"], writes=["rstd"])
            S.op("dve", lambda e: e.reciprocal(out=rstd[:], in_=rstd[:]), reads=["rstd"], writes=["rstd"])
            S.op("dve", lambda e: e.tensor_scalar(out=r[:], in0=r[:], scalar1=mv[:, 0:1], scalar2=rstd[:, 0:1],
                                                  op0=ALU.subtract, op1=ALU.mult),
                 reads=[rkey, "mv", "rstd"], writes=[rkey])
            S.op("pool", lambda e: e.tensor_tensor(out=r[:], in0=r[:], in1=lnrep[:, 0, :], op=ALU.mult),
                 reads=[rkey, ("lnrep", 0)], writes=[rkey])
            S.op("dve", lambda e: e.tensor_tensor(out=out[:], in0=r[:], in1=lnrep[:, 1, :], op=ALU.add),
                 reads=[rkey, ("lnrep", 1)], writes=[okey])

        def ffn(tag, xsrc, wg, wu, wd, gi, store, xTout):
            with contextlib.ExitStack() as st2:
                sb2 = lambda name, shape, dt=F32: st2.enter_context(nc.sbuf_tensor(tag + name, list(shape), dt))
                MT = 9
                xT = sb2("xT", [128, 8, MT * 128], BF16)
                hT = sb2("hT", [128, NFC, MT * 128], BF16)
                wgb = [sb2("wgb%d" % i, [128, 8, 256], BF16) for i in range(2)]
                wub = [sb2("wub%d" % i, [128, 8, 256], BF16) for i in range(2)]
                wdb = sb2("wdb", [128, NFC, D], BF16)
                xs = [sb2("xs%d" % i, [128, D]) for i in range(2)]
                xb = [sb2("xb%d" % i, [128, D], BF16) for i in range(2)]
                sg = [sb2("sg%d" % i, [128, 512]) for i in range(2)]
                rr = [sb2("rr%d" % i, [128, D]) for i in range(2)]
                oo = rr
                lnrep = sb2("ln", [128, 2, D])
                for i in range(2):
                    S.dma("sp", lambda e, i=i: e.dma_start(
                        out=lnrep[:, i, :], in_=lnp[gi + i:gi + i + 1, :].partition_broadcast(128)),
                          writes=[("lnrep", i)])
                stt = sb2("st", [128, 2, 6])
                mv = sb2("mv", [128, 2])
                rstd = sb2("rstd", [128, 1])
                wgv = wg.rearrange("(kc p) f -> p kc f", p=128)
                wuv = wu.rearrange("(kc p) f -> p kc f", p=128)
                wdv = wd.rearrange("(fc p) d -> p fc d", p=128)
                for mi, tiles in enumerate((list(range(0, MT)), list(range(MT, NT)))):
                    ntl = len(tiles)
                    for li, t in enumerate(tiles):
                        b = li % 2
                        S.dma("sp", lambda e, b=b, t=t: e.dma_start(out=xs[b][:], in_=xsrc(t)), writes=[("xs", b)])
                        S.op("act", lambda e, b=b: e.copy(out=xb[b][:], in_=xs[b][:]), reads=[("xs", b)],
                             writes=[("xb", b)])
                        to_featmajor(xb[b], ("xb", b), xT, li * 128, ("xT", li))
                    if mi == 0:
                        for q in range(2):
                            S.dma("pool", lambda e, q=q: e.dma_start(out=wdb[:, q * 11:(q + 1) * 11, :],
                                                                     in_=wdv[:, q * 11:(q + 1) * 11, :]),
                                  writes=[("wdb", q)])
                    tbs = [(c0, min(512, ntl * 128 - c0)) for c0 in range(0, ntl * 128, 512)]
                    for fb in range(11):
                        wbuf = fb % 2
                        S.dma("pool", lambda e, fb=fb, wbuf=wbuf: e.dma_start(
                            out=wgb[wbuf][:], in_=wgv[:, :, fb * 256:(fb + 1) * 256]), writes=[("wgb", wbuf)])
                        S.dma("pool", lambda e, fb=fb, wbuf=wbuf: e.dma_start(
                            out=wub[wbuf][:], in_=wuv[:, :, fb * 256:(fb + 1) * 256]), writes=[("wub", wbuf)])
                        for j in range(2):
                            fc = fb * 2 + j
                            for ti, (c0, cw) in enumerate(tbs):
                                pb = (fc * len(tbs) + ti) % 2
                                pg, pu = psb[pb], psb[2 + pb]
                                xkeys = [("xT", li) for li in range(c0 // 128, (c0 + cw) // 128)]
                                for kc in range(8):
                                    S.op("pe", lambda e, pg=pg, kc=kc, wbuf=wbuf, j=j, c0=c0, cw=cw: e.matmul(
                                        pg[:, 0:cw], wgb[wbuf][:, kc, j * 128:(j + 1) * 128], xT[:, kc, c0:c0 + cw],
                                        start=(kc == 0), stop=(kc == 7)),
                                         reads=[("wgb", wbuf)] + xkeys, writes=[("psb", pb)])
                                for kc in range(8):
                                    S.op("pe", lambda e, pu=pu, kc=kc, wbuf=wbuf, j=j, c0=c0, cw=cw: e.matmul(
                                        pu[:, 0:cw], wub[wbuf][:, kc, j * 128:(j + 1) * 128], xT[:, kc, c0:c0 + cw],
                                        start=(kc == 0), stop=(kc == 7)),
                                         reads=[("wub", wbuf)] + xkeys, writes=[("psb", 2 + pb)])
                                S.op("act", lambda e, pg=pg, pb=pb, cw=cw: e.activation(
                                    out=sg[pb][:, 0:cw], in_=pg[:, 0:cw], func=AF.Silu),
                                     reads=[("psb", pb)], writes=[("sg", pb)])
                                S.op("dve", lambda e, pu=pu, pb=pb, fc=fc, c0=c0, cw=cw: e.tensor_tensor(
                                    out=hT[:, fc, c0:c0 + cw], in0=sg[pb][:, 0:cw], in1=pu[:, 0:cw], op=ALU.mult),
                                     reads=[("sg", pb), ("psb", 2 + pb)], writes=[("hT", fc, ti)])
                    for li, t in enumerate(tiles):
                        b = li % 2
                        ti = li // 4
                        S.dma("sp", lambda e, b=b, t=t: e.dma_start(out=xs[b][:], in_=xsrc(t)), writes=[("xs", b)])
                        for half in range(2):
                            pd = psb[4 + half]
                            for fc in range(NFC):
                                S.op("pe", lambda e, pd=pd, fc=fc, li=li, half=half: e.matmul(
                                    pd[:, :], hT[:, fc, li * 128:(li + 1) * 128], wdb[:, fc, half * 512:(half + 1) * 512],
                                    start=(fc == 0), stop=(fc == NFC - 1)),
                                     reads=[("hT", fc, ti), ("wdb", fc // 11)], writes=[("psb", 4 + half)])
                            S.op("dve", lambda e, pd=pd, b=b, half=half: e.scalar_tensor_tensor(
                                out=rr[b][:, half * 512:(half + 1) * 512], in0=xs[b][:, half * 512:(half + 1) * 512],
                                scalar=2.0 * ALPHA, in1=pd[:, :], op0=ALU.mult, op1=ALU.add),
                                 reads=[("xs", b), ("psb", 4 + half)], writes=[("rr", b)])
                        layernorm(rr[b], ("rr", b), lnrep, rr[b], ("rr", b), (stt, mv, rstd))
                        store(t, rr[b], ("rr", b))
                        if xTout is not None:
                            S.op("act", lambda e, b=b: e.copy(out=xb[b][:], in_=rr[b][:]), reads=[("rr", b)],
                                 writes=[("xb", b)])
                            to_featmajor(xb[b], ("xb", b), xTout, t * 128, ("xTo", t))
                S.flush()

        PAT = ((128, 1), (512, 4), (2048, 16))

        def unit_tokens(d, u):
            nblk = 16 // d
            r, n = u // nblk, u % nblk
            return r, n, nblk, r + d * 128 * n

        def attention_stage():
            with contextlib.ExitStack() as st2:
                sb2 = lambda name, shape, dt=F32: st2.enter_context(nc.sbuf_tensor("at" + name, list(shape), dt))
                attnT = [sb2("attnT%d" % i, [64, 2048], BF16) for i in range(2)]
                qT = sb2("qT", [128, 4, 2048], BF16)
                kT = sb2("kT", [128, 4, 2048], BF16)
                vaug = [sb2("vaug%d" % i, [128, 16, 8, 65], BF16) for i in range(3)]
                brev = sb2("brev", [128, 24, 256], BF16)
                jb = sb2("jb", [128, 128], BF16)
                onesf = sb2("onesf", [128, 64])
                rbx = sb2("rbx", [33, 8])
                ohs = sb2("ohs", [33, 3, 384])
                fvs = sb2("fvs", [8, 3, 384])
                winv = w_in.rearrange("(kc p) f -> p kc f", p=128)
                S.dma("pool", lambda e: e.dma_start(out=jb[:], in_=antiid[:, :]), writes=["jb"])
                S.op("pool", lambda e: e.memset(onesf[:], 1.0), writes=["onesf"])
                S.op("pool", lambda e: e.memset(rbx[32:33, :], NEG), writes=["rbx1"])
                S.dma("sp", lambda e: e.dma_start(out=rbx[0:32, :], in_=relb[:, :]), writes=["rbx0"])
                S.dma("sp", lambda e: e.dma_start(out=ohs[:], in_=onehot.rearrange("p b m -> b p m")), writes=["ohs"])
                for p in range(3 if "notables" not in stages else 0):
                    S.op("pe", lambda e, p=p: e.matmul(psb[p][0:8, 0:384], rbx[:, :], ohs[:, p, :], start=True, stop=True),
                         reads=["rbx0", "rbx1", "ohs"], writes=[("psb", p)])
                    S.op("dve", lambda e, p=p: e.tensor_copy(out=fvs[:, p, :], in_=psb[p][0:8, 0:384]),
                         reads=[("psb", p)], writes=[("fvs", p)])
                if "notables" not in stages:
                    S.dma("sp", lambda e: e.dma_start(out=fvd.rearrange("p h m -> h p m"), in_=fvs[:]),
                          reads=[("fvs", p) for p in range(3)], writes=["fvd"])
                if "nohankel" not in stages:
                    S.dma("pool", lambda e: e.dma_start(
                        out=brev[:], in_=bass.AP(fvd.tensor, 0, [[1, 128], [384, 24], [1, 256]])),
                          reads=["fvd"], writes=["brev"])
                for i in range(3):
                    S.op("pool", lambda e, i=i: e.memset(vaug[i][:, :, :, 64:65], 1.0), writes=[("vone", i)])
                with contextlib.ExitStack() as st3:
                    sb3 = lambda name, shape, dt=F32: st3.enter_context(nc.sbuf_tensor("ap" + name, list(shape), dt))
                    wb = [sb3("wb%d" % i, [128, 8, 512], BF16) for i in range(3)]
                    kvo = [sb3("kvo%d" % i, [128, 512]) for i in range(2)]
                    for blk in range(3):
                        S.dma("pool", lambda e, blk=blk: e.dma_start(out=wb[blk][:], in_=winv[:, :, blk * 512:(blk + 1) * 512]),
                              writes=[("wb", blk)])
                    cnt = 0
                    if "noproj" in stages:
                        S.flush()
                        return
                    for blk, dst in (((0, qT), (1, kT)) if "noqk" not in stages else ()):
                        for pair in range(4):
                            for tb in range(4):
                                pb = cnt % 2
                                cnt += 1
                                for kc in range(8):
                                    S.op("pe", lambda e, pb=pb, blk=blk, pair=pair, tb=tb, kc=kc: e.matmul(
                                        psb[pb][:, :], wb[blk][:, kc, pair * 128:(pair + 1) * 128],
                                        x1T[:, kc, 128 + tb * 512:128 + (tb + 1) * 512], start=(kc == 0), stop=(kc == 7)),
                                         reads=[("wb", blk)], writes=[("psb", pb)])
                                if blk == 0:
                                    S.op("act", lambda e, pb=pb, pair=pair, tb=tb: e.mul(
                                        out=qT[:, pair, tb * 512:(tb + 1) * 512], in_=psb[pb][:, :], mul=0.125),
                                         reads=[("psb", pb)], writes=[("qT", pair)])
                                else:
                                    S.op("dve", lambda e, pb=pb, pair=pair, tb=tb: e.tensor_copy(
                                        out=kT[:, pair, tb * 512:(tb + 1) * 512], in_=psb[pb][:, :]),
                                         reads=[("psb", pb)], writes=[("kT", pair)])
                    for t in range(NT if "nokv" not in stages else 0):
                        for blk in (1, 2):
                            pb = 2 + (cnt % 2)
                            cnt += 1
                            ob = blk - 1
                            for kc in range(8):
                                S.op("pe", lambda e, pb=pb, blk=blk, t=t, kc=kc: e.matmul(
                                    psb[pb][:, :], x1T[:, kc, t * 128:(t + 1) * 128], wb[blk][:, kc, :],
                                    start=(kc == 0), stop=(kc == 7)),
                                     reads=[("wb", blk)], writes=[("psb", pb)])
                            S.op("act", lambda e, pb=pb, ob=ob: e.activation(out=kvo[ob][:], in_=psb[pb][:, :], func=AF.Copy),
                                 reads=[("psb", pb)], writes=[("kvo", ob)])
                            if blk == 2 and t >= 1:
                                S.op("dve", lambda e, ob=ob, t=t: e.tensor_copy(
                                    out=vaug[0][:, t - 1, :, 0:64], in_=kvo[ob][:, :].rearrange("p (h e) -> p h e", h=8)),
                                     reads=[("kvo", ob)], writes=[("vaug", 0, t - 1)])
                            if "nokvdma" in stages:
                                continue
                            if t == 0:
                                dst = (o_ks if blk == 1 else o_vs)[0:64, :]
                                S.dma("sp", lambda e, dst=dst, ob=ob: e.dma_start(out=dst, in_=kvo[ob][0:64, :]),
                                      reads=[("kvo", ob)], writes=[("okv", blk, t)])
                            else:
                                dst = (o_kp if blk == 1 else o_vp)[(t - 1) * 128:t * 128, :]
                                S.dma("sp", lambda e, dst=dst, ob=ob: e.dma_start(out=dst, in_=kvo[ob][:, :]),
                                      reads=[("kvo", ob)], writes=[("okv", blk, t)])
                    for pi in ((1, 2) if "nodil" not in stages else ()):
                        d = PAT[pi][1]
                        for u in range(16):
                            r, n, nblk, t0 = unit_tokens(d, u)
                            pb = 2 + (cnt % 2)
                            cnt += 1
                            for kc in range(8):
                                S.op("pe", lambda e, pb=pb, kc=kc, t0=t0, d=d: e.matmul(
                                    psb[pb][:, :], x1T[:, kc, sl(128 + t0, 128, d)], wb[2][:, kc, :],
                                    start=(kc == 0), stop=(kc == 7)),
                                     reads=[("wb", 2)], writes=[("psb", pb)])
                            S.op("dve", lambda e, pb=pb, pi=pi, u=u: e.tensor_copy(
                                out=vaug[pi][:, u, :, 0:64], in_=psb[pb][:, :].rearrange("p (h e) -> p h e", h=8)),
                                 reads=[("psb", pb)], writes=[("vaug", pi, u)])
                    S.flush()
                if "noattnmain" in stages:
                    return
                with contextlib.ExitStack() as st3:
                    sb3 = lambda name, shape, dt=F32: st3.enter_context(nc.sbuf_tensor("aa" + name, list(shape), dt))
                    acc = [sb3("acc%d" % i, [65, 2048]) for i in range(2)]
                    pts = [sb3("pt%d" % i, [128, 256], BF16) for i in range(4)]
                    rcp = sb3("rcp", [65, 2048])
                    ptc = 0
                    cnt = 0
                    for h in range(8):
                        pair, base = h // 2, (h % 2) * 64
                        ab = h % 2
                        A = acc[ab]
                        units = [(pi, d, u) for pi, (win, d) in enumerate(PAT) for u in range(16)]

                        def st_part(ix, h=h, pair=pair, base=base):
                            pi, d, u = units[ix]
                            r, n, nblk, t0 = unit_tokens(d, u)
                            W = 256 if n + 1 < nblk else 128
                            ps = ix % 2
                            pt = ix % 4
                            S.op("pe", lambda e: e.matmul(
                                psb[ps][:, 0:W], kT[base:base + 64, pair, sl(t0, 128, d)],
                                qT[base:base + 64, pair, sl(t0, W, d)], start=True, stop=False),
                                 reads=[("qT", pair), ("kT", pair)], writes=[("psb", ps)])
                            S.op("pe", lambda e: e.matmul(
                                psb[ps][:, 0:W], jb[:, :], brev[:, pi * 8 + h, 0:W], start=False, stop=True),
                                 reads=["jb", "brev"], writes=[("psb", ps)])
                            S.op("act", lambda e: e.activation(
                                out=pts[pt][:, 0:W], in_=psb[ps][:, 0:W], func=AF.Exp),
                                 reads=[("psb", ps)], writes=[("pt", pt)])

                        def pv_part(ix, h=h, A=A, ab=ab):
                            pi, d, u = units[ix]
                            r, n, nblk, t0 = unit_tokens(d, u)
                            pt = ix % 4
                            prev = (ix - 1) % 4
                            po = 2 + (ix % 2)
                            first = (n == 0)
                            S.op("pe", lambda e: e.matmul(
                                psb[po][0:65, 0:128], vaug[pi][:, u, h, :], pts[pt][:, 0:128], start=True, stop=first),
                                 reads=[("vaug", pi, u), ("vone", pi), ("pt", pt)], writes=[("psb", po)])
                            if not first:
                                S.op("pe", lambda e: e.matmul(
                                    psb[po][0:65, 0:128], vaug[pi][:, u - 1, h, :], pts[prev][:, 128:256],
                                    start=False, stop=True),
                                     reads=[("vaug", pi, u - 1), ("vone", pi), ("pt", prev)], writes=[("psb", po)])
                            dst = A[:, sl(t0, 128, d)]
                            if pi == 0:
                                S.op("dve", lambda e: e.tensor_copy(out=dst, in_=psb[po][0:65, 0:128]),
                                     reads=[("psb", po)], writes=[("acc", ab)])
                            else:
                                S.op("dve", lambda e: e.tensor_tensor(
                                    out=dst, in0=dst, in1=psb[po][0:65, 0:128], op=ALU.add),
                                     reads=[("psb", po), ("acc", ab)], writes=[("acc", ab)])

                        st_part(0)
                        st_part(1)
                        for ix in range(len(units)):
                            if ix + 2 < len(units):
                                st_part(ix + 2)
                            pv_part(ix)
                        S.op("dve", lambda e, A=A: e.reciprocal(out=rcp[64:65, :], in_=A[64:65, :]),
                             reads=[("acc", ab)], writes=["rcp"])
                        for tb in range(4):
                            pr = 4 + (tb % 2)
                            S.op("pe", lambda e, pr=pr, tb=tb: e.matmul(
                                psb[pr][0:64, :], onesf[64:65, 0:64], rcp[64:65, tb * 512:(tb + 1) * 512],
                                start=True, stop=True), reads=["rcp", "onesf"], writes=[("psb", pr)])
                            S.op("dve", lambda e, pr=pr, tb=tb, A=A, ab=ab: e.tensor_tensor(
                                out=attnT[ab][:, tb * 512:(tb + 1) * 512], in0=A[0:64, tb * 512:(tb + 1) * 512],
                                in1=psb[pr][0:64, :], op=ALU.mult),
                                 reads=[("psb", pr), ("acc", ab)], writes=[("attnT", ab)])
                        S.dma("sp", lambda e, h=h, ab=ab: e.dma_start(out=attn_s[h, :, :], in_=attnT[ab][:, :]),
                              reads=[("attnT", ab)], writes=[("attn_s", h)])
                    S.flush()

        def bl(ap, n=64):
            return ap.unsqueeze(2).broadcast_to([ap.shape[0], ap.shape[1], n])

        def bm(ap, n=8):
            return ap.unsqueeze(1).broadcast_to([ap.shape[0], n, ap.shape[1]])

        def v3(ap, h=8):
            return ap.rearrange("p (h x) -> p h x", h=h)

        def gdn_prompt_stage():
            with contextlib.ExitStack() as st2:
                sb2 = lambda name, shape, dt=F32: st2.enter_context(nc.sbuf_tensor("gd" + name, list(shape), dt))
                qh = sb2("qh", [64, 8, 2048], BF16)
                kh = sb2("kh", [64, 8, 2048], BF16)
                vT = sb2("vT", [128, 4, 2048], BF16)
                gcn = sb2("gcn", [64, 7, 64])
                NEGS, NEGT, MSKT, ID64, ONES, TRI, SEL = [gcn[:, i, :] for i in range(7)]
                cwq = sb2("cwq", [64, 16, 4])
                cwv = sb2("cwv", [128, 4, 4])
                nwr = sb2("nwr", [64, 64])
                wz = sb2("wz", [128, 8, 512], BF16)
                wba = sb2("wba", [128, 8, 16], BF16)
                stc = contextlib.ExitStack()
                cwr = stc.enter_context(nc.sbuf_tensor("gdcwr", [4, 1536], F32))
                winv = w_in.rearrange("(kc p) f -> p kc f", p=128)
                S.dma("sp", lambda e: e.dma_start(out=gcn[:], in_=gconst[:, :, :]), writes=["gcn"])
                S.dma("sp", lambda e: e.dma_start(out=cwr[:], in_=convw[:, :]), writes=["cwr"])
                S.dma("sp", lambda e: e.dma_start(out=nwr[:], in_=gvec[2:3, :].partition_broadcast(64)), writes=["nwr"])
                S.dma("pool", lambda e: e.dma_start(out=wz[:], in_=winv[:, :, 3072:3584]), writes=["wz"])
                S.dma("pool", lambda e: e.dma_start(out=wba[:], in_=winv[:, :, 3584:3600]), writes=["wba"])
                for g in range(16):
                    S.op("pe", lambda e, g=g: e.transpose(psb[0][0:64, g * 4:(g + 1) * 4], cwr[0:4, g * 64:(g + 1) * 64],
                                                          identf[0:4, 0:4]), reads=["cwr", "identf"], writes=[("psb", 0)])
                for c in range(4):
                    S.op("pe", lambda e, c=c: e.transpose(psb[1][:, c * 4:(c + 1) * 4],
                                                          cwr[0:4, 1024 + c * 128:1024 + (c + 1) * 128], identf[0:4, 0:4]),
                         reads=["cwr", "identf"], writes=[("psb", 1)])
                S.op("dve", lambda e: e.tensor_copy(out=cwq[:], in_=v3(psb[0][0:64, 0:64], 16)), reads=[("psb", 0)],
                     writes=["cwq"])
                S.op("dve", lambda e: e.tensor_copy(out=cwv[:], in_=v3(psb[1][:, 0:16], 4)), reads=[("psb", 1)],
                     writes=["cwv"])
                S.flush()
                stc.close()

                with contextlib.ExitStack() as st3:
                    sb3 = lambda name, shape, dt=F32: st3.enter_context(nc.sbuf_tensor("g1" + name, list(shape), dt))
                    wb = [sb3("wb%d" % i, [128, 8, 512], BF16) for i in range(2)]
                    raws = [sb3("raw%d" % i, [128, 2051]) for i in range(2)]
                    cacs = [sb3("cac%d" % i, [128, 2048]) for i in range(2)]
                    rin = [sb3("rin%d" % i, [64, 512]) for i in range(2)]
                    for i in range(2):
                        S.op("pool", lambda e, i=i: e.memset(raws[i][:, 0:3], 0.0), writes=[("raw0", i)])
                    epsb = sb3("epsb", [64, 2])
                    S.op("pool", lambda e: e.memset(epsb[:, 0:1], 64.0e-6), writes=["epsb"])
                    S.op("pool", lambda e: e.memset(epsb[:, 1:2], 1.0e-6), writes=["epsb"])
                    cnt = 0
                    gi = 0
                    for blk in range(3):
                        wbuf = blk % 2
                        S.dma("pool", lambda e, blk=blk, wbuf=wbuf: e.dma_start(
                            out=wb[wbuf][:], in_=winv[:, :, 1536 + blk * 512:1536 + (blk + 1) * 512]),
                              writes=[("wb", wbuf)])
                        ngrp, P = (8, 64) if blk < 2 else (4, 128)
                        for g in range(ngrp):
                            rb_ = gi % 2
                            gi += 1
                            raw, cac = raws[rb_], cacs[rb_]
                            RAW, CAC = ("raw", rb_), ("cac", rb_)
                            for tb in range(4):
                                pb = cnt % 2
                                cnt += 1
                                for kc in range(8):
                                    S.op("pe", lambda e, pb=pb, wbuf=wbuf, g=g, P=P, tb=tb, kc=kc: e.matmul(
                                        psb[pb][0:P, :], wb[wbuf][:, kc, g * P:(g + 1) * P],
                                        x1T[:, kc, 128 + tb * 512:128 + (tb + 1) * 512], start=(kc == 0), stop=(kc == 7)),
                                         reads=[("wb", wbuf)], writes=[("psb", pb)])
                                S.op("act", lambda e, pb=pb, P=P, tb=tb, raw=raw: e.activation(
                                    out=raw[0:P, 3 + tb * 512:3 + (tb + 1) * 512], in_=psb[pb][0:P, :], func=AF.Copy),
                                     reads=[("psb", pb)], writes=[RAW])
                            col0 = blk * 512 + g * P
                            S.dma("sp", lambda e, P=P, col0=col0, raw=raw: e.dma_start(
                                out=o_scp[:, col0:col0 + P].rearrange("j p -> p j"), in_=raw[0:P, 2048:2051],
                                allow_slow_non_contiguous=True), reads=[RAW], writes=[("o_scp", col0)])
                            cwt = (cwq[:, blk * 8 + g, :] if blk < 2 else cwv[:, g, :])
                            CH = ("cach", rb_, 0)
                            S.op("dve", lambda e, P=P, cwt=cwt, raw=raw, cac=cac: e.tensor_scalar(
                                out=cac[0:P, :], in0=raw[0:P, 3:2051], scalar1=cwt[:, 3:4], scalar2=None, op0=ALU.mult),
                                 reads=[RAW, ("raw0", rb_), "cwq", "cwv"], writes=[CH])
                            for j in (2, 1, 0):
                                S.op("dve", lambda e, P=P, cwt=cwt, j=j, raw=raw, cac=cac: e.scalar_tensor_tensor(
                                    out=cac[0:P, :], in0=raw[0:P, j:j + 2048], scalar=cwt[:, j:j + 1], in1=cac[0:P, :],
                                    op0=ALU.mult, op1=ALU.add), reads=[RAW, ("raw0", rb_), CH], writes=[CH])
                            CHS = [("cach", rb_, 0)]
                            if blk == 2:
                                S.op("act", lambda e, g=g, cac=cac: e.activation(out=vT[:, g, :], in_=cac[:, :], func=AF.Silu),
                                     reads=CHS, writes=[("vT", g)])
                                continue
                            S.op("act", lambda e, cac=cac: e.activation(out=cac[0:64, :], in_=cac[0:64, :], func=AF.Silu),
                                 reads=CHS, writes=[CAC])
                            S.op("act", lambda e, cac=cac, raw=raw: e.square(out=raw[0:64, 3:2051], in_=cac[0:64, :]),
                                 reads=[CAC], writes=[RAW])
                            dst = qh if blk == 0 else kh
                            for tb in range(4):
                                pb = 2 + (tb % 2)
                                rb = tb % 2
                                S.op("pe", lambda e, pb=pb, tb=tb, raw=raw: e.matmul(
                                    psb[pb][0:64, :], ONES, raw[0:64, 3 + tb * 512:3 + (tb + 1) * 512], start=True, stop=True),
                                     reads=[RAW, "gcn"], writes=[("psb", pb)])
                                sc = 64.0 if blk == 0 else 1.0
                                S.op("act", lambda e, pb=pb, rb=rb, sc=sc, blk=blk: e.activation(
                                    out=rin[rb][:], in_=psb[pb][0:64, :], func=AF.Sqrt, scale=sc, bias=epsb[:, blk:blk + 1]),
                                     reads=[("psb", pb), "epsb"], writes=[("rin", rb)])
                                S.op("dve", lambda e, rb=rb: e.reciprocal(out=rin[rb][:], in_=rin[rb][:]),
                                     reads=[("rin", rb)], writes=[("rin", rb)])
                                S.op("pool", lambda e, rb=rb, tb=tb, dst=dst, g=g, cac=cac: e.tensor_tensor(
                                    out=dst[:, g, tb * 512:(tb + 1) * 512], in0=cac[0:64, tb * 512:(tb + 1) * 512],
                                    in1=rin[rb][:], op=ALU.mult),
                                     reads=[CAC, ("rin", rb)], writes=[("qk", blk, g)])
                    S.flush()

                gt = lambda name: sb2(name, [64, 32, 8])
                ba = sb2("ba", [64, 32, 16])
                beta, nbeta, gg, gc, gcl, eg, egl, ekd, nbeg, alr, dtr = [gt(n) for n in (
                    "beta", "nbeta", "gg", "gc", "gcl", "eg", "egl", "ekd", "nbeg", "alr", "dtr")]
                S.dma("sp", lambda e: e.dma_start(out=alr[:], in_=bass.AP(gvec.tensor, 0, [[0, 64], [0, 32], [1, 8]])),
                      writes=["alr"])
                S.dma("sp", lambda e: e.dma_start(out=dtr[:], in_=bass.AP(gvec.tensor, 64, [[0, 64], [0, 32], [1, 8]])),
                      writes=["dtr"])
                for n in range(32):
                    for kc in range(8):
                        S.op("pe", lambda e, n=n, kc=kc: e.matmul(
                            psb[0][0:64, n * 16:(n + 1) * 16], x1T[:, kc, 128 + n * 64:128 + (n + 1) * 64], wba[:, kc, :],
                            start=(kc == 0), stop=(kc == 7)), reads=["wba"], writes=[("psb", 0)])
                S.op("dve", lambda e: e.tensor_copy(out=ba[:], in_=v3(psb[0][0:64, :], 32)), reads=[("psb", 0)],
                     writes=["ba"])
                S.op("act", lambda e: e.activation(out=beta[:], in_=ba[:, :, 0:8], func=AF.Sigmoid), reads=["ba"],
                     writes=["beta"])
                S.op("dve", lambda e: e.tensor_scalar(out=nbeta[:], in0=beta[:], scalar1=-1.0, scalar2=None, op0=ALU.mult),
                     reads=["beta"], writes=["nbeta"])
                S.op("dve", lambda e: e.tensor_tensor(out=gg[:], in0=ba[:, :, 8:16], in1=dtr[:], op=ALU.add),
                     reads=["ba", "dtr"], writes=["gg"])
                S.op("act", lambda e: e.activation(out=gg[:], in_=gg[:], func=AF.Exp), reads=["gg"], writes=["gg"])
                S.op("act", lambda e: e.activation(out=gg[:], in_=gg[:], func=AF.Ln, bias=ONES[:, 0:1]),
                     reads=["gg", "gcn"], writes=["gg"])
                S.op("act", lambda e: e.activation(out=alr[:], in_=alr[:], func=AF.Exp), reads=["alr"], writes=["alr"])
                S.op("dve", lambda e: e.scalar_tensor_tensor(out=gg[:], in0=gg[:], scalar=-1.0, in1=alr[:], op0=ALU.mult,
                                                             op1=ALU.mult), reads=["gg", "alr"], writes=["gg"])
                gg2 = lambda t: t[:].rearrange("p n h -> p (n h)")
                S.op("pe", lambda e: e.matmul(psb[1][0:64, 0:256], TRI, gg2(gg), start=True, stop=True),
                     reads=["gg", "gcn"], writes=[("psb", 1)])
                S.op("dve", lambda e: e.tensor_copy(out=gg2(gc), in_=psb[1][0:64, 0:256]), reads=[("psb", 1)],
                     writes=["gc"])
                S.op("pe", lambda e: e.matmul(psb[2][0:64, 0:256], SEL, gg2(gc), start=True, stop=True),
                     reads=["gc", "gcn"], writes=[("psb", 2)])
                S.op("dve", lambda e: e.tensor_copy(out=gg2(gcl), in_=psb[2][0:64, 0:256]), reads=[("psb", 2)],
                     writes=["gcl"])
                S.op("act", lambda e: e.activation(out=eg[:], in_=gc[:], func=AF.Exp), reads=["gc"], writes=["eg"])
                S.op("act", lambda e: e.activation(out=egl[:], in_=gcl[:], func=AF.Exp), reads=["gcl"], writes=["egl"])
                S.op("dve", lambda e: e.tensor_tensor(out=ekd[:], in0=gcl[:], in1=gc[:], op=ALU.subtract),
                     reads=["gc", "gcl"], writes=["ekd"])
                S.op("act", lambda e: e.activation(out=ekd[:], in_=ekd[:], func=AF.Exp), reads=["ekd"], writes=["ekd"])
                S.op("dve", lambda e: e.tensor_tensor(out=nbeg[:], in0=nbeta[:], in1=eg[:], op=ALU.mult),
                     reads=["nbeta", "eg"], writes=["nbeg"])
                S.flush()

                f3 = lambda name: sb2(name, [64, 8, 64])
                b3 = lambda name: sb2(name, [64, 8, 64], BF16)
                kvns = [sb2("kvn%d" % i, [64, 1024], BF16) for i in range(2)]
                zss = [sb2("zs%d" % i, [64, 512]) for i in range(2)]
                Dg, Db, m1, e1, e2, b1, c1, Tf, vb, tt, ob, osq, Sf = [f3(n) for n in (
                    "Dg", "Db", "m1", "e1", "e2", "b1", "c1", "Tf", "vb", "tt", "ob", "osq", "Sf")]
                Bb = [b3("Bb0"), b3("Bb1")]
                Cb = [b3("Cb0"), b3("Cb1")]
                intraTs = [b3("intraT0"), b3("intraT1")]
                Tbs = [b3("Tb0"), b3("Tb1")]
                rn, vn, vns, Sb = [b3(n) for n in ("rn", "vn", "vns", "Sb")]
                go = sb2("go", [64, 512], BF16)
                ss = sb2("ss", [64, 8])
                S.op("pool", lambda e: e.memset(Sf[:], 0.0), writes=["Sf"])
                S.op("pool", lambda e: e.memset(Sb[:], 0.0), writes=["Sb"])
                ps3 = lambda i: v3(psb[i][0:64, :])
                p6b = psb[6][:, :].bitcast(BF16)

                def pre(n):
                    c0 = n * 64
                    xc0 = 128 + c0
                    q = n % 2
                    kvn, zs, intraT, Tb = kvns[q], zss[q], intraTs[q], Tbs[q]
                    KVN, ZS, INT, TB = ("kvn", q), ("zs", q), ("intraT", q), ("Tb", q)
                    for h in range(8):
                        S.op("pe", lambda e, h=h: e.transpose(pst[0:64, h * 64:(h + 1) * 64], kh[:, h, c0:c0 + 64],
                                                              identb[0:64, 0:64]),
                             reads=[("qk", 1, h), "identb"], writes=["pst"])
                    for pr in range(4):
                        S.op("pe", lambda e, pr=pr: e.transpose(pst[0:64, 512 + pr * 128:512 + (pr + 1) * 128],
                                                                vT[:, pr, c0:c0 + 64], identb[:, :]),
                             reads=[("vT", pr), "identb"], writes=["pst"])
                    S.op("dve", lambda e: e.tensor_copy(out=kvn[:], in_=pst[0:64, :]), reads=["pst"], writes=[KVN])
                    for kc in range(8):
                        S.op("pe", lambda e, kc=kc: e.matmul(psb[2][0:64, :], x1T[:, kc, xc0:xc0 + 64], wz[:, kc, :],
                                                             start=(kc == 0), stop=(kc == 7)),
                             reads=["wz"], writes=[("psb", 2)])
                    S.op("act", lambda e: e.activation(out=zs[:], in_=psb[2][0:64, :], func=AF.Silu), reads=[("psb", 2)],
                         writes=[ZS])
                    for h in range(8):
                        S.op("pe", lambda e, h=h: e.matmul(psb[0][0:64, h * 64:(h + 1) * 64], kh[:, h, c0:c0 + 64],
                                                           kh[:, h, c0:c0 + 64], start=True, stop=True),
                             reads=[("qk", 1, h)], writes=[("psb", 0)])
                    for h in range(8):
                        S.op("pe", lambda e, h=h: e.matmul(psb[1][0:64, h * 64:(h + 1) * 64], kh[:, h, c0:c0 + 64],
                                                           qh[:, h, c0:c0 + 64], start=True, stop=True),
                             reads=[("qk", 1, h), ("qk", 0, h)], writes=[("psb", 1)])
                    gcn_ = gc[:, n, :]
                    S.op("pool", lambda e: e.tensor_tensor(out=Dg[:], in0=bm(ID64), in1=bl(gcn_), op=ALU.mult),
                         reads=["gc", "gcn"], writes=["Dg"])
                    S.op("pool", lambda e: e.tensor_tensor(out=Db[:], in0=bm(ID64), in1=bl(beta[:, n, :]), op=ALU.mult),
                         reads=["beta", "gcn"], writes=["Db"])
                    S.op("pe", lambda e: e.matmul(psb[2][0:64, :], ONES, Dg[:].rearrange("p h s -> p (h s)"), start=True,
                                                  stop=True), reads=["Dg", "gcn"], writes=[("psb", 2)])
                    S.op("pe", lambda e: e.matmul(psb[3][0:64, :], ONES, Db[:].rearrange("p h s -> p (h s)"), start=True,
                                                  stop=True), reads=["Db", "gcn"], writes=[("psb", 3)])
                    S.op("pool", lambda e: e.tensor_tensor(out=m1[:], in0=bl(gcn_), in1=bm(NEGS), op=ALU.add),
                         reads=["gc", "gcn"], writes=["m1"])
                    S.op("dve", lambda e: e.scalar_tensor_tensor(out=e1[:], in0=ps3(2), scalar=-1.0, in1=m1[:], op0=ALU.mult,
                                                                 op1=ALU.add), reads=[("psb", 2), "m1"], writes=["e1"])
                    S.op("act", lambda e: e.activation(out=e1[:], in_=e1[:], func=AF.Exp), reads=["e1"], writes=["e1"])
                    S.op("pool", lambda e: e.tensor_tensor(out=m1[:], in0=bm(NEGT), in1=bl(gcn_), op=ALU.subtract),
                         reads=["gc", "gcn"], writes=["m1"])
                    S.op("dve", lambda e: e.tensor_tensor(out=e2[:], in0=ps3(2), in1=m1[:], op=ALU.add),
                         reads=[("psb", 2), "m1"], writes=["e2"])
                    S.op("act", lambda e: e.activation(out=e2[:], in_=e2[:], func=AF.Exp), reads=["e2"], writes=["e2"])
                    S.op("dve", lambda e: e.tensor_tensor(out=b1[:], in0=ps3(0), in1=e1[:], op=ALU.mult),
                         reads=[("psb", 0), "e1"], writes=["b1"])
                    S.op("pool", lambda e: e.tensor_tensor(out=Bb[0][:], in0=b1[:], in1=bl(nbeta[:, n, :]), op=ALU.mult),
                         reads=["b1", "nbeta"], writes=[("Bb", 0)])
                    S.op("dve", lambda e: e.tensor_tensor(out=c1[:], in0=ps3(0), in1=e2[:], op=ALU.mult),
                         reads=[("psb", 0), "e2"], writes=["c1"])
                    S.op("dve", lambda e: e.tensor_tensor(out=c1[:], in0=c1[:], in1=ps3(3), op=ALU.mult),
                         reads=[("psb", 3), "c1"], writes=["c1"])
                    S.op("pool", lambda e: e.tensor_tensor(out=c1[:], in0=c1[:], in1=bm(MSKT), op=ALU.mult),
                         reads=["c1", "gcn"], writes=["c1"])
                    S.op("act", lambda e: e.copy(out=Cb[0][:], in_=c1[:]), reads=["c1"], writes=[("Cb", 0)])
                    S.op("dve", lambda e: e.tensor_tensor(out=intraT[:], in0=ps3(1), in1=e2[:], op=ALU.mult),
                         reads=[("psb", 1), "e2"], writes=[INT])
                    S.op("pool", lambda e: e.tensor_tensor(out=Tf[:], in0=c1[:], in1=bm(ID64), op=ALU.add),
                         reads=["c1", "gcn"], writes=["Tf"])
                    S.op("act", lambda e: e.copy(out=Tb[:], in_=Tf[:]), reads=["Tf"], writes=[TB])
                    for k in range(1, 6):
                        cur, nxt = (k - 1) % 2, k % 2
                        for h in range(8):
                            S.op("pe", lambda e, h=h, cur=cur: e.matmul(psb[2][0:64, h * 64:(h + 1) * 64], Cb[cur][:, h, :],
                                                                        Bb[cur][:, h, :], start=True, stop=True),
                                 reads=[("Cb", cur), ("Bb", cur)], writes=[("psb", 2)])
                        if k < 5:
                            for h in range(8):
                                S.op("pe", lambda e, h=h, cur=cur: e.matmul(psb[3][0:64, h * 64:(h + 1) * 64], Bb[cur][:, h, :],
                                                                            Cb[cur][:, h, :], start=True, stop=True),
                                     reads=[("Cb", cur), ("Bb", cur)], writes=[("psb", 3)])
                        S.op("act", lambda e, nxt=nxt: e.activation(out=Bb[nxt][:], in_=ps3(2), func=AF.Copy),
                             reads=[("psb", 2)], writes=[("Bb", nxt)])
                        if k < 5:
                            S.op("dve", lambda e, nxt=nxt: e.tensor_copy(out=Cb[nxt][:], in_=ps3(3)),
                                 reads=[("psb", 3)], writes=[("Cb", nxt)])
                        for h in range(8):
                            S.op("pe", lambda e, h=h, nxt=nxt: e.matmul(psb[1][0:64, h * 64:(h + 1) * 64], Bb[nxt][:, h, :],
                                                                        Tb[:, h, :], start=True, stop=True),
                                 reads=[("Bb", nxt), TB], writes=[("psb", 1)])
                        S.op("dve", lambda e: e.tensor_tensor(out=Tf[:], in0=Tf[:], in1=ps3(1), op=ALU.add),
                             reads=[("psb", 1), "Tf"], writes=["Tf"])
                        S.op("act", lambda e: e.copy(out=Tb[:], in_=Tf[:]), reads=["Tf"], writes=[TB])

                def seq(n):
                    c0 = n * 64
                    q = n % 2
                    kvn, zs, intraT, Tb = kvns[q], zss[q], intraTs[q], Tbs[q]
                    KVN, ZS, INT, TB = ("kvn", q), ("zs", q), ("intraT", q), ("Tb", q)
                    S.op("pool", lambda e: e.tensor_tensor(out=vb[:], in0=v3(kvn[:, 512:1024]), in1=bl(beta[:, n, :]),
                                                           op=ALU.mult), reads=[KVN, "beta"], writes=["vb"])
                    for h in range(8):
                        S.op("pe", lambda e, h=h: e.matmul(psb[4][0:64, h * 64:(h + 1) * 64], kh[:, h, c0:c0 + 64],
                                                           Sb[:, h, :], start=True, stop=True),
                             reads=[("qk", 1, h), "Sb"], writes=[("psb", 4)])
                    S.op("dve", lambda e: e.tensor_tensor(out=tt[:], in0=ps3(4), in1=bl(nbeg[:, n, :]), op=ALU.mult),
                         reads=[("psb", 4), "nbeg"], writes=["tt"])
                    S.op("pool", lambda e: e.tensor_tensor(out=rn[:], in0=tt[:], in1=vb[:], op=ALU.add),
                         reads=["tt", "vb"], writes=["rn"])
                    for h in range(8):
                        S.op("pe", lambda e, h=h: e.matmul(psb[5][0:64, h * 64:(h + 1) * 64], Tb[:, h, :], rn[:, h, :],
                                                           start=True, stop=True),
                             reads=[TB, "rn"], writes=[("psb", 5)])
                    S.op("act", lambda e: e.activation(out=vn[:], in_=ps3(5), func=AF.Copy), reads=[("psb", 5)],
                         writes=["vn"])
                    S.op("pool", lambda e: e.tensor_tensor(out=vns[:], in0=vn[:], in1=bl(ekd[:, n, :]), op=ALU.mult),
                         reads=["vn", "ekd"], writes=["vns"])
                    for h in range(8):
                        S.op("pe", lambda e, h=h: e.matmul(psb[6][0:64, h * 64:(h + 1) * 64], qh[:, h, c0:c0 + 64],
                                                           Sb[:, h, :], start=True, stop=True),
                             reads=[("qk", 0, h), "Sb"], writes=[("psb", 6)])
                    for h in range(8):
                        S.op("pe", lambda e, h=h: e.matmul(psb[4][0:64, h * 64:(h + 1) * 64], intraT[:, h, :], vn[:, h, :],
                                                           start=True, stop=True),
                             reads=[INT, "vn"], writes=[("psb", 4)])
                    S.op("dve", lambda e: e.tensor_tensor(out=ob[:], in0=ps3(6), in1=bl(eg[:, n, :]), op=ALU.mult),
                         reads=[("psb", 6), "eg"], writes=["ob"])
                    S.op("dve", lambda e: e.tensor_tensor(out=ob[:], in0=ob[:], in1=ps3(4), op=ALU.add),
                         reads=[("psb", 4), "ob"], writes=["ob"])
                    for h in range(8):
                        S.op("pe", lambda e, h=h: e.matmul(psb[5][0:64, h * 64:(h + 1) * 64], kvn[:, h * 64:(h + 1) * 64],
                                                           vns[:, h, :], start=True, stop=True),
                             reads=[KVN, "vns"], writes=[("psb", 5)])
                    S.op("pool", lambda e: e.tensor_tensor(out=Sf[:], in0=Sf[:], in1=bl(egl[:, n, :]), op=ALU.mult),
                         reads=["Sf", "egl"], writes=["Sf"])
                    S.op("dve", lambda e: e.tensor_tensor(out=Sf[:], in0=Sf[:], in1=ps3(5), op=ALU.add),
                         reads=[("psb", 5), "Sf"], writes=["Sf"])
                    S.op("act", lambda e: e.copy(out=Sb[:], in_=Sf[:]), reads=["Sf"], writes=["Sb"])
                    S.op("pool", lambda e: e.tensor_tensor(out=osq[:], in0=ob[:], in1=ob[:], op=ALU.mult), reads=["ob"],
                         writes=["osq"])
                    S.op("dve", lambda e: e.tensor_reduce(out=ss[:], in_=osq[:], axis=AX.X, op=ALU.add), reads=["osq"],
                         writes=["ss"])
                    S.op("dve", lambda e: e.tensor_scalar(out=ss[:], in0=ss[:], scalar1=1.0 / 64.0, scalar2=1e-6,
                                                          op0=ALU.mult, op1=ALU.add), reads=["ss"], writes=["ss"])
                    S.op("act", lambda e: e.sqrt(out=ss[:], in_=ss[:]), reads=["ss"], writes=["ss"])
                    S.op("dve", lambda e: e.reciprocal(out=ss[:], in_=ss[:]), reads=["ss"], writes=["ss"])
                    S.op("dve", lambda e: e.tensor_tensor(out=ob[:], in0=ob[:], in1=bl(ss[:, :]), op=ALU.mult),
                         reads=["ob", "ss"], writes=["ob"])
                    S.op("pool", lambda e: e.tensor_tensor(out=ob[:], in0=ob[:], in1=bm(nwr[:, :]), op=ALU.mult),
                         reads=["ob", "nwr"], writes=["ob"])
                    S.op("dve", lambda e: e.tensor_tensor(out=v3(go[:, :]), in0=ob[:], in1=v3(zs[:, :]), op=ALU.mult),
                         reads=["ob", ZS], writes=["go"])
                    for pr in range(4):
                        S.op("pe", lambda e, pr=pr: e.transpose(p6b[:, pr * 64:(pr + 1) * 64], go[:, pr * 128:(pr + 1) * 128],
                                                                identb[0:64, 0:64]),
                             reads=["go", "identb"], writes=[("psb", 6)])
                    S.op("dve", lambda e: e.tensor_copy(out=gdnT[:, :, c0:c0 + 64], in_=v3(p6b[:, 0:256], 4)),
                         reads=[("psb", 6)], writes=[("gdnT", n)])

                pre(0)
                for n in range(32):
                    sq_ = S.capture(lambda: seq(n))
                    pr_ = S.capture(lambda: pre(n + 1)) if n + 1 < 32 else []
                    S.emit_merged(pr_, sq_)
                S.dma("sp", lambda e: e.dma_start(out=o_sgp.rearrange("h d e -> d h e"), in_=Sf[:]), reads=["Sf"],
                      writes=["o_sgp"])
                if dbg:
                    S.dma("pool", lambda e: e.dma_start(out=gdn_dbg[:, :, :], in_=gdnT[:]),
                          reads=[("gdnT", n) for n in range(32)], writes=["gdn_dbg"])
                S.flush()

        def sample_proj_stage():
            with contextlib.ExitStack() as st2:
                sb2 = lambda name, shape, dt=F32: st2.enter_context(nc.sbuf_tensor("sp" + name, list(shape), dt))
                wb = [sb2("wb%d" % i, [128, 8, 512], BF16) for i in range(2)]
                ot = [sb2("ot%d" % i, [128, 512]) for i in range(2)]
                winv = w_in.rearrange("(kc p) f -> p kc f", p=128)
                blocks = [(0, 512, qs_s[:, :], 0.125), (1536, 512, gq_s[:, 0:512], 1.0), (2048, 512, gq_s[:, 512:1024], 1.0),
                          (2560, 512, gq_s[:, 1024:1536], 1.0), (3072, 512, z_s[:, :], 1.0), (3584, 16, ba_s[:, :], 1.0)]
                for i, (c0, w, dst, sc) in enumerate(blocks):
                    b = i % 2
                    S.dma("pool", lambda e, b=b, c0=c0, w=w: e.dma_start(out=wb[b][:, :, 0:w], in_=winv[:, :, c0:c0 + w]),
                          writes=[("wb", b)])
                    for kc in range(8):
                        S.op("pe", lambda e, b=b, kc=kc, w=w: e.matmul(psb[b][:, 0:w], x1T[:, kc, 0:128], wb[b][:, kc, 0:w],
                                                                       start=(kc == 0), stop=(kc == 7)),
                             reads=[("wb", b)], writes=[("psb", b)])
                    S.op("act", lambda e, b=b, w=w, sc=sc: e.mul(out=ot[b][:, 0:w], in_=psb[b][:, 0:w], mul=sc),
                         reads=[("psb", b)], writes=[("ot", b)])
                    S.dma("sp", lambda e, b=b, w=w, dst=dst: e.dma_start(out=dst, in_=ot[b][0:64, 0:w]),
                          reads=[("ot", b)], writes=[("sscr", i)])
                S.dma("sp", lambda e: e.dma_start(
                    out=o_scs[:, :, :], in_=bass.AP(gq_s.tensor, 1536, [[4 * 1536, 16], [1536, 3], [1, 1536]])),
                      reads=[("sscr", 1), ("sscr", 2), ("sscr", 3)], writes=["o_scs"])
                S.flush()

        def sample_attn_stage():
            with contextlib.ExitStack() as st2:
                sb2 = lambda name, shape, dt=F32: st2.enter_context(nc.sbuf_tensor("sa" + name, list(shape), dt))
                kt = [sb2("kt%d" % i, [128, 8, 512]) for i in range(2)]
                vt = [sb2("vt%d" % i, [128, 8, 512]) for i in range(2)]
                qrep = [sb2("qrep%d" % i, [128, 4, 512]) for i in range(2)]
                prod = [sb2("prod%d" % i, [128, 4, 512]) for i in range(2)]
                L = sb2("L", [128, 8, 32])
                Pm = sb2("Pm", [128, 8, 32])
                Pj = sb2("Pj", [128, 32])
                mtab = sb2("mtab", [128, 8, 32])
                wcs = sb2("wcs", [32, 8, 4, 128])
                eb = sb2("eb", [32, 8])
                ones1 = sb2("ones1", [128, 1])
                osb = sb2("osb", [4, 512])
                rs = sb2("rs", [4, 8])
                S.dma("sp", lambda e: e.dma_start(out=wcs[:], in_=wcm[:, :, :, :]), writes=["wcs"])
                S.dma("sp", lambda e: e.dma_start(out=eb[:], in_=relb[:, :]), writes=["eb"])
                S.op("act", lambda e: e.activation(out=eb[:], in_=eb[:], func=AF.Exp), reads=["eb"], writes=["eb"])
                S.op("pool", lambda e: e.memset(ones1[:], 1.0), writes=["ones1"])
                for i in range(2):
                    S.op("pool", lambda e, i=i: e.memset(kt[i][:, 7, :], 0.0), writes=[("kt7", i)])
                    S.op("pool", lambda e, i=i: e.memset(vt[i][:, 7, :], 0.0), writes=[("vt7", i)])
                for j in range(8):
                    for t in range(4):
                        S.op("pe", lambda e, j=j, t=t: e.matmul(psb[0][:, (j * 4 + t) * 8:(j * 4 + t + 1) * 8], wcs[:, j, t, :],
                                                                eb[:, :], start=True, stop=True),
                             reads=["wcs", "eb"], writes=[("psb", 0)])
                S.op("dve", lambda e: e.tensor_copy(out=mtab[:].rearrange("p j x -> p (j x)"), in_=psb[0][:, 0:256]),
                     reads=[("psb", 0)], writes=["mtab"])
                for b in range(16):
                    bb = b % 2
                    for src, dstt, onew, nm in ((ck, kt[bb], o_ks, "kt"), (cv, vt[bb], o_vs, "vt")):
                        S.dma("sp", lambda e, src=src, dstt=dstt, b=b: e.dma_start(
                            out=dstt[:, 0:4, :], in_=src[b, 1536:2048, :].rearrange("(j p) e -> p j e", p=128)),
                              writes=[(nm, bb, 0)])
                        for r in range(4):
                            S.dma("act" if r % 2 else "sp", lambda e, src=src, dstt=dstt, b=b, r=r: e.dma_start(
                                out=dstt[r:128:4, 4:7, :],
                                in_=bass.AP(src.tensor, b * 2048 * 512 + r * 512, [[16 * 512, 32], [32 * 16 * 512, 3], [1, 512]])),
                                  writes=[(nm, bb, 1 + r)])
                        S.dma("act", lambda e, dstt=dstt, onew=onew, b=b: e.dma_start(
                            out=dstt[0:4, 7, :], in_=onew[b * 4:(b + 1) * 4, :]),
                              reads=[("okv", 1, 0), ("okv", 2, 0), (nm + "7", bb)], writes=[(nm, bb, 5)])
                    S.dma("sp", lambda e, b=b, bb=bb: e.dma_start(
                        out=qrep[bb][:], in_=bass.AP(qs_s.tensor, b * 4 * 512, [[0, 128], [512, 4], [1, 512]])),
                          reads=[("sscr", 0)], writes=[("qrep", bb)])
                    for j in range(8):
                        pb = j % 2
                        eng = "pool" if j in (1, 3, 5, 6, 7) else "dve"
                        S.op(eng, lambda e, bb=bb, j=j, pb=pb: e.tensor_tensor(
                            out=prod[pb][:], in0=bm(kt[bb][:, j, :], 4), in1=qrep[bb][:], op=ALU.mult),
                             reads=[("kt", bb, i_) for i_ in range(6)] + [("qrep", bb)], writes=[("prod", pb)])
                        S.op("dve", lambda e, j=j, pb=pb: e.tensor_reduce(
                            out=L[:, j, :], in_=prod[pb][:].rearrange("p t (h x) -> p (t h) x", h=8), axis=AX.X, op=ALU.add),
                             reads=[("prod", pb)], writes=[("L", j)])
                    S.op("act", lambda e: e.activation(out=Pm[:], in_=L[:], func=AF.Exp), reads=[("L", j_) for j_ in range(8)], writes=["Pm"])
                    S.op("dve", lambda e: e.tensor_tensor(out=Pm[:], in0=Pm[:], in1=mtab[:], op=ALU.mult),
                         reads=["Pm", "mtab"], writes=["Pm"])
                    S.op("dve", lambda e: e.tensor_reduce(out=Pj[:], in_=Pm[:].rearrange("p j x -> p x j"), axis=AX.X,
                                                          op=ALU.add), reads=["Pm"], writes=["Pj"])
                    for h in range(8):
                        for j in range(8):
                            S.op("pe", lambda e, h=h, j=j, bb=bb: e.matmul(
                                psb[1][0:4, h * 64:(h + 1) * 64], Pm[:, j, h:32:8], vt[bb][:, j, h * 64:(h + 1) * 64],
                                start=(j == 0), stop=(j == 7)), reads=["Pm"] + [("vt", bb, i_) for i_ in range(6)], writes=[("psb", 1)])
                        S.op("pe", lambda e, h=h: e.matmul(psb[2][0:4, h:h + 1], Pj[:, h:32:8], ones1[:, :], start=True,
                                                           stop=True), reads=["Pj", "ones1"], writes=[("psb", 2)])
                    S.op("dve", lambda e: e.reciprocal(out=rs[:], in_=psb[2][0:4, 0:8]), reads=[("psb", 2)], writes=["rs"])
                    S.op("dve", lambda e: e.tensor_tensor(out=v3(osb[:, :]), in0=v3(psb[1][0:4, :]), in1=bl(rs[:, :]),
                                                          op=ALU.mult), reads=[("psb", 1), "rs"], writes=["osb"])
                    S.dma("pool", lambda e, b=b: e.dma_start(out=heads_s[b * 4:(b + 1) * 4, 0:512], in_=osb[:, :]),
                          reads=["osb"], writes=[("heads_a", b)])
                if dbg:
                    dL = dscr("dbg_L", [128, 256]); dP = dscr("dbg_Pm", [128, 256]); dM = dscr("dbg_mtab", [128, 256])
                    dK = dscr("dbg_kt", [128, 8, 512]); dQ = dscr("dbg_qrep", [128, 4, 512])
                    S.dma("sp", lambda e: e.dma_start(out=dL[:, :], in_=L[:].rearrange("p j x -> p (j x)")), reads=[("L", j_) for j_ in range(8)], writes=["dL"])
                    S.dma("sp", lambda e: e.dma_start(out=dP[:, :], in_=Pm[:].rearrange("p j x -> p (j x)")), reads=["Pm"], writes=["dP"])
                    S.dma("sp", lambda e: e.dma_start(out=dM[:, :], in_=mtab[:].rearrange("p j x -> p (j x)")), reads=["mtab"], writes=["dM"])
                    S.dma("sp", lambda e: e.dma_start(out=dK[:, :, :], in_=kt[1][:]), reads=[("kt", 1, i_) for i_ in range(6)], writes=["dK"])
                    S.dma("sp", lambda e: e.dma_start(out=dQ[:, :, :], in_=qrep[1][:]), reads=[("qrep", 1)], writes=["dQ"])
                S.flush()

        def sample_gdn_stage():
            with contextlib.ExitStack() as st2:
                sb2 = lambda name, shape, dt=F32: st2.enter_context(nc.sbuf_tensor("sg" + name, list(shape), dt))
                Sx = sb2("S", [128, 64, 64])
                tmp = sb2("tmp", [128, 64, 64])
                xq = sb2("xq", [128, 3, 7, 64])
                zz = sb2("zz", [128, 4, 64])
                bav = sb2("bav", [128, 4, 2])
                cw = sb2("cw", [128, 3, 4, 64])
                gv = sb2("gv", [128, 2])
                nwr = sb2("nwr", [128, 64])
                cq = sb2("cq", [128, 3, 4, 64])
                ct = sb2("ct", [128, 4, 64])
                ssq = sb2("ssq", [128, 3, 4])
                beta = sb2("beta", [128, 4])
                gg = sb2("gg", [128, 4])
                eg = sb2("eg", [128, 4])
                neg = sb2("neg", [128, 4])
                ks = sb2("ks", [128, 64])
                dl = sb2("dl", [128, 64])
                oo = sb2("oo", [128, 4, 64])
                one = sb2("one", [128, 1])
                S.op("pool", lambda e: e.memset(one[:], 1.0), writes=["one"])
                S.dma("sp", lambda e: e.dma_start(out=Sx[:].rearrange("p a b -> p (a b)"), in_=sg_in[:, :]), writes=["S"])
                S.dma("sp", lambda e: e.dma_start(out=cw[:], in_=cws[:, :, :, :]), writes=["cw"])
                S.dma("sp", lambda e: e.dma_start(out=gv[:], in_=gvs[:, :]), writes=["gv"])
                S.dma("sp", lambda e: e.dma_start(out=nwr[:], in_=gvec[2:3, :].partition_broadcast(128)), writes=["nwr"])
                for b in range(16):
                    p0 = b * 8
                    for sec in range(3):
                        q = "act" if sec == 1 else "sp"
                        S.dma(q, lambda e, b=b, p0=p0, sec=sec: e.dma_start(
                            out=xq[p0:p0 + 8, sec, 0:3, :],
                            in_=bass.AP(sc_in.tensor, b * 3 * 1536 + sec * 512, [[64, 8], [1536, 3], [1, 64]])),
                              writes=["xq"])
                        S.dma(q, lambda e, b=b, p0=p0, sec=sec: e.dma_start(
                            out=xq[p0:p0 + 8, sec, 3:7, :],
                            in_=bass.AP(gq_s.tensor, b * 4 * 1536 + sec * 512, [[64, 8], [1536, 4], [1, 64]])),
                              reads=[("sscr", 1 + sec)], writes=["xq"])
                    S.dma("act", lambda e, b=b, p0=p0: e.dma_start(
                        out=zz[p0:p0 + 8, :, :], in_=bass.AP(z_s.tensor, b * 4 * 512, [[64, 8], [512, 4], [1, 64]])),
                          reads=[("sscr", 4)], writes=["zz"])
                    S.dma("act", lambda e, b=b, p0=p0: e.dma_start(
                        out=bav[p0:p0 + 8, :, :], in_=bass.AP(ba_s.tensor, b * 4 * 16, [[1, 8], [16, 4], [8, 2]]),
                        allow_slow_non_contiguous=True), reads=[("sscr", 5)], writes=["bav"])
                for sec in range(3):
                    for j in range(4):
                        wv = cw[:, sec, j, :].unsqueeze(1).broadcast_to([128, 4, 64])
                        if j == 0:
                            S.op("dve", lambda e, sec=sec, wv=wv: e.tensor_tensor(
                                out=cq[:, sec, :, :], in0=xq[:, sec, 0:4, :], in1=wv, op=ALU.mult),
                                 reads=["xq", "cw"], writes=["cq"])
                        else:
                            S.op("pool", lambda e, sec=sec, wv=wv, j=j: e.tensor_tensor(
                                out=ct[:], in0=xq[:, sec, j:j + 4, :], in1=wv, op=ALU.mult),
                                 reads=["xq", "cw"], writes=["ct"])
                            S.op("dve", lambda e, sec=sec: e.tensor_tensor(
                                out=cq[:, sec, :, :], in0=cq[:, sec, :, :], in1=ct[:], op=ALU.add),
                                 reads=["cq", "ct"], writes=["cq"])
                S.op("act", lambda e: e.activation(out=cq[:], in_=cq[:], func=AF.Silu), reads=["cq"], writes=["cq"])
                S.op("act", lambda e: e.activation(out=zz[:], in_=zz[:], func=AF.Silu), reads=["zz"], writes=["zz"])
                for sec in range(2):
                    S.op("pool", lambda e, sec=sec: e.tensor_tensor(out=ct[:], in0=cq[:, sec, :, :], in1=cq[:, sec, :, :],
                                                                    op=ALU.mult), reads=["cq"], writes=["ct"])
                    S.op("dve", lambda e, sec=sec: e.tensor_reduce(out=ssq[:, sec, :], in_=ct[:], axis=AX.X, op=ALU.add),
                         reads=["ct"], writes=["ssq"])
                    S.op("dve", lambda e, sec=sec: e.tensor_scalar(out=ssq[:, sec, :], in0=ssq[:, sec, :], scalar1=1e-6,
                                                                   scalar2=None, op0=ALU.add), reads=["ssq"], writes=["ssq"])
                    S.op("act", lambda e, sec=sec: e.sqrt(out=ssq[:, sec, :], in_=ssq[:, sec, :]), reads=["ssq"],
                         writes=["ssq"])
                    S.op("dve", lambda e, sec=sec: e.reciprocal(out=ssq[:, sec, :], in_=ssq[:, sec, :]), reads=["ssq"],
                         writes=["ssq"])
                    sc = 0.125 if sec == 0 else 1.0
                    S.op("dve", lambda e, sec=sec, sc=sc: e.scalar_tensor_tensor(
                        out=cq[:, sec, :, :], in0=cq[:, sec, :, :], scalar=sc, in1=bl(ssq[:, sec, :]), op0=ALU.mult,
                        op1=ALU.mult), reads=["cq", "ssq"], writes=["cq"])
                S.op("act", lambda e: e.activation(out=beta[:], in_=bav[:, :, 0], func=AF.Sigmoid), reads=["bav"],
                     writes=["beta"])
                S.op("dve", lambda e: e.tensor_scalar(out=gg[:], in0=bav[:, :, 1], scalar1=gv[:, 1:2], scalar2=None,
                                                      op0=ALU.add), reads=["bav", "gv"], writes=["gg"])
                S.op("act", lambda e: e.activation(out=gg[:], in_=gg[:], func=AF.Exp), reads=["gg"], writes=["gg"])
                S.op("act", lambda e: e.activation(out=gg[:], in_=gg[:], func=AF.Ln, bias=one[:, 0:1]), reads=["gg", "one"],
                     writes=["gg"])
                S.op("act", lambda e: e.activation(out=gv[:, 0:1], in_=gv[:, 0:1], func=AF.Exp), reads=["gv"], writes=["gv"])
                S.op("dve", lambda e: e.tensor_scalar(out=gg[:], in0=gg[:], scalar1=gv[:, 0:1], scalar2=-1.0, op0=ALU.mult,
                                                      op1=ALU.mult), reads=["gg", "gv"], writes=["gg"])
                S.op("act", lambda e: e.activation(out=eg[:], in_=gg[:], func=AF.Exp), reads=["gg"], writes=["eg"])
                S.op("dve", lambda e: e.tensor_scalar(out=neg[:], in0=eg[:], scalar1=-1.0, scalar2=None, op0=ALU.mult),
                     reads=["eg"], writes=["neg"])
                ST = Sx[:].rearrange("p a b -> p b a")
                for t in range(4):
                    qv, kv, vv = cq[:, 0, t, :], cq[:, 1, t, :], cq[:, 2, t, :]
                    S.op("dve", lambda e, kv=kv: e.tensor_tensor(out=tmp[:], in0=ST, in1=bm(kv, 64), op=ALU.mult),
                         reads=["S", "cq"], writes=["tmp"])
                    S.op("dve", lambda e: e.tensor_reduce(out=ks[:], in_=tmp[:], axis=AX.X, op=ALU.add), reads=["tmp"],
                         writes=["ks"])
                    S.op("dve", lambda e, t=t, vv=vv: e.scalar_tensor_tensor(
                        out=dl[:], in0=ks[:], scalar=neg[:, t:t + 1], in1=vv, op0=ALU.mult, op1=ALU.add),
                         reads=["ks", "neg", "cq"], writes=["dl"])
                    S.op("dve", lambda e, t=t: e.tensor_scalar(out=dl[:], in0=dl[:], scalar1=beta[:, t:t + 1], scalar2=None,
                                                               op0=ALU.mult), reads=["dl", "beta"], writes=["dl"])
                    S.op("pool", lambda e, kv=kv: e.tensor_tensor(out=tmp[:], in0=bl(kv, 64), in1=bm(dl[:, :], 64),
                                                                  op=ALU.mult), reads=["cq", "dl"], writes=["tmp"])
                    S.op("dve", lambda e, t=t: e.scalar_tensor_tensor(
                        out=Sx[:], in0=Sx[:], scalar=eg[:, t:t + 1], in1=tmp[:], op0=ALU.mult, op1=ALU.add),
                         reads=["S", "eg", "tmp"], writes=["S"])
                    S.op("pool", lambda e, qv=qv: e.tensor_tensor(out=tmp[:], in0=ST, in1=bm(qv, 64), op=ALU.mult),
                         reads=["S", "cq"], writes=["tmp"])
                    S.op("dve", lambda e, t=t: e.tensor_reduce(out=oo[:, t, :], in_=tmp[:], axis=AX.X, op=ALU.add),
                         reads=["tmp"], writes=["oo"])
                S.dma("sp", lambda e: e.dma_start(out=o_sgs[:, :], in_=Sx[:].rearrange("p a b -> p (a b)")), reads=["S"],
                      writes=["o_sgs"])
                S.op("pool", lambda e: e.tensor_tensor(out=ct[:], in0=oo[:], in1=oo[:], op=ALU.mult), reads=["oo"],
                     writes=["ct"])
                S.op("dve", lambda e: e.tensor_reduce(out=ssq[:, 2, :], in_=ct[:], axis=AX.X, op=ALU.add), reads=["ct"],
                     writes=["ssq"])
                S.op("dve", lambda e: e.tensor_scalar(out=ssq[:, 2, :], in0=ssq[:, 2, :], scalar1=1.0 / 64.0, scalar2=1e-6,
                                                      op0=ALU.mult, op1=ALU.add), reads=["ssq"], writes=["ssq"])
                S.op("act", lambda e: e.sqrt(out=ssq[:, 2, :], in_=ssq[:, 2, :]), reads=["ssq"], writes=["ssq"])
                S.op("dve", lambda e: e.reciprocal(out=ssq[:, 2, :], in_=ssq[:, 2, :]), reads=["ssq"], writes=["ssq"])
                S.op("dve", lambda e: e.tensor_tensor(out=oo[:], in0=oo[:], in1=bl(ssq[:, 2, :]), op=ALU.mult),
                     reads=["oo", "ssq"], writes=["oo"])
                S.op("pool", lambda e: e.tensor_tensor(out=oo[:], in0=oo[:], in1=bm(nwr[:, :], 4), op=ALU.mult),
                     reads=["oo", "nwr"], writes=["oo"])
                S.op("dve", lambda e: e.tensor_tensor(out=oo[:], in0=oo[:], in1=zz[:], op=ALU.mult), reads=["oo", "zz"],
                     writes=["oo"])
                if dbg:
                    dC = dscr("dbg_cq", [128, 3, 4, 64]); dG = dscr("dbg_gates", [128, 3, 4])
                    S.dma("sp", lambda e: e.dma_start(out=dC[:, :, :, :], in_=cq[:]), reads=["cq"], writes=["dC"])
                    S.dma("sp", lambda e: e.dma_start(out=dG[:, 0, :], in_=beta[:]), reads=["beta"], writes=["dG0"])
                    S.dma("sp", lambda e: e.dma_start(out=dG[:, 1, :], in_=gg[:]), reads=["gg"], writes=["dG1"])
                    S.dma("sp", lambda e: e.dma_start(out=dG[:, 2, :], in_=eg[:]), reads=["eg"], writes=["dG2"])
                for b in range(16):
                    S.dma("sp", lambda e, b=b: e.dma_start(
                        out=bass.AP(heads_s.tensor, b * 4 * D + 512, [[64, 8], [D, 4], [1, 64]]), in_=oo[b * 8:(b + 1) * 8, :, :]),
                          reads=["oo"], writes=[("heads_g", b)])
                S.flush()

        def wout_stage():
            with contextlib.ExitStack() as st2:
                sb2 = lambda name, shape, dt=F32: st2.enter_context(nc.sbuf_tensor("wo" + name, list(shape), dt))
                attnT = sb2("attnT", [64, 8, 2048], BF16)
                woa = sb2("woa", [64, 8, D], BF16)
                wog = sb2("wog", [128, 4, D], BF16)
                won = sb2("won", [128, 8, D], BF16)
                hsf = sb2("hsf", [128, D])
                hsb = sb2("hsb", [128, D], BF16)
                hsT = sb2("hsT", [128, 8, 128], BF16)
                xs = [sb2("xs%d" % i, [128, D]) for i in range(2)]
                rr = [sb2("rr%d" % i, [128, D]) for i in range(2)]
                lnrep = sb2("ln", [128, 2, D])
                stt = sb2("st", [128, 2, 6])
                mv = sb2("mv", [128, 2])
                rstd = sb2("rstd", [128, 1])
                for i in range(2):
                    S.dma("sp", lambda e, i=i: e.dma_start(out=lnrep[:, i, :], in_=lnp[2 + i:3 + i, :].partition_broadcast(128)),
                          writes=[("lnrep", i)])
                S.dma("sp", lambda e: e.dma_start(out=attnT[:], in_=attn_s.rearrange("h d t -> d h t")),
                      reads=[("attn_s", h) for h in range(8)], writes=["attnT"])
                S.dma("pool", lambda e: e.dma_start(out=woa[:], in_=w_out[0:512, :].rearrange("(h d) o -> d h o", d=64)),
                      writes=["woa"])
                S.dma("pool", lambda e: e.dma_start(out=wog[:], in_=w_out[512:1024, :].rearrange("(c p) o -> p c o", p=128)),
                      writes=["wog"])
                S.dma("pool", lambda e: e.dma_start(out=won[:], in_=w_out.rearrange("(c p) o -> p c o", p=128)),
                      writes=["won"])
                S.op("pool", lambda e: e.memset(hsf[:], 0.0), writes=["hsf0"])
                S.dma("sp", lambda e: e.dma_start(out=hsf[0:64, :], in_=heads_s[:, :]),
                      reads=[("heads_a", b) for b in range(16)] + [("heads_g", b) for b in range(16)] + ["hsf0"],
                      writes=["hsf"])
                S.op("act", lambda e: e.copy(out=hsb[:], in_=hsf[:]), reads=["hsf"], writes=["hsb"])
                to_featmajor(hsb, "hsb", hsT, 0, "hsT")
                for t in range(NT):
                    b = t % 2
                    S.dma("sp", lambda e, b=b, t=t: e.dma_start(out=xs[b][:], in_=x1s[t * 128:(t + 1) * 128, :]),
                          reads=[("x1s", t)], writes=[("xs", b)])
                    for half in range(2):
                        pd = psb[4 + half]
                        hs_ = slice(half * 512, (half + 1) * 512)
                        if t == 0:
                            for kc in range(8):
                                S.op("pe", lambda e, pd=pd, kc=kc, hs_=hs_: e.matmul(
                                    pd[:, :], hsT[:, kc, :], won[:, kc, hs_], start=(kc == 0), stop=(kc == 7)),
                                     reads=["hsT", "won"], writes=[("psb", 4 + half)])
                        else:
                            ts_ = slice((t - 1) * 128, t * 128)
                            for h in range(8):
                                S.op("pe", lambda e, pd=pd, h=h, hs_=hs_, ts_=ts_: e.matmul(
                                    pd[:, :], attnT[:, h, ts_], woa[:, h, hs_], start=(h == 0), stop=False),
                                     reads=["attnT", "woa"], writes=[("psb", 4 + half)])
                            for c in range(4):
                                S.op("pe", lambda e, pd=pd, c=c, hs_=hs_, ts_=ts_: e.matmul(
                                    pd[:, :], gdnT[:, c, ts_], wog[:, c, hs_], start=False, stop=(c == 3)),
                                     reads=[("gdnT", n) for n in range(32)] + ["wog"], writes=[("psb", 4 + half)])
                        S.op("dve", lambda e, pd=pd, b=b, hs_=hs_: e.scalar_tensor_tensor(
                            out=rr[b][:, hs_], in0=xs[b][:, hs_], scalar=ALPHA, in1=pd[:, :], op0=ALU.mult, op1=ALU.add),
                             reads=[("xs", b), ("psb", 4 + half)], writes=[("rr", b)])
                    layernorm(rr[b], ("rr", b), lnrep, rr[b], ("rr", b), (stt, mv, rstd), epsmul=1.0)
                    S.dma("pool", lambda e, b=b, t=t: e.dma_start(out=x2s[t * 128:(t + 1) * 128, :], in_=rr[b][:]),
                          reads=[("rr", b)], writes=[("x2s", t)])
                S.flush()

        if "ffn1" not in stages:
            x1Tin = din("x1Tin", [128, 8, NTOK])
            S.dma("pool", lambda e: e.dma_start(out=x1T[:], in_=x1Tin[:, :, :]), writes=["x1Tinit"])
            S.flush()
        if "ffn1" in stages:
            def store1(t, tile, key):
                S.dma("pool", lambda e: e.dma_start(out=x1s[t * 128:(t + 1) * 128, :], in_=tile[:]), reads=[key],
                      writes=[("x1s", t)])
            ffn("f1", lambda t: xin[t * 128:(t + 1) * 128, :], f1g, f1u, f1d, 0, store1, x1T)

        gdnT = stp.enter_context(nc.sbuf_tensor("gdnT", [128, 4, 2048], BF16))
        if "attn" in stages:
            attention_stage()
        if "sproj" in stages:
            sample_proj_stage()
        if "sattn" in stages:
            sample_attn_stage()
        if "gdn" in stages:
            gdn_prompt_stage()
        if "sgdn" in stages:
            sample_gdn_stage()
        if "wout" in stages:
            wout_stage()
        S.flush()
        stp.close()
        if "ffn2" in stages:
            def store2(t, tile, key):
                if t == 0:
                    S.dma("pool", lambda e: e.dma_start(out=o_ys[:, :], in_=tile[0:64, :]), reads=[key], writes=[("oy", t)])
                else:
                    S.dma("pool", lambda e: e.dma_start(out=o_yp[(t - 1) * 128:t * 128, :], in_=tile[:]), reads=[key],
                          writes=[("oy", t)])
            ffn("f2", lambda t: x2s[t * 128:(t + 1) * 128, :], f2g, f2u, f2d, 4, store2, None)
        S.flush()
    return nc


def used_inputs(nc):
    names = set()
    for a in nc.allocations:
        try:
            if a.kind == "ExternalInput":
                names.add(a.name)
        except Exception:
            pass
    return names


def _t5_bucket(n):
    n = np.asarray(n, np.int64)
    nf = np.maximum(n, 1).astype(np.float64)
    large = 16 + np.floor(np.log(nf / 16.0) / math.log(2048 / 16.0) * 16.0 + 1e-9).astype(np.int64)
    large = np.minimum(large, 31)
    return np.where(n < 16, n, large)


def _onehot():
    oh = np.zeros((3, 33, 384), np.float32)
    for p, d in enumerate((1, 4, 16)):
        for m in range(383):
            dl = m - 127
            b = int(_t5_bucket(dl * d)) if 0 <= dl <= 128 else 32
            oh[p, b, m] = 1.0
    return oh


def _gconst():
    i = np.arange(64)
    P, Fr = i[:, None], i[None, :]
    g = np.zeros((64, 7, 64), np.float32)
    g[:, 0] = np.where(Fr < P, 0.0, NEG)
    g[:, 1] = np.where(Fr >= P, 0.0, NEG)
    g[:, 2] = np.where(Fr > P, -1.0, 0.0)
    g[:, 3] = np.eye(64)
    g[:, 4] = 1.0
    g[:, 5] = np.where(P <= Fr, 1.0, 0.0)
    g[:, 6] = np.where(P == 63, 1.0, 0.0)
    return g


def _gvec(inp):
    g = np.zeros((3, 64), np.float32)
    g[0, 0:8] = inp["gdn_a_log"][0]
    g[1, 0:8] = inp["gdn_dt_bias"][0]
    g[2, :] = inp["gdn_norm_w"][0]
    return g


def _wcm():
    w = np.zeros((32, 8, 4, 128), np.float32)
    for j in range(8):
        for p in range(128):
            if j < 4:
                pos = 1536 + j * 128 + p
            elif j < 7:
                pos = 16 * ((j - 4) * 32 + p // 4) + p % 4
            elif p < 4:
                pos = 2048 + p
            else:
                continue
            for t in range(4):
                dist = 2048 + t - pos
                if dist < 0:
                    continue
                for (win, d) in ((128, 1), (512, 4), (2048, 16)):
                    if dist % d == 0 and dist <= win:
                        w[int(_t5_bucket(dist)), j, t, p] += 1.0
    return w


def core_inputs(inp, c, big=None):
    f = np.float32
    xin = np.zeros((NTOK, D), f)
    xin[0:64] = inp["x_sample"][16 * c:16 * c + 16].reshape(64, D)
    xin[128:] = inp["x_prompt"][c]
    lnp = np.stack([inp["ln1_g"][0], inp["ln1_b"][0], inp["ln2_g"][0], inp["ln2_b"][0],
                    inp["ln3_g"][0], inp["ln3_b"][0]]).astype(f)
    m = {
        "xin": xin,
        "f1g": np.ascontiguousarray(inp["ffn1_w_gate"][0]), "f1u": np.ascontiguousarray(inp["ffn1_w_up"][0]),
        "f1d": np.ascontiguousarray(inp["ffn1_w_down"][0]),
        "lnp": lnp, "ident": np.eye(128, dtype=f),
        "w_in": np.ascontiguousarray(inp["w_in"][0]), "relb": np.ascontiguousarray(inp["rel_bias"]),
        "onehot": _onehot(), "antiid": np.ascontiguousarray(np.eye(128, dtype=f)[::-1]),
        "gconst": _gconst(), "convw": np.ascontiguousarray(inp["gdn_conv_w"][0]),
        "gvec": _gvec(inp),
        "sg_in": np.ascontiguousarray(inp["state_gdn"][0, 16 * c:16 * c + 16]).reshape(128, 4096),
        "sc_in": np.ascontiguousarray(inp["state_conv"][0, 16 * c:16 * c + 16]),
        "wcm": _wcm(),
        "cws": np.ascontiguousarray(np.broadcast_to(
            inp["gdn_conv_w"][0].reshape(4, 3, 8, 64).transpose(2, 1, 0, 3)[None], (16, 8, 3, 4, 64)).reshape(128, 3, 4, 64)),
        "gvs": np.ascontiguousarray(np.tile(np.stack([inp["gdn_a_log"][0], inp["gdn_dt_bias"][0]], axis=1), (16, 1))).astype(f),
        "w_out": np.ascontiguousarray(inp["w_out"][0]),
        "f2g": np.ascontiguousarray(inp["ffn2_w_gate"][0]), "f2u": np.ascontiguousarray(inp["ffn2_w_up"][0]),
        "f2d": np.ascontiguousarray(inp["ffn2_w_down"][0]),
    }
    if big is not None:
        m["ck"] = np.ascontiguousarray(big["cache_attn_k"][0, 16 * c:16 * c + 16]).reshape(16, 2048, 512)
        m["cv"] = np.ascontiguousarray(big["cache_attn_v"][0, 16 * c:16 * c + 16]).reshape(16, 2048, 512)
    return m


ALL_STAGES = ("ffn1", "attn", "sproj", "sattn", "gdn", "sgdn", "wout", "ffn2")
_NC_CACHE = {}


def gather_outputs(results):
    n = len(results)
    f = np.float32
    yp = np.stack([r["o_yp"] for r in results]).astype(f)
    ys = np.concatenate([r["o_ys"].reshape(16, 4, D) for r in results]).astype(f)
    kp = np.stack([r["o_kp"].reshape(2048, 8, 64) for r in results])[None].astype(f)
    vp = np.stack([r["o_vp"].reshape(2048, 8, 64) for r in results])[None].astype(f)
    sgp = np.stack([r["o_sgp"] for r in results])[None].astype(f)
    scp = np.stack([r["o_scp"] for r in results])[None].astype(f)
    ks = np.concatenate([r["o_ks"].reshape(16, 4, 8, 64) for r in results])[None].astype(f)
    vs = np.concatenate([r["o_vs"].reshape(16, 4, 8, 64) for r in results])[None].astype(f)
    sgs = np.concatenate([r["o_sgs"].reshape(16, 8, 64, 64) for r in results])[None].astype(f)
    scs = np.concatenate([r["o_scs"] for r in results])[None].astype(f)
    return (yp, ys, kp, vp, sgp, scp, ks, vs, sgs, scs)


def kernel(**inputs):
    inp = {k: np.asarray(v) for k, v in inputs.items()}
    if "nc" not in _NC_CACHE:
        _NC_CACHE["nc"] = build_nc(dbg=False, stages=ALL_STAGES)
    nc = _NC_CACHE["nc"]
    in_maps = [core_inputs(inp, c, inp) for c in range(8)]
    res = run_bass_kernel_spmd(nc, in_maps, core_ids=list(range(8)))
    return gather_outputs(res.results)
```

```python
import contextlib
import math
import numpy as np
import concourse.bass as bass
import concourse.mybir as mybir
from concourse.bass_utils import run_bass_kernel_spmd

F32 = mybir.dt.float32
BF16 = mybir.dt.bfloat16
ALU = mybir.AluOpType
AF = mybir.ActivationFunctionType
AX = mybir.AxisListType

D = 1024
DFF = 2816
NFC = 22
NT = 17
NTOK = NT * 128
INC = 3600
ALPHA = 2.0 ** 0.25
LN_EPS = 1e-5
NEG = -30000.0


def sl(start, count, step=1):
    return slice(start, start + (count - 1) * step + 1, step)


class Sched:
    def __init__(self, nc, stack, ndma=6):
        self.nc = nc
        self.eng = {"pe": nc.tensor, "act": nc.scalar, "dve": nc.vector, "pool": nc.gpsimd, "sp": nc.sync}
        self.esem = {k: stack.enter_context(nc.semaphore("es_" + k)) for k in self.eng}
        self.tick = {k: 0 for k in self.eng}
        self.dsem = {q: [stack.enter_context(nc.semaphore("ds_%s%d" % (q, i))) for i in range(ndma)]
                     for q in ("sp", "pool", "act")}
        self.duse = {q: [0] * ndma for q in self.dsem}
        self.dcnt = {q: 0 for q in self.dsem}
        self.seen = {k: {} for k in self.eng}
        self.ops = []

    def op(self, eng, fn, reads=(), writes=()):
        self.ops.append(dict(eng=eng, fn=fn, reads=tuple(reads), writes=tuple(writes), dma=False))

    def dma(self, q, fn, reads=(), writes=()):
        self.ops.append(dict(eng=q, fn=fn, reads=tuple(reads), writes=tuple(writes), dma=True))

    def capture(self, fn):
        saved = self.ops
        self.ops = []
        fn()
        got = self.ops
        self.ops = saved
        return got

    def emit_merged(self, a, b):
        i = j = 0
        while i < len(a) or j < len(b):
            if j >= len(b) or (i < len(a) and i * len(b) <= j * len(a)):
                self.ops.append(a[i])
                i += 1
            else:
                self.ops.append(b[j])
                j += 1

    def _wait(self, e, sem, val):
        key = id(sem)
        if self.seen[e].get(key, 0) < val:
            self.eng[e].wait_ge(sem, val)
            self.seen[e][key] = val

    def flush(self, barrier=True):
        ops = self.ops
        self.ops = []
        last_w = {}
        readers = {}
        needs = [False] * len(ops)
        for i, o in enumerate(ops):
            deps = set()

            def inorder(j):
                return ops[j]["eng"] == o["eng"] == "pe" and not ops[j]["dma"] and not o["dma"]

            for r in o["reads"]:
                j = last_w.get(r)
                if j is not None and not (inorder(j) and o["eng"] == "pe"):
                    deps.add(j)
            for w in o["writes"]:
                j = last_w.get(w)
                if j is not None and not inorder(j):
                    deps.add(j)
                for j in readers.get(w, ()):
                    if not inorder(j):
                        deps.add(j)
            o["deps"] = sorted(deps)
            for j in deps:
                needs[j] = True
            for r in o["reads"]:
                readers.setdefault(r, []).append(i)
            for w in o["writes"]:
                last_w[w] = i
                readers[w] = []
        lastop = {}
        for i, o in enumerate(ops):
            if not o["dma"]:
                lastop[o["eng"]] = i
        for i in lastop.values():
            needs[i] = True
        for i, o in enumerate(ops):
            e = o["eng"]
            for j in o["deps"]:
                ev = ops[j]["event"]
                self._wait(e, ev[0], ev[1])
            if o["dma"]:
                n = len(self.dsem[e])
                slot = self.dcnt[e] % n
                self.dcnt[e] += 1
                sem = self.dsem[e][slot]
                k = self.duse[e][slot]
                if k > 0:
                    self._wait(e, sem, 16 * k)
                ins = o["fn"](self.eng[e])
                ins.then_inc(sem, 16)
                self.duse[e][slot] = k + 1
                o["event"] = (sem, 16 * (k + 1))
            else:
                ins = o["fn"](self.eng[e])
                if needs[i]:
                    self.tick[e] += 1
                    ins.then_inc(self.esem[e], 1)
                    o["event"] = (self.esem[e], self.tick[e])
                else:
                    o["event"] = None
        if barrier:
            self.barrier()

    def barrier(self):
        for e in self.eng:
            for d in self.eng:
                if d != e and self.tick[d] > 0:
                    self._wait(e, self.esem[d], self.tick[d])
            for q in self.dsem:
                for s, k in zip(self.dsem[q], self.duse[q]):
                    if k > 0:
                        self._wait(e, s, 16 * k)


def build_nc(dbg=False, stages=("ffn1",)):
    nc = bass.Bass("TRN2", target_bir_lowering=False)

    def din(name, shape, dt=F32):
        return nc.dram_tensor(name, list(shape), dt, kind="ExternalInput").ap()

    def dout(name, shape, dt=F32):
        return nc.dram_tensor(name, list(shape), dt, kind="ExternalOutput").ap()

    def dscr(name, shape, dt=F32):
        return nc.dram_tensor(name, list(shape), dt, kind="ExternalOutput" if dbg else "Internal").ap()

    xin = din("xin", [NTOK, D])
    f1g = din("f1g", [D, DFF])
    f1u = din("f1u", [D, DFF])
    f1d = din("f1d", [DFF, D])
    lnp = din("lnp", [6, D])
    ident_d = din("ident", [128, 128])
    x1s = dscr("x1s", [NTOK, D])
    w_in = din("w_in", [D, INC])
    relb = din("relb", [32, 8])
    onehot = din("onehot", [3, 33, 384])
    antiid = din("antiid", [128, 128])
    o_kp = dout("o_kp", [2048, 512])
    o_vp = dout("o_vp", [2048, 512])
    o_ks = dout("o_ks", [64, 512])
    o_vs = dout("o_vs", [64, 512])
    fvd = dscr("fvd", [3, 8, 384])
    attn_s = dscr("attn_s", [8, 64, 2048], BF16)
    gconst = din("gconst", [64, 7, 64])
    convw = din("convw", [4, 1536])
    gvec = din("gvec", [3, 64])
    o_sgp = dout("o_sgp", [8, 64, 64])
    o_scp = dout("o_scp", [3, 1536])
    gdn_dbg = dscr("gdn_dbg", [128, 4, 2048]) if dbg else None
    ck = din("ck", [16, 2048, 512])
    cv = din("cv", [16, 2048, 512])
    sg_in = din("sg_in", [128, 4096])
    sc_in = din("sc_in", [16, 3, 1536])
    wcm = din("wcm", [32, 8, 4, 128])
    cws = din("cws", [128, 3, 4, 64])
    gvs = din("gvs", [128, 2])
    w_out = din("w_out", [D, D])
    f2g = din("f2g", [D, DFF])
    f2u = din("f2u", [D, DFF])
    f2d = din("f2d", [DFF, D])
    o_ys = dout("o_ys", [64, D])
    o_yp = dout("o_yp", [2048, D])
    o_sgs = dout("o_sgs", [128, 4096])
    o_scs = dout("o_scs", [16, 3, 1536])
    qs_s = dscr("qs_s", [64, 512])
    gq_s = dscr("gq_s", [64, 1536])
    z_s = dscr("z_s", [64, 512])
    ba_s = dscr("ba_s", [64, 16])
    heads_s = dscr("heads_s", [64, D])
    x2s = dscr("x2s", [NTOK, D])

    with contextlib.ExitStack() as stack:
        S = Sched(nc, stack)
        sb = lambda name, shape, dt=F32: stack.enter_context(nc.sbuf_tensor(name, list(shape), dt))
        psb = [stack.enter_context(nc.psum_tensor("psb%d" % i, [128, 512], F32)) for i in range(7)]
        pst = stack.enter_context(nc.psum_tensor("pst", [128, 1024], BF16))

        identf = sb("identf", [128, 128])
        identb = sb("identb", [128, 128], BF16)
        stp = contextlib.ExitStack()
        x1T = stp.enter_context(nc.sbuf_tensor("x1T", [128, 8, NTOK], BF16))

        S.dma("sp", lambda e: e.dma_start(out=identf[:], in_=ident_d[:, :]), writes=["identf"])
        S.op("dve", lambda e: e.tensor_copy(out=identb[:], in_=identf[:]), reads=["identf"], writes=["identb"])
        S.flush()

        def to_featmajor(src_bf, skey, dstT, col0, dkey):
            for kc in range(8):
                S.op("pe", lambda e, kc=kc: e.transpose(pst[:, kc * 128:(kc + 1) * 128],
                                                        src_bf[:, kc * 128:(kc + 1) * 128], identb[:]),
                     reads=[skey, "identb"], writes=["pst"])
            S.op("dve", lambda e: e.tensor_copy(out=dstT[:, :, col0:col0 + 128],
                                               in_=pst[:].rearrange("p (k c) -> p k c", k=8)),
                 reads=["pst"], writes=[dkey])

        def layernorm(r, rkey, lnrep, out, okey, tmp, epsmul=4.0):
            st, mv, rstd = tmp
            for c in range(2):
                S.op("dve", lambda e, c=c: e.bn_stats(out=st[:, c, :], in_=r[:, c * 512:(c + 1) * 512]),
                     reads=[rkey], writes=[("st", c)])
            S.op("dve", lambda e: e.bn_aggr(out=mv[:], in_=st[:].rearrange("p c s -> p (c s)")),
                 reads=[("st", 0), ("st", 1)], writes=["mv"])
            S.op("dve", lambda e: e.tensor_scalar(out=rstd[:], in0=mv[:, 1:2], scalar1=epsmul * LN_EPS, scalar2=None,
                                                  op0=ALU.add), reads=["mv"], writes=["rstd"])
            S.op("act", lambda e: e.sqrt(out=rstd[:], in_=rstd[:]), reads=["rstd"], writes=["rstd"])
            S.op("dve", lambda e: e.reciprocal(out=rstd[:], in_=rstd[:]), reads=["rstd"], writes=["rstd"])
            S.op("dve", lambda e: e.tensor_scalar(out=r[:], in0=r[:], scalar1=mv[:, 0:1], scalar2=rstd[:, 0:1],
                                                  op0=ALU.subtract, op1=ALU.mult),
                 reads=[rkey, "mv", "rstd"], writes=[rkey])
            S.op("pool", lambda e: e.tensor_tensor(out=r[:], in0=r[:], in1=lnrep[:, 0, :], op=ALU.mult),
                 reads=[rkey, ("lnrep", 0)], writes=[rkey])
            S.op("dve", lambda e: e.tensor_tensor(out=out[:], in0=r[:], in1=lnrep[:, 1, :], op=ALU.add),
                 reads=[rkey, ("lnrep", 1)], writes=[okey])

        def ffn(tag, xsrc, wg, wu, wd, gi, store, xTout):
            with contextlib.ExitStack() as st2:
                sb2 = lambda name, shape, dt=F32: st2.enter_context(nc.sbuf_tensor(tag + name, list(shape), dt))
                MT = 9
                xT = sb2("xT", [128, 8, MT * 128], BF16)
                hT = sb2("hT", [128, NFC, MT * 128], BF16)
                wgb = [sb2("wgb%d" % i, [128, 8, 256], BF16) for i in range(2)]
                wub = [sb2("wub%d" % i, [128, 8, 256], BF16) for i in range(2)]
                wdb = sb2("wdb", [128, NFC, D], BF16)
                xs = [sb2("xs%d" % i, [128, D]) for i in range(2)]
                xb = [sb2("xb%d" % i, [128, D], BF16) for i in range(2)]
                sg = [sb2("sg%d" % i, [128, 512]) for i in range(2)]
                rr = [sb2("rr%d" % i, [128, D]) for i in range(2)]
                oo = rr
                lnrep = sb2("ln", [128, 2, D])
                for i in range(2):
                    S.dma("sp", lambda e, i=i: e.dma_start(
                        out=lnrep[:, i, :], in_=lnp[gi + i:gi + i + 1, :].partition_broadcast(128)),
                          writes=[("lnrep", i)])
                stt = sb2("st", [128, 2, 6])
                mv = sb2("mv", [128, 2])
                rstd = sb2("rstd", [128, 1])
                wgv = wg.rearrange("(kc p) f -> p kc f", p=128)
                wuv = wu.rearrange("(kc p) f -> p kc f", p=128)
                wdv = wd.rearrange("(fc p) d -> p fc d", p=128)
                for mi, tiles in enumerate((list(range(0, MT)), list(range(MT, NT)))):
                    ntl = len(tiles)
                    for li, t in enumerate(tiles):
                        b = li % 2
                        S.dma("sp", lambda e, b=b, t=t: e.dma_start(out=xs[b][:], in_=xsrc(t)), writes=[("xs", b)])
                        S.op("act", lambda e, b=b: e.copy(out=xb[b][:], in_=xs[b][:]), reads=[("xs", b)],
                             writes=[("xb", b)])
                        to_featmajor(xb[b], ("xb", b), xT, li * 128, ("xT", li))
                    if mi == 0:
                        for q in range(2):
                            S.dma("pool", lambda e, q=q: e.dma_start(out=wdb[:, q * 11:(q + 1) * 11, :],
                                                                     in_=wdv[:, q * 11:(q + 1) * 11, :]),
                                  writes=[("wdb", q)])
                    tbs = [(c0, min(512, ntl * 128 - c0)) for c0 in range(0, ntl * 128, 512)]
                    for fb in range(11):
                        wbuf = fb % 2
                        S.dma("pool", lambda e, fb=fb, wbuf=wbuf: e.dma_start(
                            out=wgb[wbuf][:], in_=wgv[:, :, fb * 256:(fb + 1) * 256]), writes=[("wgb", wbuf)])
                        S.dma("pool", lambda e, fb=fb, wbuf=wbuf: e.dma_start(
                            out=wub[wbuf][:], in_=wuv[:, :, fb * 256:(fb + 1) * 256]), writes=[("wub", wbuf)])
                        for j in range(2):
                            fc = fb * 2 + j
                            for ti, (c0, cw) in enumerate(tbs):
                                pb = (fc * len(tbs) + ti) % 2
                                pg, pu = psb[pb], psb[2 + pb]
                                xkeys = [("xT", li) for li in range(c0 // 128, (c0 + cw) // 128)]
                                for kc in range(8):
                                    S.op("pe", lambda e, pg=pg, kc=kc, wbuf=wbuf, j=j, c0=c0, cw=cw: e.matmul(
                                        pg[:, 0:cw], wgb[wbuf][:, kc, j * 128:(j + 1) * 128], xT[:, kc, c0:c0 + cw],
                                        start=(kc == 0), stop=(kc == 7)),
                                         reads=[("wgb", wbuf)] + xkeys, writes=[("psb", pb)])
                                for kc in range(8):
                                    S.op("pe", lambda e, pu=pu, kc=kc, wbuf=wbuf, j=j, c0=c0, cw=cw: e.matmul(
                                        pu[:, 0:cw], wub[wbuf][:, kc, j * 128:(j + 1) * 128], xT[:, kc, c0:c0 + cw],
                                        start=(kc == 0), stop=(kc == 7)),
                                         reads=[("wub", wbuf)] + xkeys, writes=[("psb", 2 + pb)])
                                S.op("act", lambda e, pg=pg, pb=pb, cw=cw: e.activation(
                                    out=sg[pb][:, 0:cw], in_=pg[:, 0:cw], func=AF.Silu),
                                     reads=[("psb", pb)], writes=[("sg", pb)])
                                S.op("dve", lambda e, pu=pu, pb=pb, fc=fc, c0=c0, cw=cw: e.tensor_tensor(
                                    out=hT[:, fc, c0:c0 + cw], in0=sg[pb][:, 0:cw], in1=pu[:, 0:cw], op=ALU.mult),
                                     reads=[("sg", pb), ("psb", 2 + pb)], writes=[("hT", fc, ti)])
                    for li, t in enumerate(tiles):
                        b = li % 2
                        ti = li // 4
                        S.dma("sp", lambda e, b=b, t=t: e.dma_start(out=xs[b][:], in_=xsrc(t)), writes=[("xs", b)])
                        for half in range(2):
                            pd = psb[4 + half]
                            for fc in range(NFC):
                                S.op("pe", lambda e, pd=pd, fc=fc, li=li, half=half: e.matmul(
                                    pd[:, :], hT[:, fc, li * 128:(li + 1) * 128], wdb[:, fc, half * 512:(half + 1) * 512],
                                    start=(fc == 0), stop=(fc == NFC - 1)),
                                     reads=[("hT", fc, ti), ("wdb", fc // 11)], writes=[("psb", 4 + half)])
                            S.op("dve", lambda e, pd=pd, b=b, half=half: e.scalar_tensor_tensor(
                                out=rr[b][:, half * 512:(half + 1) * 512], in0=xs[b][:, half * 512:(half + 1) * 512],
                                scalar=2.0 * ALPHA, in1=pd[:, :], op0=ALU.mult, op1=ALU.add),
                                 reads=[("xs", b), ("psb", 4 + half)], writes=[("rr", b)])
                        layernorm(rr[b], ("rr", b), lnrep, rr[b], ("rr", b), (stt, mv, rstd))
                        store(t, rr[b], ("rr", b))
                        if xTout is not None:
                            S.op("act", lambda e, b=b: e.copy(out=xb[b][:], in_=rr[b][:]), reads=[("rr", b)],
                                 writes=[("xb", b)])
                            to_featmajor(xb[b], ("xb", b), xTout, t * 128, ("xTo", t))
                S.flush()

        PAT = ((128, 1), (512, 4), (2048, 16))

        def unit_tokens(d, u):
            nblk = 16 // d
            r, n = u // nblk, u % nblk
            return r, n, nblk, r + d * 128 * n

        def attention_stage():
            with contextlib.ExitStack() as st2:
                sb2 = lambda name, shape, dt=F32: st2.enter_context(nc.sbuf_tensor("at" + name, list(shape), dt))
                attnT = [sb2("attnT%d" % i, [64, 2048], BF16) for i in range(2)]
                qT = sb2("qT", [128, 4, 2048], BF16)
                kT = sb2("kT", [128, 4, 2048], BF16)
                vaug = [sb2("vaug%d" % i, [128, 16, 8, 65], BF16) for i in range(3)]
                brev = sb2("brev", [128, 24, 256], BF16)
                jb = sb2("jb", [128, 128], BF16)
                onesf = sb2("onesf", [128, 64])
                rbx = sb2("rbx", [33, 8])
                ohs = sb2("ohs", [33, 3, 384])
                fvs = sb2("fvs", [8, 3, 384])
                winv = w_in.rearrange("(kc p) f -> p kc f", p=128)
                S.dma("pool", lambda e: e.dma_start(out=jb[:], in_=antiid[:, :]), writes=["jb"])
                S.op("pool", lambda e: e.memset(onesf[:], 1.0), writes=["onesf"])
                S.op("pool", lambda e: e.memset(rbx[32:33, :], NEG), writes=["rbx1"])
                S.dma("sp", lambda e: e.dma_start(out=rbx[0:32, :], in_=relb[:, :]), writes=["rbx0"])
                S.dma("sp", lambda e: e.dma_start(out=ohs[:], in_=onehot.rearrange("p b m -> b p m")), writes=["ohs"])
                for p in range(3 if "notables" not in stages else 0):
                    S.op("pe", lambda e, p=p: e.matmul(psb[p][0:8, 0:384], rbx[:, :], ohs[:, p, :], start=True, stop=True),
                         reads=["rbx0", "rbx1", "ohs"], writes=[("psb", p)])
                    S.op("dve", lambda e, p=p: e.tensor_copy(out=fvs[:, p, :], in_=psb[p][0:8, 0:384]),
                         reads=[("psb", p)], writes=[("fvs", p)])
                if "notables" not in stages:
                    S.dma("sp", lambda e: e.dma_start(out=fvd.rearrange("p h m -> h p m"), in_=fvs[:]),
                          reads=[("fvs", p) for p in range(3)], writes=["fvd"])
                if "nohankel" not in stages:
                    S.dma("pool", lambda e: e.dma_start(
                        out=brev[:], in_=bass.AP(fvd.tensor, 0, [[1, 128], [384, 24], [1, 256]])),
                          reads=["fvd"], writes=["brev"])
                for i in range(3):
                    S.op("pool", lambda e, i=i: e.memset(vaug[i][:, :, :, 64:65], 1.0), writes=[("vone", i)])
                with contextlib.ExitStack() as st3:
                    sb3 = lambda name, shape, dt=F32: st3.enter_context(nc.sbuf_tensor("ap" + name, list(shape), dt))
                    wb = [sb3("wb%d" % i, [128, 8, 512], BF16) for i in range(3)]
                    kvo = [sb3("kvo%d" % i, [128, 512]) for i in range(2)]
                    for blk in range(3):
                        S.dma("pool", lambda e, blk=blk: e.dma_start(out=wb[blk][:], in_=winv[:, :, blk * 512:(blk + 1) * 512]),
                              writes=[("wb", blk)])
                    cnt = 0
                    if "noproj" in stages:
                        S.flush()
                        return
                    for blk, dst in (((0, qT), (1, kT)) if "noqk" not in stages else ()):
                        for pair in range(4):
                            for tb in range(4):
                                pb = cnt % 2
                                cnt += 1
                                for kc in range(8):
                                    S.op("pe", lambda e, pb=pb, blk=blk, pair=pair, tb=tb, kc=kc: e.matmul(
                                        psb[pb][:, :], wb[blk][:, kc, pair * 128:(pair + 1) * 128],
                                        x1T[:, kc, 128 + tb * 512:128 + (tb + 1) * 512], start=(kc == 0), stop=(kc == 7)),
                                         reads=[("wb", blk)], writes=[("psb", pb)])
                                if blk == 0:
                                    S.op("act", lambda e, pb=pb, pair=pair, tb=tb: e.mul(
                                        out=qT[:, pair, tb * 512:(tb + 1) * 512], in_=psb[pb][:, :], mul=0.125),
                                         reads=[("psb", pb)], writes=[("qT", pair)])
                                else:
                                    S.op("dve", lambda e, pb=pb, pair=pair, tb=tb: e.tensor_copy(
                                        out=kT[:, pair, tb * 512:(tb + 1) * 512], in_=psb[pb][:, :]),
                                         reads=[("psb", pb)], writes=[("kT", pair)])
                    for t in range(NT if "nokv" not in stages else 0):
                        for blk in (1, 2):
                            pb = 2 + (cnt % 2)
                            cnt += 1
                            ob = blk - 1
                            for kc in range(8):
                                S.op("pe", lambda e, pb=pb, blk=blk, t=t, kc=kc: e.matmul(
                                    psb[pb][:, :], x1T[:, kc, t * 128:(t + 1) * 128], wb[blk][:, kc, :],
                                    start=(kc == 0), stop=(kc == 7)),
                                     reads=[("wb", blk)], writes=[("psb", pb)])
                            S.op("act", lambda e, pb=pb, ob=ob: e.activation(out=kvo[ob][:], in_=psb[pb][:, :], func=AF.Copy),
                                 reads=[("psb", pb)], writes=[("kvo", ob)])
                            if blk == 2 and t >= 1:
                                S.op("dve", lambda e, ob=ob, t=t: e.tensor_copy(
                                    out=vaug[0][:, t - 1, :, 0:64], in_=kvo[ob][:, :].rearrange("p (h e) -> p h e", h=8)),
                                     reads=[("kvo", ob)], writes=[("vaug", 0, t - 1)])
                            if "nokvdma" in stages:
                                continue
                            if t == 0:
                                dst = (o_ks if blk == 1 else o_vs)[0:64, :]
                                S.dma("sp", lambda e, dst=dst, ob=ob: e.dma_start(out=dst, in_=kvo[ob][0:64, :]),
                                      reads=[("kvo", ob)], writes=[("okv", blk, t)])
                            else:
                                dst = (o_kp if blk == 1 else o_vp)[(t - 1) * 128:t * 128, :]
                                S.dma("sp", lambda e, dst=dst, ob=ob: e.dma_start(out=dst, in_=kvo[ob][:, :]),
                                      reads=[("kvo", ob)], writes=[("okv", blk, t)])
                    for pi in ((1, 2) if "nodil" not in stages else ()):
                        d = PAT[pi][1]
                        for u in range(16):
                            r, n, nblk, t0 = unit_tokens(d, u)
                            pb = 2 + (cnt % 2)
                            cnt += 1
                            for kc in range(8):
                                S.op("pe", lambda e, pb=pb, kc=kc, t0=t0, d=d: e.matmul(
                                    psb[pb][:, :], x1T[:, kc, sl(128 + t0, 128, d)], wb[2][:, kc, :],
                                    start=(kc == 0), stop=(kc == 7)),
                                     reads=[("wb", 2)], writes=[("psb", pb)])
                            S.op("dve", lambda e, pb=pb, pi=pi, u=u: e.tensor_copy(
                                out=vaug[pi][:, u, :, 0:64], in_=psb[pb][:, :].rearrange("p (h e) -> p h e", h=8)),
                                 reads=[("psb", pb)], writes=[("vaug", pi, u)])
                    S.flush()
                if "noattnmain" in stages:
                    return
                with contextlib.ExitStack() as st3:
                    sb3 = lambda name, shape, dt=F32: st3.enter_context(nc.sbuf_tensor("aa" + name, list(shape), dt))
                    acc = [sb3("acc%d" % i, [65, 2048]) for i in range(2)]
                    pts = [sb3("pt%d" % i, [128, 256], BF16) for i in range(4)]
                    rcp = sb3("rcp", [65, 2048])
                    ptc = 0
                    cnt = 0
                    for h in range(8):
                        pair, base = h // 2, (h % 2) * 64
                        ab = h % 2
                        A = acc[ab]
                        units = [(pi, d, u) for pi, (win, d) in enumerate(PAT) for u in range(16)]

                        def st_part(ix, h=h, pair=pair, base=base):
                            pi, d, u = units[ix]
                            r, n, nblk, t0 = unit_tokens(d, u)
                            W = 256 if n + 1 < nblk else 128
                            ps = ix % 2
                            pt = ix % 4
                            S.op("pe", lambda e: e.matmul(
                                psb[ps][:, 0:W], kT[base:base + 64, pair, sl(t0, 128, d)],
                                qT[base:base + 64, pair, sl(t0, W, d)], start=True, stop=False),
                                 reads=[("qT", pair), ("kT", pair)], writes=[("psb", ps)])
                            S.op("pe", lambda e: e.matmul(
                                psb[ps][:, 0:W], jb[:, :], brev[:, pi * 8 + h, 0:W], start=False, stop=True),
                                 reads=["jb", "brev"], writes=[("psb", ps)])
                            S.op("act", lambda e: e.activation(
                                out=pts[pt][:, 0:W], in_=psb[ps][:, 0:W], func=AF.Exp),
                                 reads=[("psb", ps)], writes=[("pt", pt)])

                        def pv_part(ix, h=h, A=A, ab=ab):
                            pi, d, u = units[ix]
                            r, n, nblk, t0 = unit_tokens(d, u)
                            pt = ix % 4
                            prev = (ix - 1) % 4
                            po = 2 + (ix % 2)
                            first = (n == 0)
                            S.op("pe", lambda e: e.matmul(
                                psb[po][0:65, 0:128], vaug[pi][:, u, h, :], pts[pt][:, 0:128], start=True, stop=first),
                                 reads=[("vaug", pi, u), ("vone", pi), ("pt", pt)], writes=[("psb", po)])
                            if not first:
                                S.op("pe", lambda e: e.matmul(
                                    psb[po][0:65, 0:128], vaug[pi][:, u - 1, h, :], pts[prev][:, 128:256],
                                    start=False, stop=True),
                                     reads=[("vaug", pi, u - 1), ("vone", pi), ("pt", prev)], writes=[("psb", po)])
                            dst = A[:, sl(t0, 128, d)]
                            if pi == 0:
                                S.op("dve", lambda e: e.tensor_copy(out=dst, in_=psb[po][0:65, 0:128]),
                                     reads=[("psb", po)], writes=[("acc", ab)])
                            else:
                                S.op("dve", lambda e: e.tensor_tensor(
                                    out=dst, in0=dst, in1=psb[po][0:65, 0:128], op=ALU.add),
                                     reads=[("psb", po), ("acc", ab)], writes=[("acc", ab)])

                        st_part(0)
                        st_part(1)
                        for ix in range(len(units)):
                            if ix + 2 < len(units):
                                st_part(ix + 2)
                            pv_part(ix)
                        S.op("act", lambda e, A=A: e.activation(out=rcp[64:65, :], in_=A[64:65, :], func=AF.Ln),
                             reads=[("acc", ab)], writes=["rcp"])
                        S.op("act", lambda e: e.activation(out=rcp[64:65, :], in_=rcp[64:65, :], func=AF.Exp, scale=-1.0),
                             reads=["rcp"], writes=["rcp"])
                        for tb in range(4):
                            pr = 4 + (tb % 2)
                            S.op("pe", lambda e, pr=pr, tb=tb: e.matmul(
                                psb[pr][0:64, :], onesf[64:65, 0:64], rcp[64:65, tb * 512:(tb + 1) * 512],
                                start=True, stop=True), reads=["rcp", "onesf"], writes=[("psb", pr)])
                            S.op("dve", lambda e, pr=pr, tb=tb, A=A, ab=ab: e.tensor_tensor(
                                out=attnT[ab][:, tb * 512:(tb + 1) * 512], in0=A[0:64, tb * 512:(tb + 1) * 512],
                                in1=psb[pr][0:64, :], op=ALU.mult),
                                 reads=[("psb", pr), ("acc", ab)], writes=[("attnT", ab)])
                        S.dma("sp", lambda e, h=h, ab=ab: e.dma_start(out=attn_s[h, :, :], in_=attnT[ab][:, :]),
                              reads=[("attnT", ab)], writes=[("attn_s", h)])
                    S.flush()

        def bl(ap, n=64):
            return ap.unsqueeze(2).broadcast_to([ap.shape[0], ap.shape[1], n])

        def bm(ap, n=8):
            return ap.unsqueeze(1).broadcast_to([ap.shape[0], n, ap.shape[1]])

        def v3(ap, h=8):
            return ap.rearrange("p (h x) -> p h x", h=h)

        def gdn_prompt_stage():
            with contextlib.ExitStack() as st2:
                sb2 = lambda name, shape, dt=F32: st2.enter_context(nc.sbuf_tensor("gd" + name, list(shape), dt))
                qh = sb2("qh", [64, 8, 2048], BF16)
                kh = sb2("kh", [64, 8, 2048], BF16)
                vT = sb2("vT", [128, 4, 2048], BF16)
                gcn = sb2("gcn", [64, 7, 64])
                NEGS, NEGT, MSKT, ID64, ONES, TRI, SEL = [gcn[:, i, :] for i in range(7)]
                cwq = sb2("cwq", [64, 16, 4])
                cwv = sb2("cwv", [128, 4, 4])
                nwr = sb2("nwr", [64, 64])
                wz = sb2("wz", [128, 8, 512], BF16)
                wba = sb2("wba", [128, 8, 16], BF16)
                stc = contextlib.ExitStack()
                cwr = stc.enter_context(nc.sbuf_tensor("gdcwr", [4, 1536], F32))
                winv = w_in.rearrange("(kc p) f -> p kc f", p=128)
                S.dma("sp", lambda e: e.dma_start(out=gcn[:], in_=gconst[:, :, :]), writes=["gcn"])
                S.dma("sp", lambda e: e.dma_start(out=cwr[:], in_=convw[:, :]), writes=["cwr"])
                S.dma("sp", lambda e: e.dma_start(out=nwr[:], in_=gvec[2:3, :].partition_broadcast(64)), writes=["nwr"])
                S.dma("pool", lambda e: e.dma_start(out=wz[:], in_=winv[:, :, 3072:3584]), writes=["wz"])
                S.dma("pool", lambda e: e.dma_start(out=wba[:], in_=winv[:, :, 3584:3600]), writes=["wba"])
                for g in range(16):
                    S.op("pe", lambda e, g=g: e.transpose(psb[0][0:64, g * 4:(g + 1) * 4], cwr[0:4, g * 64:(g + 1) * 64],
                                                          identf[0:4, 0:4]), reads=["cwr", "identf"], writes=[("psb", 0)])
                for c in range(4):
                    S.op("pe", lambda e, c=c: e.transpose(psb[1][:, c * 4:(c + 1) * 4],
                                                          cwr[0:4, 1024 + c * 128:1024 + (c + 1) * 128], identf[0:4, 0:4]),
                         reads=["cwr", "identf"], writes=[("psb", 1)])
                S.op("dve", lambda e: e.tensor_copy(out=cwq[:], in_=v3(psb[0][0:64, 0:64], 16)), reads=[("psb", 0)],
                     writes=["cwq"])
                S.op("dve", lambda e: e.tensor_copy(out=cwv[:], in_=v3(psb[1][:, 0:16], 4)), reads=[("psb", 1)],
                     writes=["cwv"])
                S.flush()
                stc.close()

                with contextlib.ExitStack() as st3:
                    sb3 = lambda name, shape, dt=F32: st3.enter_context(nc.sbuf_tensor("g1" + name, list(shape), dt))
                    wb = [sb3("wb%d" % i, [128, 8, 512], BF16) for i in range(2)]
                    raws = [sb3("raw%d" % i, [128, 2051]) for i in range(2)]
                    cacs = [sb3("cac%d" % i, [128, 2048]) for i in range(2)]
                    rin = [sb3("rin%d" % i, [64, 512]) for i in range(2)]
                    for i in range(2):
                        S.op("pool", lambda e, i=i: e.memset(raws[i][:, 0:3], 0.0), writes=[("raw0", i)])
                    epsb = sb3("epsb", [64, 2])
                    S.op("pool", lambda e: e.memset(epsb[:, 0:1], 64.0e-6), writes=["epsb"])
                    S.op("pool", lambda e: e.memset(epsb[:, 1:2], 1.0e-6), writes=["epsb"])
                    cnt = 0
                    gi = 0
                    pending_tail = []
                    for blk in range(3):
                        wbuf = blk % 2
                        S.dma("pool", lambda e, blk=blk, wbuf=wbuf: e.dma_start(
                            out=wb[wbuf][:], in_=winv[:, :, 1536 + blk * 512:1536 + (blk + 1) * 512]),
                              writes=[("wb", wbuf)])
                        ngrp, P = (8, 64) if blk < 2 else (4, 128)
                        for g in range(ngrp):
                          def group_body(part, blk=blk, g=g, wbuf=wbuf, P=P, rb_=gi % 2, cnt0=cnt):
                            cnt = cnt0
                            raw, cac = raws[rb_], cacs[rb_]
                            RAW, CAC = ("raw", rb_), ("cac", rb_)
                            if part == "tail":
                                return group_tail(blk, g, P, raw, cac, RAW, CAC, rb_)
                            for tb in range(4):
                                pb = cnt % 2
                                cnt += 1
                                for kc in range(8):
                                    S.op("pe", lambda e, pb=pb, wbuf=wbuf, g=g, P=P, tb=tb, kc=kc: e.matmul(
                                        psb[pb][0:P, :], wb[wbuf][:, kc, g * P:(g + 1) * P],
                                        x1T[:, kc, 128 + tb * 512:128 + (tb + 1) * 512], start=(kc == 0), stop=(kc == 7)),
                                         reads=[("wb", wbuf)], writes=[("psb", pb)])
                                S.op("act", lambda e, pb=pb, P=P, tb=tb, raw=raw: e.activation(
                                    out=raw[0:P, 3 + tb * 512:3 + (tb + 1) * 512], in_=psb[pb][0:P, :], func=AF.Copy),
                                     reads=[("psb", pb)], writes=[RAW])
                            col0 = blk * 512 + g * P
                            S.dma("sp", lambda e, P=P, col0=col0, raw=raw: e.dma_start(
                                out=o_scp[:, col0:col0 + P].rearrange("j p -> p j"), in_=raw[0:P, 2048:2051],
                                allow_slow_non_contiguous=True), reads=[RAW], writes=[("o_scp", col0)])
                            cwt = (cwq[:, blk * 8 + g, :] if blk < 2 else cwv[:, g, :])
                            CH = CAC
                            S.op("dve", lambda e, P=P, cwt=cwt, raw=raw, cac=cac: e.tensor_scalar(
                                out=cac[0:P, :], in0=raw[0:P, 3:2051], scalar1=cwt[:, 3:4], scalar2=None, op0=ALU.mult),
                                 reads=[RAW, ("raw0", rb_), "cwq", "cwv"], writes=[CH])
                            for j in (2, 1, 0):
                                S.op("dve", lambda e, P=P, cwt=cwt, j=j, raw=raw, cac=cac: e.scalar_tensor_tensor(
                                    out=cac[0:P, :], in0=raw[0:P, j:j + 2048], scalar=cwt[:, j:j + 1], in1=cac[0:P, :],
                                    op0=ALU.mult, op1=ALU.add), reads=[RAW, ("raw0", rb_), CH], writes=[CH])
                          def group_tail(blk, g, P, raw, cac, RAW, CAC, rb_):
                            CHS = [CAC]
                            if blk == 2:
                                S.op("act", lambda e, g=g, cac=cac: e.activation(out=vT[:, g, :], in_=cac[:, :], func=AF.Silu),
                                     reads=CHS, writes=[("vT", g)])
                                return
                            S.op("act", lambda e, cac=cac: e.activation(out=cac[0:64, :], in_=cac[0:64, :], func=AF.Silu),
                                 reads=CHS, writes=[CAC])
                            S.op("act", lambda e, cac=cac, raw=raw: e.square(out=raw[0:64, 3:2051], in_=cac[0:64, :]),
                                 reads=[CAC], writes=[RAW])
                            dst = qh if blk == 0 else kh
                            for tb in range(4):
                                pb = 2 + (tb % 2)
                                rb = tb % 2
                                S.op("pe", lambda e, pb=pb, tb=tb, raw=raw: e.matmul(
                                    psb[pb][0:64, :], ONES, raw[0:64, 3 + tb * 512:3 + (tb + 1) * 512], start=True, stop=True),
                                     reads=[RAW, "gcn"], writes=[("psb", pb)])
                                sc = 64.0 if blk == 0 else 1.0
                                S.op("act", lambda e, pb=pb, rb=rb, sc=sc, blk=blk: e.activation(
                                    out=rin[rb][:], in_=psb[pb][0:64, :], func=AF.Ln, scale=sc, bias=epsb[:, blk:blk + 1]),
                                     reads=[("psb", pb), "epsb"], writes=[("rin", rb)])
                                S.op("act", lambda e, rb=rb: e.activation(out=rin[rb][:], in_=rin[rb][:], func=AF.Exp,
                                                                          scale=-0.5),
                                     reads=[("rin", rb)], writes=[("rin", rb)])
                                S.op("pool", lambda e, rb=rb, tb=tb, dst=dst, g=g, cac=cac: e.tensor_tensor(
                                    out=dst[:, g, tb * 512:(tb + 1) * 512], in0=cac[0:64, tb * 512:(tb + 1) * 512],
                                    in1=rin[rb][:], op=ALU.mult),
                                     reads=[CAC, ("rin", rb)], writes=[("qk", blk, g)])
                          head_ops = S.capture(lambda: group_body("head"))
                          S.emit_merged(head_ops, pending_tail)
                          pending_tail = S.capture(lambda: group_body("tail"))
                          gi += 1
                          cnt += 4
                    S.emit_merged([], pending_tail)
                    S.flush()

                gt = lambda name: sb2(name, [64, 32, 8])
                ba = sb2("ba", [64, 32, 16])
                beta, nbeta, gg, gc, gcl, eg, egl, ekd, nbeg, alr, dtr = [gt(n) for n in (
                    "beta", "nbeta", "gg", "gc", "gcl", "eg", "egl", "ekd", "nbeg", "alr", "dtr")]
                S.dma("sp", lambda e: e.dma_start(out=alr[:], in_=bass.AP(gvec.tensor, 0, [[0, 64], [0, 32], [1, 8]])),
                      writes=["alr"])
                S.dma("sp", lambda e: e.dma_start(out=dtr[:], in_=bass.AP(gvec.tensor, 64, [[0, 64], [0, 32], [1, 8]])),
                      writes=["dtr"])
                for n in range(32):
                    for kc in range(8):
                        S.op("pe", lambda e, n=n, kc=kc: e.matmul(
                            psb[0][0:64, n * 16:(n + 1) * 16], x1T[:, kc, 128 + n * 64:128 + (n + 1) * 64], wba[:, kc, :],
                            start=(kc == 0), stop=(kc == 7)), reads=["wba"], writes=[("psb", 0)])
                S.op("dve", lambda e: e.tensor_copy(out=ba[:], in_=v3(psb[0][0:64, :], 32)), reads=[("psb", 0)],
                     writes=["ba"])
                S.op("act", lambda e: e.activation(out=beta[:], in_=ba[:, :, 0:8], func=AF.Sigmoid), reads=["ba"],
                     writes=["beta"])
                S.op("dve", lambda e: e.tensor_scalar(out=nbeta[:], in0=beta[:], scalar1=-1.0, scalar2=None, op0=ALU.mult),
                     reads=["beta"], writes=["nbeta"])
                S.op("dve", lambda e: e.tensor_tensor(out=gg[:], in0=ba[:, :, 8:16], in1=dtr[:], op=ALU.add),
                     reads=["ba", "dtr"], writes=["gg"])
                S.op("act", lambda e: e.activation(out=gg[:], in_=gg[:], func=AF.Exp), reads=["gg"], writes=["gg"])
                S.op("act", lambda e: e.activation(out=gg[:], in_=gg[:], func=AF.Ln, bias=ONES[:, 0:1]),
                     reads=["gg", "gcn"], writes=["gg"])
                S.op("act", lambda e: e.activation(out=alr[:], in_=alr[:], func=AF.Exp), reads=["alr"], writes=["alr"])
                S.op("dve", lambda e: e.scalar_tensor_tensor(out=gg[:], in0=gg[:], scalar=-1.0, in1=alr[:], op0=ALU.mult,
                                                             op1=ALU.mult), reads=["gg", "alr"], writes=["gg"])
                gg2 = lambda t: t[:].rearrange("p n h -> p (n h)")
                S.op("pe", lambda e: e.matmul(psb[1][0:64, 0:256], TRI, gg2(gg), start=True, stop=True),
                     reads=["gg", "gcn"], writes=[("psb", 1)])
                S.op("dve", lambda e: e.tensor_copy(out=gg2(gc), in_=psb[1][0:64, 0:256]), reads=[("psb", 1)],
                     writes=["gc"])
                S.op("pe", lambda e: e.matmul(psb[2][0:64, 0:256], SEL, gg2(gc), start=True, stop=True),
                     reads=["gc", "gcn"], writes=[("psb", 2)])
                S.op("dve", lambda e: e.tensor_copy(out=gg2(gcl), in_=psb[2][0:64, 0:256]), reads=[("psb", 2)],
                     writes=["gcl"])
                S.op("act", lambda e: e.activation(out=eg[:], in_=gc[:], func=AF.Exp), reads=["gc"], writes=["eg"])
                S.op("act", lambda e: e.activation(out=egl[:], in_=gcl[:], func=AF.Exp), reads=["gcl"], writes=["egl"])
                S.op("dve", lambda e: e.tensor_tensor(out=ekd[:], in0=gcl[:], in1=gc[:], op=ALU.subtract),
                     reads=["gc", "gcl"], writes=["ekd"])
                S.op("act", lambda e: e.activation(out=ekd[:], in_=ekd[:], func=AF.Exp), reads=["ekd"], writes=["ekd"])
                S.op("dve", lambda e: e.tensor_tensor(out=nbeg[:], in0=nbeta[:], in1=eg[:], op=ALU.mult),
                     reads=["nbeta", "eg"], writes=["nbeg"])
                S.flush()

                f3 = lambda name: sb2(name, [64, 8, 64])
                b3 = lambda name: sb2(name, [64, 8, 64], BF16)
                kvns = [sb2("kvn%d" % i, [64, 1024], BF16) for i in range(2)]
                zss = [sb2("zs%d" % i, [64, 512]) for i in range(2)]
                Dg, Db, m1, e1, e2, b1, c1, Tf, vb, tt, ob, osq, Sf = [f3(n) for n in (
                    "Dg", "Db", "m1", "e1", "e2", "b1", "c1", "Tf", "vb", "tt", "ob", "osq", "Sf")]
                Bb = [b3("Bb0"), b3("Bb1")]
                Cb = [b3("Cb0"), b3("Cb1")]
                intraTs = [b3("intraT0"), b3("intraT1")]
                Tbs = [b3("Tb0"), b3("Tb1")]
                rn, vn, vns, Sb = [b3(n) for n in ("rn", "vn", "vns", "Sb")]
                go = sb2("go", [64, 512], BF16)
                ss = sb2("ss", [64, 8])
                S.op("pool", lambda e: e.memset(Sf[:], 0.0), writes=["Sf"])
                S.op("pool", lambda e: e.memset(Sb[:], 0.0), writes=["Sb"])
                ps3 = lambda i: v3(psb[i][0:64, :])
                p6b = psb[6][:, :].bitcast(BF16)

                def pre(n):
                    c0 = n * 64
                    xc0 = 128 + c0
                    q = n % 2
                    kvn, zs, intraT, Tb = kvns[q], zss[q], intraTs[q], Tbs[q]
                    KVN, ZS, INT, TB = ("kvn", q), ("zs", q), ("intraT", q), ("Tb", q)
                    for h in range(8):
                        S.op("pe", lambda e, h=h: e.transpose(pst[0:64, h * 64:(h + 1) * 64], kh[:, h, c0:c0 + 64],
                                                              identb[0:64, 0:64]),
                             reads=[("qk", 1, h), "identb"], writes=["pst"])
                    for pr in range(4):
                        S.op("pe", lambda e, pr=pr: e.transpose(pst[0:64, 512 + pr * 128:512 + (pr + 1) * 128],
                                                                vT[:, pr, c0:c0 + 64], identb[:, :]),
                             reads=[("vT", pr), "identb"], writes=["pst"])
                    S.op("dve", lambda e: e.tensor_copy(out=kvn[:], in_=pst[0:64, :]), reads=["pst"], writes=[KVN])
                    for kc in range(8):
                        S.op("pe", lambda e, kc=kc: e.matmul(psb[2][0:64, :], x1T[:, kc, xc0:xc0 + 64], wz[:, kc, :],
                                                             start=(kc == 0), stop=(kc == 7)),
                             reads=["wz"], writes=[("psb", 2)])
                    S.op("act", lambda e: e.activation(out=zs[:], in_=psb[2][0:64, :], func=AF.Silu), reads=[("psb", 2)],
                         writes=[ZS])
                    for h in range(8):
                        S.op("pe", lambda e, h=h: e.matmul(psb[0][0:64, h * 64:(h + 1) * 64], kh[:, h, c0:c0 + 64],
                                                           kh[:, h, c0:c0 + 64], start=True, stop=True),
                             reads=[("qk", 1, h)], writes=[("psb", 0)])
                    for h in range(8):
                        S.op("pe", lambda e, h=h: e.matmul(psb[1][0:64, h * 64:(h + 1) * 64], kh[:, h, c0:c0 + 64],
                                                           qh[:, h, c0:c0 + 64], start=True, stop=True),
                             reads=[("qk", 1, h), ("qk", 0, h)], writes=[("psb", 1)])
                    gcn_ = gc[:, n, :]
                    S.op("pool", lambda e: e.tensor_tensor(out=Dg[:], in0=bm(ID64), in1=bl(gcn_), op=ALU.mult),
                         reads=["gc", "gcn"], writes=["Dg"])
                    S.op("pool", lambda e: e.tensor_tensor(out=Db[:], in0=bm(ID64), in1=bl(beta[:, n, :]), op=ALU.mult),
                         reads=["beta", "gcn"], writes=["Db"])
                    S.op("pe", lambda e: e.matmul(psb[2][0:64, :], ONES, Dg[:].rearrange("p h s -> p (h s)"), start=True,
                                                  stop=True), reads=["Dg", "gcn"], writes=[("psb", 2)])
                    S.op("pe", lambda e: e.matmul(psb[3][0:64, :], ONES, Db[:].rearrange("p h s -> p (h s)"), start=True,
                                                  stop=True), reads=["Db", "gcn"], writes=[("psb", 3)])
                    S.op("pool", lambda e: e.tensor_tensor(out=m1[:], in0=bl(gcn_), in1=bm(NEGS), op=ALU.add),
                         reads=["gc", "gcn"], writes=["m1"])
                    S.op("dve", lambda e: e.scalar_tensor_tensor(out=e1[:], in0=ps3(2), scalar=-1.0, in1=m1[:], op0=ALU.mult,
                                                                 op1=ALU.add), reads=[("psb", 2), "m1"], writes=["e1"])
                    S.op("act", lambda e: e.activation(out=e1[:], in_=e1[:], func=AF.Exp), reads=["e1"], writes=["e1"])
                    S.op("pool", lambda e: e.tensor_tensor(out=m1[:], in0=bm(NEGT), in1=bl(gcn_), op=ALU.subtract),
                         reads=["gc", "gcn"], writes=["m1"])
                    S.op("dve", lambda e: e.tensor_tensor(out=e2[:], in0=ps3(2), in1=m1[:], op=ALU.add),
                         reads=[("psb", 2), "m1"], writes=["e2"])
                    S.op("act", lambda e: e.activation(out=e2[:], in_=e2[:], func=AF.Exp), reads=["e2"], writes=["e2"])
                    S.op("dve", lambda e: e.tensor_tensor(out=b1[:], in0=ps3(0), in1=e1[:], op=ALU.mult),
                         reads=[("psb", 0), "e1"], writes=["b1"])
                    S.op("pool", lambda e: e.tensor_tensor(out=Bb[0][:], in0=b1[:], in1=bl(nbeta[:, n, :]), op=ALU.mult),
                         reads=["b1", "nbeta"], writes=[("Bb", 0)])
                    S.op("dve", lambda e: e.tensor_tensor(out=c1[:], in0=ps3(0), in1=e2[:], op=ALU.mult),
                         reads=[("psb", 0), "e2"], writes=["c1"])
                    S.op("dve", lambda e: e.tensor_tensor(out=c1[:], in0=c1[:], in1=ps3(3), op=ALU.mult),
                         reads=[("psb", 3), "c1"], writes=["c1"])
                    S.op("pool", lambda e: e.tensor_tensor(out=c1[:], in0=c1[:], in1=bm(MSKT), op=ALU.mult),
                         reads=["c1", "gcn"], writes=["c1"])
                    S.op("act", lambda e: e.copy(out=Cb[0][:], in_=c1[:]), reads=["c1"], writes=[("Cb", 0)])
                    S.op("dve", lambda e: e.tensor_tensor(out=intraT[:], in0=ps3(1), in1=e2[:], op=ALU.mult),
                         reads=[("psb", 1), "e2"], writes=[INT])
                    S.op("pool", lambda e: e.tensor_tensor(out=Tf[:], in0=c1[:], in1=bm(ID64), op=ALU.add),
                         reads=["c1", "gcn"], writes=["Tf"])
                    S.op("act", lambda e: e.copy(out=Tb[:], in_=Tf[:]), reads=["Tf"], writes=[TB])
                    for k in range(1, 6):
                        cur, nxt = (k - 1) % 2, k % 2
                        for h in range(8):
                            S.op("pe", lambda e, h=h, cur=cur: e.matmul(psb[2][0:64, h * 64:(h + 1) * 64], Cb[cur][:, h, :],
                                                                        Bb[cur][:, h, :], start=True, stop=True),
                                 reads=[("Cb", cur), ("Bb", cur)], writes=[("psb", 2)])
                        if k < 5:
                            for h in range(8):
                                S.op("pe", lambda e, h=h, cur=cur: e.matmul(psb[3][0:64, h * 64:(h + 1) * 64], Bb[cur][:, h, :],
                                                                            Cb[cur][:, h, :], start=True, stop=True),
                                     reads=[("Cb", cur), ("Bb", cur)], writes=[("psb", 3)])
                        S.op("act", lambda e, nxt=nxt: e.activation(out=Bb[nxt][:], in_=ps3(2), func=AF.Copy),
                             reads=[("psb", 2)], writes=[("Bb", nxt)])
                        if k < 5:
                            S.op("dve", lambda e, nxt=nxt: e.tensor_copy(out=Cb[nxt][:], in_=ps3(3)),
                                 reads=[("psb", 3)], writes=[("Cb", nxt)])
                        for h in range(8):
                            S.op("pe", lambda e, h=h, nxt=nxt: e.matmul(psb[1][0:64, h * 64:(h + 1) * 64], Bb[nxt][:, h, :],
                                                                        Tb[:, h, :], start=True, stop=True),
                                 reads=[("Bb", nxt), TB], writes=[("psb", 1)])
                        S.op("dve", lambda e: e.tensor_tensor(out=Tf[:], in0=Tf[:], in1=ps3(1), op=ALU.add),
                             reads=[("psb", 1), "Tf"], writes=["Tf"])
                        S.op("act", lambda e: e.copy(out=Tb[:], in_=Tf[:]), reads=["Tf"], writes=[TB])

                def seq(n):
                    c0 = n * 64
                    q = n % 2
                    kvn, zs, intraT, Tb = kvns[q], zss[q], intraTs[q], Tbs[q]
                    KVN, ZS, INT, TB = ("kvn", q), ("zs", q), ("intraT", q), ("Tb", q)
                    S.op("pool", lambda e: e.tensor_tensor(out=vb[:], in0=v3(kvn[:, 512:1024]), in1=bl(beta[:, n, :]),
                                                           op=ALU.mult), reads=[KVN, "beta"], writes=["vb"])
                    for h in range(8):
                        S.op("pe", lambda e, h=h: e.matmul(psb[4][0:64, h * 64:(h + 1) * 64], kh[:, h, c0:c0 + 64],
                                                           Sb[:, h, :], start=True, stop=True),
                             reads=[("qk", 1, h), "Sb"], writes=[("psb", 4)])
                    S.op("dve", lambda e: e.tensor_tensor(out=tt[:], in0=ps3(4), in1=bl(nbeg[:, n, :]), op=ALU.mult),
                         reads=[("psb", 4), "nbeg"], writes=["tt"])
                    S.op("pool", lambda e: e.tensor_tensor(out=rn[:], in0=tt[:], in1=vb[:], op=ALU.add),
                         reads=["tt", "vb"], writes=["rn"])
                    for h in range(8):
                        S.op("pe", lambda e, h=h: e.matmul(psb[5][0:64, h * 64:(h + 1) * 64], Tb[:, h, :], rn[:, h, :],
                                                           start=True, stop=True),
                             reads=[TB, "rn"], writes=[("psb", 5)])
                    S.op("act", lambda e: e.activation(out=vn[:], in_=ps3(5), func=AF.Copy), reads=[("psb", 5)],
                         writes=["vn"])
                    S.op("pool", lambda e: e.tensor_tensor(out=vns[:], in0=vn[:], in1=bl(ekd[:, n, :]), op=ALU.mult),
                         reads=["vn", "ekd"], writes=["vns"])
                    for h in range(8):
                        S.op("pe", lambda e, h=h: e.matmul(psb[6][0:64, h * 64:(h + 1) * 64], qh[:, h, c0:c0 + 64],
                                                           Sb[:, h, :], start=True, stop=True),
                             reads=[("qk", 0, h), "Sb"], writes=[("psb", 6)])
                    for h in range(8):
                        S.op("pe", lambda e, h=h: e.matmul(psb[4][0:64, h * 64:(h + 1) * 64], intraT[:, h, :], vn[:, h, :],
                                                           start=True, stop=True),
                             reads=[INT, "vn"], writes=[("psb", 4)])
                    S.op("dve", lambda e: e.tensor_tensor(out=ob[:], in0=ps3(6), in1=bl(eg[:, n, :]), op=ALU.mult),
                         reads=[("psb", 6), "eg"], writes=["ob"])
                    S.op("dve", lambda e: e.tensor_tensor(out=ob[:], in0=ob[:], in1=ps3(4), op=ALU.add),
                         reads=[("psb", 4), "ob"], writes=["ob"])
                    for h in range(8):
                        S.op("pe", lambda e, h=h: e.matmul(psb[5][0:64, h * 64:(h + 1) * 64], kvn[:, h * 64:(h + 1) * 64],
                                                           vns[:, h, :], start=True, stop=True),
                             reads=[KVN, "vns"], writes=[("psb", 5)])
                    S.op("pool", lambda e: e.tensor_tensor(out=Sf[:], in0=Sf[:], in1=bl(egl[:, n, :]), op=ALU.mult),
                         reads=["Sf", "egl"], writes=["Sf"])
                    S.op("dve", lambda e: e.tensor_tensor(out=Sf[:], in0=Sf[:], in1=ps3(5), op=ALU.add),
                         reads=[("psb", 5), "Sf"], writes=["Sf"])
                    S.op("act", lambda e: e.copy(out=Sb[:], in_=Sf[:]), reads=["Sf"], writes=["Sb"])
                    S.op("pool", lambda e: e.tensor_tensor(out=osq[:], in0=ob[:], in1=ob[:], op=ALU.mult), reads=["ob"],
                         writes=["osq"])
                    S.op("dve", lambda e: e.tensor_reduce(out=ss[:], in_=osq[:], axis=AX.X, op=ALU.add), reads=["osq"],
                         writes=["ss"])
                    S.op("dve", lambda e: e.tensor_scalar(out=ss[:], in0=ss[:], scalar1=1.0 / 64.0, scalar2=1e-6,
                                                          op0=ALU.mult, op1=ALU.add), reads=["ss"], writes=["ss"])
                    S.op("act", lambda e: e.sqrt(out=ss[:], in_=ss[:]), reads=["ss"], writes=["ss"])
                    S.op("dve", lambda e: e.reciprocal(out=ss[:], in_=ss[:]), reads=["ss"], writes=["ss"])
                    S.op("dve", lambda e: e.tensor_tensor(out=ob[:], in0=ob[:], in1=bl(ss[:, :]), op=ALU.mult),
                         reads=["ob", "ss"], writes=["ob"])
                    S.op("pool", lambda e: e.tensor_tensor(out=ob[:], in0=ob[:], in1=bm(nwr[:, :]), op=ALU.mult),
                         reads=["ob", "nwr"], writes=["ob"])
                    S.op("dve", lambda e: e.tensor_tensor(out=v3(go[:, :]), in0=ob[:], in1=v3(zs[:, :]), op=ALU.mult),
                         reads=["ob", ZS], writes=["go"])
                    for pr in range(4):
                        S.op("pe", lambda e, pr=pr: e.transpose(p6b[:, pr * 64:(pr + 1) * 64], go[:, pr * 128:(pr + 1) * 128],
                                                                identb[0:64, 0:64]),
                             reads=["go", "identb"], writes=[("psb", 6)])
                    S.op("dve", lambda e: e.tensor_copy(out=gdnT[:, :, c0:c0 + 64], in_=v3(p6b[:, 0:256], 4)),
                         reads=[("psb", 6)], writes=[("gdnT", n)])

                pre(0)
                for n in range(32):
                    sq_ = S.capture(lambda: seq(n))
                    pr_ = S.capture(lambda: pre(n + 1)) if n + 1 < 32 else []
                    S.emit_merged(pr_, sq_)
                S.dma("sp", lambda e: e.dma_start(out=o_sgp.rearrange("h d e -> d h e"), in_=Sf[:]), reads=["Sf"],
                      writes=["o_sgp"])
                if dbg:
                    S.dma("pool", lambda e: e.dma_start(out=gdn_dbg[:, :, :], in_=gdnT[:]),
                          reads=[("gdnT", n) for n in range(32)], writes=["gdn_dbg"])
                S.flush()

        def sample_proj_stage():
            with contextlib.ExitStack() as st2:
                sb2 = lambda name, shape, dt=F32: st2.enter_context(nc.sbuf_tensor("sp" + name, list(shape), dt))
                wb = [sb2("wb%d" % i, [128, 8, 512], BF16) for i in range(2)]
                ot = [sb2("ot%d" % i, [128, 512]) for i in range(2)]
                winv = w_in.rearrange("(kc p) f -> p kc f", p=128)
                blocks = [(0, 512, qs_s[:, :], 0.125), (1536, 512, gq_s[:, 0:512], 1.0), (2048, 512, gq_s[:, 512:1024], 1.0),
                          (2560, 512, gq_s[:, 1024:1536], 1.0), (3072, 512, z_s[:, :], 1.0), (3584, 16, ba_s[:, :], 1.0)]
                for i, (c0, w, dst, sc) in enumerate(blocks):
                    b = i % 2
                    S.dma("pool", lambda e, b=b, c0=c0, w=w: e.dma_start(out=wb[b][:, :, 0:w], in_=winv[:, :, c0:c0 + w]),
                          writes=[("wb", b)])
                    for kc in range(8):
                        S.op("pe", lambda e, b=b, kc=kc, w=w: e.matmul(psb[b][:, 0:w], x1T[:, kc, 0:128], wb[b][:, kc, 0:w],
                                                                       start=(kc == 0), stop=(kc == 7)),
                             reads=[("wb", b)], writes=[("psb", b)])
                    S.op("act", lambda e, b=b, w=w, sc=sc: e.mul(out=ot[b][:, 0:w], in_=psb[b][:, 0:w], mul=sc),
                         reads=[("psb", b)], writes=[("ot", b)])
                    S.dma("sp", lambda e, b=b, w=w, dst=dst: e.dma_start(out=dst, in_=ot[b][0:64, 0:w]),
                          reads=[("ot", b)], writes=[("sscr", i)])
                S.dma("sp", lambda e: e.dma_start(
                    out=o_scs[:, :, :], in_=bass.AP(gq_s.tensor, 1536, [[4 * 1536, 16], [1536, 3], [1, 1536]])),
                      reads=[("sscr", 1), ("sscr", 2), ("sscr", 3)], writes=["o_scs"])
                S.flush()

        def sample_attn_stage():
            with contextlib.ExitStack() as st2:
                sb2 = lambda name, shape, dt=F32: st2.enter_context(nc.sbuf_tensor("sa" + name, list(shape), dt))
                kt = [sb2("kt%d" % i, [128, 8, 512]) for i in range(2)]
                vt = [sb2("vt%d" % i, [128, 8, 512]) for i in range(2)]
                ktT = [sb2("ktT%d" % i, [128, 512]) for i in range(2)]
                qsT = sb2("qsT", [128, 4, 64])
                wqb = sb2("wqb", [128, 8, 512], BF16)
                Pm = sb2("Pm", [128, 8, 32])
                Pj = sb2("Pj", [128, 32])
                mtab = sb2("mtab", [128, 8, 32])
                wcs = sb2("wcs", [32, 8, 4, 128])
                eb = sb2("eb", [32, 8])
                ones1 = sb2("ones1", [128, 1])
                osb = sb2("osb", [4, 512])
                rs = sb2("rs", [4, 8])
                S.dma("sp", lambda e: e.dma_start(out=wcs[:], in_=wcm[:, :, :, :]), writes=["wcs"])
                S.dma("sp", lambda e: e.dma_start(out=eb[:], in_=relb[:, :]), writes=["eb"])
                S.op("act", lambda e: e.activation(out=eb[:], in_=eb[:], func=AF.Exp), reads=["eb"], writes=["eb"])
                S.op("pool", lambda e: e.memset(ones1[:], 1.0), writes=["ones1"])
                S.dma("pool", lambda e: e.dma_start(out=wqb[:], in_=w_in.rearrange("(kc p) f -> p kc f", p=128)[:, :, 0:512]),
                      writes=["wqb"])
                for pr in range(4):
                    for kc in range(8):
                        S.op("pe", lambda e, pr=pr, kc=kc: e.matmul(psb[6][:, pr * 64:(pr + 1) * 64],
                                                                    wqb[:, kc, pr * 128:(pr + 1) * 128], x1T[:, kc, 0:64],
                                                                    start=(kc == 0), stop=(kc == 7)),
                             reads=["wqb"], writes=[("psb", 6)])
                S.op("act", lambda e: e.mul(out=qsT[:].rearrange("p a b -> p (a b)"), in_=psb[6][:, 0:256], mul=0.125),
                     reads=[("psb", 6)], writes=["qsT"])
                for i in range(2):
                    S.op("pool", lambda e, i=i: e.memset(kt[i][:, 7, :], 0.0), writes=[("kt7", i)])
                    S.op("pool", lambda e, i=i: e.memset(vt[i][:, 7, :], 0.0), writes=[("vt7", i)])
                for j in range(8):
                    for t in range(4):
                        S.op("pe", lambda e, j=j, t=t: e.matmul(psb[0][:, (j * 4 + t) * 8:(j * 4 + t + 1) * 8], wcs[:, j, t, :],
                                                                eb[:, :], start=True, stop=True),
                             reads=["wcs", "eb"], writes=[("psb", 0)])
                S.op("dve", lambda e: e.tensor_copy(out=mtab[:].rearrange("p j (h t) -> p j h t", h=8),
                                                    in_=psb[0][:, 0:256].rearrange("p (j t h) -> p j h t", j=8, t=4)),
                     reads=[("psb", 0)], writes=["mtab"])
                for b in range(16):
                    bb = b % 2
                    for src, dstt, onew, nm in ((ck, kt[bb], o_ks, "kt"), (cv, vt[bb], o_vs, "vt")):
                        S.dma("sp", lambda e, src=src, dstt=dstt, b=b: e.dma_start(
                            out=dstt[:, 0:4, :], in_=src[b, 1536:2048, :].rearrange("(j p) e -> p j e", p=128)),
                              writes=[(nm, bb, 0)])
                        for r in range(4):
                            S.dma("act" if r % 2 else "sp", lambda e, src=src, dstt=dstt, b=b, r=r: e.dma_start(
                                out=dstt[r:128:4, 4:7, :],
                                in_=bass.AP(src.tensor, b * 2048 * 512 + r * 512, [[16 * 512, 32], [32 * 16 * 512, 3], [1, 512]])),
                                  writes=[(nm, bb, 1 + r)])
                        S.dma("act", lambda e, dstt=dstt, onew=onew, b=b: e.dma_start(
                            out=dstt[0:4, 7, :], in_=onew[b * 4:(b + 1) * 4, :]),
                              reads=[("okv", 1, 0), ("okv", 2, 0), (nm + "7", bb)], writes=[(nm, bb, 5)])
                    for j in range(8):
                        pk = 3 + (j % 2)
                        kb = j % 2
                        for pr in range(4):
                            S.op("pe", lambda e, pk=pk, j=j, pr=pr, bb=bb: e.transpose(
                                psb[pk][:, pr * 128:(pr + 1) * 128], kt[bb][:, j, pr * 128:(pr + 1) * 128], identf[:, :]),
                                 reads=[("kt", bb, i_) for i_ in range(6)] + ["identf"], writes=[("psb", pk)])
                        if j % 2:
                            S.op("act", lambda e, pk=pk, kb=kb: e.activation(out=ktT[kb][:], in_=psb[pk][:, :], func=AF.Copy),
                                 reads=[("psb", pk)], writes=[("ktT", kb)])
                        else:
                            S.op("dve", lambda e, pk=pk, kb=kb: e.tensor_copy(out=ktT[kb][:], in_=psb[pk][:, :]),
                                 reads=[("psb", pk)], writes=[("ktT", kb)])
                        for h in range(8):
                            pr, base = h // 2, (h % 2) * 64
                            S.op("pe", lambda e, j=j, h=h, pr=pr, base=base, kb=kb, b=b: e.matmul(
                                psb[5 + b % 2][:, (j * 8 + h) * 4:(j * 8 + h) * 4 + 4], ktT[kb][base:base + 64, pr * 128:(pr + 1) * 128],
                                qsT[base:base + 64, pr, b * 4:(b + 1) * 4], start=True, stop=True),
                                 reads=[("ktT", kb), "qsT"], writes=[("psb", 5 + b % 2)])
                    S.op("act", lambda e, b=b: e.activation(out=Pm[:].rearrange("p j x -> p (j x)"), in_=psb[5 + b % 2][:, 0:256],
                                                            func=AF.Exp), reads=[("psb", 5 + b % 2)], writes=["Pm"])
                    S.op("dve", lambda e: e.tensor_tensor(out=Pm[:], in0=Pm[:], in1=mtab[:], op=ALU.mult),
                         reads=["Pm", "mtab"], writes=["Pm"])
                    S.op("dve", lambda e: e.tensor_reduce(out=Pj[:], in_=Pm[:].rearrange("p j x -> p x j"), axis=AX.X,
                                                          op=ALU.add), reads=["Pm"], writes=["Pj"])
                    for h in range(8):
                        for j in range(8):
                            S.op("pe", lambda e, h=h, j=j, bb=bb: e.matmul(
                                psb[1][0:4, h * 64:(h + 1) * 64], Pm[:, j, h * 4:(h + 1) * 4], vt[bb][:, j, h * 64:(h + 1) * 64],
                                start=(j == 0), stop=(j == 7)), reads=["Pm"] + [("vt", bb, i_) for i_ in range(6)], writes=[("psb", 1)])
                        S.op("pe", lambda e, h=h: e.matmul(psb[2][0:4, h:h + 1], Pj[:, h * 4:(h + 1) * 4], ones1[:, :], start=True,
                                                           stop=True), reads=["Pj", "ones1"], writes=[("psb", 2)])
                    S.op("dve", lambda e: e.reciprocal(out=rs[:], in_=psb[2][0:4, 0:8]), reads=[("psb", 2)], writes=["rs"])
                    S.op("dve", lambda e: e.tensor_tensor(out=v3(osb[:, :]), in0=v3(psb[1][0:4, :]), in1=bl(rs[:, :]),
                                                          op=ALU.mult), reads=[("psb", 1), "rs"], writes=["osb"])
                    S.dma("pool", lambda e, b=b: e.dma_start(out=heads_s[b * 4:(b + 1) * 4, 0:512], in_=osb[:, :]),
                          reads=["osb"], writes=[("heads_a", b)])
                S.flush()

        def sample_gdn_stage():
            with contextlib.ExitStack() as st2:
                sb2 = lambda name, shape, dt=F32: st2.enter_context(nc.sbuf_tensor("sg" + name, list(shape), dt))
                Sx = sb2("S", [128, 64, 64])
                tmp = sb2("tmp", [128, 64, 64])
                xq = sb2("xq", [128, 3, 7, 64])
                zz = sb2("zz", [128, 4, 64])
                bav = sb2("bav", [128, 4, 2])
                cw = sb2("cw", [128, 3, 4, 64])
                gv = sb2("gv", [128, 2])
                nwr = sb2("nwr", [128, 64])
                cq = sb2("cq", [128, 3, 4, 64])
                ct = sb2("ct", [128, 4, 64])
                ssq = sb2("ssq", [128, 3, 4])
                beta = sb2("beta", [128, 4])
                gg = sb2("gg", [128, 4])
                eg = sb2("eg", [128, 4])
                neg = sb2("neg", [128, 4])
                ks = sb2("ks", [128, 64])
                dl = sb2("dl", [128, 64])
                oo = sb2("oo", [128, 4, 64])
                one = sb2("one", [128, 1])
                S.op("pool", lambda e: e.memset(one[:], 1.0), writes=["one"])
                S.dma("sp", lambda e: e.dma_start(out=Sx[:].rearrange("p a b -> p (a b)"), in_=sg_in[:, :]), writes=["S"])
                S.dma("sp", lambda e: e.dma_start(out=cw[:], in_=cws[:, :, :, :]), writes=["cw"])
                S.dma("sp", lambda e: e.dma_start(out=gv[:], in_=gvs[:, :]), writes=["gv"])
                S.dma("sp", lambda e: e.dma_start(out=nwr[:], in_=gvec[2:3, :].partition_broadcast(128)), writes=["nwr"])
                for b in range(16):
                    p0 = b * 8
                    for sec in range(3):
                        q = "act" if sec == 1 else "sp"
                        S.dma(q, lambda e, b=b, p0=p0, sec=sec: e.dma_start(
                            out=xq[p0:p0 + 8, sec, 0:3, :],
                            in_=bass.AP(sc_in.tensor, b * 3 * 1536 + sec * 512, [[64, 8], [1536, 3], [1, 64]])),
                              writes=[("xq", b, sec, 0)])
                        S.dma(q, lambda e, b=b, p0=p0, sec=sec: e.dma_start(
                            out=xq[p0:p0 + 8, sec, 3:7, :],
                            in_=bass.AP(gq_s.tensor, b * 4 * 1536 + sec * 512, [[64, 8], [1536, 4], [1, 64]])),
                              reads=[("sscr", 1 + sec)], writes=[("xq", b, sec, 1)])
                    S.dma("act", lambda e, b=b, p0=p0: e.dma_start(
                        out=zz[p0:p0 + 8, :, :], in_=bass.AP(z_s.tensor, b * 4 * 512, [[64, 8], [512, 4], [1, 64]])),
                          reads=[("sscr", 4)], writes=[("zz", b)])
                    S.dma("act", lambda e, b=b, p0=p0: e.dma_start(
                        out=bav[p0:p0 + 8, :, :], in_=bass.AP(ba_s.tensor, b * 4 * 16, [[1, 8], [16, 4], [8, 2]]),
                        allow_slow_non_contiguous=True), reads=[("sscr", 5)], writes=[("bav", b)])
                for sec in range(3):
                    for j in range(4):
                        wv = cw[:, sec, j, :].unsqueeze(1).broadcast_to([128, 4, 64])
                        if j == 0:
                            S.op("dve", lambda e, sec=sec, wv=wv: e.tensor_tensor(
                                out=cq[:, sec, :, :], in0=xq[:, sec, 0:4, :], in1=wv, op=ALU.mult),
                                 reads=[("xq", b_, s_, i_) for b_ in range(16) for s_ in range(3) for i_ in range(2)] + ["cw"], writes=["cq"])
                        else:
                            S.op("pool", lambda e, sec=sec, wv=wv, j=j: e.tensor_tensor(
                                out=ct[:], in0=xq[:, sec, j:j + 4, :], in1=wv, op=ALU.mult),
                                 reads=[("xq", b_, s_, i_) for b_ in range(16) for s_ in range(3) for i_ in range(2)] + ["cw"], writes=["ct"])
                            S.op("dve", lambda e, sec=sec: e.tensor_tensor(
                                out=cq[:, sec, :, :], in0=cq[:, sec, :, :], in1=ct[:], op=ALU.add),
                                 reads=["cq", "ct"], writes=["cq"])
                S.op("act", lambda e: e.activation(out=cq[:], in_=cq[:], func=AF.Silu), reads=["cq"], writes=["cq"])
                S.op("act", lambda e: e.activation(out=zz[:], in_=zz[:], func=AF.Silu), reads=[("zz", b_) for b_ in range(16)], writes=["zz"])
                for sec in range(2):
                    S.op("pool", lambda e, sec=sec: e.tensor_tensor(out=ct[:], in0=cq[:, sec, :, :], in1=cq[:, sec, :, :],
                                                                    op=ALU.mult), reads=["cq"], writes=["ct"])
                    S.op("dve", lambda e, sec=sec: e.tensor_reduce(out=ssq[:, sec, :], in_=ct[:], axis=AX.X, op=ALU.add),
                         reads=["ct"], writes=["ssq"])
                    S.op("dve", lambda e, sec=sec: e.tensor_scalar(out=ssq[:, sec, :], in0=ssq[:, sec, :], scalar1=1e-6,
                                                                   scalar2=None, op0=ALU.add), reads=["ssq"], writes=["ssq"])
                    S.op("act", lambda e, sec=sec: e.sqrt(out=ssq[:, sec, :], in_=ssq[:, sec, :]), reads=["ssq"],
                         writes=["ssq"])
                    S.op("dve", lambda e, sec=sec: e.reciprocal(out=ssq[:, sec, :], in_=ssq[:, sec, :]), reads=["ssq"],
                         writes=["ssq"])
                    sc = 0.125 if sec == 0 else 1.0
                    S.op("dve", lambda e, sec=sec, sc=sc: e.scalar_tensor_tensor(
                        out=cq[:, sec, :, :], in0=cq[:, sec, :, :], scalar=sc, in1=bl(ssq[:, sec, :]), op0=ALU.mult,
                        op1=ALU.mult), reads=["cq", "ssq"], writes=["cq"])
                S.op("act", lambda e: e.activation(out=beta[:], in_=bav[:, :, 0], func=AF.Sigmoid), reads=[("bav", b_) for b_ in range(16)],
                     writes=["beta"])
                S.op("dve", lambda e: e.tensor_scalar(out=gg[:], in0=bav[:, :, 1], scalar1=gv[:, 1:2], scalar2=None,
                                                      op0=ALU.add), reads=[("bav", b_) for b_ in range(16)] + ["gv"], writes=["gg"])
                S.op("act", lambda e: e.activation(out=gg[:], in_=gg[:], func=AF.Exp), reads=["gg"], writes=["gg"])
                S.op("act", lambda e: e.activation(out=gg[:], in_=gg[:], func=AF.Ln, bias=one[:, 0:1]), reads=["gg", "one"],
                     writes=["gg"])
                S.op("act", lambda e: e.activation(out=gv[:, 0:1], in_=gv[:, 0:1], func=AF.Exp), reads=["gv"], writes=["gv"])
                S.op("dve", lambda e: e.tensor_scalar(out=gg[:], in0=gg[:], scalar1=gv[:, 0:1], scalar2=-1.0, op0=ALU.mult,
                                                      op1=ALU.mult), reads=["gg", "gv"], writes=["gg"])
                S.op("act", lambda e: e.activation(out=eg[:], in_=gg[:], func=AF.Exp), reads=["gg"], writes=["eg"])
                S.op("dve", lambda e: e.tensor_scalar(out=neg[:], in0=eg[:], scalar1=-1.0, scalar2=None, op0=ALU.mult),
                     reads=["eg"], writes=["neg"])
                ST = Sx[:].rearrange("p a b -> p b a")
                for t in range(4):
                    qv, kv, vv = cq[:, 0, t, :], cq[:, 1, t, :], cq[:, 2, t, :]
                    S.op("dve", lambda e, kv=kv: e.tensor_tensor(out=tmp[:], in0=ST, in1=bm(kv, 64), op=ALU.mult),
                         reads=["S", "cq"], writes=["tmp"])
                    S.op("dve", lambda e: e.tensor_reduce(out=ks[:], in_=tmp[:], axis=AX.X, op=ALU.add), reads=["tmp"],
                         writes=["ks"])
                    S.op("dve", lambda e, t=t, vv=vv: e.scalar_tensor_tensor(
                        out=dl[:], in0=ks[:], scalar=neg[:, t:t + 1], in1=vv, op0=ALU.mult, op1=ALU.add),
                         reads=["ks", "neg", "cq"], writes=["dl"])
                    S.op("dve", lambda e, t=t: e.tensor_scalar(out=dl[:], in0=dl[:], scalar1=beta[:, t:t + 1], scalar2=None,
                                                               op0=ALU.mult), reads=["dl", "beta"], writes=["dl"])
                    S.op("pool", lambda e, kv=kv: e.tensor_tensor(out=tmp[:], in0=bl(kv, 64), in1=bm(dl[:, :], 64),
                                                                  op=ALU.mult), reads=["cq", "dl"], writes=["tmp"])
                    S.op("dve", lambda e, t=t: e.scalar_tensor_tensor(
                        out=Sx[:], in0=Sx[:], scalar=eg[:, t:t + 1], in1=tmp[:], op0=ALU.mult, op1=ALU.add),
                         reads=["S", "eg", "tmp"], writes=["S"])
                    S.op("pool", lambda e, qv=qv: e.tensor_tensor(out=tmp[:], in0=ST, in1=bm(qv, 64), op=ALU.mult),
                         reads=["S", "cq"], writes=["tmp"])
                    S.op("dve", lambda e, t=t: e.tensor_reduce(out=oo[:, t, :], in_=tmp[:], axis=AX.X, op=ALU.add),
                         reads=["tmp"], writes=["oo"])
                S.dma("sp", lambda e: e.dma_start(out=o_sgs[:, :], in_=Sx[:].rearrange("p a b -> p (a b)")), reads=["S"],
                      writes=["o_sgs"])
                S.op("pool", lambda e: e.tensor_tensor(out=ct[:], in0=oo[:], in1=oo[:], op=ALU.mult), reads=["oo"],
                     writes=["ct"])
                S.op("dve", lambda e: e.tensor_reduce(out=ssq[:, 2, :], in_=ct[:], axis=AX.X, op=ALU.add), reads=["ct"],
                     writes=["ssq"])
                S.op("dve", lambda e: e.tensor_scalar(out=ssq[:, 2, :], in0=ssq[:, 2, :], scalar1=1.0 / 64.0, scalar2=1e-6,
                                                      op0=ALU.mult, op1=ALU.add), reads=["ssq"], writes=["ssq"])
                S.op("act", lambda e: e.sqrt(out=ssq[:, 2, :], in_=ssq[:, 2, :]), reads=["ssq"], writes=["ssq"])
                S.op("dve", lambda e: e.reciprocal(out=ssq[:, 2, :], in_=ssq[:, 2, :]), reads=["ssq"], writes=["ssq"])
                S.op("dve", lambda e: e.tensor_tensor(out=oo[:], in0=oo[:], in1=bl(ssq[:, 2, :]), op=ALU.mult),
                     reads=["oo", "ssq"], writes=["oo"])
                S.op("pool", lambda e: e.tensor_tensor(out=oo[:], in0=oo[:], in1=bm(nwr[:, :], 4), op=ALU.mult),
                     reads=["oo", "nwr"], writes=["oo"])
                S.op("dve", lambda e: e.tensor_tensor(out=oo[:], in0=oo[:], in1=zz[:], op=ALU.mult), reads=["oo", "zz"],
                     writes=["oo"])
                if dbg:
                    dC = dscr("dbg_cq", [128, 3, 4, 64]); dG = dscr("dbg_gates", [128, 3, 4])
                    S.dma("sp", lambda e: e.dma_start(out=dC[:, :, :, :], in_=cq[:]), reads=["cq"], writes=["dC"])
                    S.dma("sp", lambda e: e.dma_start(out=dG[:, 0, :], in_=beta[:]), reads=["beta"], writes=["dG0"])
                    S.dma("sp", lambda e: e.dma_start(out=dG[:, 1, :], in_=gg[:]), reads=["gg"], writes=["dG1"])
                    S.dma("sp", lambda e: e.dma_start(out=dG[:, 2, :], in_=eg[:]), reads=["eg"], writes=["dG2"])
                for b in range(16):
                    S.dma("sp", lambda e, b=b: e.dma_start(
                        out=bass.AP(heads_s.tensor, b * 4 * D + 512, [[64, 8], [D, 4], [1, 64]]), in_=oo[b * 8:(b + 1) * 8, :, :]),
                          reads=["oo"], writes=[("heads_g", b)])
                S.flush()

        def wout_stage():
            with contextlib.ExitStack() as st2:
                sb2 = lambda name, shape, dt=F32: st2.enter_context(nc.sbuf_tensor("wo" + name, list(shape), dt))
                attnT = sb2("attnT", [64, 8, 2048], BF16)
                woa = sb2("woa", [64, 8, D], BF16)
                wog = sb2("wog", [128, 4, D], BF16)
                won = sb2("won", [128, 8, D], BF16)
                hsf = sb2("hsf", [128, D])
                hsb = sb2("hsb", [128, D], BF16)
                hsT = sb2("hsT", [128, 8, 128], BF16)
                xs = [sb2("xs%d" % i, [128, D]) for i in range(2)]
                rr = [sb2("rr%d" % i, [128, D]) for i in range(2)]
                lnrep = sb2("ln", [128, 2, D])
                stt = sb2("st", [128, 2, 6])
                mv = sb2("mv", [128, 2])
                rstd = sb2("rstd", [128, 1])
                for i in range(2):
                    S.dma("sp", lambda e, i=i: e.dma_start(out=lnrep[:, i, :], in_=lnp[2 + i:3 + i, :].partition_broadcast(128)),
                          writes=[("lnrep", i)])
                S.dma("sp", lambda e: e.dma_start(out=attnT[:], in_=attn_s.rearrange("h d t -> d h t")),
                      reads=[("attn_s", h) for h in range(8)], writes=["attnT"])
                S.dma("pool", lambda e: e.dma_start(out=woa[:], in_=w_out[0:512, :].rearrange("(h d) o -> d h o", d=64)),
                      writes=["woa"])
                S.dma("pool", lambda e: e.dma_start(out=wog[:], in_=w_out[512:1024, :].rearrange("(c p) o -> p c o", p=128)),
                      writes=["wog"])
                S.dma("pool", lambda e: e.dma_start(out=won[:], in_=w_out.rearrange("(c p) o -> p c o", p=128)),
                      writes=["won"])
                S.op("pool", lambda e: e.memset(hsf[:], 0.0), writes=["hsf0"])
                S.dma("sp", lambda e: e.dma_start(out=hsf[0:64, :], in_=heads_s[:, :]),
                      reads=[("heads_a", b) for b in range(16)] + [("heads_g", b) for b in range(16)] + ["hsf0"],
                      writes=["hsf"])
                S.op("act", lambda e: e.copy(out=hsb[:], in_=hsf[:]), reads=["hsf"], writes=["hsb"])
                to_featmajor(hsb, "hsb", hsT, 0, "hsT")
                for t in range(NT):
                    b = t % 2
                    S.dma("sp", lambda e, b=b, t=t: e.dma_start(out=xs[b][:], in_=x1s[t * 128:(t + 1) * 128, :]),
                          reads=[("x1s", t)], writes=[("xs", b)])
                    for half in range(2):
                        pd = psb[4 + half]
                        hs_ = slice(half * 512, (half + 1) * 512)
                        if t == 0:
                            for kc in range(8):
                                S.op("pe", lambda e, pd=pd, kc=kc, hs_=hs_: e.matmul(
                                    pd[:, :], hsT[:, kc, :], won[:, kc, hs_], start=(kc == 0), stop=(kc == 7)),
                                     reads=["hsT", "won"], writes=[("psb", 4 + half)])
                        else:
                            ts_ = slice((t - 1) * 128, t * 128)
                            for h in range(8):
                                S.op("pe", lambda e, pd=pd, h=h, hs_=hs_, ts_=ts_: e.matmul(
                                    pd[:, :], attnT[:, h, ts_], woa[:, h, hs_], start=(h == 0), stop=False),
                                     reads=["attnT", "woa"], writes=[("psb", 4 + half)])
                            for c in range(4):
                                S.op("pe", lambda e, pd=pd, c=c, hs_=hs_, ts_=ts_: e.matmul(
                                    pd[:, :], gdnT[:, c, ts_], wog[:, c, hs_], start=False, stop=(c == 3)),
                                     reads=[("gdnT", n) for n in range(32)] + ["wog"], writes=[("psb", 4 + half)])
                        S.op("dve", lambda e, pd=pd, b=b, hs_=hs_: e.scalar_tensor_tensor(
                            out=rr[b][:, hs_], in0=xs[b][:, hs_], scalar=ALPHA, in1=pd[:, :], op0=ALU.mult, op1=ALU.add),
                             reads=[("xs", b), ("psb", 4 + half)], writes=[("rr", b)])
                    layernorm(rr[b], ("rr", b), lnrep, rr[b], ("rr", b), (stt, mv, rstd), epsmul=1.0)
                    S.dma("pool", lambda e, b=b, t=t: e.dma_start(out=x2s[t * 128:(t + 1) * 128, :], in_=rr[b][:]),
                          reads=[("rr", b)], writes=[("x2s", t)])
                S.flush()

        if "ffn1" not in stages:
            x1Tin = din("x1Tin", [128, 8, NTOK])
            S.dma("pool", lambda e: e.dma_start(out=x1T[:], in_=x1Tin[:, :, :]), writes=["x1Tinit"])
            S.flush()
        if "ffn1" in stages:
            def store1(t, tile, key):
                S.dma("pool", lambda e: e.dma_start(out=x1s[t * 128:(t + 1) * 128, :], in_=tile[:]), reads=[key],
                      writes=[("x1s", t)])
            ffn("f1", lambda t: xin[t * 128:(t + 1) * 128, :], f1g, f1u, f1d, 0, store1, x1T)

        gdnT = stp.enter_context(nc.sbuf_tensor("gdnT", [128, 4, 2048], BF16))
        if "attn" in stages:
            attention_stage()
        if "sproj" in stages:
            sample_proj_stage()
        if "sattn" in stages:
            sample_attn_stage()
        if "gdn" in stages:
            gdn_prompt_stage()
        if "sgdn" in stages:
            sample_gdn_stage()
        if "wout" in stages:
            wout_stage()
        S.flush()
        stp.close()
        if "ffn2" in stages:
            def store2(t, tile, key):
                if t == 0:
                    S.dma("pool", lambda e: e.dma_start(out=o_ys[:, :], in_=tile[0:64, :]), reads=[key], writes=[("oy", t)])
                else:
                    S.dma("pool", lambda e: e.dma_start(out=o_yp[(t - 1) * 128:t * 128, :], in_=tile[:]), reads=[key],
                          writes=[("oy", t)])
            ffn("f2", lambda t: x2s[t * 128:(t + 1) * 128, :], f2g, f2u, f2d, 4, store2, None)
        S.flush()
    return nc


def used_inputs(nc):
    names = set()
    for a in nc.allocations:
        try:
            if a.kind == "ExternalInput":
                names.add(a.name)
        except Exception:
            pass
    return names


def _t5_bucket(n):
    n = np.asarray(n, np.int64)
    nf = np.maximum(n, 1).astype(np.float64)
    large = 16 + np.floor(np.log(nf / 16.0) / math.log(2048 / 16.0) * 16.0 + 1e-9).astype(np.int64)
    large = np.minimum(large, 31)
    return np.where(n < 16, n, large)


def _onehot():
    oh = np.zeros((3, 33, 384), np.float32)
    for p, d in enumerate((1, 4, 16)):
        for m in range(383):
            dl = m - 127
            b = int(_t5_bucket(dl * d)) if 0 <= dl <= 128 else 32
            oh[p, b, m] = 1.0
    return oh


def _gconst():
    i = np.arange(64)
    P, Fr = i[:, None], i[None, :]
    g = np.zeros((64, 7, 64), np.float32)
    g[:, 0] = np.where(Fr < P, 0.0, NEG)
    g[:, 1] = np.where(Fr >= P, 0.0, NEG)
    g[:, 2] = np.where(Fr > P, -1.0, 0.0)
    g[:, 3] = np.eye(64)
    g[:, 4] = 1.0
    g[:, 5] = np.where(P <= Fr, 1.0, 0.0)
    g[:, 6] = np.where(P == 63, 1.0, 0.0)
    return g


def _gvec(inp):
    g = np.zeros((3, 64), np.float32)
    g[0, 0:8] = inp["gdn_a_log"][0]
    g[1, 0:8] = inp["gdn_dt_bias"][0]
    g[2, :] = inp["gdn_norm_w"][0]
    return g


def _wcm():
    w = np.zeros((32, 8, 4, 128), np.float32)
    for j in range(8):
        for p in range(128):
            if j < 4:
                pos = 1536 + j * 128 + p
            elif j < 7:
                pos = 16 * ((j - 4) * 32 + p // 4) + p % 4
            elif p < 4:
                pos = 2048 + p
            else:
                continue
            for t in range(4):
                dist = 2048 + t - pos
                if dist < 0:
                    continue
                for (win, d) in ((128, 1), (512, 4), (2048, 16)):
                    if dist % d == 0 and dist <= win:
                        w[int(_t5_bucket(dist)), j, t, p] += 1.0
    return w


def core_inputs(inp, c, big=None):
    f = np.float32
    xin = np.zeros((NTOK, D), f)
    xin[0:64] = inp["x_sample"][16 * c:16 * c + 16].reshape(64, D)
    xin[128:] = inp["x_prompt"][c]
    lnp = np.stack([inp["ln1_g"][0], inp["ln1_b"][0], inp["ln2_g"][0], inp["ln2_b"][0],
                    inp["ln3_g"][0], inp["ln3_b"][0]]).astype(f)
    m = {
        "xin": xin,
        "f1g": np.ascontiguousarray(inp["ffn1_w_gate"][0]), "f1u": np.ascontiguousarray(inp["ffn1_w_up"][0]),
        "f1d": np.ascontiguousarray(inp["ffn1_w_down"][0]),
        "lnp": lnp, "ident": np.eye(128, dtype=f),
        "w_in": np.ascontiguousarray(inp["w_in"][0]), "relb": np.ascontiguousarray(inp["rel_bias"]),
        "onehot": _onehot(), "antiid": np.ascontiguousarray(np.eye(128, dtype=f)[::-1]),
        "gconst": _gconst(), "convw": np.ascontiguousarray(inp["gdn_conv_w"][0]),
        "gvec": _gvec(inp),
        "sg_in": np.ascontiguousarray(inp["state_gdn"][0, 16 * c:16 * c + 16]).reshape(128, 4096),
        "sc_in": np.ascontiguousarray(inp["state_conv"][0, 16 * c:16 * c + 16]),
        "wcm": _wcm(),
        "cws": np.ascontiguousarray(np.broadcast_to(
            inp["gdn_conv_w"][0].reshape(4, 3, 8, 64).transpose(2, 1, 0, 3)[None], (16, 8, 3, 4, 64)).reshape(128, 3, 4, 64)),
        "gvs": np.ascontiguousarray(np.tile(np.stack([inp["gdn_a_log"][0], inp["gdn_dt_bias"][0]], axis=1), (16, 1))).astype(f),
        "w_out": np.ascontiguousarray(inp["w_out"][0]),
        "f2g": np.ascontiguousarray(inp["ffn2_w_gate"][0]), "f2u": np.ascontiguousarray(inp["ffn2_w_up"][0]),
        "f2d": np.ascontiguousarray(inp["ffn2_w_down"][0]),
    }
    if big is not None:
        m["ck"] = np.ascontiguousarray(big["cache_attn_k"][0, 16 * c:16 * c + 16]).reshape(16, 2048, 512)
        m["cv"] = np.ascontiguousarray(big["cache_attn_v"][0, 16 * c:16 * c + 16]).reshape(16, 2048, 512)
    return m


ALL_STAGES = ("ffn1", "attn", "sproj", "sattn", "gdn", "sgdn", "wout", "ffn2")
_NC_CACHE = {}


def gather_outputs(results):
    n = len(results)
    f = np.float32
    yp = np.stack([r["o_yp"] for r in results]).astype(f)
    ys = np.concatenate([r["o_ys"].reshape(16, 4, D) for r in results]).astype(f)
    kp = np.stack([r["o_kp"].reshape(2048, 8, 64) for r in results])[None].astype(f)
    vp = np.stack([r["o_vp"].reshape(2048, 8, 64) for r in results])[None].astype(f)
    sgp = np.stack([r["o_sgp"] for r in results])[None].astype(f)
    scp = np.stack([r["o_scp"] for r in results])[None].astype(f)
    ks = np.concatenate([r["o_ks"].reshape(16, 4, 8, 64) for r in results])[None].astype(f)
    vs = np.concatenate([r["o_vs"].reshape(16, 4, 8, 64) for r in results])[None].astype(f)
    sgs = np.concatenate([r["o_sgs"].reshape(16, 8, 64, 64) for r in results])[None].astype(f)
    scs = np.concatenate([r["o_scs"] for r in results])[None].astype(f)
    return (yp, ys, kp, vp, sgp, scp, ks, vs, sgs, scs)


def kernel(**inputs):
    inp = {k: np.asarray(v) for k, v in inputs.items()}
    if "nc" not in _NC_CACHE:
        _NC_CACHE["nc"] = build_nc(dbg=False, stages=ALL_STAGES)
    nc = _NC_CACHE["nc"]
    in_maps = [core_inputs(inp, c, inp) for c in range(8)]
    res = run_bass_kernel_spmd(nc, in_maps, core_ids=list(range(8)))
    return gather_outputs(res.results)
```

```python
import contextlib
import math
import numpy as np
import concourse.bass as bass
import concourse.mybir as mybir
from concourse.bass_utils import run_bass_kernel_spmd

F32 = mybir.dt.float32
BF16 = mybir.dt.bfloat16
ALU = mybir.AluOpType
AF = mybir.ActivationFunctionType
AX = mybir.AxisListType

D = 1024
DFF = 2816
NFC = 22
NT = 17
NTOK = NT * 128
INC = 3600
ALPHA = 2.0 ** 0.25
LN_EPS = 1e-5
NEG = -30000.0


def sl(start, count, step=1):
    return slice(start, start + (count - 1) * step + 1, step)


class Sched:
    def __init__(self, nc, stack, ndma=6):
        self.nc = nc
        self.eng = {"pe": nc.tensor, "act": nc.scalar, "dve": nc.vector, "pool": nc.gpsimd, "sp": nc.sync}
        self.esem = {k: stack.enter_context(nc.semaphore("es_" + k)) for k in self.eng}
        self.tick = {k: 0 for k in self.eng}
        self.dsem = {q: [stack.enter_context(nc.semaphore("ds_%s%d" % (q, i))) for i in range(ndma)]
                     for q in ("sp", "pool", "act")}
        self.duse = {q: [0] * ndma for q in self.dsem}
        self.dcnt = {q: 0 for q in self.dsem}
        self.seen = {k: {} for k in self.eng}
        self.ops = []

    def op(self, eng, fn, reads=(), writes=()):
        self.ops.append(dict(eng=eng, fn=fn, reads=tuple(reads), writes=tuple(writes), dma=False))

    def dma(self, q, fn, reads=(), writes=()):
        self.ops.append(dict(eng=q, fn=fn, reads=tuple(reads), writes=tuple(writes), dma=True))

    def capture(self, fn):
        saved = self.ops
        self.ops = []
        fn()
        got = self.ops
        self.ops = saved
        return got

    def emit_merged(self, a, b):
        i = j = 0
        while i < len(a) or j < len(b):
            if j >= len(b) or (i < len(a) and i * len(b) <= j * len(a)):
                self.ops.append(a[i])
                i += 1
            else:
                self.ops.append(b[j])
                j += 1

    def _wait(self, e, sem, val):
        key = id(sem)
        if self.seen[e].get(key, 0) < val:
            self.eng[e].wait_ge(sem, val)
            self.seen[e][key] = val

    def flush(self, barrier=True):
        ops = self.ops
        self.ops = []
        last_w = {}
        readers = {}
        needs = [False] * len(ops)
        for i, o in enumerate(ops):
            deps = set()

            def inorder(j):
                return ops[j]["eng"] == o["eng"] == "pe" and not ops[j]["dma"] and not o["dma"]

            for r in o["reads"]:
                j = last_w.get(r)
                if j is not None and not (inorder(j) and o["eng"] == "pe"):
                    deps.add(j)
            for w in o["writes"]:
                j = last_w.get(w)
                if j is not None and not inorder(j):
                    deps.add(j)
                for j in readers.get(w, ()):
                    if not inorder(j):
                        deps.add(j)
            o["deps"] = sorted(deps)
            for j in deps:
                needs[j] = True
            for r in o["reads"]:
                readers.setdefault(r, []).append(i)
            for w in o["writes"]:
                last_w[w] = i
                readers[w] = []
        lastop = {}
        for i, o in enumerate(ops):
            if not o["dma"]:
                lastop[o["eng"]] = i
        for i in lastop.values():
            needs[i] = True
        for i, o in enumerate(ops):
            e = o["eng"]
            for j in o["deps"]:
                ev = ops[j]["event"]
                self._wait(e, ev[0], ev[1])
            if o["dma"]:
                n = len(self.dsem[e])
                slot = self.dcnt[e] % n
                self.dcnt[e] += 1
                sem = self.dsem[e][slot]
                k = self.duse[e][slot]
                if k > 0:
                    self._wait(e, sem, 16 * k)
                ins = o["fn"](self.eng[e])
                ins.then_inc(sem, 16)
                self.duse[e][slot] = k + 1
                o["event"] = (sem, 16 * (k + 1))
            else:
                ins = o["fn"](self.eng[e])
                if needs[i]:
                    self.tick[e] += 1
                    ins.then_inc(self.esem[e], 1)
                    o["event"] = (self.esem[e], self.tick[e])
                else:
                    o["event"] = None
        if barrier:
            self.barrier()

    def barrier(self):
        for e in self.eng:
            for d in self.eng:
                if d != e and self.tick[d] > 0:
                    self._wait(e, self.esem[d], self.tick[d])
            for q in self.dsem:
                for s, k in zip(self.dsem[q], self.duse[q]):
                    if k > 0:
                        self._wait(e, s, 16 * k)


def build_nc(dbg=False, stages=("ffn1",)):
    nc = bass.Bass("TRN2", target_bir_lowering=False)

    def din(name, shape, dt=F32):
        return nc.dram_tensor(name, list(shape), dt, kind="ExternalInput").ap()

    def dout(name, shape, dt=F32):
        return nc.dram_tensor(name, list(shape), dt, kind="ExternalOutput").ap()

    def dscr(name, shape, dt=F32):
        return nc.dram_tensor(name, list(shape), dt, kind="ExternalOutput" if dbg else "Internal").ap()

    xin = din("xin", [NTOK, D])
    f1g = din("f1g", [D, DFF])
    f1u = din("f1u", [D, DFF])
    f1d = din("f1d", [DFF, D])
    lnp = din("lnp", [6, D])
    ident_d = din("ident", [128, 128])
    x1s = dscr("x1s", [NTOK, D])
    w_in = din("w_in", [D, INC])
    relb = din("relb", [32, 8])
    onehot = din("onehot", [3, 33, 384])
    antiid = din("antiid", [128, 128])
    o_kp = dout("o_kp", [2048, 512])
    o_vp = dout("o_vp", [2048, 512])
    o_ks = dout("o_ks", [64, 512])
    o_vs = dout("o_vs", [64, 512])
    fvd = dscr("fvd", [3, 8, 384])
    attn_s = dscr("attn_s", [8, 64, 2048], BF16)
    gconst = din("gconst", [64, 7, 64])
    convw = din("convw", [4, 1536])
    gvec = din("gvec", [3, 64])
    o_sgp = dout("o_sgp", [8, 64, 64])
    o_scp = dout("o_scp", [3, 1536])
    gdn_dbg = dscr("gdn_dbg", [128, 4, 2048]) if dbg else None
    ck = din("ck", [16, 2048, 512])
    cv = din("cv", [16, 2048, 512])
    sg_in = din("sg_in", [128, 4096])
    sc_in = din("sc_in", [16, 3, 1536])
    wcm = din("wcm", [32, 8, 4, 128])
    cws = din("cws", [128, 3, 4, 64])
    gvs = din("gvs", [128, 2])
    dmask = din("dmask", [32, 8])
    w_out = din("w_out", [D, D])
    f2g = din("f2g", [D, DFF])
    f2u = din("f2u", [D, DFF])
    f2d = din("f2d", [DFF, D])
    o_ys = dout("o_ys", [64, D])
    o_yp = dout("o_yp", [2048, D])
    o_sgs = dout("o_sgs", [128, 4096])
    o_scs = dout("o_scs", [16, 3, 1536])
    qs_s = dscr("qs_s", [64, 512])
    gq_s = dscr("gq_s", [64, 1536])
    z_s = dscr("z_s", [64, 512])
    ba_s = dscr("ba_s", [64, 16])
    heads_s = dscr("heads_s", [64, D])
    x2s = dscr("x2s", [NTOK, D])

    with contextlib.ExitStack() as stack:
        S = Sched(nc, stack)
        sb = lambda name, shape, dt=F32: stack.enter_context(nc.sbuf_tensor(name, list(shape), dt))
        psb = [stack.enter_context(nc.psum_tensor("psb%d" % i, [128, 512], F32)) for i in range(7)]
        pst = stack.enter_context(nc.psum_tensor("pst", [128, 1024], BF16))

        identf = sb("identf", [128, 128])
        identb = sb("identb", [128, 128], BF16)
        stp = contextlib.ExitStack()
        x1T = stp.enter_context(nc.sbuf_tensor("x1T", [128, 8, NTOK], BF16))

        S.dma("sp", lambda e: e.dma_start(out=identf[:], in_=ident_d[:, :]), writes=["identf"])
        S.op("dve", lambda e: e.tensor_copy(out=identb[:], in_=identf[:]), reads=["identf"], writes=["identb"])
        S.flush()

        def to_featmajor(src_bf, skey, dstT, col0, dkey):
            for kc in range(8):
                S.op("pe", lambda e, kc=kc: e.transpose(pst[:, kc * 128:(kc + 1) * 128],
                                                        src_bf[:, kc * 128:(kc + 1) * 128], identb[:]),
                     reads=[skey, "identb"], writes=["pst"])
            S.op("dve", lambda e: e.tensor_copy(out=dstT[:, :, col0:col0 + 128],
                                               in_=pst[:].rearrange("p (k c) -> p k c", k=8)),
                 reads=["pst"], writes=[dkey])

        def layernorm(r, rkey, lnrep, out, okey, tmp, epsmul=4.0):
            st, mv, rstd = tmp
            for c in range(2):
                S.op("dve", lambda e, c=c: e.bn_stats(out=st[:, c, :], in_=r[:, c * 512:(c + 1) * 512]),
                     reads=[rkey], writes=[("st", c)])
            S.op("dve", lambda e: e.bn_aggr(out=mv[:], in_=st[:].rearrange("p c s -> p (c s)")),
                 reads=[("st", 0), ("st", 1)], writes=["mv"])
            S.op("dve", lambda e: e.tensor_scalar(out=rstd[:], in0=mv[:, 1:2], scalar1=epsmul * LN_EPS, scalar2=None,
                                                  op0=ALU.add), reads=["mv"], writes=["rstd"])
            S.op("act", lambda e: e.sqrt(out=rstd[:], in_=rstd[:]), reads=["rstd"], writes=["rstd"])
            S.op("dve", lambda e: e.reciprocal(out=rstd[:], in_=rstd[:]), reads=["rstd"], writes=["rstd"])
            S.op("dve", lambda e: e.tensor_scalar(out=r[:], in0=r[:], scalar1=mv[:, 0:1], scalar2=rstd[:, 0:1],
                                                  op0=ALU.subtract, op1=ALU.mult),
                 reads=[rkey, "mv", "rstd"], writes=[rkey])
            S.op("pool", lambda e: e.tensor_tensor(out=r[:], in0=r[:], in1=lnrep[:, 0, :], op=ALU.mult),
                 reads=[rkey, ("lnrep", 0)], writes=[rkey])
            S.op("dve", lambda e: e.tensor_tensor(out=out[:], in0=r[:], in1=lnrep[:, 1, :], op=ALU.add),
                 reads=[rkey, ("lnrep", 1)], writes=[okey])

        def ffn(tag, xsrc, wg, wu, wd, gi, store, xTout):
            with contextlib.ExitStack() as st2:
                sb2 = lambda name, shape, dt=F32: st2.enter_context(nc.sbuf_tensor(tag + name, list(shape), dt))
                MT = 9
                xT = sb2("xT", [128, 8, MT * 128], BF16)
                hT = sb2("hT", [128, NFC, MT * 128], BF16)
                wgb = [sb2("wgb%d" % i, [128, 8, 256], BF16) for i in range(2)]
                wub = [sb2("wub%d" % i, [128, 8, 256], BF16) for i in range(2)]
                wdb = sb2("wdb", [128, NFC, D], BF16)
                xs = [sb2("xs%d" % i, [128, D]) for i in range(2)]
                xb = [sb2("xb%d" % i, [128, D], BF16) for i in range(2)]
                sg = [sb2("sg%d" % i, [128, 512]) for i in range(2)]
                rr = [sb2("rr%d" % i, [128, D]) for i in range(2)]
                oo = rr
                lnrep = sb2("ln", [128, 2, D])
                for i in range(2):
                    S.dma("sp", lambda e, i=i: e.dma_start(
                        out=lnrep[:, i, :], in_=lnp[gi + i:gi + i + 1, :].partition_broadcast(128)),
                          writes=[("lnrep", i)])
                stt = sb2("st", [128, 2, 6])
                mv = sb2("mv", [128, 2])
                rstd = sb2("rstd", [128, 1])
                wgv = wg.rearrange("(kc p) f -> p kc f", p=128)
                wuv = wu.rearrange("(kc p) f -> p kc f", p=128)
                wdv = wd.rearrange("(fc p) d -> p fc d", p=128)
                for mi, tiles in enumerate((list(range(0, MT)), list(range(MT, NT)))):
                    ntl = len(tiles)
                    for li, t in enumerate(tiles):
                        b = li % 2
                        S.dma("sp", lambda e, b=b, t=t: e.dma_start(out=xs[b][:], in_=xsrc(t)), writes=[("xs", b)])
                        S.op("act", lambda e, b=b: e.copy(out=xb[b][:], in_=xs[b][:]), reads=[("xs", b)],
                             writes=[("xb", b)])
                        to_featmajor(xb[b], ("xb", b), xT, li * 128, ("xT", li))
                    if mi == 0:
                        for q in range(2):
                            S.dma("pool", lambda e, q=q: e.dma_start(out=wdb[:, q * 11:(q + 1) * 11, :],
                                                                     in_=wdv[:, q * 11:(q + 1) * 11, :]),
                                  writes=[("wdb", q)])
                    tbs = [(c0, min(512, ntl * 128 - c0)) for c0 in range(0, ntl * 128, 512)]
                    for fb in range(11):
                        wbuf = fb % 2
                        S.dma("pool", lambda e, fb=fb, wbuf=wbuf: e.dma_start(
                            out=wgb[wbuf][:], in_=wgv[:, :, fb * 256:(fb + 1) * 256]), writes=[("wgb", wbuf)])
                        S.dma("pool", lambda e, fb=fb, wbuf=wbuf: e.dma_start(
                            out=wub[wbuf][:], in_=wuv[:, :, fb * 256:(fb + 1) * 256]), writes=[("wub", wbuf)])
                        for j in range(2):
                            fc = fb * 2 + j
                            for ti, (c0, cw) in enumerate(tbs):
                                pb = (fc * len(tbs) + ti) % 2
                                pg, pu = psb[pb], psb[2 + pb]
                                xkeys = [("xT", li) for li in range(c0 // 128, (c0 + cw) // 128)]
                                for kc in range(8):
                                    S.op("pe", lambda e, pg=pg, kc=kc, wbuf=wbuf, j=j, c0=c0, cw=cw: e.matmul(
                                        pg[:, 0:cw], wgb[wbuf][:, kc, j * 128:(j + 1) * 128], xT[:, kc, c0:c0 + cw],
                                        start=(kc == 0), stop=(kc == 7)),
                                         reads=[("wgb", wbuf)] + xkeys, writes=[("psb", pb)])
                                for kc in range(8):
                                    S.op("pe", lambda e, pu=pu, kc=kc, wbuf=wbuf, j=j, c0=c0, cw=cw: e.matmul(
                                        pu[:, 0:cw], wub[wbuf][:, kc, j * 128:(j + 1) * 128], xT[:, kc, c0:c0 + cw],
                                        start=(kc == 0), stop=(kc == 7)),
                                         reads=[("wub", wbuf)] + xkeys, writes=[("psb", 2 + pb)])
                                S.op("act", lambda e, pg=pg, pb=pb, cw=cw: e.activation(
                                    out=sg[pb][:, 0:cw], in_=pg[:, 0:cw], func=AF.Silu),
                                     reads=[("psb", pb)], writes=[("sg", pb)])
                                S.op("dve", lambda e, pu=pu, pb=pb, fc=fc, c0=c0, cw=cw: e.tensor_tensor(
                                    out=hT[:, fc, c0:c0 + cw], in0=sg[pb][:, 0:cw], in1=pu[:, 0:cw], op=ALU.mult),
                                     reads=[("sg", pb), ("psb", 2 + pb)], writes=[("hT", fc, ti)])
                    for li, t in enumerate(tiles):
                        b = li % 2
                        ti = li // 4
                        S.dma("sp", lambda e, b=b, t=t: e.dma_start(out=xs[b][:], in_=xsrc(t)), writes=[("xs", b)])
                        for half in range(2):
                            pd = psb[4 + half]
                            for fc in range(NFC):
                                S.op("pe", lambda e, pd=pd, fc=fc, li=li, half=half: e.matmul(
                                    pd[:, :], hT[:, fc, li * 128:(li + 1) * 128], wdb[:, fc, half * 512:(half + 1) * 512],
                                    start=(fc == 0), stop=(fc == NFC - 1)),
                                     reads=[("hT", fc, ti), ("wdb", fc // 11)], writes=[("psb", 4 + half)])
                            S.op("dve", lambda e, pd=pd, b=b, half=half: e.scalar_tensor_tensor(
                                out=rr[b][:, half * 512:(half + 1) * 512], in0=xs[b][:, half * 512:(half + 1) * 512],
                                scalar=2.0 * ALPHA, in1=pd[:, :], op0=ALU.mult, op1=ALU.add),
                                 reads=[("xs", b), ("psb", 4 + half)], writes=[("rr", b)])
                        layernorm(rr[b], ("rr", b), lnrep, rr[b], ("rr", b), (stt, mv, rstd))
                        store(t, rr[b], ("rr", b))
                        if xTout is not None:
                            S.op("act", lambda e, b=b: e.copy(out=xb[b][:], in_=rr[b][:]), reads=[("rr", b)],
                                 writes=[("xb", b)])
                            to_featmajor(xb[b], ("xb", b), xTout, t * 128, ("xTo", t))
                S.flush()

        PAT = ((128, 1), (512, 4), (2048, 16))

        def unit_tokens(d, u):
            nblk = 16 // d
            r, n = u // nblk, u % nblk
            return r, n, nblk, r + d * 128 * n

        def attention_stage():
            with contextlib.ExitStack() as st2:
                sb2 = lambda name, shape, dt=F32: st2.enter_context(nc.sbuf_tensor("at" + name, list(shape), dt))
                attnT = [sb2("attnT%d" % i, [64, 2048], BF16) for i in range(2)]
                qT = sb2("qT", [128, 4, 2048], BF16)
                kT = sb2("kT", [128, 4, 2048], BF16)
                vaug = [sb2("vaug%d" % i, [128, 16, 8, 65], BF16) for i in range(3)]
                brev = sb2("brev", [128, 24, 256], BF16)
                jb = sb2("jb", [128, 128], BF16)
                onesf = sb2("onesf", [128, 64])
                rbx = sb2("rbx", [33, 8])
                ohs = sb2("ohs", [33, 3, 384])
                fvs = sb2("fvs", [8, 3, 384])
                winv = w_in.rearrange("(kc p) f -> p kc f", p=128)
                S.dma("pool", lambda e: e.dma_start(out=jb[:], in_=antiid[:, :]), writes=["jb"])
                S.op("pool", lambda e: e.memset(onesf[:], 1.0), writes=["onesf"])
                S.op("pool", lambda e: e.memset(rbx[32:33, :], NEG), writes=["rbx1"])
                S.dma("sp", lambda e: e.dma_start(out=rbx[0:32, :], in_=relb[:, :]), writes=["rbx0"])
                S.dma("sp", lambda e: e.dma_start(out=ohs[:], in_=onehot.rearrange("p b m -> b p m")), writes=["ohs"])
                for p in range(3 if "notables" not in stages else 0):
                    S.op("pe", lambda e, p=p: e.matmul(psb[p][0:8, 0:384], rbx[:, :], ohs[:, p, :], start=True, stop=True),
                         reads=["rbx0", "rbx1", "ohs"], writes=[("psb", p)])
                    S.op("dve", lambda e, p=p: e.tensor_copy(out=fvs[:, p, :], in_=psb[p][0:8, 0:384]),
                         reads=[("psb", p)], writes=[("fvs", p)])
                if "notables" not in stages:
                    S.dma("sp", lambda e: e.dma_start(out=fvd.rearrange("p h m -> h p m"), in_=fvs[:]),
                          reads=[("fvs", p) for p in range(3)], writes=["fvd"])
                if "nohankel" not in stages:
                    S.dma("pool", lambda e: e.dma_start(
                        out=brev[:], in_=bass.AP(fvd.tensor, 0, [[1, 128], [384, 24], [1, 256]])),
                          reads=["fvd"], writes=["brev"])
                for i in range(3):
                    S.op("pool", lambda e, i=i: e.memset(vaug[i][:, :, :, 64:65], 1.0), writes=[("vone", i)])
                with contextlib.ExitStack() as st3:
                    sb3 = lambda name, shape, dt=F32: st3.enter_context(nc.sbuf_tensor("ap" + name, list(shape), dt))
                    wb = [sb3("wb%d" % i, [128, 8, 512], BF16) for i in range(3)]
                    kvo = [sb3("kvo%d" % i, [128, 512]) for i in range(2)]
                    for blk in range(3):
                        S.dma("pool", lambda e, blk=blk: e.dma_start(out=wb[blk][:], in_=winv[:, :, blk * 512:(blk + 1) * 512]),
                              writes=[("wb", blk)])
                    cnt = 0
                    if "noproj" in stages:
                        S.flush()
                        return
                    for blk, dst in (((0, qT), (1, kT)) if "noqk" not in stages else ()):
                        for pair in range(4):
                            for tb in range(4):
                                pb = cnt % 2
                                cnt += 1
                                for kc in range(8):
                                    S.op("pe", lambda e, pb=pb, blk=blk, pair=pair, tb=tb, kc=kc: e.matmul(
                                        psb[pb][:, :], wb[blk][:, kc, pair * 128:(pair + 1) * 128],
                                        x1T[:, kc, 128 + tb * 512:128 + (tb + 1) * 512], start=(kc == 0), stop=(kc == 7)),
                                         reads=[("wb", blk)], writes=[("psb", pb)])
                                if blk == 0:
                                    S.op("act", lambda e, pb=pb, pair=pair, tb=tb: e.mul(
                                        out=qT[:, pair, tb * 512:(tb + 1) * 512], in_=psb[pb][:, :], mul=0.125),
                                         reads=[("psb", pb)], writes=[("qT", pair)])
                                else:
                                    S.op("dve", lambda e, pb=pb, pair=pair, tb=tb: e.tensor_copy(
                                        out=kT[:, pair, tb * 512:(tb + 1) * 512], in_=psb[pb][:, :]),
                                         reads=[("psb", pb)], writes=[("kT", pair)])
                    for t in range(NT if "nokv" not in stages else 0):
                        for blk in (1, 2):
                            pb = 2 + (cnt % 2)
                            cnt += 1
                            ob = blk - 1
                            for kc in range(8):
                                S.op("pe", lambda e, pb=pb, blk=blk, t=t, kc=kc: e.matmul(
                                    psb[pb][:, :], x1T[:, kc, t * 128:(t + 1) * 128], wb[blk][:, kc, :],
                                    start=(kc == 0), stop=(kc == 7)),
                                     reads=[("wb", blk)], writes=[("psb", pb)])
                            S.op("act", lambda e, pb=pb, ob=ob: e.activation(out=kvo[ob][:], in_=psb[pb][:, :], func=AF.Copy),
                                 reads=[("psb", pb)], writes=[("kvo", ob)])
                            if blk == 2 and t >= 1:
                                S.op("dve", lambda e, ob=ob, t=t: e.tensor_copy(
                                    out=vaug[0][:, t - 1, :, 0:64], in_=kvo[ob][:, :].rearrange("p (h e) -> p h e", h=8)),
                                     reads=[("kvo", ob)], writes=[("vaug", 0, t - 1)])
                            if "nokvdma" in stages:
                                continue
                            if t == 0:
                                dst = (o_ks if blk == 1 else o_vs)[0:64, :]
                                S.dma("sp", lambda e, dst=dst, ob=ob: e.dma_start(out=dst, in_=kvo[ob][0:64, :]),
                                      reads=[("kvo", ob)], writes=[("okv", blk, t)])
                            else:
                                dst = (o_kp if blk == 1 else o_vp)[(t - 1) * 128:t * 128, :]
                                S.dma("sp", lambda e, dst=dst, ob=ob: e.dma_start(out=dst, in_=kvo[ob][:, :]),
                                      reads=[("kvo", ob)], writes=[("okv", blk, t)])
                    for pi in ((1, 2) if "nodil" not in stages else ()):
                        d = PAT[pi][1]
                        for u in range(16):
                            r, n, nblk, t0 = unit_tokens(d, u)
                            pb = 2 + (cnt % 2)
                            cnt += 1
                            for kc in range(8):
                                S.op("pe", lambda e, pb=pb, kc=kc, t0=t0, d=d: e.matmul(
                                    psb[pb][:, :], x1T[:, kc, sl(128 + t0, 128, d)], wb[2][:, kc, :],
                                    start=(kc == 0), stop=(kc == 7)),
                                     reads=[("wb", 2)], writes=[("psb", pb)])
                            S.op("dve", lambda e, pb=pb, pi=pi, u=u: e.tensor_copy(
                                out=vaug[pi][:, u, :, 0:64], in_=psb[pb][:, :].rearrange("p (h e) -> p h e", h=8)),
                                 reads=[("psb", pb)], writes=[("vaug", pi, u)])
                    S.flush()
                if "noattnmain" in stages:
                    return
                with contextlib.ExitStack() as st3:
                    sb3 = lambda name, shape, dt=F32: st3.enter_context(nc.sbuf_tensor("aa" + name, list(shape), dt))
                    acc = [sb3("acc%d" % i, [65, 2048]) for i in range(2)]
                    pts = [sb3("pt%d" % i, [128, 256], BF16) for i in range(4)]
                    rcp = sb3("rcp", [65, 2048])
                    ptc = 0
                    cnt = 0
                    for h in range(8):
                        pair, base = h // 2, (h % 2) * 64
                        ab = h % 2
                        A = acc[ab]
                        units = [(pi, d, u) for pi, (win, d) in enumerate(PAT) for u in range(16)]

                        def st_part(ix, h=h, pair=pair, base=base):
                            pi, d, u = units[ix]
                            r, n, nblk, t0 = unit_tokens(d, u)
                            W = 256 if n + 1 < nblk else 128
                            ps = ix % 2
                            pt = ix % 4
                            S.op("pe", lambda e: e.matmul(
                                psb[ps][:, 0:W], kT[base:base + 64, pair, sl(t0, 128, d)],
                                qT[base:base + 64, pair, sl(t0, W, d)], start=True, stop=False),
                                 reads=[("qT", pair), ("kT", pair)], writes=[("psb", ps)])
                            S.op("pe", lambda e: e.matmul(
                                psb[ps][:, 0:W], jb[:, :], brev[:, pi * 8 + h, 0:W], start=False, stop=True),
                                 reads=["jb", "brev"], writes=[("psb", ps)])
                            S.op("act", lambda e: e.activation(
                                out=pts[pt][:, 0:W], in_=psb[ps][:, 0:W], func=AF.Exp),
                                 reads=[("psb", ps)], writes=[("pt", pt)])

                        def pv_part(ix, h=h, A=A, ab=ab):
                            pi, d, u = units[ix]
                            r, n, nblk, t0 = unit_tokens(d, u)
                            pt = ix % 4
                            prev = (ix - 1) % 4
                            po = 2 + (ix % 2)
                            first = (n == 0)
                            S.op("pe", lambda e: e.matmul(
                                psb[po][0:65, 0:128], vaug[pi][:, u, h, :], pts[pt][:, 0:128], start=True, stop=first),
                                 reads=[("vaug", pi, u), ("vone", pi), ("pt", pt)], writes=[("psb", po)])
                            if not first:
                                S.op("pe", lambda e: e.matmul(
                                    psb[po][0:65, 0:128], vaug[pi][:, u - 1, h, :], pts[prev][:, 128:256],
                                    start=False, stop=True),
                                     reads=[("vaug", pi, u - 1), ("vone", pi), ("pt", prev)], writes=[("psb", po)])
                            dst = A[:, sl(t0, 128, d)]
                            if pi == 0:
                                S.op("dve", lambda e: e.tensor_copy(out=dst, in_=psb[po][0:65, 0:128]),
                                     reads=[("psb", po)], writes=[("acc", ab)])
                            else:
                                S.op("dve", lambda e: e.tensor_tensor(
                                    out=dst, in0=dst, in1=psb[po][0:65, 0:128], op=ALU.add),
                                     reads=[("psb", po), ("acc", ab)], writes=[("acc", ab)])

                        st_part(0)
                        st_part(1)
                        for ix in range(len(units)):
                            if ix + 2 < len(units):
                                st_part(ix + 2)
                            pv_part(ix)
                        S.op("act", lambda e, A=A: e.activation(out=rcp[64:65, :], in_=A[64:65, :], func=AF.Ln),
                             reads=[("acc", ab)], writes=["rcp"])
                        S.op("act", lambda e: e.activation(out=rcp[64:65, :], in_=rcp[64:65, :], func=AF.Exp, scale=-1.0),
                             reads=["rcp"], writes=["rcp"])
                        for tb in range(4):
                            pr = 4 + (tb % 2)
                            S.op("pe", lambda e, pr=pr, tb=tb: e.matmul(
                                psb[pr][0:64, :], onesf[64:65, 0:64], rcp[64:65, tb * 512:(tb + 1) * 512],
                                start=True, stop=True), reads=["rcp", "onesf"], writes=[("psb", pr)])
                            S.op("dve", lambda e, pr=pr, tb=tb, A=A, ab=ab: e.tensor_tensor(
                                out=attnT[ab][:, tb * 512:(tb + 1) * 512], in0=A[0:64, tb * 512:(tb + 1) * 512],
                                in1=psb[pr][0:64, :], op=ALU.mult),
                                 reads=[("psb", pr), ("acc", ab)], writes=[("attnT", ab)])
                        S.dma("sp", lambda e, h=h, ab=ab: e.dma_start(out=attn_s[h, :, :], in_=attnT[ab][:, :]),
                              reads=[("attnT", ab)], writes=[("attn_s", h)])
                    S.flush()

        def bl(ap, n=64):
            return ap.unsqueeze(2).broadcast_to([ap.shape[0], ap.shape[1], n])

        def bm(ap, n=8):
            return ap.unsqueeze(1).broadcast_to([ap.shape[0], n, ap.shape[1]])

        def v3(ap, h=8):
            return ap.rearrange("p (h x) -> p h x", h=h)

        def gdn_prompt_stage():
            with contextlib.ExitStack() as st2:
                sb2 = lambda name, shape, dt=F32: st2.enter_context(nc.sbuf_tensor("gd" + name, list(shape), dt))
                qh = sb2("qh", [64, 8, 2048], BF16)
                kh = sb2("kh", [64, 8, 2048], BF16)
                vT = sb2("vT", [128, 4, 2048], BF16)
                gcn = sb2("gcn", [64, 7, 64])
                NEGS, NEGT, MSKT, ID64, ONES, TRI, SEL = [gcn[:, i, :] for i in range(7)]
                cwq = sb2("cwq", [64, 16, 4])
                cwv = sb2("cwv", [128, 4, 4])
                nwr = sb2("nwr", [64, 64])
                wz = sb2("wz", [128, 8, 512], BF16)
                wba = sb2("wba", [128, 8, 16], BF16)
                stc = contextlib.ExitStack()
                cwr = stc.enter_context(nc.sbuf_tensor("gdcwr", [4, 1536], F32))
                winv = w_in.rearrange("(kc p) f -> p kc f", p=128)
                S.dma("sp", lambda e: e.dma_start(out=gcn[:], in_=gconst[:, :, :]), writes=["gcn"])
                S.dma("sp", lambda e: e.dma_start(out=cwr[:], in_=convw[:, :]), writes=["cwr"])
                S.dma("sp", lambda e: e.dma_start(out=nwr[:], in_=gvec[2:3, :].partition_broadcast(64)), writes=["nwr"])
                S.dma("pool", lambda e: e.dma_start(out=wz[:], in_=winv[:, :, 3072:3584]), writes=["wz"])
                S.dma("pool", lambda e: e.dma_start(out=wba[:], in_=winv[:, :, 3584:3600]), writes=["wba"])
                for g in range(16):
                    S.op("pe", lambda e, g=g: e.transpose(psb[0][0:64, g * 4:(g + 1) * 4], cwr[0:4, g * 64:(g + 1) * 64],
                                                          identf[0:4, 0:4]), reads=["cwr", "identf"], writes=[("psb", 0)])
                for c in range(4):
                    S.op("pe", lambda e, c=c: e.transpose(psb[1][:, c * 4:(c + 1) * 4],
                                                          cwr[0:4, 1024 + c * 128:1024 + (c + 1) * 128], identf[0:4, 0:4]),
                         reads=["cwr", "identf"], writes=[("psb", 1)])
                S.op("dve", lambda e: e.tensor_copy(out=cwq[:], in_=v3(psb[0][0:64, 0:64], 16)), reads=[("psb", 0)],
                     writes=["cwq"])
                S.op("dve", lambda e: e.tensor_copy(out=cwv[:], in_=v3(psb[1][:, 0:16], 4)), reads=[("psb", 1)],
                     writes=["cwv"])
                S.flush()
                stc.close()

                with contextlib.ExitStack() as st3:
                    sb3 = lambda name, shape, dt=F32: st3.enter_context(nc.sbuf_tensor("g1" + name, list(shape), dt))
                    wb = [sb3("wb%d" % i, [128, 8, 512], BF16) for i in range(2)]
                    raws = [sb3("raw%d" % i, [128, 2051]) for i in range(2)]
                    cacs = [sb3("cac%d" % i, [128, 2048]) for i in range(2)]
                    rin = [sb3("rin%d" % i, [64, 512]) for i in range(2)]
                    scpb = sb3("scpb", [128, 24, 3])
                    for i in range(2):
                        S.op("pool", lambda e, i=i: e.memset(raws[i][:, 0:3], 0.0), writes=[("raw0", i)])
                    epsb = sb3("epsb", [64, 2])
                    S.op("pool", lambda e: e.memset(epsb[:, 0:1], 64.0e-6), writes=["epsb"])
                    S.op("pool", lambda e: e.memset(epsb[:, 1:2], 1.0e-6), writes=["epsb"])
                    cnt = 0
                    gi = 0
                    pending_tail = []
                    for blk in range(3):
                        wbuf = blk % 2
                        S.dma("pool", lambda e, blk=blk, wbuf=wbuf: e.dma_start(
                            out=wb[wbuf][:], in_=winv[:, :, 1536 + blk * 512:1536 + (blk + 1) * 512]),
                              writes=[("wb", wbuf)])
                        ngrp, P = (8, 64) if blk < 2 else (4, 128)
                        for g in range(ngrp):
                          def group_body(part, blk=blk, g=g, wbuf=wbuf, P=P, rb_=gi % 2, cnt0=cnt):
                            cnt = cnt0
                            raw, cac = raws[rb_], cacs[rb_]
                            RAW, CAC = ("raw", rb_), ("cac", rb_)
                            if part == "tail":
                                return group_tail(blk, g, P, raw, cac, RAW, CAC, rb_)
                            for tb in range(4):
                                pb = cnt % 2
                                cnt += 1
                                for kc in range(8):
                                    S.op("pe", lambda e, pb=pb, wbuf=wbuf, g=g, P=P, tb=tb, kc=kc: e.matmul(
                                        psb[pb][0:P, :], wb[wbuf][:, kc, g * P:(g + 1) * P],
                                        x1T[:, kc, 128 + tb * 512:128 + (tb + 1) * 512], start=(kc == 0), stop=(kc == 7)),
                                         reads=[("wb", wbuf)], writes=[("psb", pb)])
                                S.op("act", lambda e, pb=pb, P=P, tb=tb, raw=raw: e.activation(
                                    out=raw[0:P, 3 + tb * 512:3 + (tb + 1) * 512], in_=psb[pb][0:P, :], func=AF.Copy),
                                     reads=[("psb", pb)], writes=[RAW])
                            gidx = blk * 8 + g
                            S.op("pool", lambda e, P=P, gidx=gidx, raw=raw: e.tensor_copy(
                                out=scpb[0:P, gidx, :], in_=raw[0:P, 2048:2051]), reads=[RAW], writes=[("scpb", gidx)])
                            cwt = (cwq[:, blk * 8 + g, :] if blk < 2 else cwv[:, g, :])
                            CH = CAC
                            S.op("dve", lambda e, P=P, cwt=cwt, raw=raw, cac=cac: e.tensor_scalar(
                                out=cac[0:P, :], in0=raw[0:P, 3:2051], scalar1=cwt[:, 3:4], scalar2=None, op0=ALU.mult),
                                 reads=[RAW, ("raw0", rb_), "cwq", "cwv"], writes=[CH])
                            for j in (2, 1, 0):
                                S.op("dve", lambda e, P=P, cwt=cwt, j=j, raw=raw, cac=cac: e.scalar_tensor_tensor(
                                    out=cac[0:P, :], in0=raw[0:P, j:j + 2048], scalar=cwt[:, j:j + 1], in1=cac[0:P, :],
                                    op0=ALU.mult, op1=ALU.add), reads=[RAW, ("raw0", rb_), CH], writes=[CH])
                          def group_tail(blk, g, P, raw, cac, RAW, CAC, rb_):
                            CHS = [CAC]
                            if blk == 2:
                                S.op("act", lambda e, g=g, cac=cac: e.activation(out=vT[:, g, :], in_=cac[:, :], func=AF.Silu),
                                     reads=CHS, writes=[("vT", g)])
                                return
                            S.op("act", lambda e, cac=cac: e.activation(out=cac[0:64, :], in_=cac[0:64, :], func=AF.Silu),
                                 reads=CHS, writes=[CAC])
                            S.op("act", lambda e, cac=cac, raw=raw: e.square(out=raw[0:64, 3:2051], in_=cac[0:64, :]),
                                 reads=[CAC], writes=[RAW])
                            dst = qh if blk == 0 else kh
                            for tb in range(4):
                                pb = 2 + (tb % 2)
                                rb = tb % 2
                                S.op("pe", lambda e, pb=pb, tb=tb, raw=raw: e.matmul(
                                    psb[pb][0:64, :], ONES, raw[0:64, 3 + tb * 512:3 + (tb + 1) * 512], start=True, stop=True),
                                     reads=[RAW, "gcn"], writes=[("psb", pb)])
                                sc = 64.0 if blk == 0 else 1.0
                                S.op("act", lambda e, pb=pb, rb=rb, sc=sc, blk=blk: e.activation(
                                    out=rin[rb][:], in_=psb[pb][0:64, :], func=AF.Ln, scale=sc, bias=epsb[:, blk:blk + 1]),
                                     reads=[("psb", pb), "epsb"], writes=[("rin", rb)])
                                S.op("act", lambda e, rb=rb: e.activation(out=rin[rb][:], in_=rin[rb][:], func=AF.Exp,
                                                                          scale=-0.5),
                                     reads=[("rin", rb)], writes=[("rin", rb)])
                                S.op("pool", lambda e, rb=rb, tb=tb, dst=dst, g=g, cac=cac: e.tensor_tensor(
                                    out=dst[:, g, tb * 512:(tb + 1) * 512], in0=cac[0:64, tb * 512:(tb + 1) * 512],
                                    in1=rin[rb][:], op=ALU.mult),
                                     reads=[CAC, ("rin", rb)], writes=[("qk", blk, g)])
                          head_ops = S.capture(lambda: group_body("head"))
                          S.emit_merged(head_ops, pending_tail)
                          pending_tail = S.capture(lambda: group_body("tail"))
                          gi += 1
                          cnt += 4
                    S.emit_merged([], pending_tail)
                    for blk_ in range(3):
                        ng_, P_ = (8, 64) if blk_ < 2 else (4, 128)
                        for g_ in range(ng_):
                            col0 = blk_ * 512 + g_ * P_
                            gidx = blk_ * 8 + g_
                            S.dma("sp" if g_ % 2 else "act", lambda e, P_=P_, col0=col0, gidx=gidx: e.dma_start(
                                out=o_scp[:, col0:col0 + P_].rearrange("j p -> p j"), in_=scpb[0:P_, gidx, :],
                                allow_slow_non_contiguous=True), reads=[("scpb", gidx)], writes=[("o_scp", col0)])
                    S.flush()

                gt = lambda name: sb2(name, [64, 32, 8])
                ba = sb2("ba", [64, 32, 16])
                beta, nbeta, gg, gc, gcl, eg, egl, ekd, nbeg, alr, dtr = [gt(n) for n in (
                    "beta", "nbeta", "gg", "gc", "gcl", "eg", "egl", "ekd", "nbeg", "alr", "dtr")]
                S.dma("sp", lambda e: e.dma_start(out=alr[:], in_=bass.AP(gvec.tensor, 0, [[0, 64], [0, 32], [1, 8]])),
                      writes=["alr"])
                S.dma("sp", lambda e: e.dma_start(out=dtr[:], in_=bass.AP(gvec.tensor, 64, [[0, 64], [0, 32], [1, 8]])),
                      writes=["dtr"])
                for n in range(32):
                    for kc in range(8):
                        S.op("pe", lambda e, n=n, kc=kc: e.matmul(
                            psb[0][0:64, n * 16:(n + 1) * 16], x1T[:, kc, 128 + n * 64:128 + (n + 1) * 64], wba[:, kc, :],
                            start=(kc == 0), stop=(kc == 7)), reads=["wba"], writes=[("psb", 0)])
                S.op("dve", lambda e: e.tensor_copy(out=ba[:], in_=v3(psb[0][0:64, :], 32)), reads=[("psb", 0)],
                     writes=["ba"])
                S.op("act", lambda e: e.activation(out=beta[:], in_=ba[:, :, 0:8], func=AF.Sigmoid), reads=["ba"],
                     writes=["beta"])
                S.op("dve", lambda e: e.tensor_scalar(out=nbeta[:], in0=beta[:], scalar1=-1.0, scalar2=None, op0=ALU.mult),
                     reads=["beta"], writes=["nbeta"])
                S.op("dve", lambda e: e.tensor_tensor(out=gg[:], in0=ba[:, :, 8:16], in1=dtr[:], op=ALU.add),
                     reads=["ba", "dtr"], writes=["gg"])
                S.op("act", lambda e: e.activation(out=gg[:], in_=gg[:], func=AF.Exp), reads=["gg"], writes=["gg"])
                S.op("act", lambda e: e.activation(out=gg[:], in_=gg[:], func=AF.Ln, bias=ONES[:, 0:1]),
                     reads=["gg", "gcn"], writes=["gg"])
                S.op("act", lambda e: e.activation(out=alr[:], in_=alr[:], func=AF.Exp), reads=["alr"], writes=["alr"])
                S.op("dve", lambda e: e.scalar_tensor_tensor(out=gg[:], in0=gg[:], scalar=-1.0, in1=alr[:], op0=ALU.mult,
                                                             op1=ALU.mult), reads=["gg", "alr"], writes=["gg"])
                gg2 = lambda t: t[:].rearrange("p n h -> p (n h)")
                S.op("pe", lambda e: e.matmul(psb[1][0:64, 0:256], TRI, gg2(gg), start=True, stop=True),
                     reads=["gg", "gcn"], writes=[("psb", 1)])
                S.op("dve", lambda e: e.tensor_copy(out=gg2(gc), in_=psb[1][0:64, 0:256]), reads=[("psb", 1)],
                     writes=["gc"])
                S.op("pe", lambda e: e.matmul(psb[2][0:64, 0:256], SEL, gg2(gc), start=True, stop=True),
                     reads=["gc", "gcn"], writes=[("psb", 2)])
                S.op("dve", lambda e: e.tensor_copy(out=gg2(gcl), in_=psb[2][0:64, 0:256]), reads=[("psb", 2)],
                     writes=["gcl"])
                S.op("act", lambda e: e.activation(out=eg[:], in_=gc[:], func=AF.Exp), reads=["gc"], writes=["eg"])
                S.op("act", lambda e: e.activation(out=egl[:], in_=gcl[:], func=AF.Exp), reads=["gcl"], writes=["egl"])
                S.op("dve", lambda e: e.tensor_tensor(out=ekd[:], in0=gcl[:], in1=gc[:], op=ALU.subtract),
                     reads=["gc", "gcl"], writes=["ekd"])
                S.op("act", lambda e: e.activation(out=ekd[:], in_=ekd[:], func=AF.Exp), reads=["ekd"], writes=["ekd"])
                S.op("dve", lambda e: e.tensor_tensor(out=nbeg[:], in0=nbeta[:], in1=eg[:], op=ALU.mult),
                     reads=["nbeta", "eg"], writes=["nbeg"])
                S.flush()

                f3 = lambda name: sb2(name, [64, 8, 64])
                b3 = lambda name: sb2(name, [64, 8, 64], BF16)
                kvns = [sb2("kvn%d" % i, [64, 1024], BF16) for i in range(2)]
                zss = [sb2("zs%d" % i, [64, 512]) for i in range(2)]
                Dg, Db, m1, e1, e2, b1, c1, Tf, vb, tt, ob, osq, Sf = [f3(n) for n in (
                    "Dg", "Db", "m1", "e1", "e2", "b1", "c1", "Tf", "vb", "tt", "ob", "osq", "Sf")]
                Bb = [b3("Bb0"), b3("Bb1")]
                Cb = [b3("Cb0"), b3("Cb1")]
                intraTs = [b3("intraT0"), b3("intraT1")]
                Tbs = [b3("Tb0"), b3("Tb1")]
                rn, vn, vns, Sb = [b3(n) for n in ("rn", "vn", "vns", "Sb")]
                go = sb2("go", [64, 512], BF16)
                ss = sb2("ss", [64, 8])
                S.op("pool", lambda e: e.memset(Sf[:], 0.0), writes=["Sf"])
                S.op("pool", lambda e: e.memset(Sb[:], 0.0), writes=["Sb"])
                ps3 = lambda i: v3(psb[i][0:64, :])
                p6b = psb[6][:, :].bitcast(BF16)

                def pre(n):
                    c0 = n * 64
                    xc0 = 128 + c0
                    q = n % 2
                    kvn, zs, intraT, Tb = kvns[q], zss[q], intraTs[q], Tbs[q]
                    KVN, ZS, INT, TB = ("kvn", q), ("zs", q), ("intraT", q), ("Tb", q)
                    for h in range(8):
                        S.op("pe", lambda e, h=h: e.transpose(pst[0:64, h * 64:(h + 1) * 64], kh[:, h, c0:c0 + 64],
                                                              identb[0:64, 0:64]),
                             reads=[("qk", 1, h), "identb"], writes=["pst"])
                    for pr in range(4):
                        S.op("pe", lambda e, pr=pr: e.transpose(pst[0:64, 512 + pr * 128:512 + (pr + 1) * 128],
                                                                vT[:, pr, c0:c0 + 64], identb[:, :]),
                             reads=[("vT", pr), "identb"], writes=["pst"])
                    S.op("dve", lambda e: e.tensor_copy(out=kvn[:], in_=pst[0:64, :]), reads=["pst"], writes=[KVN])
                    for kc in range(8):
                        S.op("pe", lambda e, kc=kc: e.matmul(psb[2][0:64, :], x1T[:, kc, xc0:xc0 + 64], wz[:, kc, :],
                                                             start=(kc == 0), stop=(kc == 7)),
                             reads=["wz"], writes=[("psb", 2)])
                    S.op("act", lambda e: e.activation(out=zs[:], in_=psb[2][0:64, :], func=AF.Silu), reads=[("psb", 2)],
                         writes=[ZS])
                    for h in range(8):
                        S.op("pe", lambda e, h=h: e.matmul(psb[0][0:64, h * 64:(h + 1) * 64], kh[:, h, c0:c0 + 64],
                                                           kh[:, h, c0:c0 + 64], start=True, stop=True),
                             reads=[("qk", 1, h)], writes=[("psb", 0)])
                    for h in range(8):
                        S.op("pe", lambda e, h=h: e.matmul(psb[1][0:64, h * 64:(h + 1) * 64], kh[:, h, c0:c0 + 64],
                                                           qh[:, h, c0:c0 + 64], start=True, stop=True),
                             reads=[("qk", 1, h), ("qk", 0, h)], writes=[("psb", 1)])
                    gcn_ = gc[:, n, :]
                    S.op("pool", lambda e: e.tensor_tensor(out=Dg[:], in0=bm(ID64), in1=bl(gcn_), op=ALU.mult),
                         reads=["gc", "gcn"], writes=["Dg"])
                    S.op("pool", lambda e: e.tensor_tensor(out=Db[:], in0=bm(ID64), in1=bl(beta[:, n, :]), op=ALU.mult),
                         reads=["beta", "gcn"], writes=["Db"])
                    S.op("pe", lambda e: e.matmul(psb[2][0:64, :], ONES, Dg[:].rearrange("p h s -> p (h s)"), start=True,
                                                  stop=True), reads=["Dg", "gcn"], writes=[("psb", 2)])
                    S.op("pe", lambda e: e.matmul(psb[3][0:64, :], ONES, Db[:].rearrange("p h s -> p (h s)"), start=True,
                                                  stop=True), reads=["Db", "gcn"], writes=[("psb", 3)])
                    S.op("pool", lambda e: e.tensor_tensor(out=m1[:], in0=bl(gcn_), in1=bm(NEGS), op=ALU.add),
                         reads=["gc", "gcn"], writes=["m1"])
                    S.op("dve", lambda e: e.scalar_tensor_tensor(out=e1[:], in0=ps3(2), scalar=-1.0, in1=m1[:], op0=ALU.mult,
                                                                 op1=ALU.add), reads=[("psb", 2), "m1"], writes=["e1"])
                    S.op("act", lambda e: e.activation(out=e1[:], in_=e1[:], func=AF.Exp), reads=["e1"], writes=["e1"])
                    S.op("pool", lambda e: e.tensor_tensor(out=m1[:], in0=bm(NEGT), in1=bl(gcn_), op=ALU.subtract),
                         reads=["gc", "gcn"], writes=["m1"])
                    S.op("dve", lambda e: e.tensor_tensor(out=e2[:], in0=ps3(2), in1=m1[:], op=ALU.add),
                         reads=[("psb", 2), "m1"], writes=["e2"])
                    S.op("act", lambda e: e.activation(out=e2[:], in_=e2[:], func=AF.Exp), reads=["e2"], writes=["e2"])
                    S.op("dve", lambda e: e.tensor_tensor(out=b1[:], in0=ps3(0), in1=e1[:], op=ALU.mult),
                         reads=[("psb", 0), "e1"], writes=["b1"])
                    S.op("pool", lambda e: e.tensor_tensor(out=Bb[0][:], in0=b1[:], in1=bl(nbeta[:, n, :]), op=ALU.mult),
                         reads=["b1", "nbeta"], writes=[("Bb", 0)])
                    S.op("dve", lambda e: e.tensor_tensor(out=c1[:], in0=ps3(0), in1=e2[:], op=ALU.mult),
                         reads=[("psb", 0), "e2"], writes=["c1"])
                    S.op("dve", lambda e: e.tensor_tensor(out=c1[:], in0=c1[:], in1=ps3(3), op=ALU.mult),
                         reads=[("psb", 3), "c1"], writes=["c1"])
                    S.op("pool", lambda e: e.tensor_tensor(out=c1[:], in0=c1[:], in1=bm(MSKT), op=ALU.mult),
                         reads=["c1", "gcn"], writes=["c1"])
                    S.op("act", lambda e: e.copy(out=Cb[0][:], in_=c1[:]), reads=["c1"], writes=[("Cb", 0)])
                    S.op("dve", lambda e: e.tensor_tensor(out=intraT[:], in0=ps3(1), in1=e2[:], op=ALU.mult),
                         reads=[("psb", 1), "e2"], writes=[INT])
                    S.op("pool", lambda e: e.tensor_tensor(out=Tf[:], in0=c1[:], in1=bm(ID64), op=ALU.add),
                         reads=["c1", "gcn"], writes=["Tf"])
                    S.op("act", lambda e: e.copy(out=Tb[:], in_=Tf[:]), reads=["Tf"], writes=[TB])
                    for k in range(1, 6):
                        cur, nxt = (k - 1) % 2, k % 2
                        for h in range(8):
                            S.op("pe", lambda e, h=h, cur=cur: e.matmul(psb[2][0:64, h * 64:(h + 1) * 64], Cb[cur][:, h, :],
                                                                        Bb[cur][:, h, :], start=True, stop=True),
                                 reads=[("Cb", cur), ("Bb", cur)], writes=[("psb", 2)])
                        if k < 5:
                            for h in range(8):
                                S.op("pe", lambda e, h=h, cur=cur: e.matmul(psb[3][0:64, h * 64:(h + 1) * 64], Bb[cur][:, h, :],
                                                                            Cb[cur][:, h, :], start=True, stop=True),
                                     reads=[("Cb", cur), ("Bb", cur)], writes=[("psb", 3)])
                        S.op("act", lambda e, nxt=nxt: e.activation(out=Bb[nxt][:], in_=ps3(2), func=AF.Copy),
                             reads=[("psb", 2)], writes=[("Bb", nxt)])
                        if k < 5:
                            S.op("dve", lambda e, nxt=nxt: e.tensor_copy(out=Cb[nxt][:], in_=ps3(3)),
                                 reads=[("psb", 3)], writes=[("Cb", nxt)])
                        for h in range(8):
                            S.op("pe", lambda e, h=h, nxt=nxt: e.matmul(psb[1][0:64, h * 64:(h + 1) * 64], Bb[nxt][:, h, :],
                                                                        Tb[:, h, :], start=True, stop=True),
                                 reads=[("Bb", nxt), TB], writes=[("psb", 1)])
                        S.op("dve", lambda e: e.tensor_tensor(out=Tb[:], in0=Tb[:], in1=ps3(1), op=ALU.add),
                             reads=[("psb", 1), TB], writes=[TB])

                def seq(n):
                    c0 = n * 64
                    q = n % 2
                    kvn, zs, intraT, Tb = kvns[q], zss[q], intraTs[q], Tbs[q]
                    KVN, ZS, INT, TB = ("kvn", q), ("zs", q), ("intraT", q), ("Tb", q)
                    S.op("pool", lambda e: e.tensor_tensor(out=vb[:], in0=v3(kvn[:, 512:1024]), in1=bl(beta[:, n, :]),
                                                           op=ALU.mult), reads=[KVN, "beta"], writes=["vb"])
                    for h in range(8):
                        S.op("pe", lambda e, h=h: e.matmul(psb[4][0:64, h * 64:(h + 1) * 64], kh[:, h, c0:c0 + 64],
                                                           Sb[:, h, :], start=True, stop=True),
                             reads=[("qk", 1, h), "Sb"], writes=[("psb", 4)])
                    S.op("dve", lambda e: e.tensor_tensor(out=tt[:], in0=ps3(4), in1=bl(nbeg[:, n, :]), op=ALU.mult),
                         reads=[("psb", 4), "nbeg"], writes=["tt"])
                    S.op("pool", lambda e: e.tensor_tensor(out=rn[:], in0=tt[:], in1=vb[:], op=ALU.add),
                         reads=["tt", "vb"], writes=["rn"])
                    for h in range(8):
                        S.op("pe", lambda e, h=h: e.matmul(psb[5][0:64, h * 64:(h + 1) * 64], Tb[:, h, :], rn[:, h, :],
                                                           start=True, stop=True),
                             reads=[TB, "rn"], writes=[("psb", 5)])
                    S.op("act", lambda e: e.activation(out=vn[:], in_=ps3(5), func=AF.Copy), reads=[("psb", 5)],
                         writes=["vn"])
                    S.op("pool", lambda e: e.tensor_tensor(out=vns[:], in0=vn[:], in1=bl(ekd[:, n, :]), op=ALU.mult),
                         reads=["vn", "ekd"], writes=["vns"])
                    for h in range(8):
                        S.op("pe", lambda e, h=h: e.matmul(psb[6][0:64, h * 64:(h + 1) * 64], qh[:, h, c0:c0 + 64],
                                                           Sb[:, h, :], start=True, stop=True),
                             reads=[("qk", 0, h), "Sb"], writes=[("psb", 6)])
                    for h in range(8):
                        S.op("pe", lambda e, h=h: e.matmul(psb[4][0:64, h * 64:(h + 1) * 64], intraT[:, h, :], vn[:, h, :],
                                                           start=True, stop=True),
                             reads=[INT, "vn"], writes=[("psb", 4)])
                    S.op("dve", lambda e: e.tensor_tensor(out=ob[:], in0=ps3(6), in1=bl(eg[:, n, :]), op=ALU.mult),
                         reads=[("psb", 6), "eg"], writes=["ob"])
                    S.op("dve", lambda e: e.tensor_tensor(out=ob[:], in0=ob[:], in1=ps3(4), op=ALU.add),
                         reads=[("psb", 4), "ob"], writes=["ob"])
                    for h in range(8):
                        S.op("pe", lambda e, h=h: e.matmul(psb[5][0:64, h * 64:(h + 1) * 64], kvn[:, h * 64:(h + 1) * 64],
                                                           vns[:, h, :], start=True, stop=True),
                             reads=[KVN, "vns"], writes=[("psb", 5)])
                    S.op("pool", lambda e: e.tensor_tensor(out=Sf[:], in0=Sf[:], in1=bl(egl[:, n, :]), op=ALU.mult),
                         reads=["Sf", "egl"], writes=["Sf"])
                    S.op("dve", lambda e: e.tensor_tensor(out=Sf[:], in0=Sf[:], in1=ps3(5), op=ALU.add),
                         reads=[("psb", 5), "Sf"], writes=["Sf"])
                    S.op("act", lambda e: e.copy(out=Sb[:], in_=Sf[:]), reads=["Sf"], writes=["Sb"])
                    S.op("pool", lambda e: e.tensor_tensor(out=osq[:], in0=ob[:], in1=ob[:], op=ALU.mult), reads=["ob"],
                         writes=["osq"])
                    S.op("dve", lambda e: e.tensor_reduce(out=ss[:], in_=osq[:], axis=AX.X, op=ALU.add), reads=["osq"],
                         writes=["ss"])
                    S.op("dve", lambda e: e.tensor_scalar(out=ss[:], in0=ss[:], scalar1=1.0 / 64.0, scalar2=1e-6,
                                                          op0=ALU.mult, op1=ALU.add), reads=["ss"], writes=["ss"])
                    S.op("act", lambda e: e.activation(out=ss[:], in_=ss[:], func=AF.Ln), reads=["ss"], writes=["ss"])
                    S.op("act", lambda e: e.activation(out=ss[:], in_=ss[:], func=AF.Exp, scale=-0.5), reads=["ss"],
                         writes=["ss"])
                    S.op("dve", lambda e: e.tensor_tensor(out=ob[:], in0=ob[:], in1=bl(ss[:, :]), op=ALU.mult),
                         reads=["ob", "ss"], writes=["ob"])
                    S.op("pool", lambda e: e.tensor_tensor(out=ob[:], in0=ob[:], in1=bm(nwr[:, :]), op=ALU.mult),
                         reads=["ob", "nwr"], writes=["ob"])
                    S.op("dve", lambda e: e.tensor_tensor(out=v3(go[:, :]), in0=ob[:], in1=v3(zs[:, :]), op=ALU.mult),
                         reads=["ob", ZS], writes=["go"])
                    for pr in range(4):
                        S.op("pe", lambda e, pr=pr: e.transpose(p6b[:, pr * 64:(pr + 1) * 64], go[:, pr * 128:(pr + 1) * 128],
                                                                identb[0:64, 0:64]),
                             reads=["go", "identb"], writes=[("psb", 6)])
                    S.op("dve", lambda e: e.tensor_copy(out=gdnT[:, :, c0:c0 + 64], in_=v3(p6b[:, 0:256], 4)),
                         reads=[("psb", 6)], writes=[("gdnT", n)])

                pre(0)
                for n in range(32):
                    sq_ = S.capture(lambda: seq(n))
                    pr_ = S.capture(lambda: pre(n + 1)) if n + 1 < 32 else []
                    S.emit_merged(pr_, sq_)
                S.dma("sp", lambda e: e.dma_start(out=o_sgp.rearrange("h d e -> d h e"), in_=Sf[:]), reads=["Sf"],
                      writes=["o_sgp"])
                if dbg:
                    S.dma("pool", lambda e: e.dma_start(out=gdn_dbg[:, :, :], in_=gdnT[:]),
                          reads=[("gdnT", n) for n in range(32)], writes=["gdn_dbg"])
                S.flush()

        def sample_proj_stage():
            with contextlib.ExitStack() as st2:
                sb2 = lambda name, shape, dt=F32: st2.enter_context(nc.sbuf_tensor("sp" + name, list(shape), dt))
                wb = [sb2("wb%d" % i, [128, 8, 512], BF16) for i in range(2)]
                ot = [sb2("ot%d" % i, [128, 512]) for i in range(2)]
                winv = w_in.rearrange("(kc p) f -> p kc f", p=128)
                blocks = [(0, 512, qs_s[:, :], 0.125), (1536, 512, gq_s[:, 0:512], 1.0), (2048, 512, gq_s[:, 512:1024], 1.0),
                          (2560, 512, gq_s[:, 1024:1536], 1.0), (3072, 512, z_s[:, :], 1.0), (3584, 16, ba_s[:, :], 1.0)]
                for i, (c0, w, dst, sc) in enumerate(blocks):
                    b = i % 2
                    S.dma("pool", lambda e, b=b, c0=c0, w=w: e.dma_start(out=wb[b][:, :, 0:w], in_=winv[:, :, c0:c0 + w]),
                          writes=[("wb", b)])
                    for kc in range(8):
                        S.op("pe", lambda e, b=b, kc=kc, w=w: e.matmul(psb[b][:, 0:w], x1T[:, kc, 0:128], wb[b][:, kc, 0:w],
                                                                       start=(kc == 0), stop=(kc == 7)),
                             reads=[("wb", b)], writes=[("psb", b)])
                    S.op("act", lambda e, b=b, w=w, sc=sc: e.mul(out=ot[b][:, 0:w], in_=psb[b][:, 0:w], mul=sc),
                         reads=[("psb", b)], writes=[("ot", b)])
                    S.dma("sp", lambda e, b=b, w=w, dst=dst: e.dma_start(out=dst, in_=ot[b][0:64, 0:w]),
                          reads=[("ot", b)], writes=[("sscr", i)])
                S.dma("sp", lambda e: e.dma_start(
                    out=o_scs[:, :, :], in_=bass.AP(gq_s.tensor, 1536, [[4 * 1536, 16], [1536, 3], [1, 1536]])),
                      reads=[("sscr", 1), ("sscr", 2), ("sscr", 3)], writes=["o_scs"])
                S.flush()

        def sample_attn_stage():
            with contextlib.ExitStack() as st2:
                sb2 = lambda name, shape, dt=F32: st2.enter_context(nc.sbuf_tensor("sa" + name, list(shape), dt))
                kt = [sb2("kt%d" % i, [128, 8, 512]) for i in range(2)]
                vt = [sb2("vt%d" % i, [128, 8, 512]) for i in range(2)]
                vtb = [sb2("vtb%d" % i, [128, 8, 512], BF16) for i in range(2)]
                ktT = [sb2("ktT%d" % i, [128, 512], BF16) for i in range(2)]
                qsT = sb2("qsT", [128, 4, 64])
                qblk = sb2("qblk", [128, 4, 16, 8], BF16)
                wqb = sb2("wqb", [128, 8, 512], BF16)
                Pm = sb2("Pm", [128, 8, 32])
                Pmb = sb2("Pmb", [128, 8, 32], BF16)
                Pj = sb2("Pj", [128, 32])
                mtab = sb2("mtab", [128, 8, 32])
                wcs = sb2("wcs", [32, 8, 4, 128])
                eb = sb2("eb", [32, 8])
                ones1 = sb2("ones1", [128, 1])
                dmk = sb2("dmk", [32, 8])
                osel = sb2("osel", [32, 8, 64])
                osb = sb2("osb", [32, 64])
                rs = sb2("rs", [32, 1])
                S.dma("sp", lambda e: e.dma_start(out=wcs[:], in_=wcm[:, :, :, :]), writes=["wcs"])
                S.dma("sp", lambda e: e.dma_start(out=eb[:], in_=relb[:, :]), writes=["eb"])
                S.dma("sp", lambda e: e.dma_start(out=dmk[:], in_=dmask[:, :]), writes=["dmk"])
                S.op("act", lambda e: e.activation(out=eb[:], in_=eb[:], func=AF.Exp), reads=["eb"], writes=["eb"])
                S.op("pool", lambda e: e.memset(ones1[:], 1.0), writes=["ones1"])
                S.op("pool", lambda e: e.memset(qblk[:], 0.0), writes=["qblk0"])
                S.dma("pool", lambda e: e.dma_start(out=wqb[:], in_=w_in.rearrange("(kc p) f -> p kc f", p=128)[:, :, 0:512]),
                      writes=["wqb"])
                for pr in range(4):
                    for kc in range(8):
                        S.op("pe", lambda e, pr=pr, kc=kc: e.matmul(psb[6][:, pr * 64:(pr + 1) * 64],
                                                                    wqb[:, kc, pr * 128:(pr + 1) * 128], x1T[:, kc, 0:64],
                                                                    start=(kc == 0), stop=(kc == 7)),
                             reads=["wqb"], writes=[("psb", 6)])
                S.op("act", lambda e: e.mul(out=qsT[:].rearrange("p a b -> p (a b)"), in_=psb[6][:, 0:256], mul=0.125),
                     reads=[("psb", 6)], writes=["qsT"])
                qv = qsT[:].rearrange("p a (b t) -> p a b t", t=4)
                S.op("dve", lambda e: e.tensor_copy(out=qblk[0:64, :, :, 0:4], in_=qv[0:64]), reads=["qsT", "qblk0"],
                     writes=["qblkA"])
                S.op("dve", lambda e: e.tensor_copy(out=qblk[64:128, :, :, 4:8], in_=qv[64:128]), reads=["qsT", "qblk0"],
                     writes=["qblkB"])
                for i in range(2):
                    S.op("pool", lambda e, i=i: e.memset(kt[i][:, 7, :], 0.0), writes=[("kt7", i)])
                    S.op("pool", lambda e, i=i: e.memset(vt[i][:, 7, :], 0.0), writes=[("vt7", i)])
                for j in range(8):
                    for t in range(4):
                        S.op("pe", lambda e, j=j, t=t: e.matmul(psb[0][:, (j * 4 + t) * 8:(j * 4 + t + 1) * 8], wcs[:, j, t, :],
                                                                eb[:, :], start=True, stop=True),
                             reads=["wcs", "eb"], writes=[("psb", 0)])
                S.op("dve", lambda e: e.tensor_copy(out=mtab[:].rearrange("p j (h t) -> p j h t", h=8),
                                                    in_=psb[0][:, 0:256].rearrange("p (j t h) -> p j h t", j=8, t=4)),
                     reads=[("psb", 0)], writes=["mtab"])
                for b in range(16):
                    bb = b % 2
                    for src, dstt, onew, nm in ((ck, kt[bb], o_ks, "kt"), (cv, vt[bb], o_vs, "vt")):
                        S.dma("sp", lambda e, src=src, dstt=dstt, b=b: e.dma_start(
                            out=dstt[:, 0:4, :], in_=src[b, 1536:2048, :].rearrange("(j p) e -> p j e", p=128)),
                              writes=[(nm, bb, 0)])
                        for r in range(4):
                            S.dma("act" if r % 2 else "sp", lambda e, src=src, dstt=dstt, b=b, r=r: e.dma_start(
                                out=dstt[r:128:4, 4:7, :],
                                in_=bass.AP(src.tensor, b * 2048 * 512 + r * 512, [[16 * 512, 32], [32 * 16 * 512, 3], [1, 512]])),
                                  writes=[(nm, bb, 1 + r)])
                        S.dma("act", lambda e, dstt=dstt, onew=onew, b=b: e.dma_start(
                            out=dstt[0:4, 7, :], in_=onew[b * 4:(b + 1) * 4, :]),
                              reads=[("okv", 1, 0), ("okv", 2, 0), (nm + "7", bb)], writes=[(nm, bb, 5)])
                    pL = 5 + b % 2
                    for j in range(8):
                        pk = 3 + (j % 2)
                        kb = j % 2
                        for pr in range(4):
                            S.op("pe", lambda e, pk=pk, j=j, pr=pr, bb=bb: e.transpose(
                                psb[pk][:, pr * 128:(pr + 1) * 128], kt[bb][:, j, pr * 128:(pr + 1) * 128], identf[:, :]),
                                 reads=[("kt", bb, i_) for i_ in range(6)] + ["identf"], writes=[("psb", pk)])
                        if j % 2:
                            S.op("act", lambda e, pk=pk, kb=kb: e.activation(out=ktT[kb][:], in_=psb[pk][:, :], func=AF.Copy),
                                 reads=[("psb", pk)], writes=[("ktT", kb)])
                        else:
                            S.op("dve", lambda e, pk=pk, kb=kb: e.tensor_copy(out=ktT[kb][:], in_=psb[pk][:, :]),
                                 reads=[("psb", pk)], writes=[("ktT", kb)])
                        if j % 2:
                            S.op("pool", lambda e, j=j, bb=bb: e.tensor_copy(out=vtb[bb][:, j, :], in_=vt[bb][:, j, :]),
                                 reads=[("vt", bb, i_) for i_ in range(6)], writes=[("vtb", bb, j)])
                        else:
                            S.op("act", lambda e, j=j, bb=bb: e.copy(out=vtb[bb][:, j, :], in_=vt[bb][:, j, :]),
                                 reads=[("vt", bb, i_) for i_ in range(6)], writes=[("vtb", bb, j)])
                        for pr in range(4):
                            S.op("pe", lambda e, j=j, pr=pr, kb=kb, b=b, pL=pL: e.matmul(
                                psb[pL][:, (j * 8 + 2 * pr) * 4:(j * 8 + 2 * pr) * 4 + 8], ktT[kb][:, pr * 128:(pr + 1) * 128],
                                qblk[:, pr, b, :], start=True, stop=True),
                                 reads=[("ktT", kb), "qblkA", "qblkB", "qblk0"], writes=[("psb", pL)])
                    S.op("act", lambda e, pL=pL: e.activation(out=Pm[:].rearrange("p j x -> p (j x)"), in_=psb[pL][:, 0:256],
                                                             func=AF.Exp), reads=[("psb", pL)], writes=["Pm"])
                    S.op("dve", lambda e: e.tensor_tensor(out=Pmb[:], in0=Pm[:], in1=mtab[:], op=ALU.mult),
                         reads=["Pm", "mtab"], writes=["Pmb"])
                    S.op("dve", lambda e: e.tensor_reduce(out=Pj[:], in_=Pmb[:].rearrange("p j x -> p x j"), axis=AX.X,
                                                          op=ALU.add), reads=["Pmb"], writes=["Pj"])
                    for j in range(8):
                        S.op("pe", lambda e, j=j, bb=bb: e.matmul(
                            psb[1][0:32, :], Pmb[:, j, :], vtb[bb][:, j, :], start=(j == 0), stop=(j == 7)),
                             reads=["Pmb", ("vtb", bb, j)], writes=[("psb", 1)])
                    S.op("pe", lambda e: e.matmul(psb[2][0:32, 0:1], Pj[:, :], ones1[:, :], start=True, stop=True),
                         reads=["Pj", "ones1"], writes=[("psb", 2)])
                    S.op("dve", lambda e: e.reciprocal(out=rs[:], in_=psb[2][0:32, 0:1]), reads=[("psb", 2)], writes=["rs"])
                    S.op("dve", lambda e: e.tensor_tensor(out=osel[:], in0=v3(psb[1][0:32, :]), in1=bl(dmk[:, :]),
                                                          op=ALU.mult), reads=[("psb", 1), "dmk"], writes=["osel"])
                    S.op("dve", lambda e: e.tensor_reduce(out=osb[:], in_=osel[:].rearrange("p h e -> p e h"), axis=AX.X,
                                                          op=ALU.add), reads=["osel"], writes=["osb"])
                    S.op("dve", lambda e: e.tensor_scalar(out=osb[:], in0=osb[:], scalar1=rs[:, 0:1], scalar2=None,
                                                          op0=ALU.mult), reads=["osb", "rs"], writes=["osb"])
                    S.dma("pool", lambda e, b=b: e.dma_start(
                        out=bass.AP(heads_s.tensor, b * 4 * D, [[64, 8], [D, 4], [1, 64]]), in_=osb[:, :]),
                          reads=["osb"], writes=[("heads_a", b)])
                S.flush()

        def sample_gdn_stage():
            with contextlib.ExitStack() as st2:
                sb2 = lambda name, shape, dt=F32: st2.enter_context(nc.sbuf_tensor("sg" + name, list(shape), dt))
                Sx = sb2("S", [128, 64, 64])
                tmp = sb2("tmp", [128, 64, 64])
                xq = sb2("xq", [128, 3, 7, 64])
                zz = sb2("zz", [128, 4, 64])
                bav = sb2("bav", [128, 4, 2])
                cw = sb2("cw", [128, 3, 4, 64])
                gv = sb2("gv", [128, 2])
                nwr = sb2("nwr", [128, 64])
                cq = sb2("cq", [128, 3, 4, 64])
                ct = sb2("ct", [128, 4, 64])
                ssq = sb2("ssq", [128, 3, 4])
                beta = sb2("beta", [128, 4])
                gg = sb2("gg", [128, 4])
                eg = sb2("eg", [128, 4])
                neg = sb2("neg", [128, 4])
                ks = sb2("ks", [128, 64])
                dl = sb2("dl", [128, 64])
                oo = sb2("oo", [128, 4, 64])
                one = sb2("one", [128, 1])
                S.op("pool", lambda e: e.memset(one[:], 1.0), writes=["one"])
                S.dma("sp", lambda e: e.dma_start(out=Sx[:].rearrange("p a b -> p (a b)"), in_=sg_in[:, :]), writes=["S"])
                S.dma("sp", lambda e: e.dma_start(out=cw[:], in_=cws[:, :, :, :]), writes=["cw"])
                S.dma("sp", lambda e: e.dma_start(out=gv[:], in_=gvs[:, :]), writes=["gv"])
                S.dma("sp", lambda e: e.dma_start(out=nwr[:], in_=gvec[2:3, :].partition_broadcast(128)), writes=["nwr"])
                for b in range(16):
                    p0 = b * 8
                    for sec in range(3):
                        q = "act" if sec == 1 else "sp"
                        S.dma(q, lambda e, b=b, p0=p0, sec=sec: e.dma_start(
                            out=xq[p0:p0 + 8, sec, 0:3, :],
                            in_=bass.AP(sc_in.tensor, b * 3 * 1536 + sec * 512, [[64, 8], [1536, 3], [1, 64]])),
                              writes=[("xq", b, sec, 0)])
                        S.dma(q, lambda e, b=b, p0=p0, sec=sec: e.dma_start(
                            out=xq[p0:p0 + 8, sec, 3:7, :],
                            in_=bass.AP(gq_s.tensor, b * 4 * 1536 + sec * 512, [[64, 8], [1536, 4], [1, 64]])),
                              reads=[("sscr", 1 + sec)], writes=[("xq", b, sec, 1)])
                    S.dma("act", lambda e, b=b, p0=p0: e.dma_start(
                        out=zz[p0:p0 + 8, :, :], in_=bass.AP(z_s.tensor, b * 4 * 512, [[64, 8], [512, 4], [1, 64]])),
                          reads=[("sscr", 4)], writes=[("zz", b)])
                    S.dma("act", lambda e, b=b, p0=p0: e.dma_start(
                        out=bav[p0:p0 + 8, :, :], in_=bass.AP(ba_s.tensor, b * 4 * 16, [[1, 8], [16, 4], [8, 2]]),
                        allow_slow_non_contiguous=True), reads=[("sscr", 5)], writes=[("bav", b)])
                for sec in range(3):
                    for j in range(4):
                        wv = cw[:, sec, j, :].unsqueeze(1).broadcast_to([128, 4, 64])
                        if j == 0:
                            S.op("dve", lambda e, sec=sec, wv=wv: e.tensor_tensor(
                                out=cq[:, sec, :, :], in0=xq[:, sec, 0:4, :], in1=wv, op=ALU.mult),
                                 reads=[("xq", b_, s_, i_) for b_ in range(16) for s_ in range(3) for i_ in range(2)] + ["cw"], writes=["cq"])
                        else:
                            S.op("pool", lambda e, sec=sec, wv=wv, j=j: e.tensor_tensor(
                                out=ct[:], in0=xq[:, sec, j:j + 4, :], in1=wv, op=ALU.mult),
                                 reads=[("xq", b_, s_, i_) for b_ in range(16) for s_ in range(3) for i_ in range(2)] + ["cw"], writes=["ct"])
                            S.op("dve", lambda e, sec=sec: e.tensor_tensor(
                                out=cq[:, sec, :, :], in0=cq[:, sec, :, :], in1=ct[:], op=ALU.add),
                                 reads=["cq", "ct"], writes=["cq"])
                S.op("act", lambda e: e.activation(out=cq[:], in_=cq[:], func=AF.Silu), reads=["cq"], writes=["cq"])
                S.op("act", lambda e: e.activation(out=zz[:], in_=zz[:], func=AF.Silu), reads=[("zz", b_) for b_ in range(16)], writes=["zz"])
                for sec in range(2):
                    S.op("pool", lambda e, sec=sec: e.tensor_tensor(out=ct[:], in0=cq[:, sec, :, :], in1=cq[:, sec, :, :],
                                                                    op=ALU.mult), reads=["cq"], writes=["ct"])
                    S.op("dve", lambda e, sec=sec: e.tensor_reduce(out=ssq[:, sec, :], in_=ct[:], axis=AX.X, op=ALU.add),
                         reads=["ct"], writes=["ssq"])
                    S.op("dve", lambda e, sec=sec: e.tensor_scalar(out=ssq[:, sec, :], in0=ssq[:, sec, :], scalar1=1e-6,
                                                                   scalar2=None, op0=ALU.add), reads=["ssq"], writes=["ssq"])
                    S.op("act", lambda e, sec=sec: e.sqrt(out=ssq[:, sec, :], in_=ssq[:, sec, :]), reads=["ssq"],
                         writes=["ssq"])
                    S.op("dve", lambda e, sec=sec: e.reciprocal(out=ssq[:, sec, :], in_=ssq[:, sec, :]), reads=["ssq"],
                         writes=["ssq"])
                    sc = 0.125 if sec == 0 else 1.0
                    S.op("dve", lambda e, sec=sec, sc=sc: e.scalar_tensor_tensor(
                        out=cq[:, sec, :, :], in0=cq[:, sec, :, :], scalar=sc, in1=bl(ssq[:, sec, :]), op0=ALU.mult,
                        op1=ALU.mult), reads=["cq", "ssq"], writes=["cq"])
                S.op("act", lambda e: e.activation(out=beta[:], in_=bav[:, :, 0], func=AF.Sigmoid), reads=[("bav", b_) for b_ in range(16)],
                     writes=["beta"])
                S.op("dve", lambda e: e.tensor_scalar(out=gg[:], in0=bav[:, :, 1], scalar1=gv[:, 1:2], scalar2=None,
                                                      op0=ALU.add), reads=[("bav", b_) for b_ in range(16)] + ["gv"], writes=["gg"])
                S.op("act", lambda e: e.activation(out=gg[:], in_=gg[:], func=AF.Exp), reads=["gg"], writes=["gg"])
                S.op("act", lambda e: e.activation(out=gg[:], in_=gg[:], func=AF.Ln, bias=one[:, 0:1]), reads=["gg", "one"],
                     writes=["gg"])
                S.op("act", lambda e: e.activation(out=gv[:, 0:1], in_=gv[:, 0:1], func=AF.Exp), reads=["gv"], writes=["gv"])
                S.op("dve", lambda e: e.tensor_scalar(out=gg[:], in0=gg[:], scalar1=gv[:, 0:1], scalar2=-1.0, op0=ALU.mult,
                                                      op1=ALU.mult), reads=["gg", "gv"], writes=["gg"])
                S.op("act", lambda e: e.activation(out=eg[:], in_=gg[:], func=AF.Exp), reads=["gg"], writes=["eg"])
                S.op("dve", lambda e: e.tensor_scalar(out=neg[:], in0=eg[:], scalar1=-1.0, scalar2=None, op0=ALU.mult),
                     reads=["eg"], writes=["neg"])
                ST = Sx[:].rearrange("p a b -> p b a")
                for t in range(4):
                    qv, kv, vv = cq[:, 0, t, :], cq[:, 1, t, :], cq[:, 2, t, :]
                    S.op("dve", lambda e, kv=kv: e.tensor_tensor(out=tmp[:], in0=ST, in1=bm(kv, 64), op=ALU.mult),
                         reads=["S", "cq"], writes=["tmp"])
                    S.op("dve", lambda e: e.tensor_reduce(out=ks[:], in_=tmp[:], axis=AX.X, op=ALU.add), reads=["tmp"],
                         writes=["ks"])
                    S.op("dve", lambda e, t=t, vv=vv: e.scalar_tensor_tensor(
                        out=dl[:], in0=ks[:], scalar=neg[:, t:t + 1], in1=vv, op0=ALU.mult, op1=ALU.add),
                         reads=["ks", "neg", "cq"], writes=["dl"])
                    S.op("dve", lambda e, t=t: e.tensor_scalar(out=dl[:], in0=dl[:], scalar1=beta[:, t:t + 1], scalar2=None,
                                                               op0=ALU.mult), reads=["dl", "beta"], writes=["dl"])
                    S.op("pool", lambda e, kv=kv: e.tensor_tensor(out=tmp[:], in0=bl(kv, 64), in1=bm(dl[:, :], 64),
                                                                  op=ALU.mult), reads=["cq", "dl"], writes=["tmp"])
                    S.op("dve", lambda e, t=t: e.scalar_tensor_tensor(
                        out=Sx[:], in0=Sx[:], scalar=eg[:, t:t + 1], in1=tmp[:], op0=ALU.mult, op1=ALU.add),
                         reads=["S", "eg", "tmp"], writes=["S"])
                    S.op("pool", lambda e, qv=qv: e.tensor_tensor(out=tmp[:], in0=ST, in1=bm(qv, 64), op=ALU.mult),
                         reads=["S", "cq"], writes=["tmp"])
                    S.op("dve", lambda e, t=t: e.tensor_reduce(out=oo[:, t, :], in_=tmp[:], axis=AX.X, op=ALU.add),
                         reads=["tmp"], writes=["oo"])
                S.dma("sp", lambda e: e.dma_start(out=o_sgs[:, :], in_=Sx[:].rearrange("p a b -> p (a b)")), reads=["S"],
                      writes=["o_sgs"])
                S.op("pool", lambda e: e.tensor_tensor(out=ct[:], in0=oo[:], in1=oo[:], op=ALU.mult), reads=["oo"],
                     writes=["ct"])
                S.op("dve", lambda e: e.tensor_reduce(out=ssq[:, 2, :], in_=ct[:], axis=AX.X, op=ALU.add), reads=["ct"],
                     writes=["ssq"])
                S.op("dve", lambda e: e.tensor_scalar(out=ssq[:, 2, :], in0=ssq[:, 2, :], scalar1=1.0 / 64.0, scalar2=1e-6,
                                                      op0=ALU.mult, op1=ALU.add), reads=["ssq"], writes=["ssq"])
                S.op("act", lambda e: e.sqrt(out=ssq[:, 2, :], in_=ssq[:, 2, :]), reads=["ssq"], writes=["ssq"])
                S.op("dve", lambda e: e.reciprocal(out=ssq[:, 2, :], in_=ssq[:, 2, :]), reads=["ssq"], writes=["ssq"])
                S.op("dve", lambda e: e.tensor_tensor(out=oo[:], in0=oo[:], in1=bl(ssq[:, 2, :]), op=ALU.mult),
                     reads=["oo", "ssq"], writes=["oo"])
                S.op("pool", lambda e: e.tensor_tensor(out=oo[:], in0=oo[:], in1=bm(nwr[:, :], 4), op=ALU.mult),
                     reads=["oo", "nwr"], writes=["oo"])
                S.op("dve", lambda e: e.tensor_tensor(out=oo[:], in0=oo[:], in1=zz[:], op=ALU.mult), reads=["oo", "zz"],
                     writes=["oo"])
                if dbg:
                    dC = dscr("dbg_cq", [128, 3, 4, 64]); dG = dscr("dbg_gates", [128, 3, 4])
                    S.dma("sp", lambda e: e.dma_start(out=dC[:, :, :, :], in_=cq[:]), reads=["cq"], writes=["dC"])
                    S.dma("sp", lambda e: e.dma_start(out=dG[:, 0, :], in_=beta[:]), reads=["beta"], writes=["dG0"])
                    S.dma("sp", lambda e: e.dma_start(out=dG[:, 1, :], in_=gg[:]), reads=["gg"], writes=["dG1"])
                    S.dma("sp", lambda e: e.dma_start(out=dG[:, 2, :], in_=eg[:]), reads=["eg"], writes=["dG2"])
                for b in range(16):
                    S.dma("sp", lambda e, b=b: e.dma_start(
                        out=bass.AP(heads_s.tensor, b * 4 * D + 512, [[64, 8], [D, 4], [1, 64]]), in_=oo[b * 8:(b + 1) * 8, :, :]),
                          reads=["oo"], writes=[("heads_g", b)])
                S.flush()

        def wout_stage():
            with contextlib.ExitStack() as st2:
                sb2 = lambda name, shape, dt=F32: st2.enter_context(nc.sbuf_tensor("wo" + name, list(shape), dt))
                attnT = sb2("attnT", [64, 8, 2048], BF16)
                woa = sb2("woa", [64, 8, D], BF16)
                wog = sb2("wog", [128, 4, D], BF16)
                won = sb2("won", [128, 8, D], BF16)
                hsf = sb2("hsf", [128, D])
                hsb = sb2("hsb", [128, D], BF16)
                hsT = sb2("hsT", [128, 8, 128], BF16)
                xs = [sb2("xs%d" % i, [128, D]) for i in range(2)]
                rr = [sb2("rr%d" % i, [128, D]) for i in range(2)]
                lnrep = sb2("ln", [128, 2, D])
                stt = sb2("st", [128, 2, 6])
                mv = sb2("mv", [128, 2])
                rstd = sb2("rstd", [128, 1])
                for i in range(2):
                    S.dma("sp", lambda e, i=i: e.dma_start(out=lnrep[:, i, :], in_=lnp[2 + i:3 + i, :].partition_broadcast(128)),
                          writes=[("lnrep", i)])
                S.dma("sp", lambda e: e.dma_start(out=attnT[:], in_=attn_s.rearrange("h d t -> d h t")),
                      reads=[("attn_s", h) for h in range(8)], writes=["attnT"])
                S.dma("pool", lambda e: e.dma_start(out=woa[:], in_=w_out[0:512, :].rearrange("(h d) o -> d h o", d=64)),
                      writes=["woa"])
                S.dma("pool", lambda e: e.dma_start(out=wog[:], in_=w_out[512:1024, :].rearrange("(c p) o -> p c o", p=128)),
                      writes=["wog"])
                S.dma("pool", lambda e: e.dma_start(out=won[:], in_=w_out.rearrange("(c p) o -> p c o", p=128)),
                      writes=["won"])
                S.op("pool", lambda e: e.memset(hsf[:], 0.0), writes=["hsf0"])
                S.dma("sp", lambda e: e.dma_start(out=hsf[0:64, :], in_=heads_s[:, :]),
                      reads=[("heads_a", b) for b in range(16)] + [("heads_g", b) for b in range(16)] + ["hsf0"],
                      writes=["hsf"])
                S.op("act", lambda e: e.copy(out=hsb[:], in_=hsf[:]), reads=["hsf"], writes=["hsb"])
                to_featmajor(hsb, "hsb", hsT, 0, "hsT")
                for t in range(NT):
                    b = t % 2
                    S.dma("sp", lambda e, b=b, t=t: e.dma_start(out=xs[b][:], in_=x1s[t * 128:(t + 1) * 128, :]),
                          reads=[("x1s", t)], writes=[("xs", b)])
                    for half in range(2):
                        pd = psb[4 + half]
                        hs_ = slice(half * 512, (half + 1) * 512)
                        if t == 0:
                            for kc in range(8):
                                S.op("pe", lambda e, pd=pd, kc=kc, hs_=hs_: e.matmul(
                                    pd[:, :], hsT[:, kc, :], won[:, kc, hs_], start=(kc == 0), stop=(kc == 7)),
                                     reads=["hsT", "won"], writes=[("psb", 4 + half)])
                        else:
                            ts_ = slice((t - 1) * 128, t * 128)
                            for h in range(8):
                                S.op("pe", lambda e, pd=pd, h=h, hs_=hs_, ts_=ts_: e.matmul(
                                    pd[:, :], attnT[:, h, ts_], woa[:, h, hs_], start=(h == 0), stop=False),
                                     reads=["attnT", "woa"], writes=[("psb", 4 + half)])
                            for c in range(4):
                                S.op("pe", lambda e, pd=pd, c=c, hs_=hs_, ts_=ts_: e.matmul(
                                    pd[:, :], gdnT[:, c, ts_], wog[:, c, hs_], start=False, stop=(c == 3)),
                                     reads=[("gdnT", n) for n in range(32)] + ["wog"], writes=[("psb", 4 + half)])
                        S.op("dve", lambda e, pd=pd, b=b, hs_=hs_: e.scalar_tensor_tensor(
                            out=rr[b][:, hs_], in0=xs[b][:, hs_], scalar=ALPHA, in1=pd[:, :], op0=ALU.mult, op1=ALU.add),
                             reads=[("xs", b), ("psb", 4 + half)], writes=[("rr", b)])
                    layernorm(rr[b], ("rr", b), lnrep, rr[b], ("rr", b), (stt, mv, rstd), epsmul=1.0)
                    S.dma("pool", lambda e, b=b, t=t: e.dma_start(out=x2s[t * 128:(t + 1) * 128, :], in_=rr[b][:]),
                          reads=[("rr", b)], writes=[("x2s", t)])
                S.flush()

        if "ffn1" not in stages:
            x1Tin = din("x1Tin", [128, 8, NTOK])
            S.dma("pool", lambda e: e.dma_start(out=x1T[:], in_=x1Tin[:, :, :]), writes=["x1Tinit"])
            S.flush()
        if "ffn1" in stages:
            def store1(t, tile, key):
                S.dma("pool", lambda e: e.dma_start(out=x1s[t * 128:(t + 1) * 128, :], in_=tile[:]), reads=[key],
                      writes=[("x1s", t)])
            ffn("f1", lambda t: xin[t * 128:(t + 1) * 128, :], f1g, f1u, f1d, 0, store1, x1T)

        gdnT = stp.enter_context(nc.sbuf_tensor("gdnT", [128, 4, 2048], BF16))
        if "attn" in stages:
            attention_stage()
        if "sproj" in stages:
            sample_proj_stage()
        if "sattn" in stages:
            sample_attn_stage()
        if "gdn" in stages:
            gdn_prompt_stage()
        if "sgdn" in stages:
            sample_gdn_stage()
        if "wout" in stages:
            wout_stage()
        S.flush()
        stp.close()
        if "ffn2" in stages:
            def store2(t, tile, key):
                if t == 0:
                    S.dma("pool", lambda e: e.dma_start(out=o_ys[:, :], in_=tile[0:64, :]), reads=[key], writes=[("oy", t)])
                else:
                    S.dma("pool", lambda e: e.dma_start(out=o_yp[(t - 1) * 128:t * 128, :], in_=tile[:]), reads=[key],
                          writes=[("oy", t)])
            ffn("f2", lambda t: x2s[t * 128:(t + 1) * 128, :], f2g, f2u, f2d, 4, store2, None)
        S.flush()
    return nc


def used_inputs(nc):
    names = set()
    for a in nc.allocations:
        try:
            if a.kind == "ExternalInput":
                names.add(a.name)
        except Exception:
            pass
    return names


def _t5_bucket(n):
    n = np.asarray(n, np.int64)
    nf = np.maximum(n, 1).astype(np.float64)
    large = 16 + np.floor(np.log(nf / 16.0) / math.log(2048 / 16.0) * 16.0 + 1e-9).astype(np.int64)
    large = np.minimum(large, 31)
    return np.where(n < 16, n, large)


def _onehot():
    oh = np.zeros((3, 33, 384), np.float32)
    for p, d in enumerate((1, 4, 16)):
        for m in range(383):
            dl = m - 127
            b = int(_t5_bucket(dl * d)) if 0 <= dl <= 128 else 32
            oh[p, b, m] = 1.0
    return oh


def _gconst():
    i = np.arange(64)
    P, Fr = i[:, None], i[None, :]
    g = np.zeros((64, 7, 64), np.float32)
    g[:, 0] = np.where(Fr < P, 0.0, NEG)
    g[:, 1] = np.where(Fr >= P, 0.0, NEG)
    g[:, 2] = np.where(Fr > P, -1.0, 0.0)
    g[:, 3] = np.eye(64)
    g[:, 4] = 1.0
    g[:, 5] = np.where(P <= Fr, 1.0, 0.0)
    g[:, 6] = np.where(P == 63, 1.0, 0.0)
    return g


def _gvec(inp):
    g = np.zeros((3, 64), np.float32)
    g[0, 0:8] = inp["gdn_a_log"][0]
    g[1, 0:8] = inp["gdn_dt_bias"][0]
    g[2, :] = inp["gdn_norm_w"][0]
    return g


def _wcm():
    w = np.zeros((32, 8, 4, 128), np.float32)
    for j in range(8):
        for p in range(128):
            if j < 4:
                pos = 1536 + j * 128 + p
            elif j < 7:
                pos = 16 * ((j - 4) * 32 + p // 4) + p % 4
            elif p < 4:
                pos = 2048 + p
            else:
                continue
            for t in range(4):
                dist = 2048 + t - pos
                if dist < 0:
                    continue
                for (win, d) in ((128, 1), (512, 4), (2048, 16)):
                    if dist % d == 0 and dist <= win:
                        w[int(_t5_bucket(dist)), j, t, p] += 1.0
    return w


def core_inputs(inp, c, big=None):
    f = np.float32
    xin = np.zeros((NTOK, D), f)
    xin[0:64] = inp["x_sample"][16 * c:16 * c + 16].reshape(64, D)
    xin[128:] = inp["x_prompt"][c]
    lnp = np.stack([inp["ln1_g"][0], inp["ln1_b"][0], inp["ln2_g"][0], inp["ln2_b"][0],
                    inp["ln3_g"][0], inp["ln3_b"][0]]).astype(f)
    m = {
        "xin": xin,
        "f1g": np.ascontiguousarray(inp["ffn1_w_gate"][0]), "f1u": np.ascontiguousarray(inp["ffn1_w_up"][0]),
        "f1d": np.ascontiguousarray(inp["ffn1_w_down"][0]),
        "lnp": lnp, "ident": np.eye(128, dtype=f),
        "w_in": np.ascontiguousarray(inp["w_in"][0]), "relb": np.ascontiguousarray(inp["rel_bias"]),
        "onehot": _onehot(), "antiid": np.ascontiguousarray(np.eye(128, dtype=f)[::-1]),
        "gconst": _gconst(), "convw": np.ascontiguousarray(inp["gdn_conv_w"][0]),
        "gvec": _gvec(inp),
        "sg_in": np.ascontiguousarray(inp["state_gdn"][0, 16 * c:16 * c + 16]).reshape(128, 4096),
        "sc_in": np.ascontiguousarray(inp["state_conv"][0, 16 * c:16 * c + 16]),
        "wcm": _wcm(),
        "dmask": np.ascontiguousarray(np.repeat(np.eye(8, dtype=f), 4, axis=0)),
        "cws": np.ascontiguousarray(np.broadcast_to(
            inp["gdn_conv_w"][0].reshape(4, 3, 8, 64).transpose(2, 1, 0, 3)[None], (16, 8, 3, 4, 64)).reshape(128, 3, 4, 64)),
        "gvs": np.ascontiguousarray(np.tile(np.stack([inp["gdn_a_log"][0], inp["gdn_dt_bias"][0]], axis=1), (16, 1))).astype(f),
        "w_out": np.ascontiguousarray(inp["w_out"][0]),
        "f2g": np.ascontiguousarray(inp["ffn2_w_gate"][0]), "f2u": np.ascontiguousarray(inp["ffn2_w_up"][0]),
        "f2d": np.ascontiguousarray(inp["ffn2_w_down"][0]),
    }
    if big is not None:
        m["ck"] = np.ascontiguousarray(big["cache_attn_k"][0, 16 * c:16 * c + 16]).reshape(16, 2048, 512)
        m["cv"] = np.ascontiguousarray(big["cache_attn_v"][0, 16 * c:16 * c + 16]).reshape(16, 2048, 512)
    return m


ALL_STAGES = ("ffn1", "attn", "sproj", "sattn", "gdn", "sgdn", "wout", "ffn2")
_NC_CACHE = {}


def gather_outputs(results):
    n = len(results)
    f = np.float32
    yp = np.stack([r["o_yp"] for r in results]).astype(f)
    ys = np.concatenate([r["o_ys"].reshape(16, 4, D) for r in results]).astype(f)
    kp = np.stack([r["o_kp"].reshape(2048, 8, 64) for r in results])[None].astype(f)
    vp = np.stack([r["o_vp"].reshape(2048, 8, 64) for r in results])[None].astype(f)
    sgp = np.stack([r["o_sgp"] for r in results])[None].astype(f)
    scp = np.stack([r["o_scp"] for r in results])[None].astype(f)
    ks = np.concatenate([r["o_ks"].reshape(16, 4, 8, 64) for r in results])[None].astype(f)
    vs = np.concatenate([r["o_vs"].reshape(16, 4, 8, 64) for r in results])[None].astype(f)
    sgs = np.concatenate([r["o_sgs"].reshape(16, 8, 64, 64) for r in results])[None].astype(f)
    scs = np.concatenate([r["o_scs"] for r in results])[None].astype(f)
    return (yp, ys, kp, vp, sgp, scp, ks, vs, sgs, scs)


def kernel(**inputs):
    inp = {k: np.asarray(v) for k, v in inputs.items()}
    if "nc" not in _NC_CACHE:
        _NC_CACHE["nc"] = build_nc(dbg=False, stages=ALL_STAGES)
    nc = _NC_CACHE["nc"]
    in_maps = [core_inputs(inp, c, inp) for c in range(8)]
    res = run_bass_kernel_spmd(nc, in_maps, core_ids=list(range(8)))
    return gather_outputs(res.results)
```

```python
import contextlib
import math
import numpy as np
import concourse.bass as bass
import concourse.mybir as mybir
from concourse.bass_utils import run_bass_kernel_spmd

F32 = mybir.dt.float32
BF16 = mybir.dt.bfloat16
ALU = mybir.AluOpType
AF = mybir.ActivationFunctionType
AX = mybir.AxisListType

D = 1024
DFF = 2816
NFC = 22
NT = 17
NTOK = NT * 128
INC = 3600
ALPHA = 2.0 ** 0.25
LN_EPS = 1e-5
NEG = -30000.0


def sl(start, count, step=1):
    return slice(start, start + (count - 1) * step + 1, step)


class Sched:
    def __init__(self, nc, stack, ndma=6):
        self.nc = nc
        self.eng = {"pe": nc.tensor, "act": nc.scalar, "dve": nc.vector, "pool": nc.gpsimd, "sp": nc.sync}
        self.esem = {k: stack.enter_context(nc.semaphore("es_" + k)) for k in self.eng}
        self.tick = {k: 0 for k in self.eng}
        self.dsem = {q: [stack.enter_context(nc.semaphore("ds_%s%d" % (q, i))) for i in range(ndma)]
                     for q in ("sp", "pool", "act")}
        self.duse = {q: [0] * ndma for q in self.dsem}
        self.dcnt = {q: 0 for q in self.dsem}
        self.seen = {k: {} for k in self.eng}
        self.ops = []

    def op(self, eng, fn, reads=(), writes=()):
        self.ops.append(dict(eng=eng, fn=fn, reads=tuple(reads), writes=tuple(writes), dma=False))

    def dma(self, q, fn, reads=(), writes=()):
        self.ops.append(dict(eng=q, fn=fn, reads=tuple(reads), writes=tuple(writes), dma=True))

    def capture(self, fn):
        saved = self.ops
        self.ops = []
        fn()
        got = self.ops
        self.ops = saved
        return got

    def emit_merged(self, a, b):
        i = j = 0
        while i < len(a) or j < len(b):
            if j >= len(b) or (i < len(a) and i * len(b) <= j * len(a)):
                self.ops.append(a[i])
                i += 1
            else:
                self.ops.append(b[j])
                j += 1

    def _wait(self, e, sem, val):
        key = id(sem)
        if self.seen[e].get(key, 0) < val:
            self.eng[e].wait_ge(sem, val)
            self.seen[e][key] = val

    def flush(self, barrier=True):
        ops = self.ops
        self.ops = []
        last_w = {}
        readers = {}
        needs = [False] * len(ops)
        for i, o in enumerate(ops):
            deps = set()

            def inorder(j):
                return ops[j]["eng"] == o["eng"] == "pe" and not ops[j]["dma"] and not o["dma"]

            for r in o["reads"]:
                j = last_w.get(r)
                if j is not None and not (inorder(j) and o["eng"] == "pe"):
                    deps.add(j)
            for w in o["writes"]:
                j = last_w.get(w)
                if j is not None and not inorder(j):
                    deps.add(j)
                for j in readers.get(w, ()):
                    if not inorder(j):
                        deps.add(j)
            o["deps"] = sorted(deps)
            for j in deps:
                needs[j] = True
            for r in o["reads"]:
                readers.setdefault(r, []).append(i)
            for w in o["writes"]:
                last_w[w] = i
                readers[w] = []
        lastop = {}
        for i, o in enumerate(ops):
            if not o["dma"]:
                lastop[o["eng"]] = i
        for i in lastop.values():
            needs[i] = True
        for i, o in enumerate(ops):
            e = o["eng"]
            for j in o["deps"]:
                ev = ops[j]["event"]
                self._wait(e, ev[0], ev[1])
            if o["dma"]:
                n = len(self.dsem[e])
                slot = self.dcnt[e] % n
                self.dcnt[e] += 1
                sem = self.dsem[e][slot]
                k = self.duse[e][slot]
                if k > 0:
                    self._wait(e, sem, 16 * k)
                ins = o["fn"](self.eng[e])
                ins.then_inc(sem, 16)
                self.duse[e][slot] = k + 1
                o["event"] = (sem, 16 * (k + 1))
            else:
                ins = o["fn"](self.eng[e])
                if needs[i]:
                    self.tick[e] += 1
                    ins.then_inc(self.esem[e], 1)
                    o["event"] = (self.esem[e], self.tick[e])
                else:
                    o["event"] = None
        if barrier:
            self.barrier()

    def barrier(self):
        for e in self.eng:
            for d in self.eng:
                if d != e and self.tick[d] > 0:
                    self._wait(e, self.esem[d], self.tick[d])
            for q in self.dsem:
                for s, k in zip(self.dsem[q], self.duse[q]):
                    if k > 0:
                        self._wait(e, s, 16 * k)


def build_nc(dbg=False, stages=("ffn1",)):
    nc = bass.Bass("TRN2", target_bir_lowering=False)

    def din(name, shape, dt=F32):
        return nc.dram_tensor(name, list(shape), dt, kind="ExternalInput").ap()

    def dout(name, shape, dt=F32):
        return nc.dram_tensor(name, list(shape), dt, kind="ExternalOutput").ap()

    def dscr(name, shape, dt=F32):
        return nc.dram_tensor(name, list(shape), dt, kind="ExternalOutput" if dbg else "Internal").ap()

    xin = din("xin", [NTOK, D])
    f1g = din("f1g", [D, DFF])
    f1u = din("f1u", [D, DFF])
    f1d = din("f1d", [DFF, D])
    lnp = din("lnp", [6, D])
    ident_d = din("ident", [128, 128])
    x1s = dscr("x1s", [NTOK, D])
    w_in = din("w_in", [D, INC])
    relb = din("relb", [32, 8])
    onehot = din("onehot", [3, 33, 384])
    antiid = din("antiid", [128, 128])
    o_kp = dout("o_kp", [2048, 512])
    o_vp = dout("o_vp", [2048, 512])
    o_ks = dout("o_ks", [64, 512])
    o_vs = dout("o_vs", [64, 512])
    fvd = dscr("fvd", [3, 8, 384])
    attn_s = dscr("attn_s", [8, 64, 2048], BF16)
    gconst = din("gconst", [64, 7, 64])
    convw = din("convw", [4, 1536])
    gvec = din("gvec", [3, 64])
    o_sgp = dout("o_sgp", [8, 64, 64])
    o_scp = dout("o_scp", [3, 1536])
    gdn_dbg = dscr("gdn_dbg", [128, 4, 2048]) if dbg else None
    ck = din("ck", [16, 2048, 512])
    cv = din("cv", [16, 2048, 512])
    sg_in = din("sg_in", [128, 4096])
    sc_in = din("sc_in", [16, 3, 1536])
    wcm = din("wcm", [32, 8, 4, 128])
    cws = din("cws", [128, 3, 4, 64])
    gvs = din("gvs", [128, 2])
    dmask = din("dmask", [32, 8])
    w_out = din("w_out", [D, D])
    f2g = din("f2g", [D, DFF])
    f2u = din("f2u", [D, DFF])
    f2d = din("f2d", [DFF, D])
    o_ys = dout("o_ys", [64, D])
    o_yp = dout("o_yp", [2048, D])
    o_sgs = dout("o_sgs", [128, 4096])
    o_scs = dout("o_scs", [16, 3, 1536])
    qs_s = dscr("qs_s", [64, 512])
    gq_s = dscr("gq_s", [64, 1536])
    z_s = dscr("z_s", [64, 512])
    ba_s = dscr("ba_s", [64, 16])
    heads_s = dscr("heads_s", [64, D])
    x2s = dscr("x2s", [NTOK, D])

    with contextlib.ExitStack() as stack:
        S = Sched(nc, stack)
        sb = lambda name, shape, dt=F32: stack.enter_context(nc.sbuf_tensor(name, list(shape), dt))
        psb = [stack.enter_context(nc.psum_tensor("psb%d" % i, [128, 512], F32)) for i in range(7)]
        pst = stack.enter_context(nc.psum_tensor("pst", [128, 1024], BF16))

        identf = sb("identf", [128, 128])
        identb = sb("identb", [128, 128], BF16)
        stp = contextlib.ExitStack()
        x1T = stp.enter_context(nc.sbuf_tensor("x1T", [128, 8, NTOK], BF16))

        S.dma("sp", lambda e: e.dma_start(out=identf[:], in_=ident_d[:, :]), writes=["identf"])
        S.op("dve", lambda e: e.tensor_copy(out=identb[:], in_=identf[:]), reads=["identf"], writes=["identb"])
        S.flush()

        pst_alt = psb[6][:, :].bitcast(BF16)

        def to_featmajor(src_bf, skey, dstT, col0, dkey, alt=False):
            pt_, pk_ = (pst_alt, ("psb", 6)) if alt else (pst[:], "pst")
            for kc in range(8):
                S.op("pe", lambda e, kc=kc: e.transpose(pt_[:, kc * 128:(kc + 1) * 128],
                                                        src_bf[:, kc * 128:(kc + 1) * 128], identb[:]),
                     reads=[skey, "identb"], writes=[pk_])
            S.op("dve", lambda e: e.tensor_copy(out=dstT[:, :, col0:col0 + 128],
                                               in_=pt_.rearrange("p (k c) -> p k c", k=8)),
                 reads=[pk_], writes=[dkey])

        def layernorm(r, rkey, lnrep, out, okey, tmp, epsmul=4.0):
            st, mv, rstd = tmp
            for c in range(2):
                S.op("dve", lambda e, c=c: e.bn_stats(out=st[:, c, :], in_=r[:, c * 512:(c + 1) * 512]),
                     reads=[rkey], writes=[("st", c)])
            S.op("dve", lambda e: e.bn_aggr(out=mv[:], in_=st[:].rearrange("p c s -> p (c s)")),
                 reads=[("st", 0), ("st", 1)], writes=["mv"])
            S.op("dve", lambda e: e.tensor_scalar(out=rstd[:], in0=mv[:, 1:2], scalar1=epsmul * LN_EPS, scalar2=None,
                                                  op0=ALU.add), reads=["mv"], writes=["rstd"])
            S.op("act", lambda e: e.sqrt(out=rstd[:], in_=rstd[:]), reads=["rstd"], writes=["rstd"])
            S.op("dve", lambda e: e.reciprocal(out=rstd[:], in_=rstd[:]), reads=["rstd"], writes=["rstd"])
            S.op("dve", lambda e: e.tensor_scalar(out=r[:], in0=r[:], scalar1=mv[:, 0:1], scalar2=rstd[:, 0:1],
                                                  op0=ALU.subtract, op1=ALU.mult),
                 reads=[rkey, "mv", "rstd"], writes=[rkey])
            S.op("pool", lambda e: e.tensor_tensor(out=r[:], in0=r[:], in1=lnrep[:, 0, :], op=ALU.mult),
                 reads=[rkey, ("lnrep", 0)], writes=[rkey])
            S.op("dve", lambda e: e.tensor_tensor(out=out[:], in0=r[:], in1=lnrep[:, 1, :], op=ALU.add),
                 reads=[rkey, ("lnrep", 1)], writes=[okey])

        def ffn(tag, xsrc, wg, wu, wd, gi, store, xTout):
            with contextlib.ExitStack() as st2:
                sb2 = lambda name, shape, dt=F32: st2.enter_context(nc.sbuf_tensor(tag + name, list(shape), dt))
                MT = 9
                xT = sb2("xT", [128, 8, MT * 128], BF16)
                hT = sb2("hT", [128, NFC, MT * 128], BF16)
                wgb = [sb2("wgb%d" % i, [128, 8, 256], BF16) for i in range(2)]
                wub = [sb2("wub%d" % i, [128, 8, 256], BF16) for i in range(2)]
                wdb = sb2("wdb", [128, NFC, D], BF16)
                xs = [sb2("xs%d" % i, [128, D]) for i in range(2)]
                xb = [sb2("xb%d" % i, [128, D], BF16) for i in range(2)]
                sg = [sb2("sg%d" % i, [128, 512]) for i in range(2)]
                rr = [sb2("rr%d" % i, [128, D]) for i in range(2)]
                oo = rr
                lnrep = sb2("ln", [128, 2, D])
                for i in range(2):
                    S.dma("sp", lambda e, i=i: e.dma_start(
                        out=lnrep[:, i, :], in_=lnp[gi + i:gi + i + 1, :].partition_broadcast(128)),
                          writes=[("lnrep", i)])
                stt = sb2("st", [128, 2, 6])
                mv = sb2("mv", [128, 2])
                rstd = sb2("rstd", [128, 1])
                wgv = wg.rearrange("(kc p) f -> p kc f", p=128)
                wuv = wu.rearrange("(kc p) f -> p kc f", p=128)
                wdv = wd.rearrange("(fc p) d -> p fc d", p=128)
                for mi, tiles in enumerate((list(range(0, MT)), list(range(MT, NT)))):
                    ntl = len(tiles)
                    for li, t in enumerate(tiles):
                        b = li % 2
                        S.dma("sp", lambda e, b=b, t=t: e.dma_start(out=xs[b][:], in_=xsrc(t)), writes=[("xs", b)])
                        S.op("act", lambda e, b=b: e.copy(out=xb[b][:], in_=xs[b][:]), reads=[("xs", b)],
                             writes=[("xb", b)])
                        to_featmajor(xb[b], ("xb", b), xT, li * 128, ("xT", li), alt=bool(li % 2))
                    if mi == 0:
                        for q in range(2):
                            S.dma("pool", lambda e, q=q: e.dma_start(out=wdb[:, q * 11:(q + 1) * 11, :],
                                                                     in_=wdv[:, q * 11:(q + 1) * 11, :]),
                                  writes=[("wdb", q)])
                    tbs = [(c0, min(512, ntl * 128 - c0)) for c0 in range(0, ntl * 128, 512)]
                    for fb in range(11):
                        wbuf = fb % 2
                        S.dma("pool", lambda e, fb=fb, wbuf=wbuf: e.dma_start(
                            out=wgb[wbuf][:], in_=wgv[:, :, fb * 256:(fb + 1) * 256]), writes=[("wgb", wbuf)])
                        S.dma("pool", lambda e, fb=fb, wbuf=wbuf: e.dma_start(
                            out=wub[wbuf][:], in_=wuv[:, :, fb * 256:(fb + 1) * 256]), writes=[("wub", wbuf)])
                        for j in range(2):
                            fc = fb * 2 + j
                            for ti, (c0, cw) in enumerate(tbs):
                                pb = (fc * len(tbs) + ti) % 2
                                pg, pu = psb[pb], psb[2 + pb]
                                xkeys = [("xT", li) for li in range(c0 // 128, (c0 + cw) // 128)]
                                for kc in range(8):
                                    S.op("pe", lambda e, pg=pg, kc=kc, wbuf=wbuf, j=j, c0=c0, cw=cw: e.matmul(
                                        pg[:, 0:cw], wgb[wbuf][:, kc, j * 128:(j + 1) * 128], xT[:, kc, c0:c0 + cw],
                                        start=(kc == 0), stop=(kc == 7)),
                                         reads=[("wgb", wbuf)] + xkeys, writes=[("psb", pb)])
                                for kc in range(8):
                                    S.op("pe", lambda e, pu=pu, kc=kc, wbuf=wbuf, j=j, c0=c0, cw=cw: e.matmul(
                                        pu[:, 0:cw], wub[wbuf][:, kc, j * 128:(j + 1) * 128], xT[:, kc, c0:c0 + cw],
                                        start=(kc == 0), stop=(kc == 7)),
                                         reads=[("wub", wbuf)] + xkeys, writes=[("psb", 2 + pb)])
                                S.op("act", lambda e, pg=pg, pb=pb, cw=cw: e.activation(
                                    out=sg[pb][:, 0:cw], in_=pg[:, 0:cw], func=AF.Silu),
                                     reads=[("psb", pb)], writes=[("sg", pb)])
                                S.op("dve", lambda e, pu=pu, pb=pb, fc=fc, c0=c0, cw=cw: e.tensor_tensor(
                                    out=hT[:, fc, c0:c0 + cw], in0=sg[pb][:, 0:cw], in1=pu[:, 0:cw], op=ALU.mult),
                                     reads=[("sg", pb), ("psb", 2 + pb)], writes=[("hT", fc, ti)])
                    pend = None

                    def emit_pending(pend):
                        if pend is None or xTout is None:
                            return
                        b_, t_ = pend
                        S.op("act", lambda e: e.copy(out=xb[b_][:], in_=rr[b_][:]), reads=[("rr", b_)], writes=[("xb", b_)])
                        to_featmajor(xb[b_], ("xb", b_), xTout, t_ * 128, ("xTo", t_), alt=bool(t_ % 2))

                    for li, t in enumerate(tiles):
                        b = li % 2
                        ti = li // 4
                        S.dma("sp", lambda e, b=b, t=t: e.dma_start(out=xs[b][:], in_=xsrc(t)), writes=[("xs", b)])
                        for half in range(2):
                            pd = psb[4 + half]
                            for fc in range(NFC):
                                S.op("pe", lambda e, pd=pd, fc=fc, li=li, half=half: e.matmul(
                                    pd[:, :], hT[:, fc, li * 128:(li + 1) * 128], wdb[:, fc, half * 512:(half + 1) * 512],
                                    start=(fc == 0), stop=(fc == NFC - 1)),
                                     reads=[("hT", fc, ti), ("wdb", fc // 11)], writes=[("psb", 4 + half)])
                            S.op("dve", lambda e, pd=pd, b=b, half=half: e.scalar_tensor_tensor(
                                out=rr[b][:, half * 512:(half + 1) * 512], in0=xs[b][:, half * 512:(half + 1) * 512],
                                scalar=2.0 * ALPHA, in1=pd[:, :], op0=ALU.mult, op1=ALU.add),
                                 reads=[("xs", b), ("psb", 4 + half)], writes=[("rr", b)])
                            if half == 1:
                                emit_pending(pend)
                                pend = None
                        layernorm(rr[b], ("rr", b), lnrep, rr[b], ("rr", b), (stt, mv, rstd))
                        store(t, rr[b], ("rr", b))
                        pend = (b, t)
                    emit_pending(pend)
                S.flush()

        PAT = ((128, 1), (512, 4), (2048, 16))

        def unit_tokens(d, u):
            nblk = 16 // d
            r, n = u // nblk, u % nblk
            return r, n, nblk, r + d * 128 * n

        def attention_stage():
            with contextlib.ExitStack() as st2:
                sb2 = lambda name, shape, dt=F32: st2.enter_context(nc.sbuf_tensor("at" + name, list(shape), dt))
                attnT = [sb2("attnT%d" % i, [64, 2048], BF16) for i in range(2)]
                qT = sb2("qT", [128, 4, 2048], BF16)
                kT = sb2("kT", [128, 4, 2048], BF16)
                vaug = [sb2("vaug%d" % i, [128, 16, 8, 65], BF16) for i in range(3)]
                brev = sb2("brev", [128, 24, 256], BF16)
                jb = sb2("jb", [128, 128], BF16)
                onesf = sb2("onesf", [128, 64])
                rbx = sb2("rbx", [33, 8])
                ohs = sb2("ohs", [33, 3, 384])
                fvs = sb2("fvs", [8, 3, 384])
                winv = w_in.rearrange("(kc p) f -> p kc f", p=128)
                S.dma("pool", lambda e: e.dma_start(out=jb[:], in_=antiid[:, :]), writes=["jb"])
                S.op("pool", lambda e: e.memset(onesf[:], 1.0), writes=["onesf"])
                S.op("pool", lambda e: e.memset(rbx[32:33, :], NEG), writes=["rbx1"])
                S.dma("sp", lambda e: e.dma_start(out=rbx[0:32, :], in_=relb[:, :]), writes=["rbx0"])
                S.dma("sp", lambda e: e.dma_start(out=ohs[:], in_=onehot.rearrange("p b m -> b p m")), writes=["ohs"])
                for p in range(3 if "notables" not in stages else 0):
                    S.op("pe", lambda e, p=p: e.matmul(psb[p][0:8, 0:384], rbx[:, :], ohs[:, p, :], start=True, stop=True),
                         reads=["rbx0", "rbx1", "ohs"], writes=[("psb", p)])
                    S.op("dve", lambda e, p=p: e.tensor_copy(out=fvs[:, p, :], in_=psb[p][0:8, 0:384]),
                         reads=[("psb", p)], writes=[("fvs", p)])
                if "notables" not in stages:
                    S.dma("sp", lambda e: e.dma_start(out=fvd.rearrange("p h m -> h p m"), in_=fvs[:]),
                          reads=[("fvs", p) for p in range(3)], writes=["fvd"])
                if "nohankel" not in stages:
                    S.dma("pool", lambda e: e.dma_start(
                        out=brev[:], in_=bass.AP(fvd.tensor, 0, [[1, 128], [384, 24], [1, 256]])),
                          reads=["fvd"], writes=["brev"])
                for i in range(3):
                    S.op("pool", lambda e, i=i: e.memset(vaug[i][:, :, :, 64:65], 1.0), writes=[("vone", i)])
                with contextlib.ExitStack() as st3:
                    sb3 = lambda name, shape, dt=F32: st3.enter_context(nc.sbuf_tensor("ap" + name, list(shape), dt))
                    wb = [sb3("wb%d" % i, [128, 8, 512], BF16) for i in range(3)]
                    kvo = [sb3("kvo%d" % i, [128, 512]) for i in range(2)]
                    for blk in range(3):
                        S.dma("pool", lambda e, blk=blk: e.dma_start(out=wb[blk][:], in_=winv[:, :, blk * 512:(blk + 1) * 512]),
                              writes=[("wb", blk)])
                    cnt = 0
                    if "noproj" in stages:
                        S.flush()
                        return
                    for blk, dst in (((0, qT), (1, kT)) if "noqk" not in stages else ()):
                        for pair in range(4):
                            for tb in range(4):
                                pb = cnt % 2
                                cnt += 1
                                for kc in range(8):
                                    S.op("pe", lambda e, pb=pb, blk=blk, pair=pair, tb=tb, kc=kc: e.matmul(
                                        psb[pb][:, :], wb[blk][:, kc, pair * 128:(pair + 1) * 128],
                                        x1T[:, kc, 128 + tb * 512:128 + (tb + 1) * 512], start=(kc == 0), stop=(kc == 7)),
                                         reads=[("wb", blk)], writes=[("psb", pb)])
                                if blk == 0:
                                    S.op("act", lambda e, pb=pb, pair=pair, tb=tb: e.mul(
                                        out=qT[:, pair, tb * 512:(tb + 1) * 512], in_=psb[pb][:, :], mul=0.125),
                                         reads=[("psb", pb)], writes=[("qT", pair)])
                                else:
                                    S.op("dve", lambda e, pb=pb, pair=pair, tb=tb: e.tensor_copy(
                                        out=kT[:, pair, tb * 512:(tb + 1) * 512], in_=psb[pb][:, :]),
                                         reads=[("psb", pb)], writes=[("kT", pair)])
                    for t in range(NT if "nokv" not in stages else 0):
                        for blk in (1, 2):
                            pb = 2 + (cnt % 2)
                            cnt += 1
                            ob = blk - 1
                            for kc in range(8):
                                S.op("pe", lambda e, pb=pb, blk=blk, t=t, kc=kc: e.matmul(
                                    psb[pb][:, :], x1T[:, kc, t * 128:(t + 1) * 128], wb[blk][:, kc, :],
                                    start=(kc == 0), stop=(kc == 7)),
                                     reads=[("wb", blk)], writes=[("psb", pb)])
                            S.op("act", lambda e, pb=pb, ob=ob: e.activation(out=kvo[ob][:], in_=psb[pb][:, :], func=AF.Copy),
                                 reads=[("psb", pb)], writes=[("kvo", ob)])
                            if blk == 2 and t >= 1:
                                S.op("dve", lambda e, ob=ob, t=t: e.tensor_copy(
                                    out=vaug[0][:, t - 1, :, 0:64], in_=kvo[ob][:, :].rearrange("p (h e) -> p h e", h=8)),
                                     reads=[("kvo", ob)], writes=[("vaug", 0, t - 1)])
                            if "nokvdma" in stages:
                                continue
                            if t == 0:
                                dst = (o_ks if blk == 1 else o_vs)[0:64, :]
                                S.dma("sp", lambda e, dst=dst, ob=ob: e.dma_start(out=dst, in_=kvo[ob][0:64, :]),
                                      reads=[("kvo", ob)], writes=[("okv", blk, t)])
                            else:
                                dst = (o_kp if blk == 1 else o_vp)[(t - 1) * 128:t * 128, :]
                                S.dma("sp", lambda e, dst=dst, ob=ob: e.dma_start(out=dst, in_=kvo[ob][:, :]),
                                      reads=[("kvo", ob)], writes=[("okv", blk, t)])
                    for pi in ((1, 2) if "nodil" not in stages else ()):
                        d = PAT[pi][1]
                        for u in range(16):
                            r, n, nblk, t0 = unit_tokens(d, u)
                            pb = 2 + (cnt % 2)
                            cnt += 1
                            for kc in range(8):
                                S.op("pe", lambda e, pb=pb, kc=kc, t0=t0, d=d: e.matmul(
                                    psb[pb][:, :], x1T[:, kc, sl(128 + t0, 128, d)], wb[2][:, kc, :],
                                    start=(kc == 0), stop=(kc == 7)),
                                     reads=[("wb", 2)], writes=[("psb", pb)])
                            S.op("dve", lambda e, pb=pb, pi=pi, u=u: e.tensor_copy(
                                out=vaug[pi][:, u, :, 0:64], in_=psb[pb][:, :].rearrange("p (h e) -> p h e", h=8)),
                                 reads=[("psb", pb)], writes=[("vaug", pi, u)])
                    S.flush()
                if "noattnmain" in stages:
                    return
                with contextlib.ExitStack() as st3:
                    sb3 = lambda name, shape, dt=F32: st3.enter_context(nc.sbuf_tensor("aa" + name, list(shape), dt))
                    acc = [sb3("acc%d" % i, [65, 2048]) for i in range(2)]
                    pts = [sb3("pt%d" % i, [128, 256], BF16) for i in range(4)]
                    rcp = sb3("rcp", [65, 2048])
                    ptc = 0
                    cnt = 0
                    for h in range(8):
                        pair, base = h // 2, (h % 2) * 64
                        ab = h % 2
                        A = acc[ab]
                        units = [(pi, d, u) for pi, (win, d) in enumerate(PAT) for u in range(16)]

                        def st_part(ix, h=h, pair=pair, base=base):
                            pi, d, u = units[ix]
                            r, n, nblk, t0 = unit_tokens(d, u)
                            W = 256 if n + 1 < nblk else 128
                            ps = ix % 2
                            pt = ix % 4
                            S.op("pe", lambda e: e.matmul(
                                psb[ps][:, 0:W], kT[base:base + 64, pair, sl(t0, 128, d)],
                                qT[base:base + 64, pair, sl(t0, W, d)], start=True, stop=False),
                                 reads=[("qT", pair), ("kT", pair)], writes=[("psb", ps)])
                            S.op("pe", lambda e: e.matmul(
                                psb[ps][:, 0:W], jb[:, :], brev[:, pi * 8 + h, 0:W], start=False, stop=True),
                                 reads=["jb", "brev"], writes=[("psb", ps)])
                            S.op("act", lambda e: e.activation(
                                out=pts[pt][:, 0:W], in_=psb[ps][:, 0:W], func=AF.Exp),
                                 reads=[("psb", ps)], writes=[("pt", pt)])

                        def pv_part(ix, h=h, A=A, ab=ab):
                            pi, d, u = units[ix]
                            r, n, nblk, t0 = unit_tokens(d, u)
                            pt = ix % 4
                            prev = (ix - 1) % 4
                            po = 2 + (ix % 2)
                            first = (n == 0)
                            S.op("pe", lambda e: e.matmul(
                                psb[po][0:65, 0:128], vaug[pi][:, u, h, :], pts[pt][:, 0:128], start=True, stop=first),
                                 reads=[("vaug", pi, u), ("vone", pi), ("pt", pt)], writes=[("psb", po)])
                            if not first:
                                S.op("pe", lambda e: e.matmul(
                                    psb[po][0:65, 0:128], vaug[pi][:, u - 1, h, :], pts[prev][:, 128:256],
                                    start=False, stop=True),
                                     reads=[("vaug", pi, u - 1), ("vone", pi), ("pt", prev)], writes=[("psb", po)])
                            dst = A[:, sl(t0, 128, d)]
                            if pi == 0:
                                S.op("dve", lambda e: e.tensor_copy(out=dst, in_=psb[po][0:65, 0:128]),
                                     reads=[("psb", po)], writes=[("acc", ab)])
                            else:
                                S.op("dve", lambda e: e.tensor_tensor(
                                    out=dst, in0=dst, in1=psb[po][0:65, 0:128], op=ALU.add),
                                     reads=[("psb", po), ("acc", ab)], writes=[("acc", ab)])

                        st_part(0)
                        st_part(1)
                        for ix in range(len(units)):
                            if ix + 2 < len(units):
                                st_part(ix + 2)
                            pv_part(ix)
                        S.op("act", lambda e, A=A: e.activation(out=rcp[64:65, :], in_=A[64:65, :], func=AF.Ln),
                             reads=[("acc", ab)], writes=["rcp"])
                        S.op("act", lambda e: e.activation(out=rcp[64:65, :], in_=rcp[64:65, :], func=AF.Exp, scale=-1.0),
                             reads=["rcp"], writes=["rcp"])
                        for tb in range(4):
                            pr = 4 + (tb % 2)
                            S.op("pe", lambda e, pr=pr, tb=tb: e.matmul(
                                psb[pr][0:64, :], onesf[64:65, 0:64], rcp[64:65, tb * 512:(tb + 1) * 512],
                                start=True, stop=True), reads=["rcp", "onesf"], writes=[("psb", pr)])
                            S.op("dve", lambda e, pr=pr, tb=tb, A=A, ab=ab: e.tensor_tensor(
                                out=attnT[ab][:, tb * 512:(tb + 1) * 512], in0=A[0:64, tb * 512:(tb + 1) * 512],
                                in1=psb[pr][0:64, :], op=ALU.mult),
                                 reads=[("psb", pr), ("acc", ab)], writes=[("attnT", ab)])
                        S.dma("sp", lambda e, h=h, ab=ab: e.dma_start(out=attn_s[h, :, :], in_=attnT[ab][:, :]),
                              reads=[("attnT", ab)], writes=[("attn_s", h)])
                    S.flush()

        def bl(ap, n=64):
            return ap.unsqueeze(2).broadcast_to([ap.shape[0], ap.shape[1], n])

        def bm(ap, n=8):
            return ap.unsqueeze(1).broadcast_to([ap.shape[0], n, ap.shape[1]])

        def v3(ap, h=8):
            return ap.rearrange("p (h x) -> p h x", h=h)

        def gdn_prompt_stage():
            with contextlib.ExitStack() as st2:
                sb2 = lambda name, shape, dt=F32: st2.enter_context(nc.sbuf_tensor("gd" + name, list(shape), dt))
                qh = sb2("qh", [64, 8, 2048], BF16)
                kh = sb2("kh", [64, 8, 2048], BF16)
                vT = sb2("vT", [128, 4, 2048], BF16)
                gcn = sb2("gcn", [64, 7, 64])
                NEGS, NEGT, MSKT, ID64, ONES, TRI, SEL = [gcn[:, i, :] for i in range(7)]
                cwq = sb2("cwq", [64, 16, 4])
                cwv = sb2("cwv", [128, 4, 4])
                nwr = sb2("nwr", [64, 64])
                wz = sb2("wz", [128, 8, 512], BF16)
                wba = sb2("wba", [128, 8, 16], BF16)
                stc = contextlib.ExitStack()
                cwr = stc.enter_context(nc.sbuf_tensor("gdcwr", [4, 1536], F32))
                winv = w_in.rearrange("(kc p) f -> p kc f", p=128)
                S.dma("sp", lambda e: e.dma_start(out=gcn[:], in_=gconst[:, :, :]), writes=["gcn"])
                S.dma("sp", lambda e: e.dma_start(out=cwr[:], in_=convw[:, :]), writes=["cwr"])
                S.dma("sp", lambda e: e.dma_start(out=nwr[:], in_=gvec[2:3, :].partition_broadcast(64)), writes=["nwr"])
                S.dma("pool", lambda e: e.dma_start(out=wz[:], in_=winv[:, :, 3072:3584]), writes=["wz"])
                S.dma("pool", lambda e: e.dma_start(out=wba[:], in_=winv[:, :, 3584:3600]), writes=["wba"])
                for g in range(16):
                    S.op("pe", lambda e, g=g: e.transpose(psb[0][0:64, g * 4:(g + 1) * 4], cwr[0:4, g * 64:(g + 1) * 64],
                                                          identf[0:4, 0:4]), reads=["cwr", "identf"], writes=[("psb", 0)])
                for c in range(4):
                    S.op("pe", lambda e, c=c: e.transpose(psb[1][:, c * 4:(c + 1) * 4],
                                                          cwr[0:4, 1024 + c * 128:1024 + (c + 1) * 128], identf[0:4, 0:4]),
                         reads=["cwr", "identf"], writes=[("psb", 1)])
                S.op("dve", lambda e: e.tensor_copy(out=cwq[:], in_=v3(psb[0][0:64, 0:64], 16)), reads=[("psb", 0)],
                     writes=["cwq"])
                S.op("dve", lambda e: e.tensor_copy(out=cwv[:], in_=v3(psb[1][:, 0:16], 4)), reads=[("psb", 1)],
                     writes=["cwv"])
                S.flush()
                stc.close()

                with contextlib.ExitStack() as st3:
                    sb3 = lambda name, shape, dt=F32: st3.enter_context(nc.sbuf_tensor("g1" + name, list(shape), dt))
                    wb = [sb3("wb%d" % i, [128, 8, 512], BF16) for i in range(2)]
                    raws = [sb3("raw%d" % i, [128, 2051]) for i in range(2)]
                    cacs = [sb3("cac%d" % i, [128, 2048]) for i in range(2)]
                    rin = [sb3("rin%d" % i, [64, 512]) for i in range(2)]
                    scpb = sb3("scpb", [128, 24, 3])
                    for i in range(2):
                        S.op("pool", lambda e, i=i: e.memset(raws[i][:, 0:3], 0.0), writes=[("raw0", i)])
                    epsb = sb3("epsb", [64, 2])
                    S.op("pool", lambda e: e.memset(epsb[:, 0:1], 64.0e-6), writes=["epsb"])
                    S.op("pool", lambda e: e.memset(epsb[:, 1:2], 1.0e-6), writes=["epsb"])
                    cnt = 0
                    gi = 0
                    pending_tail = []
                    for blk in range(3):
                        wbuf = blk % 2
                        S.dma("pool", lambda e, blk=blk, wbuf=wbuf: e.dma_start(
                            out=wb[wbuf][:], in_=winv[:, :, 1536 + blk * 512:1536 + (blk + 1) * 512]),
                              writes=[("wb", wbuf)])
                        ngrp, P = (8, 64) if blk < 2 else (4, 128)
                        for g in range(ngrp):
                          def group_body(part, blk=blk, g=g, wbuf=wbuf, P=P, rb_=gi % 2, cnt0=cnt):
                            cnt = cnt0
                            raw, cac = raws[rb_], cacs[rb_]
                            RAW, CAC = ("raw", rb_), ("cac", rb_)
                            if part == "tail":
                                return group_tail(blk, g, P, raw, cac, RAW, CAC, rb_)
                            for tb in range(4):
                                pb = cnt % 2
                                cnt += 1
                                for kc in range(8):
                                    S.op("pe", lambda e, pb=pb, wbuf=wbuf, g=g, P=P, tb=tb, kc=kc: e.matmul(
                                        psb[pb][0:P, :], wb[wbuf][:, kc, g * P:(g + 1) * P],
                                        x1T[:, kc, 128 + tb * 512:128 + (tb + 1) * 512], start=(kc == 0), stop=(kc == 7)),
                                         reads=[("wb", wbuf)], writes=[("psb", pb)])
                                S.op("act", lambda e, pb=pb, P=P, tb=tb, raw=raw: e.activation(
                                    out=raw[0:P, 3 + tb * 512:3 + (tb + 1) * 512], in_=psb[pb][0:P, :], func=AF.Copy),
                                     reads=[("psb", pb)], writes=[RAW])
                            gidx = blk * 8 + g
                            S.op("pool", lambda e, P=P, gidx=gidx, raw=raw: e.tensor_copy(
                                out=scpb[0:P, gidx, :], in_=raw[0:P, 2048:2051]), reads=[RAW], writes=[("scpb", gidx)])
                            cwt = (cwq[:, blk * 8 + g, :] if blk < 2 else cwv[:, g, :])
                            CH = CAC
                            S.op("dve", lambda e, P=P, cwt=cwt, raw=raw, cac=cac: e.tensor_scalar(
                                out=cac[0:P, :], in0=raw[0:P, 3:2051], scalar1=cwt[:, 3:4], scalar2=None, op0=ALU.mult),
                                 reads=[RAW, ("raw0", rb_), "cwq", "cwv"], writes=[CH])
                            for j in (2, 1, 0):
                                S.op("dve", lambda e, P=P, cwt=cwt, j=j, raw=raw, cac=cac: e.scalar_tensor_tensor(
                                    out=cac[0:P, :], in0=raw[0:P, j:j + 2048], scalar=cwt[:, j:j + 1], in1=cac[0:P, :],
                                    op0=ALU.mult, op1=ALU.add), reads=[RAW, ("raw0", rb_), CH], writes=[CH])
                          def group_tail(blk, g, P, raw, cac, RAW, CAC, rb_):
                            CHS = [CAC]
                            if blk == 2:
                                S.op("act", lambda e, g=g, cac=cac: e.activation(out=vT[:, g, :], in_=cac[:, :], func=AF.Silu),
                                     reads=CHS, writes=[("vT", g)])
                                return
                            S.op("act", lambda e, cac=cac: e.activation(out=cac[0:64, :], in_=cac[0:64, :], func=AF.Silu),
                                 reads=CHS, writes=[CAC])
                            S.op("act", lambda e, cac=cac, raw=raw: e.square(out=raw[0:64, 3:2051], in_=cac[0:64, :]),
                                 reads=[CAC], writes=[RAW])
                            dst = qh if blk == 0 else kh
                            for tb in range(4):
                                pb = 2 + (tb % 2)
                                rb = tb % 2
                                S.op("pe", lambda e, pb=pb, tb=tb, raw=raw: e.matmul(
                                    psb[pb][0:64, :], ONES, raw[0:64, 3 + tb * 512:3 + (tb + 1) * 512], start=True, stop=True),
                                     reads=[RAW, "gcn"], writes=[("psb", pb)])
                                sc = 64.0 if blk == 0 else 1.0
                                S.op("act", lambda e, pb=pb, rb=rb, sc=sc, blk=blk: e.activation(
                                    out=rin[rb][:], in_=psb[pb][0:64, :], func=AF.Ln, scale=sc, bias=epsb[:, blk:blk + 1]),
                                     reads=[("psb", pb), "epsb"], writes=[("rin", rb)])
                                S.op("act", lambda e, rb=rb: e.activation(out=rin[rb][:], in_=rin[rb][:], func=AF.Exp,
                                                                          scale=-0.5),
                                     reads=[("rin", rb)], writes=[("rin", rb)])
                                S.op("pool", lambda e, rb=rb, tb=tb, dst=dst, g=g, cac=cac: e.tensor_tensor(
                                    out=dst[:, g, tb * 512:(tb + 1) * 512], in0=cac[0:64, tb * 512:(tb + 1) * 512],
                                    in1=rin[rb][:], op=ALU.mult),
                                     reads=[CAC, ("rin", rb)], writes=[("qk", blk, g)])
                          head_ops = S.capture(lambda: group_body("head"))
                          S.emit_merged(head_ops, pending_tail)
                          pending_tail = S.capture(lambda: group_body("tail"))
                          gi += 1
                          cnt += 4
                    S.emit_merged([], pending_tail)
                    for blk_ in range(3):
                        ng_, P_ = (8, 64) if blk_ < 2 else (4, 128)
                        for g_ in range(ng_):
                            col0 = blk_ * 512 + g_ * P_
                            gidx = blk_ * 8 + g_
                            S.dma("sp" if g_ % 2 else "act", lambda e, P_=P_, col0=col0, gidx=gidx: e.dma_start(
                                out=o_scp[:, col0:col0 + P_].rearrange("j p -> p j"), in_=scpb[0:P_, gidx, :],
                                allow_slow_non_contiguous=True), reads=[("scpb", gidx)], writes=[("o_scp", col0)])
                    S.flush()

                gt = lambda name: sb2(name, [64, 32, 8])
                ba = sb2("ba", [64, 32, 16])
                beta, nbeta, gg, gc, gcl, eg, egl, ekd, nbeg, alr, dtr = [gt(n) for n in (
                    "beta", "nbeta", "gg", "gc", "gcl", "eg", "egl", "ekd", "nbeg", "alr", "dtr")]
                S.dma("sp", lambda e: e.dma_start(out=alr[:], in_=bass.AP(gvec.tensor, 0, [[0, 64], [0, 32], [1, 8]])),
                      writes=["alr"])
                S.dma("sp", lambda e: e.dma_start(out=dtr[:], in_=bass.AP(gvec.tensor, 64, [[0, 64], [0, 32], [1, 8]])),
                      writes=["dtr"])
                for n in range(32):
                    for kc in range(8):
                        S.op("pe", lambda e, n=n, kc=kc: e.matmul(
                            psb[0][0:64, n * 16:(n + 1) * 16], x1T[:, kc, 128 + n * 64:128 + (n + 1) * 64], wba[:, kc, :],
                            start=(kc == 0), stop=(kc == 7)), reads=["wba"], writes=[("psb", 0)])
                S.op("dve", lambda e: e.tensor_copy(out=ba[:], in_=v3(psb[0][0:64, :], 32)), reads=[("psb", 0)],
                     writes=["ba"])
                S.op("act", lambda e: e.activation(out=beta[:], in_=ba[:, :, 0:8], func=AF.Sigmoid), reads=["ba"],
                     writes=["beta"])
                S.op("dve", lambda e: e.tensor_scalar(out=nbeta[:], in0=beta[:], scalar1=-1.0, scalar2=None, op0=ALU.mult),
                     reads=["beta"], writes=["nbeta"])
                S.op("dve", lambda e: e.tensor_tensor(out=gg[:], in0=ba[:, :, 8:16], in1=dtr[:], op=ALU.add),
                     reads=["ba", "dtr"], writes=["gg"])
                S.op("act", lambda e: e.activation(out=gg[:], in_=gg[:], func=AF.Exp), reads=["gg"], writes=["gg"])
                S.op("act", lambda e: e.activation(out=gg[:], in_=gg[:], func=AF.Ln, bias=ONES[:, 0:1]),
                     reads=["gg", "gcn"], writes=["gg"])
                S.op("act", lambda e: e.activation(out=alr[:], in_=alr[:], func=AF.Exp), reads=["alr"], writes=["alr"])
                S.op("dve", lambda e: e.scalar_tensor_tensor(out=gg[:], in0=gg[:], scalar=-1.0, in1=alr[:], op0=ALU.mult,
                                                             op1=ALU.mult), reads=["gg", "alr"], writes=["gg"])
                gg2 = lambda t: t[:].rearrange("p n h -> p (n h)")
                S.op("pe", lambda e: e.matmul(psb[1][0:64, 0:256], TRI, gg2(gg), start=True, stop=True),
                     reads=["gg", "gcn"], writes=[("psb", 1)])
                S.op("dve", lambda e: e.tensor_copy(out=gg2(gc), in_=psb[1][0:64, 0:256]), reads=[("psb", 1)],
                     writes=["gc"])
                S.op("pe", lambda e: e.matmul(psb[2][0:64, 0:256], SEL, gg2(gc), start=True, stop=True),
                     reads=["gc", "gcn"], writes=[("psb", 2)])
                S.op("dve", lambda e: e.tensor_copy(out=gg2(gcl), in_=psb[2][0:64, 0:256]), reads=[("psb", 2)],
                     writes=["gcl"])
                S.op("act", lambda e: e.activation(out=eg[:], in_=gc[:], func=AF.Exp), reads=["gc"], writes=["eg"])
                S.op("act", lambda e: e.activation(out=egl[:], in_=gcl[:], func=AF.Exp), reads=["gcl"], writes=["egl"])
                S.op("dve", lambda e: e.tensor_tensor(out=ekd[:], in0=gcl[:], in1=gc[:], op=ALU.subtract),
                     reads=["gc", "gcl"], writes=["ekd"])
                S.op("act", lambda e: e.activation(out=ekd[:], in_=ekd[:], func=AF.Exp), reads=["ekd"], writes=["ekd"])
                S.op("dve", lambda e: e.tensor_tensor(out=nbeg[:], in0=nbeta[:], in1=eg[:], op=ALU.mult),
                     reads=["nbeta", "eg"], writes=["nbeg"])
                S.flush()

                f3 = lambda name: sb2(name, [64, 8, 64])
                b3 = lambda name: sb2(name, [64, 8, 64], BF16)
                kvns = [sb2("kvn%d" % i, [64, 1024], BF16) for i in range(2)]
                zss = [sb2("zs%d" % i, [64, 512]) for i in range(2)]
                Dg, Db, m1, e1, e2, b1, c1, Tf, vb, tt, ob, osq, Sf = [f3(n) for n in (
                    "Dg", "Db", "m1", "e1", "e2", "b1", "c1", "Tf", "vb", "tt", "ob", "osq", "Sf")]
                Bb = [b3("Bb0"), b3("Bb1")]
                Cb = [b3("Cb0"), b3("Cb1")]
                intraTs = [b3("intraT0"), b3("intraT1")]
                Tbs = [b3("Tb0"), b3("Tb1")]
                rn, vn, vns, Sb = [b3(n) for n in ("rn", "vn", "vns", "Sb")]
                go = sb2("go", [64, 512], BF16)
                ss = sb2("ss", [64, 8])
                S.op("pool", lambda e: e.memset(Sf[:], 0.0), writes=["Sf"])
                S.op("pool", lambda e: e.memset(Sb[:], 0.0), writes=["Sb"])
                ps3 = lambda i: v3(psb[i][0:64, :])
                p6b = psb[6][:, :].bitcast(BF16)

                def pre(n):
                    c0 = n * 64
                    xc0 = 128 + c0
                    q = n % 2
                    kvn, zs, intraT, Tb = kvns[q], zss[q], intraTs[q], Tbs[q]
                    KVN, ZS, INT, TB = ("kvn", q), ("zs", q), ("intraT", q), ("Tb", q)
                    for h in range(8):
                        S.op("pe", lambda e, h=h: e.transpose(pst[0:64, h * 64:(h + 1) * 64], kh[:, h, c0:c0 + 64],
                                                              identb[0:64, 0:64]),
                             reads=[("qk", 1, h), "identb"], writes=["pst"])
                    for pr in range(4):
                        S.op("pe", lambda e, pr=pr: e.transpose(pst[0:64, 512 + pr * 128:512 + (pr + 1) * 128],
                                                                vT[:, pr, c0:c0 + 64], identb[:, :]),
                             reads=[("vT", pr), "identb"], writes=["pst"])
                    S.op("dve", lambda e: e.tensor_copy(out=kvn[:], in_=pst[0:64, :]), reads=["pst"], writes=[KVN])
                    for kc in range(8):
                        S.op("pe", lambda e, kc=kc: e.matmul(psb[2][0:64, :], x1T[:, kc, xc0:xc0 + 64], wz[:, kc, :],
                                                             start=(kc == 0), stop=(kc == 7)),
                             reads=["wz"], writes=[("psb", 2)])
                    S.op("act", lambda e: e.activation(out=zs[:], in_=psb[2][0:64, :], func=AF.Silu), reads=[("psb", 2)],
                         writes=[ZS])
                    for h in range(8):
                        S.op("pe", lambda e, h=h: e.matmul(psb[0][0:64, h * 64:(h + 1) * 64], kh[:, h, c0:c0 + 64],
                                                           kh[:, h, c0:c0 + 64], start=True, stop=True),
                             reads=[("qk", 1, h)], writes=[("psb", 0)])
                    for h in range(8):
                        S.op("pe", lambda e, h=h: e.matmul(psb[1][0:64, h * 64:(h + 1) * 64], kh[:, h, c0:c0 + 64],
                                                           qh[:, h, c0:c0 + 64], start=True, stop=True),
                             reads=[("qk", 1, h), ("qk", 0, h)], writes=[("psb", 1)])
                    gcn_ = gc[:, n, :]
                    S.op("pool", lambda e: e.tensor_tensor(out=Dg[:], in0=bm(ID64), in1=bl(gcn_), op=ALU.mult),
                         reads=["gc", "gcn"], writes=["Dg"])
                    S.op("pool", lambda e: e.tensor_tensor(out=Db[:], in0=bm(ID64), in1=bl(beta[:, n, :]), op=ALU.mult),
                         reads=["beta", "gcn"], writes=["Db"])
                    S.op("pe", lambda e: e.matmul(psb[2][0:64, :], ONES, Dg[:].rearrange("p h s -> p (h s)"), start=True,
                                                  stop=True), reads=["Dg", "gcn"], writes=[("psb", 2)])
                    S.op("pe", lambda e: e.matmul(psb[3][0:64, :], ONES, Db[:].rearrange("p h s -> p (h s)"), start=True,
                                                  stop=True), reads=["Db", "gcn"], writes=[("psb", 3)])
                    S.op("pool", lambda e: e.tensor_tensor(out=m1[:], in0=bl(gcn_), in1=bm(NEGS), op=ALU.add),
                         reads=["gc", "gcn"], writes=["m1"])
                    S.op("dve", lambda e: e.scalar_tensor_tensor(out=e1[:], in0=ps3(2), scalar=-1.0, in1=m1[:], op0=ALU.mult,
                                                                 op1=ALU.add), reads=[("psb", 2), "m1"], writes=["e1"])
                    S.op("act", lambda e: e.activation(out=e1[:], in_=e1[:], func=AF.Exp), reads=["e1"], writes=["e1"])
                    S.op("pool", lambda e: e.tensor_tensor(out=m1[:], in0=bm(NEGT), in1=bl(gcn_), op=ALU.subtract),
                         reads=["gc", "gcn"], writes=["m1"])
                    S.op("dve", lambda e: e.tensor_tensor(out=e2[:], in0=ps3(2), in1=m1[:], op=ALU.add),
                         reads=[("psb", 2), "m1"], writes=["e2"])
                    S.op("act", lambda e: e.activation(out=e2[:], in_=e2[:], func=AF.Exp), reads=["e2"], writes=["e2"])
                    S.op("dve", lambda e: e.tensor_tensor(out=b1[:], in0=ps3(0), in1=e1[:], op=ALU.mult),
                         reads=[("psb", 0), "e1"], writes=["b1"])
                    S.op("pool", lambda e: e.tensor_tensor(out=Bb[0][:], in0=b1[:], in1=bl(nbeta[:, n, :]), op=ALU.mult),
                         reads=["b1", "nbeta"], writes=[("Bb", 0)])
                    S.op("dve", lambda e: e.tensor_tensor(out=c1[:], in0=ps3(0), in1=e2[:], op=ALU.mult),
                         reads=[("psb", 0), "e2"], writes=["c1"])
                    S.op("dve", lambda e: e.tensor_tensor(out=c1[:], in0=c1[:], in1=ps3(3), op=ALU.mult),
                         reads=[("psb", 3), "c1"], writes=["c1"])
                    S.op("pool", lambda e: e.tensor_tensor(out=c1[:], in0=c1[:], in1=bm(MSKT), op=ALU.mult),
                         reads=["c1", "gcn"], writes=["c1"])
                    S.op("act", lambda e: e.copy(out=Cb[0][:], in_=c1[:]), reads=["c1"], writes=[("Cb", 0)])
                    S.op("dve", lambda e: e.tensor_tensor(out=intraT[:], in0=ps3(1), in1=e2[:], op=ALU.mult),
                         reads=[("psb", 1), "e2"], writes=[INT])
                    S.op("pool", lambda e: e.tensor_tensor(out=Tf[:], in0=c1[:], in1=bm(ID64), op=ALU.add),
                         reads=["c1", "gcn"], writes=["Tf"])
                    S.op("act", lambda e: e.copy(out=Tb[:], in_=Tf[:]), reads=["Tf"], writes=[TB])
                    for k in range(1, 6):
                        cur, nxt = (k - 1) % 2, k % 2
                        for h in range(8):
                            S.op("pe", lambda e, h=h, cur=cur: e.matmul(psb[2][0:64, h * 64:(h + 1) * 64], Cb[cur][:, h, :],
                                                                        Bb[cur][:, h, :], start=True, stop=True),
                                 reads=[("Cb", cur), ("Bb", cur)], writes=[("psb", 2)])
                        if k < 5:
                            for h in range(8):
                                S.op("pe", lambda e, h=h, cur=cur: e.matmul(psb[3][0:64, h * 64:(h + 1) * 64], Bb[cur][:, h, :],
                                                                            Cb[cur][:, h, :], start=True, stop=True),
                                     reads=[("Cb", cur), ("Bb", cur)], writes=[("psb", 3)])
                        S.op("act", lambda e, nxt=nxt: e.activation(out=Bb[nxt][:], in_=ps3(2), func=AF.Copy),
                             reads=[("psb", 2)], writes=[("Bb", nxt)])
                        if k < 5:
                            S.op("dve", lambda e, nxt=nxt: e.tensor_copy(out=Cb[nxt][:], in_=ps3(3)),
                                 reads=[("psb", 3)], writes=[("Cb", nxt)])
                        for h in range(8):
                            S.op("pe", lambda e, h=h, nxt=nxt: e.matmul(psb[1][0:64, h * 64:(h + 1) * 64], Bb[nxt][:, h, :],
                                                                        Tb[:, h, :], start=True, stop=True),
                                 reads=[("Bb", nxt), TB], writes=[("psb", 1)])
                        S.op("dve", lambda e: e.tensor_tensor(out=Tb[:], in0=Tb[:], in1=ps3(1), op=ALU.add),
                             reads=[("psb", 1), TB], writes=[TB])

                def seq(n):
                    c0 = n * 64
                    q = n % 2
                    kvn, zs, intraT, Tb = kvns[q], zss[q], intraTs[q], Tbs[q]
                    KVN, ZS, INT, TB = ("kvn", q), ("zs", q), ("intraT", q), ("Tb", q)
                    S.op("pool", lambda e: e.tensor_tensor(out=vb[:], in0=v3(kvn[:, 512:1024]), in1=bl(beta[:, n, :]),
                                                           op=ALU.mult), reads=[KVN, "beta"], writes=["vb"])
                    for h in range(8):
                        S.op("pe", lambda e, h=h: e.matmul(psb[4][0:64, h * 64:(h + 1) * 64], kh[:, h, c0:c0 + 64],
                                                           Sb[:, h, :], start=True, stop=True),
                             reads=[("qk", 1, h), "Sb"], writes=[("psb", 4)])
                    S.op("dve", lambda e: e.tensor_tensor(out=tt[:], in0=ps3(4), in1=bl(nbeg[:, n, :]), op=ALU.mult),
                         reads=[("psb", 4), "nbeg"], writes=["tt"])
                    S.op("pool", lambda e: e.tensor_tensor(out=rn[:], in0=tt[:], in1=vb[:], op=ALU.add),
                         reads=["tt", "vb"], writes=["rn"])
                    for h in range(8):
                        S.op("pe", lambda e, h=h: e.matmul(psb[5][0:64, h * 64:(h + 1) * 64], Tb[:, h, :], rn[:, h, :],
                                                           start=True, stop=True),
                             reads=[TB, "rn"], writes=[("psb", 5)])
                    S.op("act", lambda e: e.activation(out=vn[:], in_=ps3(5), func=AF.Copy), reads=[("psb", 5)],
                         writes=["vn"])
                    S.op("pool", lambda e: e.tensor_tensor(out=vns[:], in0=vn[:], in1=bl(ekd[:, n, :]), op=ALU.mult),
                         reads=["vn", "ekd"], writes=["vns"])
                    for h in range(8):
                        S.op("pe", lambda e, h=h: e.matmul(psb[6][0:64, h * 64:(h + 1) * 64], qh[:, h, c0:c0 + 64],
                                                           Sb[:, h, :], start=True, stop=True),
                             reads=[("qk", 0, h), "Sb"], writes=[("psb", 6)])
                    for h in range(8):
                        S.op("pe", lambda e, h=h: e.matmul(psb[4][0:64, h * 64:(h + 1) * 64], intraT[:, h, :], vn[:, h, :],
                                                           start=True, stop=True),
                             reads=[INT, "vn"], writes=[("psb", 4)])
                    S.op("dve", lambda e: e.tensor_tensor(out=ob[:], in0=ps3(6), in1=bl(eg[:, n, :]), op=ALU.mult),
                         reads=[("psb", 6), "eg"], writes=["ob"])
                    S.op("dve", lambda e: e.tensor_tensor(out=ob[:], in0=ob[:], in1=ps3(4), op=ALU.add),
                         reads=[("psb", 4), "ob"], writes=["ob"])
                    for h in range(8):
                        S.op("pe", lambda e, h=h: e.matmul(psb[5][0:64, h * 64:(h + 1) * 64], kvn[:, h * 64:(h + 1) * 64],
                                                           vns[:, h, :], start=True, stop=True),
                             reads=[KVN, "vns"], writes=[("psb", 5)])
                    S.op("pool", lambda e: e.tensor_tensor(out=Sf[:], in0=Sf[:], in1=bl(egl[:, n, :]), op=ALU.mult),
                         reads=["Sf", "egl"], writes=["Sf"])
                    S.op("dve", lambda e: e.tensor_tensor(out=Sf[:], in0=Sf[:], in1=ps3(5), op=ALU.add),
                         reads=[("psb", 5), "Sf"], writes=["Sf"])
                    S.op("act", lambda e: e.copy(out=Sb[:], in_=Sf[:]), reads=["Sf"], writes=["Sb"])
                    S.op("pool", lambda e: e.tensor_tensor(out=osq[:], in0=ob[:], in1=ob[:], op=ALU.mult), reads=["ob"],
                         writes=["osq"])
                    S.op("dve", lambda e: e.tensor_reduce(out=ss[:], in_=osq[:], axis=AX.X, op=ALU.add), reads=["osq"],
                         writes=["ss"])
                    S.op("dve", lambda e: e.tensor_scalar(out=ss[:], in0=ss[:], scalar1=1.0 / 64.0, scalar2=1e-6,
                                                          op0=ALU.mult, op1=ALU.add), reads=["ss"], writes=["ss"])
                    S.op("act", lambda e: e.activation(out=ss[:], in_=ss[:], func=AF.Ln), reads=["ss"], writes=["ss"])
                    S.op("act", lambda e: e.activation(out=ss[:], in_=ss[:], func=AF.Exp, scale=-0.5), reads=["ss"],
                         writes=["ss"])
                    S.op("dve", lambda e: e.tensor_tensor(out=ob[:], in0=ob[:], in1=bl(ss[:, :]), op=ALU.mult),
                         reads=["ob", "ss"], writes=["ob"])
                    S.op("pool", lambda e: e.tensor_tensor(out=ob[:], in0=ob[:], in1=bm(nwr[:, :]), op=ALU.mult),
                         reads=["ob", "nwr"], writes=["ob"])
                    S.op("dve", lambda e: e.tensor_tensor(out=v3(go[:, :]), in0=ob[:], in1=v3(zs[:, :]), op=ALU.mult),
                         reads=["ob", ZS], writes=["go"])
                    for pr in range(4):
                        S.op("pe", lambda e, pr=pr: e.transpose(p6b[:, pr * 64:(pr + 1) * 64], go[:, pr * 128:(pr + 1) * 128],
                                                                identb[0:64, 0:64]),
                             reads=["go", "identb"], writes=[("psb", 6)])
                    S.op("dve", lambda e: e.tensor_copy(out=gdnT[:, :, c0:c0 + 64], in_=v3(p6b[:, 0:256], 4)),
                         reads=[("psb", 6)], writes=[("gdnT", n)])

                pre(0)
                for n in range(32):
                    sq_ = S.capture(lambda: seq(n))
                    pr_ = S.capture(lambda: pre(n + 1)) if n + 1 < 32 else []
                    S.emit_merged(pr_, sq_)
                S.dma("sp", lambda e: e.dma_start(out=o_sgp.rearrange("h d e -> d h e"), in_=Sf[:]), reads=["Sf"],
                      writes=["o_sgp"])
                if dbg:
                    S.dma("pool", lambda e: e.dma_start(out=gdn_dbg[:, :, :], in_=gdnT[:]),
                          reads=[("gdnT", n) for n in range(32)], writes=["gdn_dbg"])
                S.flush()

        def sample_proj_stage():
            with contextlib.ExitStack() as st2:
                sb2 = lambda name, shape, dt=F32: st2.enter_context(nc.sbuf_tensor("sp" + name, list(shape), dt))
                wb = [sb2("wb%d" % i, [128, 8, 512], BF16) for i in range(2)]
                ot = [sb2("ot%d" % i, [128, 512]) for i in range(2)]
                winv = w_in.rearrange("(kc p) f -> p kc f", p=128)
                blocks = [(0, 512, qs_s[:, :], 0.125), (1536, 512, gq_s[:, 0:512], 1.0), (2048, 512, gq_s[:, 512:1024], 1.0),
                          (2560, 512, gq_s[:, 1024:1536], 1.0), (3072, 512, z_s[:, :], 1.0), (3584, 16, ba_s[:, :], 1.0)]
                for i, (c0, w, dst, sc) in enumerate(blocks):
                    b = i % 2
                    S.dma("pool", lambda e, b=b, c0=c0, w=w: e.dma_start(out=wb[b][:, :, 0:w], in_=winv[:, :, c0:c0 + w]),
                          writes=[("wb", b)])
                    for kc in range(8):
                        S.op("pe", lambda e, b=b, kc=kc, w=w: e.matmul(psb[b][:, 0:w], x1T[:, kc, 0:128], wb[b][:, kc, 0:w],
                                                                       start=(kc == 0), stop=(kc == 7)),
                             reads=[("wb", b)], writes=[("psb", b)])
                    S.op("act", lambda e, b=b, w=w, sc=sc: e.mul(out=ot[b][:, 0:w], in_=psb[b][:, 0:w], mul=sc),
                         reads=[("psb", b)], writes=[("ot", b)])
                    S.dma("sp", lambda e, b=b, w=w, dst=dst: e.dma_start(out=dst, in_=ot[b][0:64, 0:w]),
                          reads=[("ot", b)], writes=[("sscr", i)])
                S.dma("sp", lambda e: e.dma_start(
                    out=o_scs[:, :, :], in_=bass.AP(gq_s.tensor, 1536, [[4 * 1536, 16], [1536, 3], [1, 1536]])),
                      reads=[("sscr", 1), ("sscr", 2), ("sscr", 3)], writes=["o_scs"])
                S.flush()

        def sample_attn_stage():
            with contextlib.ExitStack() as st2:
                sb2 = lambda name, shape, dt=F32: st2.enter_context(nc.sbuf_tensor("sa" + name, list(shape), dt))
                kt = [sb2("kt%d" % i, [128, 8, 512]) for i in range(2)]
                vt = [sb2("vt%d" % i, [128, 8, 512]) for i in range(2)]
                vtb = [sb2("vtb%d" % i, [128, 8, 512], BF16) for i in range(2)]
                ktT = [sb2("ktT%d" % i, [128, 512], BF16) for i in range(2)]
                qsT = sb2("qsT", [128, 4, 64])
                qblk = sb2("qblk", [128, 4, 16, 8], BF16)
                wqb = sb2("wqb", [128, 8, 512], BF16)
                Pm = sb2("Pm", [128, 8, 32])
                Pmb = sb2("Pmb", [128, 8, 32], BF16)
                Pj = sb2("Pj", [128, 32])
                mtab = sb2("mtab", [128, 8, 32])
                wcs = sb2("wcs", [32, 8, 4, 128])
                eb = sb2("eb", [32, 8])
                ones1 = sb2("ones1", [128, 1])
                dmk = sb2("dmk", [32, 8])
                osel = sb2("osel", [32, 8, 64])
                osb = sb2("osb", [32, 64])
                rs = sb2("rs", [32, 1])
                S.dma("sp", lambda e: e.dma_start(out=wcs[:], in_=wcm[:, :, :, :]), writes=["wcs"])
                S.dma("sp", lambda e: e.dma_start(out=eb[:], in_=relb[:, :]), writes=["eb"])
                S.dma("sp", lambda e: e.dma_start(out=dmk[:], in_=dmask[:, :]), writes=["dmk"])
                S.op("act", lambda e: e.activation(out=eb[:], in_=eb[:], func=AF.Exp), reads=["eb"], writes=["eb"])
                S.op("pool", lambda e: e.memset(ones1[:], 1.0), writes=["ones1"])
                S.op("pool", lambda e: e.memset(qblk[:], 0.0), writes=["qblk0"])
                S.dma("pool", lambda e: e.dma_start(out=wqb[:], in_=w_in.rearrange("(kc p) f -> p kc f", p=128)[:, :, 0:512]),
                      writes=["wqb"])
                for pr in range(4):
                    for kc in range(8):
                        S.op("pe", lambda e, pr=pr, kc=kc: e.matmul(psb[6][:, pr * 64:(pr + 1) * 64],
                                                                    wqb[:, kc, pr * 128:(pr + 1) * 128], x1T[:, kc, 0:64],
                                                                    start=(kc == 0), stop=(kc == 7)),
                             reads=["wqb"], writes=[("psb", 6)])
                S.op("act", lambda e: e.mul(out=qsT[:].rearrange("p a b -> p (a b)"), in_=psb[6][:, 0:256], mul=0.125),
                     reads=[("psb", 6)], writes=["qsT"])
                qv = qsT[:].rearrange("p a (b t) -> p a b t", t=4)
                S.op("dve", lambda e: e.tensor_copy(out=qblk[0:64, :, :, 0:4], in_=qv[0:64]), reads=["qsT", "qblk0"],
                     writes=["qblkA"])
                S.op("dve", lambda e: e.tensor_copy(out=qblk[64:128, :, :, 4:8], in_=qv[64:128]), reads=["qsT", "qblk0"],
                     writes=["qblkB"])
                for i in range(2):
                    S.op("pool", lambda e, i=i: e.memset(kt[i][:, 7, :], 0.0), writes=[("kt7", i)])
                    S.op("pool", lambda e, i=i: e.memset(vt[i][:, 7, :], 0.0), writes=[("vt7", i)])
                for j in range(8):
                    for t in range(4):
                        S.op("pe", lambda e, j=j, t=t: e.matmul(psb[0][:, (j * 4 + t) * 8:(j * 4 + t + 1) * 8], wcs[:, j, t, :],
                                                                eb[:, :], start=True, stop=True),
                             reads=["wcs", "eb"], writes=[("psb", 0)])
                S.op("dve", lambda e: e.tensor_copy(out=mtab[:].rearrange("p j (h t) -> p j h t", h=8),
                                                    in_=psb[0][:, 0:256].rearrange("p (j t h) -> p j h t", j=8, t=4)),
                     reads=[("psb", 0)], writes=["mtab"])
                for b in range(16):
                    bb = b % 2
                    for src, dstt, onew, nm in ((ck, kt[bb], o_ks, "kt"), (cv, vt[bb], o_vs, "vt")):
                        S.dma("sp", lambda e, src=src, dstt=dstt, b=b: e.dma_start(
                            out=dstt[:, 0:4, :], in_=src[b, 1536:2048, :].rearrange("(j p) e -> p j e", p=128)),
                              writes=[(nm, bb, 0)])
                        for r in range(4):
                            S.dma("act" if r % 2 else "sp", lambda e, src=src, dstt=dstt, b=b, r=r: e.dma_start(
                                out=dstt[r:128:4, 4:7, :],
                                in_=bass.AP(src.tensor, b * 2048 * 512 + r * 512, [[16 * 512, 32], [32 * 16 * 512, 3], [1, 512]])),
                                  writes=[(nm, bb, 1 + r)])
                        S.dma("act", lambda e, dstt=dstt, onew=onew, b=b: e.dma_start(
                            out=dstt[0:4, 7, :], in_=onew[b * 4:(b + 1) * 4, :]),
                              reads=[("okv", 1, 0), ("okv", 2, 0), (nm + "7", bb)], writes=[(nm, bb, 5)])
                    pL = 5 + b % 2
                    for j in range(8):
                        pk = 3 + (j % 2)
                        kb = j % 2
                        for pr in range(4):
                            S.op("pe", lambda e, pk=pk, j=j, pr=pr, bb=bb: e.transpose(
                                psb[pk][:, pr * 128:(pr + 1) * 128], kt[bb][:, j, pr * 128:(pr + 1) * 128], identf[:, :]),
                                 reads=[("kt", bb, i_) for i_ in range(6)] + ["identf"], writes=[("psb", pk)])
                        if j % 2:
                            S.op("act", lambda e, pk=pk, kb=kb: e.activation(out=ktT[kb][:], in_=psb[pk][:, :], func=AF.Copy),
                                 reads=[("psb", pk)], writes=[("ktT", kb)])
                        else:
                            S.op("dve", lambda e, pk=pk, kb=kb: e.tensor_copy(out=ktT[kb][:], in_=psb[pk][:, :]),
                                 reads=[("psb", pk)], writes=[("ktT", kb)])
                        if j % 2:
                            S.op("pool", lambda e, j=j, bb=bb: e.tensor_copy(out=vtb[bb][:, j, :], in_=vt[bb][:, j, :]),
                                 reads=[("vt", bb, i_) for i_ in range(6)], writes=[("vtb", bb, j)])
                        else:
                            S.op("act", lambda e, j=j, bb=bb: e.copy(out=vtb[bb][:, j, :], in_=vt[bb][:, j, :]),
                                 reads=[("vt", bb, i_) for i_ in range(6)], writes=[("vtb", bb, j)])
                        for pr in range(4):
                            S.op("pe", lambda e, j=j, pr=pr, kb=kb, b=b, pL=pL: e.matmul(
                                psb[pL][:, (j * 8 + 2 * pr) * 4:(j * 8 + 2 * pr) * 4 + 8], ktT[kb][:, pr * 128:(pr + 1) * 128],
                                qblk[:, pr, b, :], start=True, stop=True),
                                 reads=[("ktT", kb), "qblkA", "qblkB", "qblk0"], writes=[("psb", pL)])
                    S.op("act", lambda e, pL=pL: e.activation(out=Pm[:].rearrange("p j x -> p (j x)"), in_=psb[pL][:, 0:256],
                                                             func=AF.Exp), reads=[("psb", pL)], writes=["Pm"])
                    S.op("dve", lambda e: e.tensor_tensor(out=Pmb[:], in0=Pm[:], in1=mtab[:], op=ALU.mult),
                         reads=["Pm", "mtab"], writes=["Pmb"])
                    S.op("dve", lambda e: e.tensor_reduce(out=Pj[:], in_=Pmb[:].rearrange("p j x -> p x j"), axis=AX.X,
                                                          op=ALU.add), reads=["Pmb"], writes=["Pj"])
                    for j in range(8):
                        S.op("pe", lambda e, j=j, bb=bb: e.matmul(
                            psb[1][0:32, :], Pmb[:, j, :], vtb[bb][:, j, :], start=(j == 0), stop=(j == 7)),
                             reads=["Pmb", ("vtb", bb, j)], writes=[("psb", 1)])
                    S.op("pe", lambda e: e.matmul(psb[2][0:32, 0:1], Pj[:, :], ones1[:, :], start=True, stop=True),
                         reads=["Pj", "ones1"], writes=[("psb", 2)])
                    S.op("dve", lambda e: e.reciprocal(out=rs[:], in_=psb[2][0:32, 0:1]), reads=[("psb", 2)], writes=["rs"])
                    S.op("dve", lambda e: e.tensor_tensor(out=osel[:], in0=v3(psb[1][0:32, :]), in1=bl(dmk[:, :]),
                                                          op=ALU.mult), reads=[("psb", 1), "dmk"], writes=["osel"])
                    S.op("dve", lambda e: e.tensor_reduce(out=osb[:], in_=osel[:].rearrange("p h e -> p e h"), axis=AX.X,
                                                          op=ALU.add), reads=["osel"], writes=["osb"])
                    S.op("dve", lambda e: e.tensor_scalar(out=osb[:], in0=osb[:], scalar1=rs[:, 0:1], scalar2=None,
                                                          op0=ALU.mult), reads=["osb", "rs"], writes=["osb"])
                    S.dma("pool", lambda e, b=b: e.dma_start(
                        out=bass.AP(heads_s.tensor, b * 4 * D, [[64, 8], [D, 4], [1, 64]]), in_=osb[:, :]),
                          reads=["osb"], writes=[("heads_a", b)])
                S.flush()

        def sample_gdn_stage():
            with contextlib.ExitStack() as st2:
                sb2 = lambda name, shape, dt=F32: st2.enter_context(nc.sbuf_tensor("sg" + name, list(shape), dt))
                Sx = sb2("S", [128, 64, 64])
                tmp = sb2("tmp", [128, 64, 64])
                xq = sb2("xq", [128, 3, 7, 64])
                zz = sb2("zz", [128, 4, 64])
                bav = sb2("bav", [128, 4, 2])
                cw = sb2("cw", [128, 3, 4, 64])
                gv = sb2("gv", [128, 2])
                nwr = sb2("nwr", [128, 64])
                cq = sb2("cq", [128, 3, 4, 64])
                ct = sb2("ct", [128, 4, 64])
                ssq = sb2("ssq", [128, 3, 4])
                beta = sb2("beta", [128, 4])
                gg = sb2("gg", [128, 4])
                eg = sb2("eg", [128, 4])
                neg = sb2("neg", [128, 4])
                ks = sb2("ks", [128, 64])
                dl = sb2("dl", [128, 64])
                oo = sb2("oo", [128, 4, 64])
                one = sb2("one", [128, 1])
                S.op("pool", lambda e: e.memset(one[:], 1.0), writes=["one"])
                S.dma("sp", lambda e: e.dma_start(out=Sx[:].rearrange("p a b -> p (a b)"), in_=sg_in[:, :]), writes=["S"])
                S.dma("sp", lambda e: e.dma_start(out=cw[:], in_=cws[:, :, :, :]), writes=["cw"])
                S.dma("sp", lambda e: e.dma_start(out=gv[:], in_=gvs[:, :]), writes=["gv"])
                S.dma("sp", lambda e: e.dma_start(out=nwr[:], in_=gvec[2:3, :].partition_broadcast(128)), writes=["nwr"])
                for b in range(16):
                    p0 = b * 8
                    for sec in range(3):
                        q = "act" if sec == 1 else "sp"
                        S.dma(q, lambda e, b=b, p0=p0, sec=sec: e.dma_start(
                            out=xq[p0:p0 + 8, sec, 0:3, :],
                            in_=bass.AP(sc_in.tensor, b * 3 * 1536 + sec * 512, [[64, 8], [1536, 3], [1, 64]])),
                              writes=[("xq", b, sec, 0)])
                        S.dma(q, lambda e, b=b, p0=p0, sec=sec: e.dma_start(
                            out=xq[p0:p0 + 8, sec, 3:7, :],
                            in_=bass.AP(gq_s.tensor, b * 4 * 1536 + sec * 512, [[64, 8], [1536, 4], [1, 64]])),
                              reads=[("sscr", 1 + sec)], writes=[("xq", b, sec, 1)])
                    S.dma("act", lambda e, b=b, p0=p0: e.dma_start(
                        out=zz[p0:p0 + 8, :, :], in_=bass.AP(z_s.tensor, b * 4 * 512, [[64, 8], [512, 4], [1, 64]])),
                          reads=[("sscr", 4)], writes=[("zz", b)])
                    S.dma("act", lambda e, b=b, p0=p0: e.dma_start(
                        out=bav[p0:p0 + 8, :, :], in_=bass.AP(ba_s.tensor, b * 4 * 16, [[1, 8], [16, 4], [8, 2]]),
                        allow_slow_non_contiguous=True), reads=[("sscr", 5)], writes=[("bav", b)])
                for sec in range(3):
                    for j in range(4):
                        wv = cw[:, sec, j, :].unsqueeze(1).broadcast_to([128, 4, 64])
                        if j == 0:
                            S.op("dve", lambda e, sec=sec, wv=wv: e.tensor_tensor(
                                out=cq[:, sec, :, :], in0=xq[:, sec, 0:4, :], in1=wv, op=ALU.mult),
                                 reads=[("xq", b_, s_, i_) for b_ in range(16) for s_ in range(3) for i_ in range(2)] + ["cw"], writes=["cq"])
                        else:
                            S.op("pool", lambda e, sec=sec, wv=wv, j=j: e.tensor_tensor(
                                out=ct[:], in0=xq[:, sec, j:j + 4, :], in1=wv, op=ALU.mult),
                                 reads=[("xq", b_, s_, i_) for b_ in range(16) for s_ in range(3) for i_ in range(2)] + ["cw"], writes=["ct"])
                            S.op("dve", lambda e, sec=sec: e.tensor_tensor(
                                out=cq[:, sec, :, :], in0=cq[:, sec, :, :], in1=ct[:], op=ALU.add),
                                 reads=["cq", "ct"], writes=["cq"])
                S.op("act", lambda e: e.activation(out=cq[:], in_=cq[:], func=AF.Silu), reads=["cq"], writes=["cq"])
                S.op("act", lambda e: e.activation(out=zz[:], in_=zz[:], func=AF.Silu), reads=[("zz", b_) for b_ in range(16)], writes=["zz"])
                for sec in range(2):
                    S.op("pool", lambda e, sec=sec: e.tensor_tensor(out=ct[:], in0=cq[:, sec, :, :], in1=cq[:, sec, :, :],
                                                                    op=ALU.mult), reads=["cq"], writes=["ct"])
                    S.op("dve", lambda e, sec=sec: e.tensor_reduce(out=ssq[:, sec, :], in_=ct[:], axis=AX.X, op=ALU.add),
                         reads=["ct"], writes=["ssq"])
                    S.op("dve", lambda e, sec=sec: e.tensor_scalar(out=ssq[:, sec, :], in0=ssq[:, sec, :], scalar1=1e-6,
                                                                   scalar2=None, op0=ALU.add), reads=["ssq"], writes=["ssq"])
                    S.op("act", lambda e, sec=sec: e.sqrt(out=ssq[:, sec, :], in_=ssq[:, sec, :]), reads=["ssq"],
                         writes=["ssq"])
                    S.op("dve", lambda e, sec=sec: e.reciprocal(out=ssq[:, sec, :], in_=ssq[:, sec, :]), reads=["ssq"],
                         writes=["ssq"])
                    sc = 0.125 if sec == 0 else 1.0
                    S.op("dve", lambda e, sec=sec, sc=sc: e.scalar_tensor_tensor(
                        out=cq[:, sec, :, :], in0=cq[:, sec, :, :], scalar=sc, in1=bl(ssq[:, sec, :]), op0=ALU.mult,
                        op1=ALU.mult), reads=["cq", "ssq"], writes=["cq"])
                S.op("act", lambda e: e.activation(out=beta[:], in_=bav[:, :, 0], func=AF.Sigmoid), reads=[("bav", b_) for b_ in range(16)],
                     writes=["beta"])
                S.op("dve", lambda e: e.tensor_scalar(out=gg[:], in0=bav[:, :, 1], scalar1=gv[:, 1:2], scalar2=None,
                                                      op0=ALU.add), reads=[("bav", b_) for b_ in range(16)] + ["gv"], writes=["gg"])
                S.op("act", lambda e: e.activation(out=gg[:], in_=gg[:], func=AF.Exp), reads=["gg"], writes=["gg"])
                S.op("act", lambda e: e.activation(out=gg[:], in_=gg[:], func=AF.Ln, bias=one[:, 0:1]), reads=["gg", "one"],
                     writes=["gg"])
                S.op("act", lambda e: e.activation(out=gv[:, 0:1], in_=gv[:, 0:1], func=AF.Exp), reads=["gv"], writes=["gv"])
                S.op("dve", lambda e: e.tensor_scalar(out=gg[:], in0=gg[:], scalar1=gv[:, 0:1], scalar2=-1.0, op0=ALU.mult,
                                                      op1=ALU.mult), reads=["gg", "gv"], writes=["gg"])
                S.op("act", lambda e: e.activation(out=eg[:], in_=gg[:], func=AF.Exp), reads=["gg"], writes=["eg"])
                S.op("dve", lambda e: e.tensor_scalar(out=neg[:], in0=eg[:], scalar1=-1.0, scalar2=None, op0=ALU.mult),
                     reads=["eg"], writes=["neg"])
                ST = Sx[:].rearrange("p a b -> p b a")
                for t in range(4):
                    qv, kv, vv = cq[:, 0, t, :], cq[:, 1, t, :], cq[:, 2, t, :]
                    S.op("dve", lambda e, kv=kv: e.tensor_tensor(out=tmp[:], in0=ST, in1=bm(kv, 64), op=ALU.mult),
                         reads=["S", "cq"], writes=["tmp"])
                    S.op("dve", lambda e: e.tensor_reduce(out=ks[:], in_=tmp[:], axis=AX.X, op=ALU.add), reads=["tmp"],
                         writes=["ks"])
                    S.op("dve", lambda e, t=t, vv=vv: e.scalar_tensor_tensor(
                        out=dl[:], in0=ks[:], scalar=neg[:, t:t + 1], in1=vv, op0=ALU.mult, op1=ALU.add),
                         reads=["ks", "neg", "cq"], writes=["dl"])
                    S.op("dve", lambda e, t=t: e.tensor_scalar(out=dl[:], in0=dl[:], scalar1=beta[:, t:t + 1], scalar2=None,
                                                               op0=ALU.mult), reads=["dl", "beta"], writes=["dl"])
                    S.op("pool", lambda e, kv=kv: e.tensor_tensor(out=tmp[:], in0=bl(kv, 64), in1=bm(dl[:, :], 64),
                                                                  op=ALU.mult), reads=["cq", "dl"], writes=["tmp"])
                    S.op("dve", lambda e, t=t: e.scalar_tensor_tensor(
                        out=Sx[:], in0=Sx[:], scalar=eg[:, t:t + 1], in1=tmp[:], op0=ALU.mult, op1=ALU.add),
                         reads=["S", "eg", "tmp"], writes=["S"])
                    S.op("pool", lambda e, qv=qv: e.tensor_tensor(out=tmp[:], in0=ST, in1=bm(qv, 64), op=ALU.mult),
                         reads=["S", "cq"], writes=["tmp"])
                    S.op("dve", lambda e, t=t: e.tensor_reduce(out=oo[:, t, :], in_=tmp[:], axis=AX.X, op=ALU.add),
                         reads=["tmp"], writes=["oo"])
                S.dma("sp", lambda e: e.dma_start(out=o_sgs[:, :], in_=Sx[:].rearrange("p a b -> p (a b)")), reads=["S"],
                      writes=["o_sgs"])
                S.op("pool", lambda e: e.tensor_tensor(out=ct[:], in0=oo[:], in1=oo[:], op=ALU.mult), reads=["oo"],
                     writes=["ct"])
                S.op("dve", lambda e: e.tensor_reduce(out=ssq[:, 2, :], in_=ct[:], axis=AX.X, op=ALU.add), reads=["ct"],
                     writes=["ssq"])
                S.op("dve", lambda e: e.tensor_scalar(out=ssq[:, 2, :], in0=ssq[:, 2, :], scalar1=1.0 / 64.0, scalar2=1e-6,
                                                      op0=ALU.mult, op1=ALU.add), reads=["ssq"], writes=["ssq"])
                S.op("act", lambda e: e.sqrt(out=ssq[:, 2, :], in_=ssq[:, 2, :]), reads=["ssq"], writes=["ssq"])
                S.op("dve", lambda e: e.reciprocal(out=ssq[:, 2, :], in_=ssq[:, 2, :]), reads=["ssq"], writes=["ssq"])
                S.op("dve", lambda e: e.tensor_tensor(out=oo[:], in0=oo[:], in1=bl(ssq[:, 2, :]), op=ALU.mult),
                     reads=["oo", "ssq"], writes=["oo"])
                S.op("pool", lambda e: e.tensor_tensor(out=oo[:], in0=oo[:], in1=bm(nwr[:, :], 4), op=ALU.mult),
                     reads=["oo", "nwr"], writes=["oo"])
                S.op("dve", lambda e: e.tensor_tensor(out=oo[:], in0=oo[:], in1=zz[:], op=ALU.mult), reads=["oo", "zz"],
                     writes=["oo"])
                if dbg:
                    dC = dscr("dbg_cq", [128, 3, 4, 64]); dG = dscr("dbg_gates", [128, 3, 4])
                    S.dma("sp", lambda e: e.dma_start(out=dC[:, :, :, :], in_=cq[:]), reads=["cq"], writes=["dC"])
                    S.dma("sp", lambda e: e.dma_start(out=dG[:, 0, :], in_=beta[:]), reads=["beta"], writes=["dG0"])
                    S.dma("sp", lambda e: e.dma_start(out=dG[:, 1, :], in_=gg[:]), reads=["gg"], writes=["dG1"])
                    S.dma("sp", lambda e: e.dma_start(out=dG[:, 2, :], in_=eg[:]), reads=["eg"], writes=["dG2"])
                for b in range(16):
                    S.dma("sp", lambda e, b=b: e.dma_start(
                        out=bass.AP(heads_s.tensor, b * 4 * D + 512, [[64, 8], [D, 4], [1, 64]]), in_=oo[b * 8:(b + 1) * 8, :, :]),
                          reads=["oo"], writes=[("heads_g", b)])
                S.flush()

        def wout_stage():
            with contextlib.ExitStack() as st2:
                sb2 = lambda name, shape, dt=F32: st2.enter_context(nc.sbuf_tensor("wo" + name, list(shape), dt))
                attnT = sb2("attnT", [64, 8, 2048], BF16)
                woa = sb2("woa", [64, 8, D], BF16)
                wog = sb2("wog", [128, 4, D], BF16)
                won = sb2("won", [128, 8, D], BF16)
                hsf = sb2("hsf", [128, D])
                hsb = sb2("hsb", [128, D], BF16)
                hsT = sb2("hsT", [128, 8, 128], BF16)
                xs = [sb2("xs%d" % i, [128, D]) for i in range(2)]
                rr = [sb2("rr%d" % i, [128, D]) for i in range(2)]
                lnrep = sb2("ln", [128, 2, D])
                stt = sb2("st", [128, 2, 6])
                mv = sb2("mv", [128, 2])
                rstd = sb2("rstd", [128, 1])
                for i in range(2):
                    S.dma("sp", lambda e, i=i: e.dma_start(out=lnrep[:, i, :], in_=lnp[2 + i:3 + i, :].partition_broadcast(128)),
                          writes=[("lnrep", i)])
                S.dma("sp", lambda e: e.dma_start(out=attnT[:], in_=attn_s.rearrange("h d t -> d h t")),
                      reads=[("attn_s", h) for h in range(8)], writes=["attnT"])
                S.dma("pool", lambda e: e.dma_start(out=woa[:], in_=w_out[0:512, :].rearrange("(h d) o -> d h o", d=64)),
                      writes=["woa"])
                S.dma("pool", lambda e: e.dma_start(out=wog[:], in_=w_out[512:1024, :].rearrange("(c p) o -> p c o", p=128)),
                      writes=["wog"])
                S.dma("pool", lambda e: e.dma_start(out=won[:], in_=w_out.rearrange("(c p) o -> p c o", p=128)),
                      writes=["won"])
                S.op("pool", lambda e: e.memset(hsf[:], 0.0), writes=["hsf0"])
                S.dma("sp", lambda e: e.dma_start(out=hsf[0:64, :], in_=heads_s[:, :]),
                      reads=[("heads_a", b) for b in range(16)] + [("heads_g", b) for b in range(16)] + ["hsf0"],
                      writes=["hsf"])
                S.op("act", lambda e: e.copy(out=hsb[:], in_=hsf[:]), reads=["hsf"], writes=["hsb"])
                to_featmajor(hsb, "hsb", hsT, 0, "hsT")
                for t in range(NT):
                    b = t % 2
                    S.dma("sp", lambda e, b=b, t=t: e.dma_start(out=xs[b][:], in_=x1s[t * 128:(t + 1) * 128, :]),
                          reads=[("x1s", t)], writes=[("xs", b)])
                    for half in range(2):
                        pd = psb[4 + half]
                        hs_ = slice(half * 512, (half + 1) * 512)
                        if t == 0:
                            for kc in range(8):
                                S.op("pe", lambda e, pd=pd, kc=kc, hs_=hs_: e.matmul(
                                    pd[:, :], hsT[:, kc, :], won[:, kc, hs_], start=(kc == 0), stop=(kc == 7)),
                                     reads=["hsT", "won"], writes=[("psb", 4 + half)])
                        else:
                            ts_ = slice((t - 1) * 128, t * 128)
                            for h in range(8):
                                S.op("pe", lambda e, pd=pd, h=h, hs_=hs_, ts_=ts_: e.matmul(
                                    pd[:, :], attnT[:, h, ts_], woa[:, h, hs_], start=(h == 0), stop=False),
                                     reads=["attnT", "woa"], writes=[("psb", 4 + half)])
                            for c in range(4):
                                S.op("pe", lambda e, pd=pd, c=c, hs_=hs_, ts_=ts_: e.matmul(
                                    pd[:, :], gdnT[:, c, ts_], wog[:, c, hs_], start=False, stop=(c == 3)),
                                     reads=[("gdnT", n) for n in range(32)] + ["wog"], writes=[("psb", 4 + half)])
                        S.op("dve", lambda e, pd=pd, b=b, hs_=hs_: e.scalar_tensor_tensor(
                            out=rr[b][:, hs_], in0=xs[b][:, hs_], scalar=ALPHA, in1=pd[:, :], op0=ALU.mult, op1=ALU.add),
                             reads=[("xs", b), ("psb", 4 + half)], writes=[("rr", b)])
                    layernorm(rr[b], ("rr", b), lnrep, rr[b], ("rr", b), (stt, mv, rstd), epsmul=1.0)
                    S.dma("pool", lambda e, b=b, t=t: e.dma_start(out=x2s[t * 128:(t + 1) * 128, :], in_=rr[b][:]),
                          reads=[("rr", b)], writes=[("x2s", t)])
                S.flush()

        if "ffn1" not in stages:
            x1Tin = din("x1Tin", [128, 8, NTOK])
            S.dma("pool", lambda e: e.dma_start(out=x1T[:], in_=x1Tin[:, :, :]), writes=["x1Tinit"])
            S.flush()
        if "ffn1" in stages:
            def store1(t, tile, key):
                S.dma("pool", lambda e: e.dma_start(out=x1s[t * 128:(t + 1) * 128, :], in_=tile[:]), reads=[key],
                      writes=[("x1s", t)])
            ffn("f1", lambda t: xin[t * 128:(t + 1) * 128, :], f1g, f1u, f1d, 0, store1, x1T)

        gdnT = stp.enter_context(nc.sbuf_tensor("gdnT", [128, 4, 2048], BF16))
        if "attn" in stages:
            attention_stage()
        if "sproj" in stages:
            sample_proj_stage()
        if "sattn" in stages:
            sample_attn_stage()
        if "gdn" in stages:
            gdn_prompt_stage()
        if "sgdn" in stages:
            sample_gdn_stage()
        if "wout" in stages:
            wout_stage()
        S.flush()
        stp.close()
        if "ffn2" in stages:
            def store2(t, tile, key):
                if t == 0:
                    S.dma("pool", lambda e: e.dma_start(out=o_ys[:, :], in_=tile[0:64, :]), reads=[key], writes=[("oy", t)])
                else:
                    S.dma("pool", lambda e: e.dma_start(out=o_yp[(t - 1) * 128:t * 128, :], in_=tile[:]), reads=[key],
                          writes=[("oy", t)])
            ffn("f2", lambda t: x2s[t * 128:(t + 1) * 128, :], f2g, f2u, f2d, 4, store2, None)
        S.flush()
    return nc


def used_inputs(nc):
    names = set()
    for a in nc.allocations:
        try:
            if a.kind == "ExternalInput":
                names.add(a.name)
        except Exception:
            pass
    return names


def _t5_bucket(n):
    n = np.asarray(n, np.int64)
    nf = np.maximum(n, 1).astype(np.float64)
    large = 16 + np.floor(np.log(nf / 16.0) / math.log(2048 / 16.0) * 16.0 + 1e-9).astype(np.int64)
    large = np.minimum(large, 31)
    return np.where(n < 16, n, large)


def _onehot():
    oh = np.zeros((3, 33, 384), np.float32)
    for p, d in enumerate((1, 4, 16)):
        for m in range(383):
            dl = m - 127
            b = int(_t5_bucket(dl * d)) if 0 <= dl <= 128 else 32
            oh[p, b, m] = 1.0
    return oh


def _gconst():
    i = np.arange(64)
    P, Fr = i[:, None], i[None, :]
    g = np.zeros((64, 7, 64), np.float32)
    g[:, 0] = np.where(Fr < P, 0.0, NEG)
    g[:, 1] = np.where(Fr >= P, 0.0, NEG)
    g[:, 2] = np.where(Fr > P, -1.0, 0.0)
    g[:, 3] = np.eye(64)
    g[:, 4] = 1.0
    g[:, 5] = np.where(P <= Fr, 1.0, 0.0)
    g[:, 6] = np.where(P == 63, 1.0, 0.0)
    return g


def _gvec(inp):
    g = np.zeros((3, 64), np.float32)
    g[0, 0:8] = inp["gdn_a_log"][0]
    g[1, 0:8] = inp["gdn_dt_bias"][0]
    g[2, :] = inp["gdn_norm_w"][0]
    return g


def _wcm():
    w = np.zeros((32, 8, 4, 128), np.float32)
    for j in range(8):
        for p in range(128):
            if j < 4:
                pos = 1536 + j * 128 + p
            elif j < 7:
                pos = 16 * ((j - 4) * 32 + p // 4) + p % 4
            elif p < 4:
                pos = 2048 + p
            else:
                continue
            for t in range(4):
                dist = 2048 + t - pos
                if dist < 0:
                    continue
                for (win, d) in ((128, 1), (512, 4), (2048, 16)):
                    if dist % d == 0 and dist <= win:
                        w[int(_t5_bucket(dist)), j, t, p] += 1.0
    return w


def core_inputs(inp, c, big=None):
    f = np.float32
    xin = np.zeros((NTOK, D), f)
    xin[0:64] = inp["x_sample"][16 * c:16 * c + 16].reshape(64, D)
    xin[128:] = inp["x_prompt"][c]
    lnp = np.stack([inp["ln1_g"][0], inp["ln1_b"][0], inp["ln2_g"][0], inp["ln2_b"][0],
                    inp["ln3_g"][0], inp["ln3_b"][0]]).astype(f)
    m = {
        "xin": xin,
        "f1g": np.ascontiguousarray(inp["ffn1_w_gate"][0]), "f1u": np.ascontiguousarray(inp["ffn1_w_up"][0]),
        "f1d": np.ascontiguousarray(inp["ffn1_w_down"][0]),
        "lnp": lnp, "ident": np.eye(128, dtype=f),
        "w_in": np.ascontiguousarray(inp["w_in"][0]), "relb": np.ascontiguousarray(inp["rel_bias"]),
        "onehot": _onehot(), "antiid": np.ascontiguousarray(np.eye(128, dtype=f)[::-1]),
        "gconst": _gconst(), "convw": np.ascontiguousarray(inp["gdn_conv_w"][0]),
        "gvec": _gvec(inp),
        "sg_in": np.ascontiguousarray(inp["state_gdn"][0, 16 * c:16 * c + 16]).reshape(128, 4096),
        "sc_in": np.ascontiguousarray(inp["state_conv"][0, 16 * c:16 * c + 16]),
        "wcm": _wcm(),
        "dmask": np.ascontiguousarray(np.repeat(np.eye(8, dtype=f), 4, axis=0)),
        "cws": np.ascontiguousarray(np.broadcast_to(
            inp["gdn_conv_w"][0].reshape(4, 3, 8, 64).transpose(2, 1, 0, 3)[None], (16, 8, 3, 4, 64)).reshape(128, 3, 4, 64)),
        "gvs": np.ascontiguousarray(np.tile(np.stack([inp["gdn_a_log"][0], inp["gdn_dt_bias"][0]], axis=1), (16, 1))).astype(f),
        "w_out": np.ascontiguousarray(inp["w_out"][0]),
        "f2g": np.ascontiguousarray(inp["ffn2_w_gate"][0]), "f2u": np.ascontiguousarray(inp["ffn2_w_up"][0]),
        "f2d": np.ascontiguousarray(inp["ffn2_w_down"][0]),
    }
    if big is not None:
        m["ck"] = np.ascontiguousarray(big["cache_attn_k"][0, 16 * c:16 * c + 16]).reshape(16, 2048, 512)
        m["cv"] = np.ascontiguousarray(big["cache_attn_v"][0, 16 * c:16 * c + 16]).reshape(16, 2048, 512)
    return m


ALL_STAGES = ("ffn1", "attn", "sproj", "sattn", "gdn", "sgdn", "wout", "ffn2")
_NC_CACHE = {}


def gather_outputs(results):
    n = len(results)
    f = np.float32
    yp = np.stack([r["o_yp"] for r in results]).astype(f)
    ys = np.concatenate([r["o_ys"].reshape(16, 4, D) for r in results]).astype(f)
    kp = np.stack([r["o_kp"].reshape(2048, 8, 64) for r in results])[None].astype(f)
    vp = np.stack([r["o_vp"].reshape(2048, 8, 64) for r in results])[None].astype(f)
    sgp = np.stack([r["o_sgp"] for r in results])[None].astype(f)
    scp = np.stack([r["o_scp"] for r in results])[None].astype(f)
    ks = np.concatenate([r["o_ks"].reshape(16, 4, 8, 64) for r in results])[None].astype(f)
    vs = np.concatenate([r["o_vs"].reshape(16, 4, 8, 64) for r in results])[None].astype(f)
    sgs = np.concatenate([r["o_sgs"].reshape(16, 8, 64, 64) for r in results])[None].astype(f)
    scs = np.concatenate([r["o_scs"] for r in results])[None].astype(f)
    return (yp, ys, kp, vp, sgp, scp, ks, vs, sgs, scs)


def kernel(**inputs):
    inp = {k: np.asarray(v) for k, v in inputs.items()}
    if "nc" not in _NC_CACHE:
        _NC_CACHE["nc"] = build_nc(dbg=False, stages=ALL_STAGES)
    nc = _NC_CACHE["nc"]
    in_maps = [core_inputs(inp, c, inp) for c in range(8)]
    res = run_bass_kernel_spmd(nc, in_maps, core_ids=list(range(8)))
    return gather_outputs(res.results)
```
